# Optimizing a Trainium2 kernel written in Bass

```python
import math
import jax
import jax.numpy as jnp
from jax import lax
import numpy as np

D_MODEL = 1024
BATCH = 8
SEQ = 4096
DEPTH = 1

PLE_DIM = 256
N_ATT_HEADS = 8
ATT_HEAD_DIM = 64
ATT_WIDTH = N_ATT_HEADS * ATT_HEAD_DIM
KV_RANK = 128
N_IDX_HEADS = 8
IDX_HEAD_DIM = 64
TOPK_MAX = 256
Q_BLOCK = 128
D_CONV = 512
CONV_GROUPS = 8
CONV_K = 3
D_FF = 2816
FFN_CONV_K = 3
N_BRANCHES = 2
LN_EPS = 1e-5
DEEPNORM_ALPHA = (2.0 * DEPTH) ** 0.25
DEEPNORM_BETA = (8.0 * DEPTH) ** -0.25
IN_SPLITS = (ATT_WIDTH, KV_RANK, N_IDX_HEADS * IDX_HEAD_DIM, IDX_HEAD_DIM, N_IDX_HEADS,
             D_CONV, D_CONV, D_CONV, D_MODEL, D_MODEL)
IN_WIDTH = sum(IN_SPLITS)

kernel_name = "hybrid_dsa_shortconv_convffn_deepnorm_ple"


def _split_points():
    pts, acc = [], 0
    for w in IN_SPLITS[:-1]:
        acc += w
        pts.append(acc)
    return pts


def layer_norm(x, g, b):
    xf = x.astype(jnp.float32)
    mu = jnp.mean(xf, axis=-1, keepdims=True)
    var = jnp.mean(jnp.square(xf - mu), axis=-1, keepdims=True)
    y = (xf - mu) * lax.rsqrt(var + LN_EPS) * g.astype(jnp.float32) + b.astype(jnp.float32)
    return y.astype(x.dtype)


def rms_norm(x, g):
    xf = x.astype(jnp.float32)
    y = xf * lax.rsqrt(jnp.mean(jnp.square(xf), axis=-1, keepdims=True) + LN_EPS) * g.astype(jnp.float32)
    return y.astype(x.dtype)


def causal_dwconv(u, w, b):
    c = u.shape[-1]
    k = w.shape[0]
    out = lax.conv_general_dilated(
        u, w[:, None, :].astype(u.dtype), window_strides=(1,), padding=[(k - 1, 0)],
        dimension_numbers=('NWC', 'WIO', 'NWC'), feature_group_count=c)
    return out + b.astype(u.dtype)


def dsa_sparse_attention(q_lat, c_kv, q_idx, k_idx, w_idx):
    bsz, seq = c_kv.shape[0], c_kv.shape[1]
    topk = min(TOPK_MAX, seq // 4)
    n_blocks = seq // Q_BLOCK
    key_pos = jnp.arange(seq)
    att_scale = ATT_HEAD_DIM ** -0.5
    idx_scale = IDX_HEAD_DIM ** -0.5

    def to_blocks(a):
        return jnp.swapaxes(a.reshape((bsz, n_blocks, Q_BLOCK) + a.shape[2:]), 0, 1)

    def one_block(args):
        blk, qb, qib, wb = args
        q_pos = blk * Q_BLOCK + jnp.arange(Q_BLOCK)
        causal = key_pos[None, :] <= q_pos[:, None]
        raw = jnp.einsum('bqhd,bsd->bqhs', qib, k_idx).astype(jnp.float32) * idx_scale
        score = jnp.einsum('bqhs,bqh->bqs', jax.nn.relu(raw), wb.astype(jnp.float32))
        score = jnp.where(causal[None], score, -jnp.inf)
        _, sel = lax.top_k(score, topk)
        valid = sel <= q_pos[None, :, None]
        c_sel = jax.vmap(lambda c, i: c[i])(c_kv, sel)
        logits = jnp.einsum('bqhr,bqkr->bqhk', qb, c_sel).astype(jnp.float32) * att_scale
        logits = jnp.where(valid[:, :, None, :], logits, -jnp.inf)
        probs = jax.nn.softmax(logits, axis=-1).astype(c_sel.dtype)
        return jnp.einsum('bqhk,bqkr->bqhr', probs, c_sel)

    out = lax.map(one_block, (jnp.arange(n_blocks), to_blocks(q_lat), to_blocks(q_idx), to_blocks(w_idx)))
    return jnp.swapaxes(out, 0, 1).reshape((bsz, seq) + q_lat.shape[2:])


def setup_inputs(seed: int = 0) -> dict:
    key = jax.random.key(seed)
    ks = iter(jax.random.split(key, 40))
    f32 = jnp.float32

    def nrm(shape, scale):
        return jax.random.normal(next(ks), shape, f32) * scale

    def gain(shape):
        return 1.0 + 0.02 * jax.random.normal(next(ks), shape, f32)

    def bias(shape):
        return 0.02 * jax.random.normal(next(ks), shape, f32)

    L = DEPTH
    beta = DEEPNORM_BETA
    return {
        "x": jax.random.normal(next(ks), (BATCH, SEQ, D_MODEL), f32),
        "p": jax.random.normal(next(ks), (DEPTH, BATCH, SEQ, PLE_DIM), f32),
        "ln_emb_g": gain((D_MODEL,)),
        "ln_emb_b": bias((D_MODEL,)),
        "w_in": nrm((L, D_MODEL, IN_WIDTH), D_MODEL ** -0.5),
        "b_gate": bias((L, N_BRANCHES, D_MODEL)),
        "kv_norm_g": gain((L, KV_RANK)),
        "w_uk": nrm((L, N_ATT_HEADS, KV_RANK, ATT_HEAD_DIM), KV_RANK ** -0.5),
        "w_uv": nrm((L, N_ATT_HEADS, KV_RANK, ATT_HEAD_DIM), KV_RANK ** -0.5 * beta),
        "k_idx_ln_g": gain((L, IDX_HEAD_DIM)),
        "k_idx_ln_b": bias((L, IDX_HEAD_DIM)),
        "mix_conv_w": nrm((L, CONV_K, D_CONV), CONV_K ** -0.5),
        "mix_conv_b": bias((L, D_CONV)),
        "w_br_att": nrm((L, ATT_WIDTH, D_MODEL), ATT_WIDTH ** -0.5 * beta),
        "w_br_conv": nrm((L, D_CONV, D_MODEL), D_CONV ** -0.5 * beta),
        "w_o": nrm((L, D_MODEL, D_MODEL), D_MODEL ** -0.5 * beta),
        "ln1_g": gain((L, D_MODEL)),
        "ln1_b": bias((L, D_MODEL)),
        "w_ffn_up": nrm((L, D_MODEL, 2 * D_FF), D_MODEL ** -0.5),
        "ffn_conv_w": nrm((L, FFN_CONV_K, 2 * D_FF), FFN_CONV_K ** -0.5),
        "ffn_conv_b": bias((L, 2 * D_FF)),
        "w_ffn_down": nrm((L, D_FF, D_MODEL), D_FF ** -0.5 * beta),
        "w_ple_gate": nrm((L, D_MODEL, D_MODEL), D_MODEL ** -0.5),
        "b_ple_gate": bias((L, D_MODEL)),
        "w_ple": nrm((L, PLE_DIM, D_MODEL), PLE_DIM ** -0.5 * beta),
        "ln2_g": gain((L, D_MODEL)),
        "ln2_b": bias((L, D_MODEL)),
    }


def reference(x, p, ln_emb_g, ln_emb_b, w_in, b_gate, kv_norm_g, w_uk, w_uv, k_idx_ln_g, k_idx_ln_b,
              mix_conv_w, mix_conv_b, w_br_att, w_br_conv, w_o, ln1_g, ln1_b, w_ffn_up, ffn_conv_w,
              ffn_conv_b, w_ffn_down, w_ple_gate, b_ple_gate, w_ple, ln2_g, ln2_b):
    bsz, seq = x.shape[0], x.shape[1]
    split_points = _split_points()
    h = layer_norm(x, ln_emb_g, ln_emb_b)
    for i in range(DEPTH):
        proj = h @ w_in[i]
        (q_att, c_kv, q_idx, k_idx, w_idx, cv_b, cv_c, cv_x, g_att, g_conv) = jnp.split(proj, split_points, axis=-1)

        q_att = q_att.reshape(bsz, seq, N_ATT_HEADS, ATT_HEAD_DIM)
        c_kv = rms_norm(c_kv, kv_norm_g[i])
        q_lat = jnp.einsum('bshd,hrd->bshr', q_att, w_uk[i])
        q_idx = q_idx.reshape(bsz, seq, N_IDX_HEADS, IDX_HEAD_DIM)
        k_idx = layer_norm(k_idx, k_idx_ln_g[i], k_idx_ln_b[i])
        w_idx = w_idx * (N_IDX_HEADS ** -0.5)
        o_lat = dsa_sparse_attention(q_lat, c_kv, q_idx, k_idx, w_idx)
        att = jnp.einsum('bshr,hrd->bshd', o_lat, w_uv[i]).reshape(bsz, seq, ATT_WIDTH)

        conv_y = cv_b * causal_dwconv(cv_c * cv_x, mix_conv_w[i], mix_conv_b[i])

        merged = (jax.nn.sigmoid(g_att + b_gate[i, 0]) * (att @ w_br_att[i])
                  + jax.nn.sigmoid(g_conv + b_gate[i, 1]) * (conv_y @ w_br_conv[i]))
        h = layer_norm(DEEPNORM_ALPHA * h + merged @ w_o[i], ln1_g[i], ln1_b[i])

        gu = causal_dwconv(h @ w_ffn_up[i], ffn_conv_w[i], ffn_conv_b[i])
        g_ffn, u_ffn = jnp.split(gu, 2, axis=-1)
        ffn = (jax.nn.silu(g_ffn) * u_ffn) @ w_ffn_down[i]
        ple = jax.nn.sigmoid(h @ w_ple_gate[i] + b_ple_gate[i]) * (p[i] @ w_ple[i])
        h = layer_norm(DEEPNORM_ALPHA * h + ffn + ple, ln2_g[i], ln2_b[i])
    return h
```

```python
import numpy as np
from contextlib import ExitStack
import concourse.bass as bass
import concourse.mybir as mybir
from concourse.bass_utils import run_bass_kernel_spmd

F32 = mybir.dt.float32
BF16 = mybir.dt.bfloat16
ALU = mybir.AluOpType
AF = mybir.ActivationFunctionType
AX = mybir.AxisListType

S = 4096
D = 1024
NCORES = 8
IN_W = 4808
DFF = 2816
LN_EPS = 1e-5
ALPHA = 2.0 ** 0.25
TOPK = 256
NEG = -1.0e30
N_BISECT = 18
C_QATT, C_CKV, C_QIDX, C_KIDX, C_WIDX, C_CVB, C_CVC, C_CVX, C_GATT, C_GCONV = (
    0, 512, 640, 1152, 1216, 1224, 1736, 2248, 2760, 3784)
W1C = 1224
W2C = IN_W - W1C
CW = (64 ** -0.5) * (8 ** -0.5)


class Res:
    __slots__ = ("name", "last_w", "readers")

    def __init__(self, name):
        self.name = name
        self.last_w = None
        self.readers = []


class Op:
    __slots__ = ("eng", "fn", "deps", "signal", "sem", "ticket", "is_dma", "idx", "eidx", "semkey")


class Sched:
    NSEM = 0

    def __init__(self, nc, es):
        self.nc = nc
        self.es = es
        self.ops = []
        self.ecount = {}
        self.engs = {"pe": nc.tensor, "act": nc.scalar, "dve": nc.vector, "pool": nc.gpsimd, "sp": nc.sync}

    def add(self, eng, fn, reads=(), writes=(), dma=False, semkey=None, nodep=False):
        op = Op()
        op.eng = eng
        op.fn = fn
        op.is_dma = dma
        op.signal = False
        op.sem = None
        op.ticket = 0
        op.idx = len(self.ops)
        op.eidx = self.ecount.get(eng, 0)
        self.ecount[eng] = op.eidx + 1
        op.semkey = semkey
        deps = {}
        for r in reads:
            if r.last_w is not None:
                deps[r.last_w.idx] = r.last_w
        for w in writes:
            if w.last_w is not None:
                deps[w.last_w.idx] = w.last_w
            for rd in w.readers:
                deps[rd.idx] = rd
        deps.pop(op.idx, None)
        if nodep:
            deps = {}
        for r in reads:
            r.readers.append(op)
        for w in writes:
            w.last_w = op
            w.readers = []
        keep = []
        for d in deps.values():
            if not d.is_dma and d.eng == eng and not dma:
                if eng == "pe":
                    continue
                if op.eidx - d.eidx > 3:
                    continue
            if d.is_dma and dma and d.eng == eng and False:
                continue
            keep.append(d)
        op.deps = keep
        self.ops.append(op)
        return op

    def emit(self):
        nc = self.nc
        for op in self.ops:
            if op.is_dma:
                op.signal = True
            for d in op.deps:
                d.signal = True
        sems = {}
        counts = {}

        def get_sem(key):
            if key not in sems:
                Sched.NSEM += 1
                sems[key] = self.es.enter_context(nc.semaphore("s%d" % Sched.NSEM))
                counts[key] = 0
            return sems[key]

        for op in self.ops:
            if not op.signal:
                continue
            if op.is_dma:
                key = ("dma", op.semkey if op.semkey is not None else op.idx)
                op.sem = get_sem(key)
                counts[key] += 16
                op.ticket = counts[key]
            else:
                key = ("eng", op.eng)
                op.sem = get_sem(key)
                counts[key] += 1
                op.ticket = counts[key]
        waited = {}
        nwait = 0
        for op in self.ops:
            e = self.engs[op.eng]
            need = {}
            for d in op.deps:
                k = id(d.sem)
                if waited.get((op.eng, k), 0) >= d.ticket:
                    continue
                if k not in need or need[k][1] < d.ticket:
                    need[k] = (d.sem, d.ticket)
            for k, (sem, val) in need.items():
                e.wait_ge(sem, val)
                waited[(op.eng, k)] = val
                nwait += 1
            ins = op.fn()
            if op.signal:
                ins.then_inc(op.sem, 16 if op.is_dma else 1)
        self.nsems = len(sems)
        self.nwait = nwait


def AP(t, off, dims):
    return bass.AP(t, off, [list(d) for d in dims])


class Kern:
    def __init__(self, phases=(1, 2, 3, 4), debug=False):
        self.phases = phases
        self.debug = debug
        self.nc = bass.Bass("TRN2", target_bir_lowering=False)
        self.es = ExitStack()
        self.ges = self.es
        self.semcount = 0
        self.dram = {}
        self.res = {}

    def din(self, name, shape, dt=F32):
        t = self.nc.dram_tensor(name, list(shape), dt, kind="ExternalInput")
        self.dram[name] = t
        return t

    def dscr(self, name, shape, dt):
        kind = "ExternalOutput" if self.debug else "Internal"
        t = self.nc.dram_tensor(name, list(shape), dt, kind=kind)
        self.dram[name] = t
        return t

    def sb(self, name, shape, dt):
        nm = "p%d_%s" % (getattr(self, "nphase", 0), name)
        return self.es.enter_context(self.nc.sbuf_tensor(nm, list(shape), dt))

    def R(self, name):
        if name not in self.res:
            self.res[name] = Res(name)
        return self.res[name]

    def Rs(self, *names):
        return [self.R(n) for n in names]

    def op(self, eng, fn, r=(), w=(), dma=False, semkey=None, nodep=False):
        return self.sc.add(eng, fn, [self.R(x) if isinstance(x, str) else x for x in r],
                           [self.R(x) if isinstance(x, str) else x for x in w], dma=dma, semkey=semkey, nodep=nodep)

    def dma(self, q, out, in_, r=(), w=(), semkey=None, nodep=False):
        e = {"sp": self.nc.sync, "pool": self.nc.gpsimd, "act": self.nc.scalar}[q]
        return self.op(q, lambda: e.dma_start(out=out, in_=in_), r, w, dma=True, semkey=semkey, nodep=nodep)

    def mm(self, out, lhsT, rhs, start, stop, r=(), w=()):
        nc = self.nc
        return self.op("pe", lambda: nc.tensor.matmul(out, lhsT=lhsT, rhs=rhs, start=start, stop=stop), r, w)

    def tr(self, out, in_, ident, r=(), w=()):
        nc = self.nc
        return self.op("pe", lambda: nc.tensor.transpose(out, in_, ident), r, w)

    def act(self, out, in_, func, r=(), w=(), **kw):
        nc = self.nc
        return self.op("act", lambda: nc.scalar.activation(out=out, in_=in_, func=func, **kw), r, w)

    def ts(self, eng, out, in0, s1, s2, op0, op1=None, r=(), w=(), accum_out=None):
        e = self.nc.vector if eng == "dve" else self.nc.gpsimd
        if op1 is None:
            return self.op(eng, lambda: e.tensor_scalar(out=out, in0=in0, scalar1=s1, scalar2=None, op0=op0), r, w)
        if accum_out is not None:
            return self.op(eng, lambda: e.tensor_scalar(out=out, in0=in0, scalar1=s1, scalar2=s2, op0=op0, op1=op1,
                                                        accum_out=accum_out), r, w)
        return self.op(eng, lambda: e.tensor_scalar(out=out, in0=in0, scalar1=s1, scalar2=s2, op0=op0, op1=op1), r, w)

    def tt(self, eng, out, in0, in1, op, r=(), w=()):
        e = self.nc.vector if eng == "dve" else self.nc.gpsimd
        return self.op(eng, lambda: e.tensor_tensor(out=out, in0=in0, in1=in1, op=op), r, w)

    def stt(self, out, in0, scalar, in1, op0, op1, r=(), w=()):
        nc = self.nc
        return self.op("dve", lambda: nc.vector.scalar_tensor_tensor(out=out, in0=in0, scalar=scalar, in1=in1,
                                                                      op0=op0, op1=op1), r, w)

    def cp(self, eng, out, in_, r=(), w=()):
        nc = self.nc
        if eng == "act":
            return self.op("act", lambda: nc.scalar.copy(out=out, in_=in_), r, w)
        e = nc.vector if eng == "dve" else nc.gpsimd
        return self.op(eng, lambda: e.tensor_copy(out=out, in_=in_), r, w)

    def memset(self, eng, ap, val, r=(), w=()):
        e = self.nc.vector if eng == "dve" else self.nc.gpsimd
        return self.op(eng, lambda: e.memset(ap, val), r, w)

    def ln_stats(self, src, tag, r, nchunk, width):
        nc = self.nc
        st = self.lnst[tag]
        stats, mv, rs = st
        resn = "lnst_" + tag
        for c in range(nchunk):
            self.op("dve", (lambda c=c: nc.vector.bn_stats(out=stats[:, 6 * c:6 * c + 6],
                                                           in_=src[:, c * width:(c + 1) * width])), r, [resn])
        self.op("dve", lambda: nc.vector.bn_aggr(out=mv[:, 0:2], in_=stats[:, 0:6 * nchunk]), [resn], [resn])
        self.ts("dve", rs[:, 0:1], mv[:, 1:2], LN_EPS, None, ALU.add, None, [resn], [resn])
        self.act(rs[:, 0:1], rs[:, 0:1], AF.Ln, [resn], [resn])
        self.act(rs[:, 0:1], rs[:, 0:1], AF.Exp, [resn], [resn], scale=-0.5)
        self.ts("dve", rs[:, 1:2], mv[:, 0:1], -1.0, rs[:, 0:1], ALU.mult, ALU.mult, [resn], [resn])
        return rs[:, 0:1], rs[:, 1:2]

    def alloc_lnst(self, tag):
        if not hasattr(self, "lnst"):
            self.lnst = {}
        self.lnst[tag] = (self.sb("lnstats_" + tag, [128, 12], F32), self.sb("lnmv_" + tag, [128, 2], F32),
                          self.sb("lnrs_" + tag, [128, 2], F32))

    def build(self):
        nc = self.nc
        k = self
        x = k.din("x", [S, D])
        p = k.din("p", [S, 256])
        w_in = k.din("w_in", [D, IN_W])
        lng_fm = k.din("lng_fm", [128, 8])
        lnb_fm = k.din("lnb_fm", [128, 8])
        lng = k.din("lng", [1, D])
        lnb = k.din("lnb", [1, D])
        bgate = k.din("bgate", [128, 16])
        kvg = k.din("kvg", [1, 128])
        wukT = k.din("wukT", [128, 4, 128])
        wuvr = k.din("wuvr", [128, 512])
        kig = k.din("kig", [1, 64])
        kib = k.din("kib", [1, 64])
        mcw = k.din("mcw", [128, 4, 3])
        mcb = k.din("mcb", [128, 4])
        wbra = k.din("wbra", [512, D])
        wbrc = k.din("wbrc", [512, D])
        wo = k.din("wo", [D, D])
        ln1g = k.din("ln1g", [1, D])
        ln1b = k.din("ln1b", [1, D])
        wup = k.din("wup", [D, 2 * DFF])
        fcw = k.din("fcw", [128, 44, 3])
        fcb = k.din("fcb", [128, 44])
        wdn = k.din("wdn", [DFF, D])
        wpg = k.din("wpg", [D, D])
        bpg = k.din("bpg", [1, D])
        wple = k.din("wple", [256, D])
        ln2g = k.din("ln2g", [1, D])
        ln2b = k.din("ln2b", [1, D])
        out = nc.dram_tensor("out", [S, D], F32, kind="ExternalOutput")
        k.dram["out"] = out
        attT_d = k.dscr("attT_d", [512, S], BF16)
        mrgT_d = k.dscr("mrgT_d", [D, S], BF16)
        r_d = k.dscr("r_d", [S, D], F32)
        h1T_d = k.dscr("h1T_d", [D, S], BF16)

        ps = k.es.enter_context(nc.psum_tensor("ps", [128, 4096], F32))
        k.ps = ps

        def bank(b, n=512, off=0, parts=128):
            return ps[0:parts, b * 512 + off: b * 512 + off + n]

        def bank_bf(b):
            return ps[:, b * 512:(b + 1) * 512].bitcast(BF16)

        k.bank = bank
        k.bank_bf = bank_bf

        k.bar_tile = k.sb("bar_tile", [128, 8], F32)
        k.bar_bf = k.sb("bar_bf", [128, 8], BF16)
        ident = k.sb("ident", [128, 128], BF16)
        k.ident = ident
        k.nphase = 0
        with ExitStack() as pes:
            k.begin_phase(pes)
            k.memset("pool", ident[:], 0.0, [], ["ident"])
            k.op("pool", lambda: nc.gpsimd.affine_select(out=ident[:], in_=ident[:], pattern=[[-1, 128]],
                                                          compare_op=ALU.not_equal, fill=1.0, base=0,
                                                          channel_multiplier=1), ["ident"], ["ident"])
            k.memset("pool", k.bar_bf[:], 0.0, [], ["bar_bf"])
            k.end_phase()
        if 1 in k.phases:
            with ExitStack() as pes:
                k.begin_phase(pes)
                k.phase1(x, w_in, lng_fm, lnb_fm, kvg, wukT, wuvr, kig, kib, attT_d)
                k.end_phase()
        if 2 in k.phases:
            with ExitStack() as pes:
                k.begin_phase(pes)
                k.phase2(x, w_in, lng_fm, lnb_fm, bgate, mcw, mcb, wbra, wbrc, attT_d, mrgT_d)
                k.end_phase()
        if 3 in k.phases:
            with ExitStack() as pes:
                k.begin_phase(pes)
                k.phase3(x, p, lng, lnb, wo, ln1g, ln1b, wpg, bpg, wple, mrgT_d, r_d, h1T_d)
                k.end_phase()
        if 4 in k.phases:
            with ExitStack() as pes:
                k.begin_phase(pes)
                k.phase4(wup, fcw, fcb, wdn, ln2g, ln2b, r_d, h1T_d, out)
                k.end_phase()
        return nc

    def begin_phase(self, pes):
        self.es = pes
        self.sc = Sched(self.nc, self.ges)
        self.res = {}
        self.lnst = {}

    def end_phase(self):
        nc = self.nc
        k = self
        self.sc.emit()
        bar = self.ges.enter_context(nc.semaphore("bar%d" % self.nphase))
        self.nphase += 1
        nc.vector.memset(k.bar_tile[:, 0:1], 0.0).then_inc(bar, 1)
        nc.gpsimd.memset(k.bar_tile[:, 1:2], 0.0).then_inc(bar, 1)
        nc.scalar.copy(out=k.bar_tile[:, 2:3], in_=k.bar_tile[:, 3:4]).then_inc(bar, 1)
        nc.tensor.matmul(k.ps[0:8, 0:8], lhsT=k.bar_bf[:, 0:8], rhs=k.bar_bf[:, 0:8], start=True, stop=True).then_inc(bar, 1)
        nc.sync.nop().then_inc(bar, 1)
        for e in (nc.vector, nc.gpsimd, nc.scalar, nc.tensor, nc.sync):
            e.wait_ge(bar, 5)

    def phase1(self, x, w_in, lng_fm, lnb_fm, kvg, wukT, wuvr, kig, kib, attT_d):
        k = self
        nc = self.nc
        bank, bank_bf, ident, ps = k.bank, k.bank_bf, k.ident, k.ps
        w1 = k.sb("w1", [128, 8, W1C], BF16)
        g_fm = k.sb("g_fm", [128, 8], F32)
        b_fm = k.sb("b_fm", [128, 8], F32)
        wuk_sb = k.sb("wuk_sb", [128, 4, 128], BF16)
        wuv_sb = k.sb("wuv_sb", [128, 512], BF16)
        kvg_bc = k.sb("kvg_bc", [128, 128], F32)
        kig_bc = k.sb("kig_bc", [128, 64], F32)
        kib_bc = k.sb("kib_bc", [128, 64], F32)
        negm = k.sb("negm", [128, 128], F32)
        pow2 = k.sb("pow2", [128, N_BISECT], F32)
        kT2 = k.sb("kT2", [128, S], BF16)
        ckvT = k.sb("ckvT", [128, S], BF16)
        vext = k.sb("vext", [128, 32, 8, 65], BF16)
        xbuf = [k.sb("xbuf%d" % i, [128, D], F32) for i in range(2)]
        xn_bf = k.sb("xn_bf", [128, D], BF16)
        hT = k.sb("hT", [128, 8, 512], BF16)
        qattT = k.sb("qattT", [128, 4, 512], BF16)
        qlatT = k.sb("qlatT", [128, 8, 512], BF16)
        qidxT = k.sb("qidxT", [128, 4, 512], BF16)
        absw4 = k.sb("absw4", [128, 4, 8], F32)
        sgn4 = k.sb("sgn4", [128, 4, 8], F32)
        dsgn = k.sb("dsgn", [128, 8, 128], BF16)
        relu_sb = [k.sb("relu%d" % i, [128, 512], BF16) for i in range(3)]
        score = k.sb("score", [128, S], F32)
        junk = k.sb("junk", [128, S], BF16)
        mask01 = k.sb("mask01", [128, S], BF16)
        maskT = k.sb("maskT", [128, 32, 128], BF16)
        PT = [k.sb("PT%d" % i, [128, 4, 128], BF16) for i in range(4)]
        ckv_tm = k.sb("ckv_tm", [128, 128], BF16)
        craw = k.sb("craw", [128, 128], F32)
        craw2 = k.sb("craw2", [128, 128], F32)
        kn_f = k.sb("kn_f", [128, 64], F32)
        kn2 = k.sb("kn2", [128, 128], BF16)
        sm = k.sb("sm", [128, 16], F32)
        wks = k.sb("wks", [128, N_BISECT], F32)
        bis = k.sb("bis", [128, 8], F32)
        pv_sb = k.sb("pv_sb", [65, 1024], F32)
        rden = k.sb("rden", [65, 1024], F32)
        ones_r = k.sb("ones_r", [65, 64], F32)
        att_n = [k.sb("att_n%d" % i, [64, 8, 128], BF16) for i in range(2)]
        k.alloc_lnst("x")
        k.alloc_lnst("k")

        for kk in range(8):
            k.dma("pool", w1[:, kk, :], w_in[kk * 128:(kk + 1) * 128, 0:W1C], [], ["w1"], semkey="w1", nodep=True)
        W1R = ["w1"]
        k.dma("sp", g_fm[:], lng_fm[:], [], ["g_fm"])
        k.dma("sp", b_fm[:], lnb_fm[:], [], ["b_fm"])
        k.dma("pool", wuk_sb[:], wukT[:], [], ["wuk"])
        k.dma("pool", wuv_sb[:], wuvr[:], [], ["wuv"])
        k.dma("sp", kvg_bc[:], AP(kvg, 0, [[0, 128], [1, 128]]), [], ["kvg_bc"])
        k.dma("sp", kig_bc[:], AP(kig, 0, [[0, 128], [1, 64]]), [], ["kig_bc"])
        k.dma("sp", kib_bc[:], AP(kib, 0, [[0, 128], [1, 64]]), [], ["kib_bc"])
        k.memset("pool", negm[:], 0.0, [], ["negm"])
        k.op("pool", lambda: nc.gpsimd.affine_select(out=negm[:], in_=negm[:], pattern=[[-1, 128]],
                                                      compare_op=ALU.is_ge, fill=NEG, base=0,
                                                      channel_multiplier=1), ["negm"], ["negm"])
        for i in range(N_BISECT):
            k.memset("pool", pow2[:, i:i + 1], 2.0 ** (-(i + 1)), [], ["pow2"])
        k.memset("pool", vext[:, :, :, 64:65], 1.0, [], ["vext_ones"])
        k.memset("pool", ones_r[:], 1.0, [], ["ones_r"])

        BT, BR0, BR1, BSC, BL0, BL1, BV0, BV1 = 0, 1, 2, 3, 4, 5, 6, 7
        ring = [BR0, BR1]
        rstate = {"i": 0, "relu": 0, "pt": 0, "lg": 0, "xb": 0, "an": 0}

        def next_ring():
            b = ring[rstate["i"] % 2]
            rstate["i"] += 1
            return b

        for st in range(8):
            T0 = st * 512
            for tt in range(4):
                t0 = T0 + tt * 128
                xb_i = rstate["xb"] % 2
                rstate["xb"] += 1
                xb = xbuf[xb_i]
                XR = "xbuf%d" % xb_i
                k.dma("sp", xb[:], x[t0:t0 + 128, :], [], [XR], semkey=XR)
                rstd, nmr = k.ln_stats(xb, "x", [XR], 2, 512)
                k.act(xn_bf[:], xb[:], AF.Identity, ["lnst_x", XR], ["xn_bf"], scale=rstd, bias=nmr)
                tb = bank_bf(BT)
                for kk in range(8):
                    k.tr(tb[:, kk * 128:(kk + 1) * 128], xn_bf[:, kk * 128:(kk + 1) * 128], ident[:],
                         ["xn_bf", "ident"], ["bank0", "bank0b"])
                for kk in range(8):
                    k.act(hT[:, kk, tt * 128:(tt + 1) * 128], tb[:, kk * 128:(kk + 1) * 128], AF.Identity,
                          ["bank0", "bank0b", "g_fm", "b_fm"], ["hT"], scale=g_fm[:, kk:kk + 1], bias=b_fm[:, kk:kk + 1])
            for j in range(4):
                b = next_ring()
                for kk in range(8):
                    k.mm(bank(b), w1[:, kk, C_QATT + 128 * j:C_QATT + 128 * (j + 1)], hT[:, kk, :], kk == 0, kk == 7,
                         ["hT"] + W1R, ["bank%d" % b])
                k.cp("dve", qattT[:, j, :], bank(b), ["bank%d" % b], ["qattT"])
            for j in range(4):
                b = next_ring()
                for kk in range(8):
                    k.mm(bank(b), w1[:, kk, C_QIDX + 128 * j:C_QIDX + 128 * (j + 1)], hT[:, kk, :], kk == 0, kk == 7,
                         ["hT"] + W1R, ["bank%d" % b])
                k.cp("act", qidxT[:, j, :], bank(b), ["bank%d" % b], ["qidxT"])
            for tt in range(4):
                blk = st * 4 + tt
                pck = bank(BT, 128, 0)
                pkw = bank(BT, 72, 128)
                for kk in range(8):
                    k.mm(pck, hT[:, kk, tt * 128:(tt + 1) * 128], w1[:, kk, C_CKV:C_CKV + 128], kk == 0, kk == 7,
                         ["hT"] + W1R, ["bank0"])
                for kk in range(8):
                    k.mm(pkw, hT[:, kk, tt * 128:(tt + 1) * 128], w1[:, kk, C_KIDX:C_KIDX + 72], kk == 0, kk == 7,
                         ["hT"] + W1R, ["bank0"])
                k.cp("act", craw[:], pck, ["bank0"], ["craw"])
                k.op("dve", lambda: nc.vector.scalar_tensor_tensor(out=craw2[:], in0=craw[:], scalar=1.0, in1=craw[:],
                                                                    op0=ALU.mult, op1=ALU.mult, accum_out=sm[:, 0:1]),
                     ["craw"], ["craw2", "sm_c"])
                k.ts("dve", sm[:, 1:2], sm[:, 0:1], 1.0 / 128.0, LN_EPS, ALU.mult, ALU.add, ["sm_c"], ["sm_c"])
                k.act(sm[:, 2:3], sm[:, 1:2], AF.Ln, ["sm_c"], ["sm_c"])
                k.act(sm[:, 2:3], sm[:, 2:3], AF.Exp, ["sm_c"], ["sm_c"], scale=-0.5)
                k.stt(ckv_tm[:], craw[:], sm[:, 2:3], kvg_bc[:], ALU.mult, ALU.mult, ["craw", "sm_c", "kvg_bc"], ["ckv_tm"])
                rstd_k, nmr_k = k.ln_stats(pkw, "k", ["bank0"], 1, 64)
                k.act(kn_f[:], pkw[:, 0:64], AF.Identity, ["bank0", "lnst_k"], ["kn_f"], scale=rstd_k, bias=nmr_k)
                k.tt("pool", kn_f[:], kn_f[:], kig_bc[:], ALU.mult, ["kn_f", "kig_bc"], ["kn_f"])
                k.tt("pool", kn2[:, 0:64], kn_f[:], kib_bc[:], ALU.add, ["kn_f", "kib_bc"], ["kn2"])
                k.tt("pool", kn2[:, 64:128], kn_f[:], kib_bc[:], ALU.add, ["kn_f", "kib_bc"], ["kn2"])
                k.act(sm[:, 8:16], pkw[:, 64:72], AF.Copy, ["bank0"], ["sm_w"], scale=CW)
                k.stt(absw4[:, tt, :], sm[:, 8:16], -1.0, sm[:, 8:16], ALU.mult, ALU.max, ["sm_w"], ["absw4"])
                k.act(sgn4[:, tt, :], pkw[:, 64:72], AF.Sign, ["bank0"], ["sgn4"])
                tb = bank_bf(BT)
                k.tr(tb[:, 512:640], ckv_tm[:], ident[:], ["ckv_tm", "ident"], ["bank0b"])
                k.tr(tb[:, 640:768], kn2[:], ident[:], ["kn2", "ident"], ["bank0b"])
                k.cp("act", ckvT[:, blk * 128:(blk + 1) * 128], tb[:, 512:640], ["bank0b"], ["ckvT"])
                k.cp("act", kT2[:, blk * 128:(blk + 1) * 128], tb[:, 640:768], ["bank0b"], ["kT2"])
                b = next_ring()
                k.mm(bank(b), ckvT[:, blk * 128:(blk + 1) * 128], wuv_sb[:], True, True, ["ckvT", "wuv"], ["bank%d" % b])
                k.cp("dve", vext[:, blk, :, 0:64], bank(b).rearrange("p (h d) -> p h d", h=8), ["bank%d" % b], ["vext"])
            for h in range(8):
                e, j = h % 2, h // 2
                b = next_ring()
                k.mm(bank(b), wuk_sb[64 * e:64 * e + 64, j, :], qattT[64 * e:64 * e + 64, j, :], True, True,
                     ["qattT", "wuk"], ["bank%d" % b])
                k.act(qlatT[:, h, :], bank(b), AF.Copy, ["bank%d" % b], ["qlatT"], scale=0.125)
            for i in range(4):
                I = st * 4 + i
                nk = 128 * (I + 1)
                q0 = i * 128
                for h in range(8):
                    k.ts("dve", dsgn[:, h, :], ident[:], sgn4[:, i, h:h + 1], None, ALU.mult, None,
                         ["ident", "sgn4"], ["dsgn"])
                nkb = (nk + 511) // 512
                for kb in range(nkb):
                    wk = min(512, nk - 512 * kb)
                    for h in range(8):
                        e, j = h % 2, h // 2
                        b = next_ring()
                        k.mm(bank(b, wk), qidxT[64 * e:64 * e + 64, j, q0:q0 + 128],
                             kT2[64 * e:64 * e + 64, 512 * kb:512 * kb + wk], True, True,
                             ["qidxT", "kT2"], ["bank%d" % b])
                        ri = rstate["relu"] % 3
                        rstate["relu"] += 1
                        k.act(relu_sb[ri][:, 0:wk], bank(b, wk), AF.Relu, ["bank%d" % b, "absw4"], ["relu%d" % ri],
                              scale=absw4[:, i, h:h + 1])
                        k.mm(bank(BSC, wk), dsgn[:, h, :], relu_sb[ri][:, 0:wk], h == 0, h == 7,
                             ["dsgn", "relu%d" % ri], ["bank%d" % BSC])
                    last = (kb == nkb - 1)
                    ncopy = wk - 128 if last else wk
                    if ncopy > 0:
                        k.cp("act", score[:, 512 * kb:512 * kb + ncopy], bank(BSC, ncopy), ["bank%d" % BSC], ["score"])
                    if last:
                        k.tt("dve", score[:, nk - 128:nk], bank(BSC, 128, wk - 128), negm[:], ALU.add,
                             ["bank%d" % BSC, "negm"], ["score"])
                if I >= 2:
                    k.op("dve", lambda nk=nk: nc.vector.tensor_reduce(out=bis[:, 0:1], in_=score[:, 0:nk], axis=AX.X,
                                                                      op=ALU.max), ["score"], ["bis"])
                    k.op("dve", lambda: nc.vector.tensor_reduce(out=bis[:, 1:2], in_=score[:, 0:256], axis=AX.X,
                                                                op=ALU.min), ["score"], ["bis"])
                    k.tt("dve", bis[:, 2:3], bis[:, 0:1], bis[:, 1:2], ALU.subtract, ["bis"], ["bis"])
                    k.ts("dve", bis[:, 2:3], bis[:, 2:3], 1.001, 1e-6, ALU.mult, ALU.add, ["bis"], ["bis"])
                    k.ts("dve", wks[:], pow2[:], bis[:, 2:3], None, ALU.mult, None, ["bis", "pow2"], ["wks"])
                    for it in range(N_BISECT):
                        k.tt("dve", bis[:, 3:4], bis[:, 1:2], wks[:, it:it + 1], ALU.add, ["bis", "wks"], ["bis"])
                        k.ts("dve", junk[:, 0:nk], score[:, 0:nk], bis[:, 3:4], 0.0, ALU.is_ge, ALU.add,
                             ["score", "bis"], ["junk", "bis"], accum_out=bis[:, 4:5])
                        k.ts("dve", bis[:, 5:6], bis[:, 4:5], float(TOPK) - 0.5, wks[:, it:it + 1], ALU.is_ge, ALU.mult,
                             ["bis", "wks"], ["bis"])
                        k.tt("dve", bis[:, 1:2], bis[:, 1:2], bis[:, 5:6], ALU.add, ["bis"], ["bis"])
                    k.ts("dve", mask01[:, 0:nk], score[:, 0:nk], bis[:, 1:2], None, ALU.is_ge, None,
                         ["score", "bis"], ["mask01"])
                else:
                    k.ts("dve", mask01[:, 0:nk], score[:, 0:nk], -1.0e29, None, ALU.is_ge, None, ["score"], ["mask01"])
                tb = bank_bf(BT)
                for g0 in range(0, I + 1, 8):
                    g1 = min(I + 1, g0 + 8)
                    for jb in range(g0, g1):
                        k.tr(tb[:, (jb - g0) * 128:(jb - g0 + 1) * 128], mask01[:, jb * 128:(jb + 1) * 128], ident[:],
                             ["mask01", "ident"], ["bank0", "bank0b"])
                    k.cp("act", maskT[:, g0:g1, :], tb[:, 0:(g1 - g0) * 128].rearrange("p (a b) -> p a b", b=128),
                         ["bank0", "bank0b"], ["maskT"])
                pv = ps[0:65, BV0 * 512:BV0 * 512 + 1024].rearrange("p (h q) -> p h q", h=8)
                k.op("dve", lambda: nc.vector.memset(ps[0:65, BV0 * 512:BV0 * 512 + 1024], 0.0), [], ["bankpv"])
                for jb in range(I + 1):
                    for g in range(2):
                        lb = [BL0, BL1][rstate["lg"] % 2]
                        rstate["lg"] += 1
                        k.mm(bank(lb), ckvT[:, jb * 128:(jb + 1) * 128], qlatT[:, 4 * g:4 * g + 4, q0:q0 + 128],
                             True, True, ["ckvT", "qlatT"], ["bank%d" % lb])
                        pi = rstate["pt"] % 4
                        rstate["pt"] += 1
                        k.act(PT[pi][:], bank(lb).rearrange("p (h q) -> p h q", h=4), AF.Exp, ["bank%d" % lb],
                              ["PT%d" % pi])
                        k.tt("pool", PT[pi][:], PT[pi][:], AP(maskT, jb * 128, [[32 * 128, 128], [0, 4], [1, 128]]),
                             ALU.mult, ["PT%d" % pi, "maskT"], ["PT%d" % pi])
                        for hh in range(4):
                            h = 4 * g + hh
                            k.op("pe", (lambda o=pv[:, h, :], l=vext[:, jb, h, :], rr=PT[pi][:, hh, :], sp_=(jb == I):
                                        nc.tensor.matmul(o, lhsT=l, rhs=rr, start=False, stop=sp_,
                                                         skip_group_check=True)),
                                 ["vext", "vext_ones", "PT%d" % pi], ["bankpv"])
                k.cp("act", pv_sb[:], ps[0:65, BV0 * 512:BV0 * 512 + 1024], ["bankpv"], ["pv_sb"])
                k.act(rden[64:65, :], pv_sb[64:65, :], AF.Ln, ["pv_sb"], ["rden"])
                k.act(rden[64:65, :], rden[64:65, :], AF.Exp, ["rden"], ["rden"], scale=-1.0)
                ai = rstate["an"] % 2
                rstate["an"] += 1
                for g in range(2):
                    lb = [BL0, BL1][rstate["lg"] % 2]
                    rstate["lg"] += 1
                    k.mm(bank(lb, 512, 0, 64), ones_r[64:65, :], rden[64:65, g * 512:(g + 1) * 512], True, True,
                         ["ones_r", "rden"], ["bank%d" % lb])
                    k.tt("dve", att_n[ai][:, 4 * g:4 * g + 4, :],
                         pv_sb[0:64, g * 512:(g + 1) * 512].rearrange("p (h q) -> p h q", h=4),
                         bank(lb, 512, 0, 64).rearrange("p (h q) -> p h q", h=4), ALU.mult,
                         ["pv_sb", "bank%d" % lb], ["att_n%d" % ai])
                tok0 = T0 + q0
                k.dma("sp", AP(attT_d, tok0, [[S, 64], [64 * S, 8], [1, 128]]), att_n[ai][:],
                      ["att_n%d" % ai], ["attT_d"], semkey="att_n%d" % ai)
        k.final_wait(["att_n0", "att_n1"])

    def ln_hT(self, x, t0, tt, xbuf, rstate, xn_bf, hT, g_fm, b_fm, BT=0):
        k = self
        nc = self.nc
        xb_i = rstate["xb"] % 2
        rstate["xb"] += 1
        xb = xbuf[xb_i]
        XR = "xbuf%d" % xb_i
        k.dma("sp", xb[:], x[t0:t0 + 128, :], [], [XR], semkey=XR)
        rstd, nmr = k.ln_stats(xb, "x", [XR], 2, 512)
        k.act(xn_bf[:], xb[:], AF.Identity, ["lnst_x", XR], ["xn_bf"], scale=rstd, bias=nmr)
        tb = k.bank_bf(BT)
        for kk in range(8):
            k.tr(tb[:, kk * 128:(kk + 1) * 128], xn_bf[:, kk * 128:(kk + 1) * 128], k.ident[:],
                 ["xn_bf", "ident"], ["bank0", "bank0b"])
        for kk in range(8):
            k.act(hT[:, kk, tt * 128:(tt + 1) * 128], tb[:, kk * 128:(kk + 1) * 128], AF.Identity,
                  ["bank0", "bank0b", "g_fm", "b_fm"], ["hT"], scale=g_fm[:, kk:kk + 1], bias=b_fm[:, kk:kk + 1])

    def phase2(self, x, w_in, lng_fm, lnb_fm, bgate, mcw, mcb, wbra, wbrc, attT_d, mrgT_d):
        k = self
        nc = self.nc
        bank, bank_bf, ident, ps = k.bank, k.bank_bf, k.ident, k.ps
        w2 = k.sb("w2", [128, 8, W2C], BF16)
        wa = k.sb("wa", [128, 4, D], BF16)
        wc = k.sb("wc", [128, 4, D], BF16)
        g_fm = k.sb("g_fm", [128, 8], F32)
        b_fm = k.sb("b_fm", [128, 8], F32)
        hb = k.sb("hb", [128, 16], F32)
        mcw_sb = k.sb("mcw_sb", [128, 4, 3], F32)
        mcb_sb = k.sb("mcb_sb", [128, 4], F32)
        xbuf = [k.sb("xbuf%d" % i, [128, D], F32) for i in range(2)]
        xn_bf = k.sb("xn_bf", [128, D], BF16)
        hT = k.sb("hT", [128, 8, 512], BF16)
        att_in = k.sb("att_in", [128, 4, 512], BF16)
        u = k.sb("u", [128, 4, 514], F32)
        tmpc = k.sb("tmpc", [128, 512], F32)
        a_sb = k.sb("a_sb", [128, 512], F32)
        cyT = k.sb("cyT", [128, 4, 512], BF16)
        ta = k.sb("ta", [128, 512], F32)
        tc2 = k.sb("tc2", [128, 512], F32)
        m1 = k.sb("m1", [128, 512], F32)
        m2 = k.sb("m2", [128, 512], F32)
        mrg = [k.sb("mrg%d" % i, [128, 8, 512], BF16) for i in range(2)]
        k.alloc_lnst("x")
        for kk in range(8):
            k.dma("pool", w2[:, kk, :], w_in[kk * 128:(kk + 1) * 128, W1C:IN_W], [], ["w2"], semkey="w2", nodep=True)
        W2R = ["w2"]
        k.dma("pool", wa[:], wbra.rearrange("(k p) f -> p k f", p=128), [], ["wa"])
        k.dma("pool", wc[:], wbrc.rearrange("(k p) f -> p k f", p=128), [], ["wc"])
        k.dma("sp", g_fm[:], lng_fm[:], [], ["g_fm"])
        k.dma("sp", b_fm[:], lnb_fm[:], [], ["b_fm"])
        k.dma("sp", hb[:], bgate[:], [], ["hb"])
        k.dma("sp", mcw_sb[:], mcw[:], [], ["mcw"])
        k.dma("sp", mcb_sb[:], mcb[:], [], ["mcb"])
        k.ts("dve", hb[:], hb[:], 0.5, None, ALU.mult, None, ["hb"], ["hb"])
        k.memset("pool", u[:], 0.0, [], ["u"])
        rstate = {"xb": 0, "ring": 0, "mrg": 0}
        ringb = [1, 2, 3, 4, 5, 6, 7]

        def nb():
            b = ringb[rstate["ring"] % 7]
            rstate["ring"] += 1
            return b

        def proj(col0, b):
            c0 = col0 - W1C
            for kk in range(8):
                k.mm(bank(b), w2[:, kk, c0:c0 + 128], hT[:, kk, :], kk == 0, kk == 7, ["hT"] + W2R, ["bank%d" % b])

        for st in range(8):
            T0 = st * 512
            for tt in range(4):
                k.ln_hT(x, T0 + tt * 128, tt, xbuf, rstate, xn_bf, hT, g_fm, b_fm)
            k.dma("sp", att_in[:], AP(attT_d, T0, [[S, 128], [128 * S, 4], [1, 512]]), [], ["att_in"], semkey="att_in")
            for j in range(4):
                bc_, bx_, bb_ = nb(), nb(), nb()
                proj(C_CVC + 128 * j, bc_)
                proj(C_CVX + 128 * j, bx_)
                proj(C_CVB + 128 * j, bb_)
                if st > 0:
                    k.cp("pool", u[:, j, 0:2], u[:, j, 512:514], ["u"], ["u"])
                k.cp("act", tmpc[:], bank(bc_), ["bank%d" % bc_], ["tmpc"])
                k.tt("dve", u[:, j, 2:514], tmpc[:], bank(bx_), ALU.mult, ["tmpc", "bank%d" % bx_], ["u"])
                k.act(a_sb[:], u[:, j, 2:514], AF.Identity, ["u", "mcw", "mcb"], ["a_sb"],
                      scale=mcw_sb[:, j, 2:3], bias=mcb_sb[:, j:j + 1])
                k.stt(a_sb[:], u[:, j, 1:513], mcw_sb[:, j, 1:2], a_sb[:], ALU.mult, ALU.add, ["u", "a_sb", "mcw"], ["a_sb"])
                k.stt(a_sb[:], u[:, j, 0:512], mcw_sb[:, j, 0:1], a_sb[:], ALU.mult, ALU.add, ["u", "a_sb", "mcw"], ["a_sb"])
                k.tt("dve", cyT[:, j, :], a_sb[:], bank(bb_), ALU.mult, ["a_sb", "bank%d" % bb_], ["cyT"])
            mi = rstate["mrg"] % 2
            rstate["mrg"] += 1
            for c in range(8):
                bga, bgc, bra, brc = nb(), nb(), nb(), nb()
                proj(C_GATT + 128 * c, bga)
                proj(C_GCONV + 128 * c, bgc)
                for kk in range(4):
                    k.mm(bank(bra), wa[:, kk, 128 * c:128 * (c + 1)], att_in[:, kk, :], kk == 0, kk == 3,
                         ["wa", "att_in"], ["bank%d" % bra])
                for kk in range(4):
                    k.mm(bank(brc), wc[:, kk, 128 * c:128 * (c + 1)], cyT[:, kk, :], kk == 0, kk == 3,
                         ["wc", "cyT"], ["bank%d" % brc])
                k.act(ta[:], bank(bga), AF.Tanh, ["bank%d" % bga, "hb"], ["ta"], scale=0.5, bias=hb[:, c:c + 1])
                k.act(tc2[:], bank(bgc), AF.Tanh, ["bank%d" % bgc, "hb"], ["tc2"], scale=0.5, bias=hb[:, 8 + c:9 + c])
                k.stt(m1[:], ta[:], 1.0, bank(bra), ALU.add, ALU.mult, ["ta", "bank%d" % bra], ["m1"])
                k.stt(m2[:], tc2[:], 1.0, bank(brc), ALU.add, ALU.mult, ["tc2", "bank%d" % brc], ["m2"])
                k.tt("pool", mrg[mi][:, c, :], m1[:], m2[:], ALU.add, ["m1", "m2"], ["mrg%d" % mi])
            k.dma("sp", AP(mrgT_d, T0, [[S, 128], [128 * S, 8], [1, 512]]), mrg[mi][:], ["mrg%d" % mi], ["mrgT_d"],
                  semkey="mrg%d" % mi)
        k.final_wait(["mrg0", "mrg1"])

    def phase3(self, x, p, lng, lnb, wo, ln1g, ln1b, wpg, bpg, wple, mrgT_d, r_d, h1T_d):
        k = self
        nc = self.nc
        bank, bank_bf, ident, ps = k.bank, k.bank_bf, k.ident, k.ps
        wo_sb = k.sb("wo_sb", [128, 8, D], BF16)
        wpg_sb = k.sb("wpg_sb", [128, 8, D], BF16)
        wpl_sb = k.sb("wpl_sb", [128, 2, D], BF16)
        Ga = k.sb("Ga", [128, D], F32)
        Ba = k.sb("Ba", [128, D], F32)
        G1 = k.sb("G1", [128, D], F32)
        B1 = k.sb("B1", [128, D], F32)
        HB = k.sb("HB", [128, D], F32)
        xbuf = [k.sb("xbuf%d" % i, [128, D], F32) for i in range(2)]
        pbuf = [k.sb("pbuf%d" % i, [128, 256], F32) for i in range(2)]
        m_in = [k.sb("m_in%d" % i, [128, 8, 128], BF16) for i in range(2)]
        hA = k.sb("hA", [128, D], F32)
        y = k.sb("y", [128, D], F32)
        h1 = k.sb("h1", [128, D], F32)
        h1_bf = k.sb("h1_bf", [128, D], BF16)
        h1T = [k.sb("h1T%d" % i, [128, 8, 128], BF16) for i in range(2)]
        p_bf = k.sb("p_bf", [128, 256], BF16)
        pT = k.sb("pT", [128, 2, 128], BF16)
        tg = k.sb("tg", [128, D], F32)
        pl2 = k.sb("pl2", [128, D], F32)
        r2 = [k.sb("r2_%d" % i, [128, D], F32) for i in range(2)]
        k.alloc_lnst("x")
        k.alloc_lnst("y")
        k.dma("pool", wo_sb[:], wo.rearrange("(k p) f -> p k f", p=128), [], ["wo"])
        k.dma("pool", wpg_sb[:], wpg.rearrange("(k p) f -> p k f", p=128), [], ["wpg"])
        k.dma("pool", wpl_sb[:], wple.rearrange("(k p) f -> p k f", p=128), [], ["wpl"])
        for t_, src, nm in ((Ga, lng, "Ga"), (Ba, lnb, "Ba"), (G1, ln1g, "G1"), (B1, ln1b, "B1"), (HB, bpg, "HB")):
            k.dma("sp", t_[:], AP(src, 0, [[0, 128], [1, D]]), [], [nm])
        k.ts("dve", Ga[:], Ga[:], ALPHA, None, ALU.mult, None, ["Ga"], ["Ga"])
        k.ts("dve", Ba[:], Ba[:], ALPHA, None, ALU.mult, None, ["Ba"], ["Ba"])
        k.ts("dve", HB[:], HB[:], 0.5, None, ALU.mult, None, ["HB"], ["HB"])
        for t in range(32):
            t0 = t * 128
            bi = t % 2
            XR, PR, MR = "xbuf%d" % bi, "pbuf%d" % bi, "m_in%d" % bi
            k.dma("sp", xbuf[bi][:], x[t0:t0 + 128, :], [], [XR], semkey=XR)
            k.dma("sp", pbuf[bi][:], p[t0:t0 + 128, :], [], [PR], semkey=PR)
            k.dma("sp", m_in[bi][:], AP(mrgT_d, t0, [[S, 128], [128 * S, 8], [1, 128]]), [], [MR], semkey=MR)
            rstd, nmr = k.ln_stats(xbuf[bi], "x", [XR], 2, 512)
            k.act(hA[:], xbuf[bi][:], AF.Identity, ["lnst_x", XR], ["hA"], scale=rstd, bias=nmr)
            k.tt("pool", hA[:], hA[:], Ga[:], ALU.mult, ["hA", "Ga"], ["hA"])
            k.tt("pool", hA[:], hA[:], Ba[:], ALU.add, ["hA", "Ba"], ["hA"])
            for n in range(2):
                b = 1 + n
                for kk in range(8):
                    k.mm(bank(b), m_in[bi][:, kk, :], wo_sb[:, kk, 512 * n:512 * (n + 1)], kk == 0, kk == 7,
                         [MR, "wo"], ["bank%d" % b])
                k.stt(y[:, 512 * n:512 * (n + 1)], bank(b), 0.5, hA[:, 512 * n:512 * (n + 1)], ALU.mult, ALU.add,
                      ["bank%d" % b, "hA"], ["y"])
            rstd1, nmr1 = k.ln_stats(y, "y", ["y"], 2, 512)
            k.act(h1[:], y[:], AF.Identity, ["lnst_y", "y"], ["h1"], scale=rstd1, bias=nmr1)
            k.tt("pool", h1[:], h1[:], G1[:], ALU.mult, ["h1", "G1"], ["h1"])
            k.tt("pool", h1[:], h1[:], B1[:], ALU.add, ["h1", "B1"], ["h1"])
            k.cp("pool", h1_bf[:], h1[:], ["h1"], ["h1_bf"])
            tb = bank_bf(0)
            for kk in range(8):
                k.tr(tb[:, kk * 128:(kk + 1) * 128], h1_bf[:, kk * 128:(kk + 1) * 128], ident[:], ["h1_bf", "ident"], ["bank0"])
            HR = "h1T%d" % bi
            k.cp("act", h1T[bi][:], tb[:, 0:1024].rearrange("p (a b) -> p a b", b=128), ["bank0"], [HR])
            k.dma("sp", AP(h1T_d, t0, [[S, 128], [128 * S, 8], [1, 128]]), h1T[bi][:], [HR], ["h1T_d"], semkey=HR)
            for n in range(2):
                b = 3 + n
                for kk in range(8):
                    k.mm(bank(b), h1T[bi][:, kk, :], wpg_sb[:, kk, 512 * n:512 * (n + 1)], kk == 0, kk == 7,
                         [HR, "wpg"], ["bank%d" % b])
                k.stt(tg[:, 512 * n:512 * (n + 1)], bank(b), 0.5, HB[:, 512 * n:512 * (n + 1)], ALU.mult, ALU.add,
                      ["bank%d" % b, "HB"], ["tg"])
            k.act(tg[:], tg[:], AF.Tanh, ["tg"], ["tg"])
            k.cp("pool", p_bf[:], pbuf[bi][:], [PR], ["p_bf"])
            tb2 = bank_bf(7)
            for kk in range(2):
                k.tr(tb2[:, kk * 128:(kk + 1) * 128], p_bf[:, kk * 128:(kk + 1) * 128], ident[:], ["p_bf", "ident"], ["bank7"])
            k.cp("act", pT[:], tb2[:, 0:256].rearrange("p (a b) -> p a b", b=128), ["bank7"], ["pT"])
            RR = "r2_%d" % bi
            for n in range(2):
                b = 5 + n
                for kk in range(2):
                    k.mm(bank(b), pT[:, kk, :], wpl_sb[:, kk, 512 * n:512 * (n + 1)], kk == 0, kk == 1,
                         ["pT", "wpl"], ["bank%d" % b])
                k.stt(pl2[:, 512 * n:512 * (n + 1)], tg[:, 512 * n:512 * (n + 1)], 1.0, bank(b), ALU.add, ALU.mult,
                      ["tg", "bank%d" % b], ["pl2"])
            k.stt(r2[bi][:], h1[:], 2.0 * ALPHA, pl2[:], ALU.mult, ALU.add, ["h1", "pl2"], [RR])
            k.dma("sp", r_d[t0:t0 + 128, :], r2[bi][:], [RR], ["r_d"], semkey=RR)
        k.final_wait(["r2_0", "r2_1", "h1T0", "h1T1"])

    def phase4(self, wup, fcw, fcb, wdn, ln2g, ln2b, r_d, h1T_d, out):
        k = self
        nc = self.nc
        bank, bank_bf, ident, ps = k.bank, k.bank_bf, k.ident, k.ps
        wup_sb = k.sb("wup_sb", [128, 8, 2 * DFF], BF16)
        wdn_sb = k.sb("wdn_sb", [128, 22, D], BF16)
        fcw_sb = k.sb("fcw_sb", [128, 44, 3], F32)
        fcb_sb = k.sb("fcb_sb", [128, 44], F32)
        G2 = k.sb("G2", [128, D], F32)
        B2 = k.sb("B2", [128, D], F32)
        h1T = k.sb("h1T", [128, 8, 512], BF16)
        actT = k.sb("actT", [128, 22, 512], BF16)
        gbuf = [k.sb("gbuf%d" % i, [128, 514], F32) for i in range(2)]
        abuf = [k.sb("abuf%d" % i, [128, 512], F32) for i in range(2)]
        sg = k.sb("sg", [128, 512], F32)
        halo = k.sb("halo", [128, 44, 2], F32)
        r2 = k.sb("r2", [128, D], F32)
        yb = [k.sb("yb%d" % i, [128, D], F32) for i in range(2)]
        k.alloc_lnst("y")
        for kk in range(8):
            k.dma("pool", wup_sb[:, kk, :], wup[kk * 128:(kk + 1) * 128, :], [], ["wup"], semkey="wup", nodep=True)
        WUR = ["wup"]
        for c in range(22):
            k.dma("pool", wdn_sb[:, c, :], wdn[c * 128:(c + 1) * 128, :], [], ["wdn"], semkey="wdn", nodep=True)
        WDR = ["wdn"]
        k.dma("sp", fcw_sb[:], fcw[:], [], ["fcw"])
        k.dma("sp", fcb_sb[:], fcb[:], [], ["fcb"])
        k.dma("sp", G2[:], AP(ln2g, 0, [[0, 128], [1, D]]), [], ["G2"])
        k.dma("sp", B2[:], AP(ln2b, 0, [[0, 128], [1, D]]), [], ["B2"])
        k.memset("pool", halo[:], 0.0, [], ["halo"])
        ring = {"i": 0}

        def nb():
            b = ring["i"] % 4
            ring["i"] += 1
            return b

        for st in range(8):
            T0 = st * 512
            k.dma("sp", h1T[:], AP(h1T_d, T0, [[S, 128], [128 * S, 8], [1, 512]]), [], ["h1T"], semkey="h1T")
            for c in range(22):
                res_ab = []
                for half in range(2):
                    ch = c + 22 * half
                    b = nb()
                    for kk in range(8):
                        k.mm(bank(b), wup_sb[:, kk, 128 * ch:128 * (ch + 1)], h1T[:, kk, :], kk == 0, kk == 7,
                             ["h1T"] + WUR, ["bank%d" % b])
                    gb, ab = gbuf[half], abuf[half]
                    GR, AR = "gbuf%d" % half, "abuf%d" % half
                    k.cp("pool", gb[:, 0:2], halo[:, ch, :], ["halo"], [GR])
                    k.cp("act", gb[:, 2:514], bank(b), ["bank%d" % b], [GR])
                    k.cp("pool", halo[:, ch, :], gb[:, 512:514], [GR], ["halo"])
                    k.act(ab[:], gb[:, 2:514], AF.Identity, [GR, "fcw", "fcb"], [AR],
                          scale=fcw_sb[:, ch, 2:3], bias=fcb_sb[:, ch:ch + 1])
                    k.stt(ab[:], gb[:, 1:513], fcw_sb[:, ch, 1:2], ab[:], ALU.mult, ALU.add, [GR, AR, "fcw"], [AR])
                    k.stt(ab[:], gb[:, 0:512], fcw_sb[:, ch, 0:1], ab[:], ALU.mult, ALU.add, [GR, AR, "fcw"], [AR])
                k.act(sg[:], abuf[0][:], AF.Silu, ["abuf0"], ["sg"])
                k.tt("pool", actT[:, c, :], sg[:], abuf[1][:], ALU.mult, ["sg", "abuf1"], ["actT"])
            for tt in range(4):
                t0 = T0 + tt * 128
                k.dma("sp", r2[:], r_d[t0:t0 + 128, :], [], ["r2"], semkey="r2")
                yi = (st * 4 + tt) % 2
                YR = "yb%d" % yi
                yv = yb[yi]
                for n in range(2):
                    b = 4 + 2 * (tt % 2) + n
                    for c in range(22):
                        k.mm(bank(b), actT[:, c, tt * 128:(tt + 1) * 128], wdn_sb[:, c, 512 * n:512 * (n + 1)],
                             c == 0, c == 21, ["actT"] + WDR, ["bank%d" % b])
                    k.stt(yv[:, 512 * n:512 * (n + 1)], r2[:, 512 * n:512 * (n + 1)], 0.5, bank(b), ALU.mult, ALU.add,
                          ["r2", "bank%d" % b], [YR])
                rstd, nmr = k.ln_stats(yv, "y", [YR], 2, 512)
                k.act(yv[:], yv[:], AF.Identity, ["lnst_y", YR], [YR], scale=rstd, bias=nmr)
                k.tt("pool", yv[:], yv[:], G2[:], ALU.mult, [YR, "G2"], [YR])
                k.tt("pool", yv[:], yv[:], B2[:], ALU.add, [YR, "B2"], [YR])
                k.dma("sp", out[t0:t0 + 128, :], yv[:], [YR], ["out_d"], semkey=YR)
        k.final_wait(["yb0", "yb1"])

    def final_wait(self, names):
        nc = self.nc
        self.op("sp", lambda: nc.sync.nop(), names, names)


def _prep_inputs(inputs, b):
    f = lambda a: np.ascontiguousarray(np.asarray(a, dtype=np.float32))
    m = {}
    m["x"] = f(inputs["x"][b])
    m["p"] = f(inputs["p"][0, b])
    m["w_in"] = f(inputs["w_in"][0])
    m["lng_fm"] = f(np.asarray(inputs["ln_emb_g"]).reshape(8, 128).T)
    m["lnb_fm"] = f(np.asarray(inputs["ln_emb_b"]).reshape(8, 128).T)
    m["lng"] = f(np.asarray(inputs["ln_emb_g"]).reshape(1, D))
    m["lnb"] = f(np.asarray(inputs["ln_emb_b"]).reshape(1, D))
    m["bgate"] = f(np.asarray(inputs["b_gate"][0]).reshape(2, 8, 128).transpose(2, 0, 1).reshape(128, 16))
    m["kvg"] = f(np.asarray(inputs["kv_norm_g"][0]).reshape(1, 128))
    wuk = np.asarray(inputs["w_uk"][0])
    m["wukT"] = f(wuk.reshape(4, 2, 128, 64).transpose(1, 3, 0, 2).reshape(128, 4, 128))
    wuv = np.asarray(inputs["w_uv"][0])
    m["wuvr"] = f(wuv.transpose(1, 0, 2).reshape(128, 512))
    m["kig"] = f(np.asarray(inputs["k_idx_ln_g"][0]).reshape(1, 64))
    m["kib"] = f(np.asarray(inputs["k_idx_ln_b"][0]).reshape(1, 64))
    m["mcw"] = f(np.asarray(inputs["mix_conv_w"][0]).reshape(3, 4, 128).transpose(2, 1, 0))
    m["mcb"] = f(np.asarray(inputs["mix_conv_b"][0]).reshape(4, 128).T)
    m["wbra"] = f(inputs["w_br_att"][0])
    m["wbrc"] = f(inputs["w_br_conv"][0])
    m["wo"] = f(inputs["w_o"][0])
    m["ln1g"] = f(np.asarray(inputs["ln1_g"][0]).reshape(1, D))
    m["ln1b"] = f(np.asarray(inputs["ln1_b"][0]).reshape(1, D))
    m["wup"] = f(inputs["w_ffn_up"][0])
    m["fcw"] = f(np.asarray(inputs["ffn_conv_w"][0]).reshape(3, 44, 128).transpose(2, 1, 0))
    m["fcb"] = f(np.asarray(inputs["ffn_conv_b"][0]).reshape(44, 128).T)
    m["wdn"] = f(inputs["w_ffn_down"][0])
    m["wpg"] = f(inputs["w_ple_gate"][0])
    m["bpg"] = f(np.asarray(inputs["b_ple_gate"][0]).reshape(1, D))
    m["wple"] = f(inputs["w_ple"][0])
    m["ln2g"] = f(np.asarray(inputs["ln2_g"][0]).reshape(1, D))
    m["ln2b"] = f(np.asarray(inputs["ln2_b"][0]).reshape(1, D))
    return m


def kernel(**inputs):
    kern = Kern()
    nc = kern.build()
    in_maps = [_prep_inputs(inputs, b) for b in range(NCORES)]
    res = run_bass_kernel_spmd(nc, in_maps, core_ids=list(range(NCORES)))
    return np.stack([np.asarray(r["out"], dtype=np.float32) for r in res.results], axis=0)
```

```python
import numpy as np
from contextlib import ExitStack
import concourse.bass as bass
import concourse.mybir as mybir
from concourse.bass_utils import run_bass_kernel_spmd

F32 = mybir.dt.float32
BF16 = mybir.dt.bfloat16
ALU = mybir.AluOpType
AF = mybir.ActivationFunctionType
AX = mybir.AxisListType

S = 4096
D = 1024
NCORES = 8
IN_W = 4808
DFF = 2816
LN_EPS = 1e-5
ALPHA = 2.0 ** 0.25
TOPK = 256
NEG = -1.0e30
N_BISECT = 18
C_QATT, C_CKV, C_QIDX, C_KIDX, C_WIDX, C_CVB, C_CVC, C_CVX, C_GATT, C_GCONV = (
    0, 512, 640, 1152, 1216, 1224, 1736, 2248, 2760, 3784)
W1C = 1224
W2C = IN_W - W1C
CW = (64 ** -0.5) * (8 ** -0.5)


class Res:
    __slots__ = ("name", "last_w", "readers")

    def __init__(self, name):
        self.name = name
        self.last_w = None
        self.readers = []


class Op:
    __slots__ = ("eng", "fn", "deps", "alldeps", "orderdeps", "signal", "sem", "ticket", "is_dma", "idx", "eidx",
                 "semkey", "cost", "lat", "start")


class Sched:
    NSEM = 0
    ENGS = ("pe", "act", "dve", "pool", "sp")

    def __init__(self, nc, es, reorder=True):
        self.nc = nc
        self.es = es
        self.ops = []
        self.last_dma = {}
        self.reorder = reorder
        self.engs = {"pe": nc.tensor, "act": nc.scalar, "dve": nc.vector, "pool": nc.gpsimd, "sp": nc.sync}

    def add(self, eng, fn, reads=(), writes=(), dma=False, semkey=None, nodep=False, cost=200.0, lat=0.0):
        op = Op()
        op.eng = eng
        op.fn = fn
        op.is_dma = dma
        op.signal = False
        op.sem = None
        op.ticket = 0
        op.idx = len(self.ops)
        op.eidx = 0
        op.semkey = semkey
        op.cost = cost
        op.lat = lat
        op.start = 0.0
        deps = {}
        for r in reads:
            if r.last_w is not None:
                deps[r.last_w.idx] = r.last_w
        for w in writes:
            if w.last_w is not None:
                deps[w.last_w.idx] = w.last_w
            for rd in w.readers:
                deps[rd.idx] = rd
        deps.pop(op.idx, None)
        if nodep:
            deps = {}
        for r in reads:
            r.readers.append(op)
        for w in writes:
            w.last_w = op
            w.readers = []
        op.alldeps = list(deps.values())
        op.orderdeps = []
        if dma:
            prev = self.last_dma.get(eng)
            if prev is not None:
                op.orderdeps.append(prev)
            self.last_dma[eng] = op
        self.ops.append(op)
        return op

    def schedule(self):
        import heapq
        ops = self.ops
        n = len(ops)
        succ = [[] for _ in range(n)]
        ndeps = [0] * n
        for op in ops:
            ds = set(d.idx for d in op.alldeps) | set(d.idx for d in op.orderdeps)
            ndeps[op.idx] = len(ds)
            for di in ds:
                succ[di].append(op.idx)
        ready = [0.0] * n
        finish = [0.0] * n
        eng_free = {e: 0.0 for e in self.ENGS}
        future = {e: [] for e in self.ENGS}
        avail = {e: [] for e in self.ENGS}
        for op in ops:
            if ndeps[op.idx] == 0:
                heapq.heappush(future[op.eng], (0.0, op.idx))
        order = []
        XLAT = 150.0
        while len(order) < n:
            best = None
            for e in self.ENGS:
                T = eng_free[e]
                fu, av = future[e], avail[e]
                while fu and fu[0][0] <= T:
                    _, i = heapq.heappop(fu)
                    heapq.heappush(av, i)
                if av:
                    st = T
                elif fu:
                    st = fu[0][0]
                else:
                    continue
                if best is None or st < best[0]:
                    best = (st, e)
            st, e = best
            if avail[e]:
                i = heapq.heappop(avail[e])
            else:
                _, i = heapq.heappop(future[e])
            op = ops[i]
            op.start = st
            eng_free[e] = st + op.cost
            finish[i] = st + op.cost + op.lat
            order.append(op)
            for j in succ[i]:
                r_ = finish[i] + (XLAT if ops[j].eng != e or op.is_dma else 0.0)
                if r_ > ready[j]:
                    ready[j] = r_
                ndeps[j] -= 1
                if ndeps[j] == 0:
                    heapq.heappush(future[ops[j].eng], (ready[j], j))
        self.est_ns = max(finish) if n else 0.0
        self.ops = order

    def emit(self):
        nc = self.nc
        if self.reorder:
            self.schedule()
        ecount = {}
        for op in self.ops:
            op.eidx = ecount.get(op.eng, 0)
            ecount[op.eng] = op.eidx + 1
        for op in self.ops:
            keep = []
            for d in op.alldeps:
                if not d.is_dma and d.eng == op.eng and not op.is_dma:
                    if op.eng == "pe":
                        continue
                    if op.eidx - d.eidx > 3:
                        continue
                keep.append(d)
            op.deps = keep
        for op in self.ops:
            if op.is_dma:
                op.signal = True
            for d in op.deps:
                d.signal = True
        sems = {}
        counts = {}

        def get_sem(key):
            if key not in sems:
                Sched.NSEM += 1
                sems[key] = self.es.enter_context(nc.semaphore("s%d" % Sched.NSEM))
                counts[key] = 0
            return sems[key]

        for op in self.ops:
            if not op.signal:
                continue
            if op.is_dma:
                key = ("dma", op.semkey if op.semkey is not None else op.idx)
                op.sem = get_sem(key)
                counts[key] += 16
                op.ticket = counts[key]
            else:
                key = ("eng", op.eng)
                op.sem = get_sem(key)
                counts[key] += 1
                op.ticket = counts[key]
        waited = {}
        nwait = 0
        for op in self.ops:
            e = self.engs[op.eng]
            need = {}
            for d in op.deps:
                k = id(d.sem)
                if waited.get((op.eng, k), 0) >= d.ticket:
                    continue
                if k not in need or need[k][1] < d.ticket:
                    need[k] = (d.sem, d.ticket)
            for k, (sem, val) in need.items():
                e.wait_ge(sem, val)
                waited[(op.eng, k)] = val
                nwait += 1
            ins = op.fn()
            if op.signal:
                ins.then_inc(op.sem, 16 if op.is_dma else 1)
        self.nsems = len(sems)
        self.nwait = nwait


def fsz(ap):
    n = 1
    for d in ap.shape[1:]:
        n *= int(d)
    return n


def AP(t, off, dims):
    return bass.AP(t, off, [list(d) for d in dims])


class Kern:
    def __init__(self, phases=(1, 2, 3, 4), debug=False, reorder=True):
        self.reorder = reorder
        self.phases = phases
        self.debug = debug
        self.nc = bass.Bass("TRN2", target_bir_lowering=False)
        self.es = ExitStack()
        self.ges = self.es
        self.semcount = 0
        self.dram = {}
        self.res = {}

    def din(self, name, shape, dt=F32):
        t = self.nc.dram_tensor(name, list(shape), dt, kind="ExternalInput")
        self.dram[name] = t
        return t

    def dscr(self, name, shape, dt):
        kind = "ExternalOutput" if self.debug else "Internal"
        t = self.nc.dram_tensor(name, list(shape), dt, kind=kind)
        self.dram[name] = t
        return t

    def sb(self, name, shape, dt):
        nm = "p%d_%s" % (getattr(self, "nphase", 0), name)
        return self.es.enter_context(self.nc.sbuf_tensor(nm, list(shape), dt))

    def guard(self, kb=8):
        with self.nc.sbuf_tensor("guard%d" % self.nphase, [128, kb * 256], F32):
            pass

    def R(self, name):
        if name not in self.res:
            self.res[name] = Res(name)
        return self.res[name]

    def Rs(self, *names):
        return [self.R(n) for n in names]

    def op(self, eng, fn, r=(), w=(), dma=False, semkey=None, nodep=False, cost=200.0, lat=0.0):
        return self.sc.add(eng, fn, [self.R(x) if isinstance(x, str) else x for x in r],
                           [self.R(x) if isinstance(x, str) else x for x in w], dma=dma, semkey=semkey, nodep=nodep,
                           cost=cost, lat=lat)

    def dma(self, q, out, in_, r=(), w=(), semkey=None, nodep=False):
        e = {"sp": self.nc.sync, "pool": self.nc.gpsimd, "act": self.nc.scalar}[q]
        nbytes = fsz(out) * int(out.shape[0]) * 4
        return self.op(q, lambda: e.dma_start(out=out, in_=in_), r, w, dma=True, semkey=semkey, nodep=nodep,
                       cost=(600.0 if q == "pool" else 100.0), lat=2500.0 + nbytes / 250.0)

    def mm(self, out, lhsT, rhs, start, stop, r=(), w=()):
        nc = self.nc
        return self.op("pe", lambda: nc.tensor.matmul(out, lhsT=lhsT, rhs=rhs, start=start, stop=stop), r, w,
                       cost=64.0 + 0.5 * fsz(rhs))

    def tr(self, out, in_, ident, r=(), w=()):
        nc = self.nc
        return self.op("pe", lambda: nc.tensor.transpose(out, in_, ident), r, w, cost=130.0)

    def act(self, out, in_, func, r=(), w=(), **kw):
        nc = self.nc
        return self.op("act", lambda: nc.scalar.activation(out=out, in_=in_, func=func, **kw), r, w,
                       cost=200.0 + 0.85 * fsz(out))

    def ts(self, eng, out, in0, s1, s2, op0, op1=None, r=(), w=(), accum_out=None):
        e = self.nc.vector if eng == "dve" else self.nc.gpsimd
        c = 70.0 + 1.05 * fsz(out)
        if op1 is None:
            return self.op(eng, lambda: e.tensor_scalar(out=out, in0=in0, scalar1=s1, scalar2=None, op0=op0), r, w, cost=c)
        if accum_out is not None:
            return self.op(eng, lambda: e.tensor_scalar(out=out, in0=in0, scalar1=s1, scalar2=s2, op0=op0, op1=op1,
                                                        accum_out=accum_out), r, w, cost=c)
        return self.op(eng, lambda: e.tensor_scalar(out=out, in0=in0, scalar1=s1, scalar2=s2, op0=op0, op1=op1), r, w, cost=c)

    def tt(self, eng, out, in0, in1, op, r=(), w=()):
        e = self.nc.vector if eng == "dve" else self.nc.gpsimd
        c = (70.0 + 1.05 * fsz(out)) if eng == "dve" else (150.0 + 2.0 * fsz(out))
        return self.op(eng, lambda: e.tensor_tensor(out=out, in0=in0, in1=in1, op=op), r, w, cost=c)

    def stt(self, out, in0, scalar, in1, op0, op1, r=(), w=()):
        nc = self.nc
        return self.op("dve", lambda: nc.vector.scalar_tensor_tensor(out=out, in0=in0, scalar=scalar, in1=in1,
                                                                      op0=op0, op1=op1), r, w, cost=70.0 + 1.05 * fsz(out))

    def cp(self, eng, out, in_, r=(), w=()):
        nc = self.nc
        if eng == "act":
            return self.op("act", lambda: nc.scalar.copy(out=out, in_=in_), r, w, cost=200.0 + 0.85 * fsz(out))
        e = nc.vector if eng == "dve" else nc.gpsimd
        c = (70.0 + 1.05 * fsz(out)) if eng == "dve" else (150.0 + 1.1 * fsz(out))
        return self.op(eng, lambda: e.tensor_copy(out=out, in_=in_), r, w, cost=c)

    def memset(self, eng, ap, val, r=(), w=()):
        e = self.nc.vector if eng == "dve" else self.nc.gpsimd
        return self.op(eng, lambda: e.memset(ap, val), r, w, cost=100.0 + 1.0 * fsz(ap))

    def ln_stats(self, src, tag, r, nchunk, width):
        nc = self.nc
        st = self.lnst[tag]
        stats, mv, rs = st
        resn = "lnst_" + tag
        for c in range(nchunk):
            self.op("dve", (lambda c=c: nc.vector.bn_stats(out=stats[:, 6 * c:6 * c + 6],
                                                           in_=src[:, c * width:(c + 1) * width])), r, [resn],
                    cost=100.0 + 1.1 * width)
        self.op("dve", lambda: nc.vector.bn_aggr(out=mv[:, 0:2], in_=stats[:, 0:6 * nchunk]), [resn], [resn])
        self.ts("dve", rs[:, 0:1], mv[:, 1:2], LN_EPS, None, ALU.add, None, [resn], [resn])
        self.act(rs[:, 0:1], rs[:, 0:1], AF.Ln, [resn], [resn])
        self.act(rs[:, 0:1], rs[:, 0:1], AF.Exp, [resn], [resn], scale=-0.5)
        self.ts("dve", rs[:, 1:2], mv[:, 0:1], -1.0, rs[:, 0:1], ALU.mult, ALU.mult, [resn], [resn])
        return rs[:, 0:1], rs[:, 1:2]

    def alloc_lnst(self, tag):
        if not hasattr(self, "lnst"):
            self.lnst = {}
        self.lnst[tag] = (self.sb("lnstats_" + tag, [128, 12], F32), self.sb("lnmv_" + tag, [128, 2], F32),
                          self.sb("lnrs_" + tag, [128, 2], F32))

    def build(self):
        nc = self.nc
        k = self
        x = k.din("x", [S, D])
        p = k.din("p", [S, 256])
        w_in = k.din("w_in", [D, IN_W])
        lng_fm = k.din("lng_fm", [128, 8])
        lnb_fm = k.din("lnb_fm", [128, 8])
        lng = k.din("lng", [1, D])
        lnb = k.din("lnb", [1, D])
        bgate = k.din("bgate", [128, 16])
        kvg = k.din("kvg", [1, 128])
        wukT = k.din("wukT", [128, 4, 128])
        wuvr = k.din("wuvr", [128, 512])
        kig = k.din("kig", [1, 64])
        kib = k.din("kib", [1, 64])
        mcw = k.din("mcw", [128, 4, 3])
        mcb = k.din("mcb", [128, 4])
        wbra = k.din("wbra", [512, D])
        wbrc = k.din("wbrc", [512, D])
        wo = k.din("wo", [D, D])
        ln1g = k.din("ln1g", [1, D])
        ln1b = k.din("ln1b", [1, D])
        wup = k.din("wup", [D, 2 * DFF])
        fcw = k.din("fcw", [128, 44, 3])
        fcb = k.din("fcb", [128, 44])
        wdn = k.din("wdn", [DFF, D])
        wpg = k.din("wpg", [D, D])
        bpg = k.din("bpg", [1, D])
        wple = k.din("wple", [256, D])
        ln2g = k.din("ln2g", [1, D])
        ln2b = k.din("ln2b", [1, D])
        out = nc.dram_tensor("out", [S, D], F32, kind="ExternalOutput")
        k.dram["out"] = out
        attT_d = k.dscr("attT_d", [512, S], BF16)
        mrgT_d = k.dscr("mrgT_d", [D, S], BF16)
        r_d = k.dscr("r_d", [S, D], F32)
        h1T_d = k.dscr("h1T_d", [D, S], BF16)

        ps = k.es.enter_context(nc.psum_tensor("ps", [128, 4096], F32))
        k.ps = ps

        def bank(b, n=512, off=0, parts=128):
            return ps[0:parts, b * 512 + off: b * 512 + off + n]

        def bank_bf(b):
            return ps[:, b * 512:(b + 1) * 512].bitcast(BF16)

        k.bank = bank
        k.bank_bf = bank_bf

        k.bar_tile = k.sb("bar_tile", [128, 8], F32)
        k.bar_bf = k.sb("bar_bf", [128, 8], BF16)
        ident = k.sb("ident", [128, 128], BF16)
        k.ident = ident
        k.nphase = 0
        with ExitStack() as pes:
            k.begin_phase(pes)
            k.memset("pool", ident[:], 0.0, [], ["ident"])
            k.op("pool", lambda: nc.gpsimd.affine_select(out=ident[:], in_=ident[:], pattern=[[-1, 128]],
                                                          compare_op=ALU.not_equal, fill=1.0, base=0,
                                                          channel_multiplier=1), ["ident"], ["ident"])
            k.memset("pool", k.bar_bf[:], 0.0, [], ["bar_bf"])
            k.end_phase()
        if 1 in k.phases:
            with ExitStack() as pes:
                k.begin_phase(pes)
                k.phase1(x, w_in, lng_fm, lnb_fm, kvg, wukT, wuvr, kig, kib, attT_d)
                k.end_phase()
        if 2 in k.phases:
            with ExitStack() as pes:
                k.begin_phase(pes)
                k.phase2(x, w_in, lng_fm, lnb_fm, bgate, mcw, mcb, wbra, wbrc, attT_d, mrgT_d)
                k.end_phase()
        if 3 in k.phases:
            with ExitStack() as pes:
                k.begin_phase(pes)
                k.phase3(x, p, lng, lnb, wo, ln1g, ln1b, wpg, bpg, wple, mrgT_d, r_d, h1T_d)
                k.end_phase()
        if 4 in k.phases:
            with ExitStack() as pes:
                k.begin_phase(pes)
                k.phase4(wup, fcw, fcb, wdn, ln2g, ln2b, r_d, h1T_d, out)
                k.end_phase()
        return nc

    def begin_phase(self, pes):
        self.es = pes
        self.sc = Sched(self.nc, self.ges, reorder=self.reorder)
        self.res = {}
        self.lnst = {}

    def end_phase(self):
        nc = self.nc
        k = self
        self.sc.emit()
        bar = self.ges.enter_context(nc.semaphore("bar%d" % self.nphase))
        self.nphase += 1
        nc.vector.memset(k.bar_tile[:, 0:1], 0.0).then_inc(bar, 1)
        nc.gpsimd.memset(k.bar_tile[:, 1:2], 0.0).then_inc(bar, 1)
        nc.scalar.copy(out=k.bar_tile[:, 2:3], in_=k.bar_tile[:, 3:4]).then_inc(bar, 1)
        nc.tensor.matmul(k.ps[0:8, 0:8], lhsT=k.bar_bf[:, 0:8], rhs=k.bar_bf[:, 0:8], start=True, stop=True).then_inc(bar, 1)
        nc.sync.nop().then_inc(bar, 1)
        for e in (nc.vector, nc.gpsimd, nc.scalar, nc.tensor, nc.sync):
            e.wait_ge(bar, 5)

    def phase1(self, x, w_in, lng_fm, lnb_fm, kvg, wukT, wuvr, kig, kib, attT_d):
        k = self
        nc = self.nc
        bank, bank_bf, ident, ps = k.bank, k.bank_bf, k.ident, k.ps
        w1 = k.sb("w1", [128, 8, W1C], BF16)
        g_fm = k.sb("g_fm", [128, 8], F32)
        b_fm = k.sb("b_fm", [128, 8], F32)
        wuk_sb = k.sb("wuk_sb", [128, 4, 128], BF16)
        wuv_sb = k.sb("wuv_sb", [128, 512], BF16)
        kvg_bc = k.sb("kvg_bc", [128, 128], F32)
        kig_bc = k.sb("kig_bc", [128, 64], F32)
        kib_bc = k.sb("kib_bc", [128, 64], F32)
        negm = k.sb("negm", [128, 128], F32)
        pow2 = k.sb("pow2", [128, N_BISECT], F32)
        kT2 = k.sb("kT2", [128, S], BF16)
        ckvT = k.sb("ckvT", [128, S], BF16)
        vext = k.sb("vext", [128, 32, 8, 65], BF16)
        xbuf = [k.sb("xbuf%d" % i, [128, D], F32) for i in range(2)]
        xn_bf = k.sb("xn_bf", [128, D], BF16)
        hT2 = [k.sb("hT0", [128, 8, 512], BF16)] * 2
        qattT = k.sb("qattT", [128, 4, 512], BF16)
        qlatT2 = [k.sb("qlatT%d" % i, [128, 8, 512], BF16) for i in range(2)]
        qidxT2 = [k.sb("qidxT%d" % i, [128, 4, 512], BF16) for i in range(2)]
        absw42 = [k.sb("absw4%d" % i, [128, 4, 8], F32) for i in range(2)]
        sgn42 = [k.sb("sgn4%d" % i, [128, 4, 8], F32) for i in range(2)]
        dsgn2 = [k.sb("dsgn%d" % i, [128, 8, 128], BF16) for i in range(2)]
        relu_sb = [k.sb("relu%d" % i, [128, 512], BF16) for i in range(3)]
        score2 = [k.sb("score%d" % i, [128, S], F32) for i in range(2)]
        mask012 = [k.sb("mask01%d" % i, [128, S], BF16) for i in range(2)]
        maskT = k.sb("maskT", [128, 32, 128], BF16)
        PT = [k.sb("PT%d" % i, [128, 4, 128], BF16) for i in range(4)]
        ckv_tm = k.sb("ckv_tm", [128, 128], BF16)
        craw = k.sb("craw", [128, 128], F32)
        craw2 = k.sb("craw2", [128, 128], F32)
        kn_f = k.sb("kn_f", [128, 64], F32)
        kn2 = k.sb("kn2", [128, 128], BF16)
        sm = k.sb("sm", [128, 16], F32)
        wks = k.sb("wks", [128, N_BISECT], F32)
        bis = k.sb("bis", [128, 8], F32)
        pv_sb2 = [k.sb("pv_sb0", [65, 1024], F32)] * 2
        rden2 = [k.sb("rden0", [65, 1024], F32)] * 2
        ones_r = k.sb("ones_r", [65, 64], F32)
        att_n = [k.sb("att_n%d" % i, [64, 8, 128], BF16) for i in range(2)]
        k.alloc_lnst("x")
        k.alloc_lnst("k")
        k.guard()

        for kk in range(8):
            k.dma("pool", w1[:, kk, :], w_in[kk * 128:(kk + 1) * 128, 0:W1C], [], ["w1"], semkey="w1", nodep=True)
        W1R = ["w1"]
        k.dma("sp", g_fm[:], lng_fm[:], [], ["g_fm"])
        k.dma("sp", b_fm[:], lnb_fm[:], [], ["b_fm"])
        k.dma("pool", wuk_sb[:], wukT[:], [], ["wuk"])
        k.dma("pool", wuv_sb[:], wuvr[:], [], ["wuv"])
        k.dma("sp", kvg_bc[:], AP(kvg, 0, [[0, 128], [1, 128]]), [], ["kvg_bc"])
        k.dma("sp", kig_bc[:], AP(kig, 0, [[0, 128], [1, 64]]), [], ["kig_bc"])
        k.dma("sp", kib_bc[:], AP(kib, 0, [[0, 128], [1, 64]]), [], ["kib_bc"])
        k.memset("pool", negm[:], 0.0, [], ["negm"])
        k.op("pool", lambda: nc.gpsimd.affine_select(out=negm[:], in_=negm[:], pattern=[[-1, 128]],
                                                      compare_op=ALU.is_ge, fill=NEG, base=0,
                                                      channel_multiplier=1), ["negm"], ["negm"])
        for i in range(N_BISECT):
            k.memset("pool", pow2[:, i:i + 1], 2.0 ** (-(i + 1)), [], ["pow2"])
        k.memset("pool", vext[:, :, :, 64:65], 1.0, [], ["vext_ones"])
        k.memset("pool", ones_r[:], 1.0, [], ["ones_r"])

        BT, BR0, BR1, BSC, BL0, BL1, BV0, BV1 = 0, 1, 2, 3, 4, 5, 6, 7
        ring = [BR0, BR1]
        rstate = {"i": 0, "relu": 0, "pt": 0, "lg": 0, "xb": 0, "an": 0}

        def next_ring():
            b = ring[rstate["i"] % 2]
            rstate["i"] += 1
            return b

        for st in range(8):
            T0 = st * 512
            sp_ = st % 2
            hT, qlatT, qidxT, absw4, sgn4 = hT2[sp_], qlatT2[sp_], qidxT2[sp_], absw42[sp_], sgn42[sp_]
            HT, QL, QI, AW, SG = "hT0", "qlatT%d" % sp_, "qidxT%d" % sp_, "absw4%d" % sp_, "sgn4%d" % sp_
            for tt in range(4):
                t0 = T0 + tt * 128
                xb_i = rstate["xb"] % 2
                rstate["xb"] += 1
                xb = xbuf[xb_i]
                XR = "xbuf%d" % xb_i
                k.dma("sp", xb[:], x[t0:t0 + 128, :], [], [XR], semkey=XR)
                rstd, nmr = k.ln_stats(xb, "x", [XR], 2, 512)
                k.act(xn_bf[:], xb[:], AF.Identity, ["lnst_x", XR], ["xn_bf"], scale=rstd, bias=nmr)
                tb = bank_bf(BT)
                for kk in range(8):
                    k.tr(tb[:, kk * 128:(kk + 1) * 128], xn_bf[:, kk * 128:(kk + 1) * 128], ident[:],
                         ["xn_bf", "ident"], ["bank0", "bank0b"])
                for kk in range(8):
                    k.act(hT[:, kk, tt * 128:(tt + 1) * 128], tb[:, kk * 128:(kk + 1) * 128], AF.Identity,
                          ["bank0", "bank0b", "g_fm", "b_fm"], [HT], scale=g_fm[:, kk:kk + 1], bias=b_fm[:, kk:kk + 1])
            for j in range(4):
                b = next_ring()
                for kk in range(8):
                    k.mm(bank(b), w1[:, kk, C_QATT + 128 * j:C_QATT + 128 * (j + 1)], hT[:, kk, :], kk == 0, kk == 7,
                         [HT] + W1R, ["bank%d" % b])
                k.cp("dve", qattT[:, j, :], bank(b), ["bank%d" % b], ["qattT"])
            for j in range(4):
                b = next_ring()
                for kk in range(8):
                    k.mm(bank(b), w1[:, kk, C_QIDX + 128 * j:C_QIDX + 128 * (j + 1)], hT[:, kk, :], kk == 0, kk == 7,
                         [HT] + W1R, ["bank%d" % b])
                k.cp("act", qidxT[:, j, :], bank(b), ["bank%d" % b], [QI])
            for tt in range(4):
                blk = st * 4 + tt
                bck_, bkw_ = next_ring(), next_ring()
                PCK, PKW = "bank%d" % bck_, "bank%d" % bkw_
                pck = bank(bck_, 128, 0)
                pkw = bank(bkw_, 72, 0)
                for kk in range(8):
                    k.mm(pck, hT[:, kk, tt * 128:(tt + 1) * 128], w1[:, kk, C_CKV:C_CKV + 128], kk == 0, kk == 7,
                         [HT] + W1R, [PCK])
                for kk in range(8):
                    k.mm(pkw, hT[:, kk, tt * 128:(tt + 1) * 128], w1[:, kk, C_KIDX:C_KIDX + 72], kk == 0, kk == 7,
                         [HT] + W1R, [PKW])
                k.cp("act", craw[:], pck, [PCK], ["craw"])
                k.op("dve", lambda: nc.vector.scalar_tensor_tensor(out=craw2[:], in0=craw[:], scalar=1.0, in1=craw[:],
                                                                    op0=ALU.mult, op1=ALU.mult, accum_out=sm[:, 0:1]),
                     ["craw"], ["craw2", "sm_c"])
                k.ts("dve", sm[:, 1:2], sm[:, 0:1], 1.0 / 128.0, LN_EPS, ALU.mult, ALU.add, ["sm_c"], ["sm_c"])
                k.act(sm[:, 2:3], sm[:, 1:2], AF.Ln, ["sm_c"], ["sm_c"])
                k.act(sm[:, 2:3], sm[:, 2:3], AF.Exp, ["sm_c"], ["sm_c"], scale=-0.5)
                k.stt(ckv_tm[:], craw[:], sm[:, 2:3], kvg_bc[:], ALU.mult, ALU.mult, ["craw", "sm_c", "kvg_bc"], ["ckv_tm"])
                rstd_k, nmr_k = k.ln_stats(pkw, "k", [PKW], 1, 64)
                k.act(kn_f[:], pkw[:, 0:64], AF.Identity, [PKW, "lnst_k"], ["kn_f"], scale=rstd_k, bias=nmr_k)
                k.tt("pool", kn_f[:], kn_f[:], kig_bc[:], ALU.mult, ["kn_f", "kig_bc"], ["kn_f"])
                k.tt("pool", kn2[:, 0:64], kn_f[:], kib_bc[:], ALU.add, ["kn_f", "kib_bc"], ["kn2"])
                k.tt("pool", kn2[:, 64:128], kn_f[:], kib_bc[:], ALU.add, ["kn_f", "kib_bc"], ["kn2"])
                k.act(sm[:, 8:16], pkw[:, 64:72], AF.Copy, [PKW], ["sm_w"], scale=CW)
                k.stt(absw4[:, tt, :], sm[:, 8:16], -1.0, sm[:, 8:16], ALU.mult, ALU.max, ["sm_w"], [AW])
                k.act(sgn4[:, tt, :], pkw[:, 64:72], AF.Sign, [PKW], [SG])
                tb = bank_bf(BT)
                k.tr(tb[:, 512:640], ckv_tm[:], ident[:], ["ckv_tm", "ident"], ["bank0b"])
                k.tr(tb[:, 640:768], kn2[:], ident[:], ["kn2", "ident"], ["bank0b"])
                k.cp("act", ckvT[:, blk * 128:(blk + 1) * 128], tb[:, 512:640], ["bank0b"], ["ckvT"])
                k.cp("act", kT2[:, blk * 128:(blk + 1) * 128], tb[:, 640:768], ["bank0b"], ["kT2"])
                b = next_ring()
                k.mm(bank(b), ckvT[:, blk * 128:(blk + 1) * 128], wuv_sb[:], True, True, ["ckvT", "wuv"], ["bank%d" % b])
                k.cp("dve", vext[:, blk, :, 0:64], bank(b).rearrange("p (h d) -> p h d", h=8), ["bank%d" % b], ["vext"])
            for h in range(8):
                e, j = h % 2, h // 2
                b = next_ring()
                k.mm(bank(b), wuk_sb[64 * e:64 * e + 64, j, :], qattT[64 * e:64 * e + 64, j, :], True, True,
                     ["qattT", "wuk"], ["bank%d" % b])
                k.act(qlatT[:, h, :], bank(b), AF.Copy, ["bank%d" % b], [QL], scale=0.125)
            for i in range(4):
                I = st * 4 + i
                nk = 128 * (I + 1)
                q0 = i * 128
                ip_ = I % 2
                score, mask01, dsgn, pv_sb, rden = score2[ip_], mask012[ip_], dsgn2[ip_], pv_sb2[ip_], rden2[ip_]
                junk = mask01
                SC, MK, DS, PVS, RD = "score%d" % ip_, "mask01%d" % ip_, "dsgn%d" % ip_, "pv_sb0", "rden0"
                for h in range(8):
                    k.ts("dve", dsgn[:, h, :], ident[:], sgn4[:, i, h:h + 1], None, ALU.mult, None,
                         ["ident", SG], [DS])
                nkb = (nk + 511) // 512
                for kb in range(nkb):
                    wk = min(512, nk - 512 * kb)
                    for h in range(8):
                        e, j = h % 2, h // 2
                        b = next_ring()
                        k.mm(bank(b, wk), qidxT[64 * e:64 * e + 64, j, q0:q0 + 128],
                             kT2[64 * e:64 * e + 64, 512 * kb:512 * kb + wk], True, True,
                             [QI, "kT2"], ["bank%d" % b])
                        ri = rstate["relu"] % 3
                        rstate["relu"] += 1
                        k.act(relu_sb[ri][:, 0:wk], bank(b, wk), AF.Relu, ["bank%d" % b, AW], ["relu%d" % ri],
                              scale=absw4[:, i, h:h + 1])
                        k.mm(bank(BSC, wk), dsgn[:, h, :], relu_sb[ri][:, 0:wk], h == 0, h == 7,
                             [DS, "relu%d" % ri], ["bank%d" % BSC])
                    last = (kb == nkb - 1)
                    ncopy = wk - 128 if last else wk
                    if ncopy > 0:
                        k.cp("act", score[:, 512 * kb:512 * kb + ncopy], bank(BSC, ncopy), ["bank%d" % BSC], [SC])
                    if last:
                        k.tt("dve", score[:, nk - 128:nk], bank(BSC, 128, wk - 128), negm[:], ALU.add,
                             ["bank%d" % BSC, "negm"], [SC])
                if I >= 2:
                    k.op("dve", lambda nk=nk, sc_=score: nc.vector.tensor_reduce(out=bis[:, 0:1], in_=sc_[:, 0:nk], axis=AX.X,
                                                                                 op=ALU.max), [SC], ["bis"],
                         cost=100.0 + 1.05 * nk)
                    k.op("dve", lambda sc_=score: nc.vector.tensor_reduce(out=bis[:, 1:2], in_=sc_[:, 0:256], axis=AX.X,
                                                                          op=ALU.min), [SC], ["bis"], cost=400.0)
                    k.tt("dve", bis[:, 2:3], bis[:, 0:1], bis[:, 1:2], ALU.subtract, ["bis"], ["bis"])
                    k.ts("dve", bis[:, 2:3], bis[:, 2:3], 1.001, 1e-6, ALU.mult, ALU.add, ["bis"], ["bis"])
                    k.ts("dve", wks[:], pow2[:], bis[:, 2:3], None, ALU.mult, None, ["bis", "pow2"], ["wks"])
                    for it in range(N_BISECT):
                        k.tt("dve", bis[:, 3:4], bis[:, 1:2], wks[:, it:it + 1], ALU.add, ["bis", "wks"], ["bis"])
                        k.ts("dve", junk[:, 0:nk], score[:, 0:nk], bis[:, 3:4], 0.0, ALU.is_ge, ALU.add,
                             [SC, "bis"], [MK, "bis"], accum_out=bis[:, 4:5])
                        k.ts("dve", bis[:, 5:6], bis[:, 4:5], float(TOPK) - 0.5, wks[:, it:it + 1], ALU.is_ge, ALU.mult,
                             ["bis", "wks"], ["bis"])
                        k.tt("dve", bis[:, 1:2], bis[:, 1:2], bis[:, 5:6], ALU.add, ["bis"], ["bis"])
                    k.ts("dve", mask01[:, 0:nk], score[:, 0:nk], bis[:, 1:2], None, ALU.is_ge, None,
                         [SC, "bis"], [MK])
                else:
                    k.ts("dve", mask01[:, 0:nk], score[:, 0:nk], -1.0e29, None, ALU.is_ge, None, [SC], [MK])
                tb = bank_bf(BT)
                for g0 in range(0, I + 1, 8):
                    g1 = min(I + 1, g0 + 8)
                    for jb in range(g0, g1):
                        k.tr(tb[:, (jb - g0) * 128:(jb - g0 + 1) * 128], mask01[:, jb * 128:(jb + 1) * 128], ident[:],
                             [MK, "ident"], ["bank0", "bank0b"])
                    k.cp("act", maskT[:, g0:g1, :], tb[:, 0:(g1 - g0) * 128].rearrange("p (a b) -> p a b", b=128),
                         ["bank0", "bank0b"], ["maskT"])
                pv = ps[0:65, BV0 * 512:BV0 * 512 + 1024].rearrange("p (h q) -> p h q", h=8)
                k.op("dve", lambda: nc.vector.memset(ps[0:65, BV0 * 512:BV0 * 512 + 1024], 0.0), [], ["bankpv"])
                for jb in range(I + 1):
                    for g in range(2):
                        lb = [BL0, BL1][rstate["lg"] % 2]
                        rstate["lg"] += 1
                        k.mm(bank(lb), ckvT[:, jb * 128:(jb + 1) * 128], qlatT[:, 4 * g:4 * g + 4, q0:q0 + 128],
                             True, True, ["ckvT", QL], ["bank%d" % lb])
                        pi = rstate["pt"] % 4
                        rstate["pt"] += 1
                        k.act(PT[pi][:], bank(lb).rearrange("p (h q) -> p h q", h=4), AF.Exp, ["bank%d" % lb],
                              ["PT%d" % pi])
                        k.tt("pool", PT[pi][:], PT[pi][:], AP(maskT, jb * 128, [[32 * 128, 128], [0, 4], [1, 128]]),
                             ALU.mult, ["PT%d" % pi, "maskT"], ["PT%d" % pi])
                        for hh in range(4):
                            h = 4 * g + hh
                            k.op("pe", (lambda o=pv[:, h, :], l=vext[:, jb, h, :], rr=PT[pi][:, hh, :], sp_=(jb == I):
                                        nc.tensor.matmul(o, lhsT=l, rhs=rr, start=False, stop=sp_,
                                                         skip_group_check=True)),
                                 ["vext", "vext_ones", "PT%d" % pi], ["bankpv"])
                k.cp("act", pv_sb[:], ps[0:65, BV0 * 512:BV0 * 512 + 1024], ["bankpv"], [PVS])
                k.act(rden[64:65, :], pv_sb[64:65, :], AF.Ln, [PVS], [RD])
                k.act(rden[64:65, :], rden[64:65, :], AF.Exp, [RD], [RD], scale=-1.0)
                ai = rstate["an"] % 2
                rstate["an"] += 1
                for g in range(2):
                    lb = [BL0, BL1][rstate["lg"] % 2]
                    rstate["lg"] += 1
                    k.mm(bank(lb, 512, 0, 64), ones_r[64:65, :], rden[64:65, g * 512:(g + 1) * 512], True, True,
                         ["ones_r", RD], ["bank%d" % lb])
                    k.tt("dve", att_n[ai][:, 4 * g:4 * g + 4, :],
                         pv_sb[0:64, g * 512:(g + 1) * 512].rearrange("p (h q) -> p h q", h=4),
                         bank(lb, 512, 0, 64).rearrange("p (h q) -> p h q", h=4), ALU.mult,
                         [PVS, "bank%d" % lb], ["att_n%d" % ai])
                tok0 = T0 + q0
                k.dma("sp", AP(attT_d, tok0, [[S, 64], [64 * S, 8], [1, 128]]), att_n[ai][:],
                      ["att_n%d" % ai], ["attT_d"], semkey="att_n%d" % ai)
        k.final_wait(["att_n0", "att_n1"])

    def ln_hT(self, x, t0, tt, xbuf, rstate, xn_bf, hT, g_fm, b_fm, BT=0):
        k = self
        nc = self.nc
        xb_i = rstate["xb"] % 2
        rstate["xb"] += 1
        xb = xbuf[xb_i]
        XR = "xbuf%d" % xb_i
        k.dma("sp", xb[:], x[t0:t0 + 128, :], [], [XR], semkey=XR)
        rstd, nmr = k.ln_stats(xb, "x", [XR], 2, 512)
        k.act(xn_bf[:], xb[:], AF.Identity, ["lnst_x", XR], ["xn_bf"], scale=rstd, bias=nmr)
        tb = k.bank_bf(BT)
        for kk in range(8):
            k.tr(tb[:, kk * 128:(kk + 1) * 128], xn_bf[:, kk * 128:(kk + 1) * 128], k.ident[:],
                 ["xn_bf", "ident"], ["bank0", "bank0b"])
        for kk in range(8):
            k.act(hT[:, kk, tt * 128:(tt + 1) * 128], tb[:, kk * 128:(kk + 1) * 128], AF.Identity,
                  ["bank0", "bank0b", "g_fm", "b_fm"], ["hT"], scale=g_fm[:, kk:kk + 1], bias=b_fm[:, kk:kk + 1])

    def phase2(self, x, w_in, lng_fm, lnb_fm, bgate, mcw, mcb, wbra, wbrc, attT_d, mrgT_d):
        k = self
        nc = self.nc
        bank, bank_bf, ident, ps = k.bank, k.bank_bf, k.ident, k.ps
        w2 = k.sb("w2", [128, 8, W2C], BF16)
        wa = k.sb("wa", [128, 4, D], BF16)
        wc = k.sb("wc", [128, 4, D], BF16)
        g_fm = k.sb("g_fm", [128, 8], F32)
        b_fm = k.sb("b_fm", [128, 8], F32)
        hb = k.sb("hb", [128, 16], F32)
        mcw_sb = k.sb("mcw_sb", [128, 4, 3], F32)
        mcb_sb = k.sb("mcb_sb", [128, 4], F32)
        xbuf = [k.sb("xbuf%d" % i, [128, D], F32) for i in range(2)]
        xn_bf = k.sb("xn_bf", [128, D], BF16)
        hT = k.sb("hT", [128, 8, 512], BF16)
        att_in = k.sb("att_in", [128, 4, 512], BF16)
        u = k.sb("u", [128, 4, 514], F32)
        tmpc = k.sb("tmpc", [128, 512], F32)
        a_sb = k.sb("a_sb", [128, 512], F32)
        cyT = k.sb("cyT", [128, 4, 512], BF16)
        ta = k.sb("ta", [128, 512], F32)
        tc2 = k.sb("tc2", [128, 512], F32)
        m1 = k.sb("m1", [128, 512], F32)
        m2 = k.sb("m2", [128, 512], F32)
        mrg = [k.sb("mrg%d" % i, [128, 8, 512], BF16) for i in range(2)]
        k.alloc_lnst("x")
        k.guard()
        for kk in range(8):
            k.dma("pool", w2[:, kk, :], w_in[kk * 128:(kk + 1) * 128, W1C:IN_W], [], ["w2"], semkey="w2", nodep=True)
        W2R = ["w2"]
        k.dma("pool", wa[:], wbra.rearrange("(k p) f -> p k f", p=128), [], ["wa"])
        k.dma("pool", wc[:], wbrc.rearrange("(k p) f -> p k f", p=128), [], ["wc"])
        k.dma("sp", g_fm[:], lng_fm[:], [], ["g_fm"])
        k.dma("sp", b_fm[:], lnb_fm[:], [], ["b_fm"])
        k.dma("sp", hb[:], bgate[:], [], ["hb"])
        k.dma("sp", mcw_sb[:], mcw[:], [], ["mcw"])
        k.dma("sp", mcb_sb[:], mcb[:], [], ["mcb"])
        k.ts("dve", hb[:], hb[:], 0.5, None, ALU.mult, None, ["hb"], ["hb"])
        k.memset("pool", u[:], 0.0, [], ["u"])
        rstate = {"xb": 0, "ring": 0, "mrg": 0}
        ringb = [1, 2, 3, 4, 5, 6, 7]

        def nb():
            b = ringb[rstate["ring"] % 7]
            rstate["ring"] += 1
            return b

        def proj(col0, b):
            c0 = col0 - W1C
            for kk in range(8):
                k.mm(bank(b), w2[:, kk, c0:c0 + 128], hT[:, kk, :], kk == 0, kk == 7, ["hT"] + W2R, ["bank%d" % b])

        for st in range(8):
            T0 = st * 512
            for tt in range(4):
                k.ln_hT(x, T0 + tt * 128, tt, xbuf, rstate, xn_bf, hT, g_fm, b_fm)
            k.dma("sp", att_in[:], AP(attT_d, T0, [[S, 128], [128 * S, 4], [1, 512]]), [], ["att_in"], semkey="att_in")
            for j in range(4):
                bc_, bx_, bb_ = nb(), nb(), nb()
                proj(C_CVC + 128 * j, bc_)
                proj(C_CVX + 128 * j, bx_)
                proj(C_CVB + 128 * j, bb_)
                if st > 0:
                    k.cp("pool", u[:, j, 0:2], u[:, j, 512:514], ["u"], ["u"])
                k.cp("act", tmpc[:], bank(bc_), ["bank%d" % bc_], ["tmpc"])
                k.tt("dve", u[:, j, 2:514], tmpc[:], bank(bx_), ALU.mult, ["tmpc", "bank%d" % bx_], ["u"])
                k.act(a_sb[:], u[:, j, 2:514], AF.Identity, ["u", "mcw", "mcb"], ["a_sb"],
                      scale=mcw_sb[:, j, 2:3], bias=mcb_sb[:, j:j + 1])
                k.stt(a_sb[:], u[:, j, 1:513], mcw_sb[:, j, 1:2], a_sb[:], ALU.mult, ALU.add, ["u", "a_sb", "mcw"], ["a_sb"])
                k.stt(a_sb[:], u[:, j, 0:512], mcw_sb[:, j, 0:1], a_sb[:], ALU.mult, ALU.add, ["u", "a_sb", "mcw"], ["a_sb"])
                k.tt("dve", cyT[:, j, :], a_sb[:], bank(bb_), ALU.mult, ["a_sb", "bank%d" % bb_], ["cyT"])
            mi = rstate["mrg"] % 2
            rstate["mrg"] += 1
            for c in range(8):
                bga, bgc, bra, brc = nb(), nb(), nb(), nb()
                proj(C_GATT + 128 * c, bga)
                proj(C_GCONV + 128 * c, bgc)
                for kk in range(4):
                    k.mm(bank(bra), wa[:, kk, 128 * c:128 * (c + 1)], att_in[:, kk, :], kk == 0, kk == 3,
                         ["wa", "att_in"], ["bank%d" % bra])
                for kk in range(4):
                    k.mm(bank(brc), wc[:, kk, 128 * c:128 * (c + 1)], cyT[:, kk, :], kk == 0, kk == 3,
                         ["wc", "cyT"], ["bank%d" % brc])
                k.act(ta[:], bank(bga), AF.Tanh, ["bank%d" % bga, "hb"], ["ta"], scale=0.5, bias=hb[:, c:c + 1])
                k.act(tc2[:], bank(bgc), AF.Tanh, ["bank%d" % bgc, "hb"], ["tc2"], scale=0.5, bias=hb[:, 8 + c:9 + c])
                k.stt(m1[:], ta[:], 1.0, bank(bra), ALU.add, ALU.mult, ["ta", "bank%d" % bra], ["m1"])
                k.stt(m2[:], tc2[:], 1.0, bank(brc), ALU.add, ALU.mult, ["tc2", "bank%d" % brc], ["m2"])
                k.tt("pool", mrg[mi][:, c, :], m1[:], m2[:], ALU.add, ["m1", "m2"], ["mrg%d" % mi])
            k.dma("sp", AP(mrgT_d, T0, [[S, 128], [128 * S, 8], [1, 512]]), mrg[mi][:], ["mrg%d" % mi], ["mrgT_d"],
                  semkey="mrg%d" % mi)
        k.final_wait(["mrg0", "mrg1"])

    def phase3(self, x, p, lng, lnb, wo, ln1g, ln1b, wpg, bpg, wple, mrgT_d, r_d, h1T_d):
        k = self
        nc = self.nc
        bank, bank_bf, ident, ps = k.bank, k.bank_bf, k.ident, k.ps
        wo_sb = k.sb("wo_sb", [128, 8, D], BF16)
        wpg_sb = k.sb("wpg_sb", [128, 8, D], BF16)
        wpl_sb = k.sb("wpl_sb", [128, 2, D], BF16)
        Ga = k.sb("Ga", [128, D], F32)
        Ba = k.sb("Ba", [128, D], F32)
        G1 = k.sb("G1", [128, D], F32)
        B1 = k.sb("B1", [128, D], F32)
        HB = k.sb("HB", [128, D], F32)
        xbuf = [k.sb("xbuf%d" % i, [128, D], F32) for i in range(2)]
        pbuf = [k.sb("pbuf%d" % i, [128, 256], F32) for i in range(2)]
        m_in = [k.sb("m_in%d" % i, [128, 8, 128], BF16) for i in range(2)]
        hA = k.sb("hA", [128, D], F32)
        y = k.sb("y", [128, D], F32)
        h1 = k.sb("h1", [128, D], F32)
        h1_bf = k.sb("h1_bf", [128, D], BF16)
        h1T = [k.sb("h1T%d" % i, [128, 8, 128], BF16) for i in range(2)]
        p_bf = k.sb("p_bf", [128, 256], BF16)
        pT = k.sb("pT", [128, 2, 128], BF16)
        tg = k.sb("tg", [128, D], F32)
        pl2 = k.sb("pl2", [128, D], F32)
        r2 = [k.sb("r2_%d" % i, [128, D], F32) for i in range(2)]
        k.alloc_lnst("x")
        k.alloc_lnst("y")
        k.guard()
        k.dma("pool", wo_sb[:], wo.rearrange("(k p) f -> p k f", p=128), [], ["wo"])
        k.dma("pool", wpg_sb[:], wpg.rearrange("(k p) f -> p k f", p=128), [], ["wpg"])
        k.dma("pool", wpl_sb[:], wple.rearrange("(k p) f -> p k f", p=128), [], ["wpl"])
        for t_, src, nm in ((Ga, lng, "Ga"), (Ba, lnb, "Ba"), (G1, ln1g, "G1"), (B1, ln1b, "B1"), (HB, bpg, "HB")):
            k.dma("sp", t_[:], AP(src, 0, [[0, 128], [1, D]]), [], [nm])
        k.ts("dve", Ga[:], Ga[:], ALPHA, None, ALU.mult, None, ["Ga"], ["Ga"])
        k.ts("dve", Ba[:], Ba[:], ALPHA, None, ALU.mult, None, ["Ba"], ["Ba"])
        k.ts("dve", HB[:], HB[:], 0.5, None, ALU.mult, None, ["HB"], ["HB"])
        for t in range(32):
            t0 = t * 128
            bi = t % 2
            XR, PR, MR = "xbuf%d" % bi, "pbuf%d" % bi, "m_in%d" % bi
            k.dma("sp", xbuf[bi][:], x[t0:t0 + 128, :], [], [XR], semkey=XR)
            k.dma("sp", pbuf[bi][:], p[t0:t0 + 128, :], [], [PR], semkey=PR)
            k.dma("sp", m_in[bi][:], AP(mrgT_d, t0, [[S, 128], [128 * S, 8], [1, 128]]), [], [MR], semkey=MR)
            rstd, nmr = k.ln_stats(xbuf[bi], "x", [XR], 2, 512)
            k.act(hA[:], xbuf[bi][:], AF.Identity, ["lnst_x", XR], ["hA"], scale=rstd, bias=nmr)
            k.tt("pool", hA[:], hA[:], Ga[:], ALU.mult, ["hA", "Ga"], ["hA"])
            k.tt("pool", hA[:], hA[:], Ba[:], ALU.add, ["hA", "Ba"], ["hA"])
            for n in range(2):
                b = 1 + n
                for kk in range(8):
                    k.mm(bank(b), m_in[bi][:, kk, :], wo_sb[:, kk, 512 * n:512 * (n + 1)], kk == 0, kk == 7,
                         [MR, "wo"], ["bank%d" % b])
                k.stt(y[:, 512 * n:512 * (n + 1)], bank(b), 0.5, hA[:, 512 * n:512 * (n + 1)], ALU.mult, ALU.add,
                      ["bank%d" % b, "hA"], ["y"])
            rstd1, nmr1 = k.ln_stats(y, "y", ["y"], 2, 512)
            k.act(h1[:], y[:], AF.Identity, ["lnst_y", "y"], ["h1"], scale=rstd1, bias=nmr1)
            k.tt("pool", h1[:], h1[:], G1[:], ALU.mult, ["h1", "G1"], ["h1"])
            k.tt("pool", h1[:], h1[:], B1[:], ALU.add, ["h1", "B1"], ["h1"])
            k.cp("pool", h1_bf[:], h1[:], ["h1"], ["h1_bf"])
            tb = bank_bf(0)
            for kk in range(8):
                k.tr(tb[:, kk * 128:(kk + 1) * 128], h1_bf[:, kk * 128:(kk + 1) * 128], ident[:], ["h1_bf", "ident"], ["bank0"])
            HR = "h1T%d" % bi
            k.cp("act", h1T[bi][:], tb[:, 0:1024].rearrange("p (a b) -> p a b", b=128), ["bank0"], [HR])
            k.dma("sp", AP(h1T_d, t0, [[S, 128], [128 * S, 8], [1, 128]]), h1T[bi][:], [HR], ["h1T_d"], semkey=HR)
            for n in range(2):
                b = 3 + n
                for kk in range(8):
                    k.mm(bank(b), h1T[bi][:, kk, :], wpg_sb[:, kk, 512 * n:512 * (n + 1)], kk == 0, kk == 7,
                         [HR, "wpg"], ["bank%d" % b])
                k.stt(tg[:, 512 * n:512 * (n + 1)], bank(b), 0.5, HB[:, 512 * n:512 * (n + 1)], ALU.mult, ALU.add,
                      ["bank%d" % b, "HB"], ["tg"])
            k.act(tg[:], tg[:], AF.Tanh, ["tg"], ["tg"])
            k.cp("pool", p_bf[:], pbuf[bi][:], [PR], ["p_bf"])
            tb2 = bank_bf(7)
            for kk in range(2):
                k.tr(tb2[:, kk * 128:(kk + 1) * 128], p_bf[:, kk * 128:(kk + 1) * 128], ident[:], ["p_bf", "ident"], ["bank7"])
            k.cp("act", pT[:], tb2[:, 0:256].rearrange("p (a b) -> p a b", b=128), ["bank7"], ["pT"])
            RR = "r2_%d" % bi
            for n in range(2):
                b = 5 + n
                for kk in range(2):
                    k.mm(bank(b), pT[:, kk, :], wpl_sb[:, kk, 512 * n:512 * (n + 1)], kk == 0, kk == 1,
                         ["pT", "wpl"], ["bank%d" % b])
                k.stt(pl2[:, 512 * n:512 * (n + 1)], tg[:, 512 * n:512 * (n + 1)], 1.0, bank(b), ALU.add, ALU.mult,
                      ["tg", "bank%d" % b], ["pl2"])
            k.stt(r2[bi][:], h1[:], 2.0 * ALPHA, pl2[:], ALU.mult, ALU.add, ["h1", "pl2"], [RR])
            k.dma("sp", r_d[t0:t0 + 128, :], r2[bi][:], [RR], ["r_d"], semkey=RR)
        k.final_wait(["r2_0", "r2_1", "h1T0", "h1T1"])

    def phase4(self, wup, fcw, fcb, wdn, ln2g, ln2b, r_d, h1T_d, out):
        k = self
        nc = self.nc
        bank, bank_bf, ident, ps = k.bank, k.bank_bf, k.ident, k.ps
        wup_sb = k.sb("wup_sb", [128, 8, 2 * DFF], BF16)
        wdn_sb = k.sb("wdn_sb", [128, 22, D], BF16)
        fcw_sb = k.sb("fcw_sb", [128, 44, 3], F32)
        fcb_sb = k.sb("fcb_sb", [128, 44], F32)
        G2 = k.sb("G2", [128, D], F32)
        B2 = k.sb("B2", [128, D], F32)
        h1T = k.sb("h1T", [128, 8, 512], BF16)
        actT = k.sb("actT", [128, 22, 512], BF16)
        gbuf = [k.sb("gbuf%d" % i, [128, 514], F32) for i in range(2)]
        abuf = [k.sb("abuf%d" % i, [128, 512], F32) for i in range(2)]
        sg = k.sb("sg", [128, 512], F32)
        halo = k.sb("halo", [128, 44, 2], F32)
        r2 = k.sb("r2", [128, D], F32)
        yb = [k.sb("yb%d" % i, [128, D], F32) for i in range(2)]
        k.alloc_lnst("y")
        k.guard()
        for kk in range(8):
            k.dma("pool", wup_sb[:, kk, :], wup[kk * 128:(kk + 1) * 128, :], [], ["wup"], semkey="wup", nodep=True)
        WUR = ["wup"]
        for c in range(22):
            k.dma("pool", wdn_sb[:, c, :], wdn[c * 128:(c + 1) * 128, :], [], ["wdn"], semkey="wdn", nodep=True)
        WDR = ["wdn"]
        k.dma("sp", fcw_sb[:], fcw[:], [], ["fcw"])
        k.dma("sp", fcb_sb[:], fcb[:], [], ["fcb"])
        k.dma("sp", G2[:], AP(ln2g, 0, [[0, 128], [1, D]]), [], ["G2"])
        k.dma("sp", B2[:], AP(ln2b, 0, [[0, 128], [1, D]]), [], ["B2"])
        k.memset("pool", halo[:], 0.0, [], ["halo"])
        ring = {"i": 0}

        def nb():
            b = ring["i"] % 4
            ring["i"] += 1
            return b

        for st in range(8):
            T0 = st * 512
            k.dma("sp", h1T[:], AP(h1T_d, T0, [[S, 128], [128 * S, 8], [1, 512]]), [], ["h1T"], semkey="h1T")
            for c in range(22):
                res_ab = []
                for half in range(2):
                    ch = c + 22 * half
                    b = nb()
                    for kk in range(8):
                        k.mm(bank(b), wup_sb[:, kk, 128 * ch:128 * (ch + 1)], h1T[:, kk, :], kk == 0, kk == 7,
                             ["h1T"] + WUR, ["bank%d" % b])
                    gb, ab = gbuf[half], abuf[half]
                    GR, AR = "gbuf%d" % half, "abuf%d" % half
                    k.cp("pool", gb[:, 0:2], halo[:, ch, :], ["halo"], [GR])
                    k.cp("act", gb[:, 2:514], bank(b), ["bank%d" % b], [GR])
                    k.cp("pool", halo[:, ch, :], gb[:, 512:514], [GR], ["halo"])
                    k.act(ab[:], gb[:, 2:514], AF.Identity, [GR, "fcw", "fcb"], [AR],
                          scale=fcw_sb[:, ch, 2:3], bias=fcb_sb[:, ch:ch + 1])
                    k.stt(ab[:], gb[:, 1:513], fcw_sb[:, ch, 1:2], ab[:], ALU.mult, ALU.add, [GR, AR, "fcw"], [AR])
                    k.stt(ab[:], gb[:, 0:512], fcw_sb[:, ch, 0:1], ab[:], ALU.mult, ALU.add, [GR, AR, "fcw"], [AR])
                k.act(sg[:], abuf[0][:], AF.Silu, ["abuf0"], ["sg"])
                k.tt("pool", actT[:, c, :], sg[:], abuf[1][:], ALU.mult, ["sg", "abuf1"], ["actT"])
            for tt in range(4):
                t0 = T0 + tt * 128
                k.dma("sp", r2[:], r_d[t0:t0 + 128, :], [], ["r2"], semkey="r2")
                yi = (st * 4 + tt) % 2
                YR = "yb%d" % yi
                yv = yb[yi]
                for n in range(2):
                    b = 4 + 2 * (tt % 2) + n
                    for c in range(22):
                        k.mm(bank(b), actT[:, c, tt * 128:(tt + 1) * 128], wdn_sb[:, c, 512 * n:512 * (n + 1)],
                             c == 0, c == 21, ["actT"] + WDR, ["bank%d" % b])
                    k.stt(yv[:, 512 * n:512 * (n + 1)], r2[:, 512 * n:512 * (n + 1)], 0.5, bank(b), ALU.mult, ALU.add,
                          ["r2", "bank%d" % b], [YR])
                rstd, nmr = k.ln_stats(yv, "y", [YR], 2, 512)
                k.act(yv[:], yv[:], AF.Identity, ["lnst_y", YR], [YR], scale=rstd, bias=nmr)
                k.tt("pool", yv[:], yv[:], G2[:], ALU.mult, [YR, "G2"], [YR])
                k.tt("pool", yv[:], yv[:], B2[:], ALU.add, [YR, "B2"], [YR])
                k.dma("sp", out[t0:t0 + 128, :], yv[:], [YR], ["out_d"], semkey=YR)
        k.final_wait(["yb0", "yb1"])

    def final_wait(self, names):
        nc = self.nc
        self.op("sp", lambda: nc.sync.nop(), names, names)


def _prep_inputs(inputs, b):
    f = lambda a: np.ascontiguousarray(np.asarray(a, dtype=np.float32))
    m = {}
    m["x"] = f(inputs["x"][b])
    m["p"] = f(inputs["p"][0, b])
    m["w_in"] = f(inputs["w_in"][0])
    m["lng_fm"] = f(np.asarray(inputs["ln_emb_g"]).reshape(8, 128).T)
    m["lnb_fm"] = f(np.asarray(inputs["ln_emb_b"]).reshape(8, 128).T)
    m["lng"] = f(np.asarray(inputs["ln_emb_g"]).reshape(1, D))
    m["lnb"] = f(np.asarray(inputs["ln_emb_b"]).reshape(1, D))
    m["bgate"] = f(np.asarray(inputs["b_gate"][0]).reshape(2, 8, 128).transpose(2, 0, 1).reshape(128, 16))
    m["kvg"] = f(np.asarray(inputs["kv_norm_g"][0]).reshape(1, 128))
    wuk = np.asarray(inputs["w_uk"][0])
    m["wukT"] = f(wuk.reshape(4, 2, 128, 64).transpose(1, 3, 0, 2).reshape(128, 4, 128))
    wuv = np.asarray(inputs["w_uv"][0])
    m["wuvr"] = f(wuv.transpose(1, 0, 2).reshape(128, 512))
    m["kig"] = f(np.asarray(inputs["k_idx_ln_g"][0]).reshape(1, 64))
    m["kib"] = f(np.asarray(inputs["k_idx_ln_b"][0]).reshape(1, 64))
    m["mcw"] = f(np.asarray(inputs["mix_conv_w"][0]).reshape(3, 4, 128).transpose(2, 1, 0))
    m["mcb"] = f(np.asarray(inputs["mix_conv_b"][0]).reshape(4, 128).T)
    m["wbra"] = f(inputs["w_br_att"][0])
    m["wbrc"] = f(inputs["w_br_conv"][0])
    m["wo"] = f(inputs["w_o"][0])
    m["ln1g"] = f(np.asarray(inputs["ln1_g"][0]).reshape(1, D))
    m["ln1b"] = f(np.asarray(inputs["ln1_b"][0]).reshape(1, D))
    m["wup"] = f(inputs["w_ffn_up"][0])
    m["fcw"] = f(np.asarray(inputs["ffn_conv_w"][0]).reshape(3, 44, 128).transpose(2, 1, 0))
    m["fcb"] = f(np.asarray(inputs["ffn_conv_b"][0]).reshape(44, 128).T)
    m["wdn"] = f(inputs["w_ffn_down"][0])
    m["wpg"] = f(inputs["w_ple_gate"][0])
    m["bpg"] = f(np.asarray(inputs["b_ple_gate"][0]).reshape(1, D))
    m["wple"] = f(inputs["w_ple"][0])
    m["ln2g"] = f(np.asarray(inputs["ln2_g"][0]).reshape(1, D))
    m["ln2b"] = f(np.asarray(inputs["ln2_b"][0]).reshape(1, D))
    return m


def kernel(**inputs):
    kern = Kern()
    nc = kern.build()
    in_maps = [_prep_inputs(inputs, b) for b in range(NCORES)]
    res = run_bass_kernel_spmd(nc, in_maps, core_ids=list(range(NCORES)))
    return np.stack([np.asarray(r["out"], dtype=np.float32) for r in res.results], axis=0)
```

```python
import numpy as np
from contextlib import ExitStack
import concourse.bass as bass
import concourse.mybir as mybir
from concourse.bass_utils import run_bass_kernel_spmd

F32 = mybir.dt.float32
BF16 = mybir.dt.bfloat16
ALU = mybir.AluOpType
AF = mybir.ActivationFunctionType
AX = mybir.AxisListType

S = 4096
D = 1024
NCORES = 8
IN_W = 4808
DFF = 2816
LN_EPS = 1e-5
ALPHA = 2.0 ** 0.25
TOPK = 256
NEG = -1.0e30
N_BISECT = 16
C_QATT, C_CKV, C_QIDX, C_KIDX, C_WIDX, C_CVB, C_CVC, C_CVX, C_GATT, C_GCONV = (
    0, 512, 640, 1152, 1216, 1224, 1736, 2248, 2760, 3784)
W1C = 1224
W2C = IN_W - W1C
CW = (64 ** -0.5) * (8 ** -0.5)


class Res:
    __slots__ = ("name", "last_w", "readers")

    def __init__(self, name):
        self.name = name
        self.last_w = None
        self.readers = []


class Op:
    __slots__ = ("eng", "fn", "deps", "alldeps", "orderdeps", "signal", "sem", "ticket", "is_dma", "idx", "eidx",
                 "semkey", "cost", "lat", "start")


class Sched:
    NSEM = 0
    ENGS = ("pe", "act", "dve", "pool", "sp")

    def __init__(self, nc, es, reorder=True):
        self.nc = nc
        self.es = es
        self.ops = []
        self.last_dma = {}
        self.reorder = reorder
        self.engs = {"pe": nc.tensor, "act": nc.scalar, "dve": nc.vector, "pool": nc.gpsimd, "sp": nc.sync}

    def add(self, eng, fn, reads=(), writes=(), dma=False, semkey=None, nodep=False, cost=200.0, lat=0.0):
        op = Op()
        op.eng = eng
        op.fn = fn
        op.is_dma = dma
        op.signal = False
        op.sem = None
        op.ticket = 0
        op.idx = len(self.ops)
        op.eidx = 0
        op.semkey = semkey
        op.cost = cost
        op.lat = lat
        op.start = 0.0
        deps = {}
        for r in reads:
            if r.last_w is not None:
                deps[r.last_w.idx] = r.last_w
        for w in writes:
            if w.last_w is not None:
                deps[w.last_w.idx] = w.last_w
            for rd in w.readers:
                deps[rd.idx] = rd
        deps.pop(op.idx, None)
        if nodep:
            deps = {}
        for r in reads:
            r.readers.append(op)
        for w in writes:
            w.last_w = op
            w.readers = []
        op.alldeps = list(deps.values())
        op.orderdeps = []
        if dma and nodep:
            prev = self.last_dma.get((eng, semkey))
            if prev is not None:
                op.orderdeps.append(prev)
            self.last_dma[(eng, semkey)] = op
        self.ops.append(op)
        return op

    def schedule(self):
        import heapq
        ops = self.ops
        n = len(ops)
        succ = [[] for _ in range(n)]
        ndeps = [0] * n
        for op in ops:
            ds = set(d.idx for d in op.alldeps) | set(d.idx for d in op.orderdeps)
            ndeps[op.idx] = len(ds)
            for di in ds:
                succ[di].append(op.idx)
        ready = [0.0] * n
        finish = [0.0] * n
        eng_free = {e: 0.0 for e in self.ENGS}
        future = {e: [] for e in self.ENGS}
        avail = {e: [] for e in self.ENGS}
        for op in ops:
            if ndeps[op.idx] == 0:
                heapq.heappush(future[op.eng], (0.0, op.idx))
        order = []
        XLAT = 150.0
        SLAT = 200.0
        while len(order) < n:
            best = None
            for e in self.ENGS:
                T = eng_free[e]
                fu, av = future[e], avail[e]
                while fu and fu[0][0] <= T:
                    _, i = heapq.heappop(fu)
                    heapq.heappush(av, i)
                if av:
                    st = T
                elif fu:
                    st = fu[0][0]
                else:
                    continue
                if best is None or st < best[0]:
                    best = (st, e)
            st, e = best
            if avail[e]:
                i = heapq.heappop(avail[e])
            else:
                _, i = heapq.heappop(future[e])
            op = ops[i]
            op.start = st
            eng_free[e] = st + op.cost
            finish[i] = st + op.cost + op.lat
            order.append(op)
            for j in succ[i]:
                r_ = finish[i] + (XLAT if ops[j].eng != e or op.is_dma else (0.0 if e == "pe" else SLAT))
                if r_ > ready[j]:
                    ready[j] = r_
                ndeps[j] -= 1
                if ndeps[j] == 0:
                    heapq.heappush(future[ops[j].eng], (ready[j], j))
        self.est_ns = max(finish) if n else 0.0
        self.ops = order

    def emit(self):
        nc = self.nc
        if self.reorder:
            self.schedule()
        ecount = {}
        for op in self.ops:
            op.eidx = ecount.get(op.eng, 0)
            ecount[op.eng] = op.eidx + 1
        for op in self.ops:
            keep = []
            for d in op.alldeps:
                if not d.is_dma and d.eng == op.eng and not op.is_dma:
                    if op.eng == "pe":
                        continue
                    if op.eidx - d.eidx > 3:
                        continue
                keep.append(d)
            op.deps = keep
        for op in self.ops:
            if op.is_dma:
                op.signal = True
            for d in op.deps:
                d.signal = True
        sems = {}
        counts = {}

        def get_sem(key):
            if key not in sems:
                Sched.NSEM += 1
                sems[key] = self.es.enter_context(nc.semaphore("s%d" % Sched.NSEM))
                counts[key] = 0
            return sems[key]

        for op in self.ops:
            if not op.signal:
                continue
            if op.is_dma:
                key = ("dma", op.semkey if op.semkey is not None else op.idx)
                op.sem = get_sem(key)
                counts[key] += 16
                op.ticket = counts[key]
            else:
                key = ("eng", op.eng)
                op.sem = get_sem(key)
                counts[key] += 1
                op.ticket = counts[key]
        waited = {}
        nwait = 0
        for op in self.ops:
            e = self.engs[op.eng]
            need = {}
            for d in op.deps:
                k = id(d.sem)
                if waited.get((op.eng, k), 0) >= d.ticket:
                    continue
                if k not in need or need[k][1] < d.ticket:
                    need[k] = (d.sem, d.ticket)
            for k, (sem, val) in need.items():
                e.wait_ge(sem, val)
                waited[(op.eng, k)] = val
                nwait += 1
            ins = op.fn()
            if op.signal:
                ins.then_inc(op.sem, 16 if op.is_dma else 1)
        self.nsems = len(sems)
        self.nwait = nwait


def fsz(ap):
    n = 1
    for d in ap.shape[1:]:
        n *= int(d)
    return n


def AP(t, off, dims):
    return bass.AP(t, off, [list(d) for d in dims])


class Kern:
    def __init__(self, phases=(1, 2, 3, 4), debug=False, reorder=True):
        self.reorder = reorder
        self.phases = phases
        self.debug = debug
        self.nc = bass.Bass("TRN2", target_bir_lowering=False)
        self.es = ExitStack()
        self.ges = self.es
        self.semcount = 0
        self.dram = {}
        self.res = {}

    def din(self, name, shape, dt=F32):
        t = self.nc.dram_tensor(name, list(shape), dt, kind="ExternalInput")
        self.dram[name] = t
        return t

    def dscr(self, name, shape, dt):
        kind = "ExternalOutput" if self.debug else "Internal"
        t = self.nc.dram_tensor(name, list(shape), dt, kind=kind)
        self.dram[name] = t
        return t

    def sb(self, name, shape, dt):
        nm = "p%d_%s" % (getattr(self, "nphase", 0), name)
        return self.es.enter_context(self.nc.sbuf_tensor(nm, list(shape), dt))

    def guard(self, kb=8):
        with self.nc.sbuf_tensor("guard%d" % self.nphase, [128, kb * 256], F32):
            pass

    def R(self, name):
        if name not in self.res:
            self.res[name] = Res(name)
        return self.res[name]

    def Rs(self, *names):
        return [self.R(n) for n in names]

    def op(self, eng, fn, r=(), w=(), dma=False, semkey=None, nodep=False, cost=200.0, lat=0.0):
        return self.sc.add(eng, fn, [self.R(x) if isinstance(x, str) else x for x in r],
                           [self.R(x) if isinstance(x, str) else x for x in w], dma=dma, semkey=semkey, nodep=nodep,
                           cost=cost, lat=lat)

    def dma(self, q, out, in_, r=(), w=(), semkey=None, nodep=False):
        e = {"sp": self.nc.sync, "pool": self.nc.gpsimd, "act": self.nc.scalar}[q]
        nbytes = fsz(out) * int(out.shape[0]) * 4
        return self.op(q, lambda: e.dma_start(out=out, in_=in_), r, w, dma=True, semkey=semkey, nodep=nodep,
                       cost=(600.0 if q == "pool" else 100.0), lat=2500.0 + nbytes / 250.0)

    def mm(self, out, lhsT, rhs, start, stop, r=(), w=()):
        nc = self.nc
        return self.op("pe", lambda: nc.tensor.matmul(out, lhsT=lhsT, rhs=rhs, start=start, stop=stop), r, w,
                       cost=64.0 + 0.5 * fsz(rhs))

    def tr(self, out, in_, ident, r=(), w=()):
        nc = self.nc
        return self.op("pe", lambda: nc.tensor.transpose(out, in_, ident), r, w, cost=130.0)

    def act(self, out, in_, func, r=(), w=(), **kw):
        nc = self.nc
        return self.op("act", lambda: nc.scalar.activation(out=out, in_=in_, func=func, **kw), r, w,
                       cost=200.0 + 0.85 * fsz(out))

    def ts(self, eng, out, in0, s1, s2, op0, op1=None, r=(), w=(), accum_out=None):
        e = self.nc.vector if eng == "dve" else self.nc.gpsimd
        c = 70.0 + 1.05 * fsz(out)
        if op1 is None:
            return self.op(eng, lambda: e.tensor_scalar(out=out, in0=in0, scalar1=s1, scalar2=None, op0=op0), r, w, cost=c)
        if accum_out is not None:
            return self.op(eng, lambda: e.tensor_scalar(out=out, in0=in0, scalar1=s1, scalar2=s2, op0=op0, op1=op1,
                                                        accum_out=accum_out), r, w, cost=c)
        return self.op(eng, lambda: e.tensor_scalar(out=out, in0=in0, scalar1=s1, scalar2=s2, op0=op0, op1=op1), r, w, cost=c)

    def tt(self, eng, out, in0, in1, op, r=(), w=()):
        e = self.nc.vector if eng == "dve" else self.nc.gpsimd
        c = (70.0 + 1.05 * fsz(out)) if eng == "dve" else (150.0 + 2.0 * fsz(out))
        return self.op(eng, lambda: e.tensor_tensor(out=out, in0=in0, in1=in1, op=op), r, w, cost=c)

    def stt(self, out, in0, scalar, in1, op0, op1, r=(), w=()):
        nc = self.nc
        return self.op("dve", lambda: nc.vector.scalar_tensor_tensor(out=out, in0=in0, scalar=scalar, in1=in1,
                                                                      op0=op0, op1=op1), r, w, cost=70.0 + 1.05 * fsz(out))

    def cp(self, eng, out, in_, r=(), w=()):
        nc = self.nc
        if eng == "act":
            return self.op("act", lambda: nc.scalar.copy(out=out, in_=in_), r, w, cost=200.0 + 0.85 * fsz(out))
        e = nc.vector if eng == "dve" else nc.gpsimd
        c = (70.0 + 1.05 * fsz(out)) if eng == "dve" else (150.0 + 1.1 * fsz(out))
        return self.op(eng, lambda: e.tensor_copy(out=out, in_=in_), r, w, cost=c)

    def memset(self, eng, ap, val, r=(), w=()):
        e = self.nc.vector if eng == "dve" else self.nc.gpsimd
        return self.op(eng, lambda: e.memset(ap, val), r, w, cost=100.0 + 1.0 * fsz(ap))

    def ln_stats(self, src, tag, r, nchunk, width):
        nc = self.nc
        st = self.lnst[tag]
        stats, mv, rs = st
        resn = "lnst_" + tag
        for c in range(nchunk):
            self.op("dve", (lambda c=c: nc.vector.bn_stats(out=stats[:, 6 * c:6 * c + 6],
                                                           in_=src[:, c * width:(c + 1) * width])), r, [resn],
                    cost=100.0 + 1.1 * width)
        self.op("dve", lambda: nc.vector.bn_aggr(out=mv[:, 0:2], in_=stats[:, 0:6 * nchunk]), [resn], [resn])
        self.ts("dve", rs[:, 0:1], mv[:, 1:2], LN_EPS, None, ALU.add, None, [resn], [resn])
        self.act(rs[:, 0:1], rs[:, 0:1], AF.Ln, [resn], [resn])
        self.act(rs[:, 0:1], rs[:, 0:1], AF.Exp, [resn], [resn], scale=-0.5)
        self.ts("dve", rs[:, 1:2], mv[:, 0:1], -1.0, rs[:, 0:1], ALU.mult, ALU.mult, [resn], [resn])
        return rs[:, 0:1], rs[:, 1:2]

    def alloc_lnst(self, tag):
        if not hasattr(self, "lnst"):
            self.lnst = {}
        self.lnst[tag] = (self.sb("lnstats_" + tag, [128, 12], F32), self.sb("lnmv_" + tag, [128, 2], F32),
                          self.sb("lnrs_" + tag, [128, 2], F32))

    def build(self):
        nc = self.nc
        k = self
        x = k.din("x", [S, D])
        p = k.din("p", [S, 256])
        w_in = k.din("w_in", [D, IN_W])
        lng_fm = k.din("lng_fm", [128, 8])
        lnb_fm = k.din("lnb_fm", [128, 8])
        lng = k.din("lng", [1, D])
        lnb = k.din("lnb", [1, D])
        bgate = k.din("bgate", [128, 16])
        kvg = k.din("kvg", [1, 128])
        wukT = k.din("wukT", [128, 4, 128])
        wuvr = k.din("wuvr", [128, 512])
        kig = k.din("kig", [1, 64])
        kib = k.din("kib", [1, 64])
        mcw = k.din("mcw", [128, 4, 3])
        mcb = k.din("mcb", [128, 4])
        wbra = k.din("wbra", [512, D])
        wbrc = k.din("wbrc", [512, D])
        wo = k.din("wo", [D, D])
        ln1g = k.din("ln1g", [1, D])
        ln1b = k.din("ln1b", [1, D])
        wup = k.din("wup", [D, 2 * DFF])
        fcw = k.din("fcw", [128, 44, 3])
        fcb = k.din("fcb", [128, 44])
        wdn = k.din("wdn", [DFF, D])
        wpg = k.din("wpg", [D, D])
        bpg = k.din("bpg", [1, D])
        wple = k.din("wple", [256, D])
        ln2g = k.din("ln2g", [1, D])
        ln2b = k.din("ln2b", [1, D])
        out = nc.dram_tensor("out", [S, D], F32, kind="ExternalOutput")
        k.dram["out"] = out
        attT_d = k.dscr("attT_d", [512, S], BF16)
        mrgT_d = k.dscr("mrgT_d", [D, S], BF16)
        r_d = k.dscr("r_d", [S, D], F32)
        h1T_d = k.dscr("h1T_d", [D, S], BF16)

        ps = k.es.enter_context(nc.psum_tensor("ps", [128, 4096], F32))
        k.ps = ps

        def bank(b, n=512, off=0, parts=128):
            return ps[0:parts, b * 512 + off: b * 512 + off + n]

        def bank_bf(b):
            return ps[:, b * 512:(b + 1) * 512].bitcast(BF16)

        k.bank = bank
        k.bank_bf = bank_bf

        k.bar_tile = k.sb("bar_tile", [128, 8], F32)
        k.bar_bf = k.sb("bar_bf", [128, 8], BF16)
        ident = k.sb("ident", [128, 128], BF16)
        k.ident = ident
        k.nphase = 0
        with ExitStack() as pes:
            k.begin_phase(pes)
            k.memset("pool", ident[:], 0.0, [], ["ident"])
            k.op("pool", lambda: nc.gpsimd.affine_select(out=ident[:], in_=ident[:], pattern=[[-1, 128]],
                                                          compare_op=ALU.not_equal, fill=1.0, base=0,
                                                          channel_multiplier=1), ["ident"], ["ident"])
            k.memset("pool", k.bar_bf[:], 0.0, [], ["bar_bf"])
            k.end_phase()
        if 1 in k.phases:
            with ExitStack() as pes:
                k.begin_phase(pes)
                k.phase1(x, w_in, lng_fm, lnb_fm, kvg, wukT, wuvr, kig, kib, attT_d)
                k.end_phase()
        if 2 in k.phases:
            with ExitStack() as pes:
                k.begin_phase(pes)
                k.phase2(x, w_in, lng_fm, lnb_fm, bgate, mcw, mcb, wbra, wbrc, attT_d, mrgT_d)
                k.end_phase()
        if 3 in k.phases:
            with ExitStack() as pes:
                k.begin_phase(pes)
                k.phase3(x, p, lng, lnb, wo, ln1g, ln1b, wpg, bpg, wple, mrgT_d, r_d, h1T_d)
                k.end_phase()
        if 4 in k.phases:
            with ExitStack() as pes:
                k.begin_phase(pes)
                k.phase4(wup, fcw, fcb, wdn, ln2g, ln2b, r_d, h1T_d, out)
                k.end_phase()
        return nc

    def begin_phase(self, pes):
        self.es = pes
        self.sc = Sched(self.nc, self.ges, reorder=self.reorder)
        self.res = {}
        self.lnst = {}

    def end_phase(self):
        nc = self.nc
        k = self
        self.sc.emit()
        bar = self.ges.enter_context(nc.semaphore("bar%d" % self.nphase))
        self.nphase += 1
        nc.vector.memset(k.bar_tile[:, 0:1], 0.0).then_inc(bar, 1)
        nc.gpsimd.memset(k.bar_tile[:, 1:2], 0.0).then_inc(bar, 1)
        nc.scalar.copy(out=k.bar_tile[:, 2:3], in_=k.bar_tile[:, 3:4]).then_inc(bar, 1)
        nc.tensor.matmul(k.ps[0:8, 0:8], lhsT=k.bar_bf[:, 0:8], rhs=k.bar_bf[:, 0:8], start=True, stop=True).then_inc(bar, 1)
        nc.sync.nop().then_inc(bar, 1)
        for e in (nc.vector, nc.gpsimd, nc.scalar, nc.tensor, nc.sync):
            e.wait_ge(bar, 5)

    def phase1(self, x, w_in, lng_fm, lnb_fm, kvg, wukT, wuvr, kig, kib, attT_d):
        k = self
        nc = self.nc
        bank, bank_bf, ident, ps = k.bank, k.bank_bf, k.ident, k.ps
        w1 = k.sb("w1", [128, 8, W1C], BF16)
        g_fm = k.sb("g_fm", [128, 8], F32)
        b_fm = k.sb("b_fm", [128, 8], F32)
        wuk_sb = k.sb("wuk_sb", [128, 4, 128], BF16)
        wuv_sb = k.sb("wuv_sb", [128, 512], BF16)
        kvg_bc = k.sb("kvg_bc", [128, 128], F32)
        kig_bc = k.sb("kig_bc", [128, 64], F32)
        kib_bc = k.sb("kib_bc", [128, 64], F32)
        negm = k.sb("negm", [128, 128], F32)
        pow2 = k.sb("pow2", [128, N_BISECT], F32)
        kT2 = k.sb("kT2", [128, S], BF16)
        ckvT = k.sb("ckvT", [128, S], BF16)
        vext = k.sb("vext", [128, 32, 8, 65], BF16)
        xbuf = [k.sb("xbuf%d" % i, [128, D], F32) for i in range(2)]
        xn_bf = k.sb("xn_bf", [128, D], BF16)
        hT2 = [k.sb("hT0", [128, 8, 512], BF16)] * 2
        qattT = k.sb("qattT", [128, 4, 512], BF16)
        qlatT2 = [k.sb("qlatT%d" % i, [128, 8, 512], BF16) for i in range(2)]
        qidxT2 = [k.sb("qidxT%d" % i, [128, 4, 512], BF16) for i in range(2)]
        absw42 = [k.sb("absw4%d" % i, [128, 4, 8], F32) for i in range(2)]
        sgn42 = [k.sb("sgn4%d" % i, [128, 4, 8], F32) for i in range(2)]
        dsgn2 = [k.sb("dsgn%d" % i, [128, 8, 128], BF16) for i in range(2)]
        relu_sb = [k.sb("relu%d" % i, [128, 512], BF16) for i in range(3)]
        score2 = [k.sb("score%d" % i, [128, S], F32) for i in range(2)]
        mask012 = [k.sb("mask01%d" % i, [128, S], BF16) for i in range(2)]
        maskT = k.sb("maskT", [128, 32, 128], BF16)
        PT = [k.sb("PT%d" % i, [128, 4, 128], BF16) for i in range(4)]
        ckv_tm = k.sb("ckv_tm", [128, 128], BF16)
        craw = k.sb("craw", [128, 128], F32)
        craw2 = k.sb("craw2", [128, 128], F32)
        kn_f = k.sb("kn_f", [128, 64], F32)
        kn2 = k.sb("kn2", [128, 128], BF16)
        sm = k.sb("sm", [128, 16], F32)
        wks = k.sb("wks", [128, N_BISECT], F32)
        bis = k.sb("bis", [128, 8], F32)
        pv_sb2 = [k.sb("pv_sb0", [65, 1024], F32)] * 2
        rden2 = [k.sb("rden0", [65, 1024], F32)] * 2
        ones_r = k.sb("ones_r", [65, 64], F32)
        att_n = [k.sb("att_n%d" % i, [64, 8, 128], BF16) for i in range(2)]
        k.alloc_lnst("x")
        k.alloc_lnst("k")
        k.guard()

        for kk in range(8):
            k.dma("pool", w1[:, kk, :], w_in[kk * 128:(kk + 1) * 128, 0:W1C], [], ["w1"], semkey="w1", nodep=True)
        W1R = ["w1"]
        k.dma("sp", g_fm[:], lng_fm[:], [], ["g_fm"])
        k.dma("sp", b_fm[:], lnb_fm[:], [], ["b_fm"])
        k.dma("pool", wuk_sb[:], wukT[:], [], ["wuk"])
        k.dma("pool", wuv_sb[:], wuvr[:], [], ["wuv"])
        k.dma("sp", kvg_bc[:], AP(kvg, 0, [[0, 128], [1, 128]]), [], ["kvg_bc"])
        k.dma("sp", kig_bc[:], AP(kig, 0, [[0, 128], [1, 64]]), [], ["kig_bc"])
        k.dma("sp", kib_bc[:], AP(kib, 0, [[0, 128], [1, 64]]), [], ["kib_bc"])
        k.memset("pool", negm[:], 0.0, [], ["negm"])
        k.op("pool", lambda: nc.gpsimd.affine_select(out=negm[:], in_=negm[:], pattern=[[-1, 128]],
                                                      compare_op=ALU.is_ge, fill=NEG, base=0,
                                                      channel_multiplier=1), ["negm"], ["negm"])
        for i in range(N_BISECT):
            k.memset("pool", pow2[:, i:i + 1], 2.0 ** (-(i + 1)), [], ["pow2"])
        k.memset("pool", vext[:, :, :, 64:65], 1.0, [], ["vext_ones"])
        k.memset("pool", ones_r[:], 1.0, [], ["ones_r"])

        BT, BR0, BR1, BSC, BL0, BL1, BV0, BV1 = 0, 1, 2, 3, 4, 5, 6, 7
        ring = [BR0, BR1]
        rstate = {"i": 0, "relu": 0, "pt": 0, "lg": 0, "xb": 0, "an": 0}

        def next_ring():
            b = ring[rstate["i"] % 2]
            rstate["i"] += 1
            return b

        for st in range(8):
            T0 = st * 512
            sp_ = st % 2
            hT, qlatT, qidxT, absw4, sgn4 = hT2[sp_], qlatT2[sp_], qidxT2[sp_], absw42[sp_], sgn42[sp_]
            HT, QL, QI, AW, SG = "hT0", "qlatT%d" % sp_, "qidxT%d" % sp_, "absw4%d" % sp_, "sgn4%d" % sp_
            for tt in range(4):
                t0 = T0 + tt * 128
                xb_i = rstate["xb"] % 2
                rstate["xb"] += 1
                xb = xbuf[xb_i]
                XR = "xbuf%d" % xb_i
                k.dma("sp", xb[:], x[t0:t0 + 128, :], [], [XR], semkey=XR)
                rstd, nmr = k.ln_stats(xb, "x", [XR], 2, 512)
                k.act(xn_bf[:], xb[:], AF.Identity, ["lnst_x", XR], ["xn_bf"], scale=rstd, bias=nmr)
                tb = bank_bf(BT)
                for kk in range(8):
                    k.tr(tb[:, kk * 128:(kk + 1) * 128], xn_bf[:, kk * 128:(kk + 1) * 128], ident[:],
                         ["xn_bf", "ident"], ["bank0", "bank0b"])
                for kk in range(8):
                    k.act(hT[:, kk, tt * 128:(tt + 1) * 128], tb[:, kk * 128:(kk + 1) * 128], AF.Identity,
                          ["bank0", "bank0b", "g_fm", "b_fm"], [HT], scale=g_fm[:, kk:kk + 1], bias=b_fm[:, kk:kk + 1])
            for j in range(4):
                b = next_ring()
                for kk in range(8):
                    k.mm(bank(b), w1[:, kk, C_QATT + 128 * j:C_QATT + 128 * (j + 1)], hT[:, kk, :], kk == 0, kk == 7,
                         [HT] + W1R, ["bank%d" % b])
                k.cp("dve", qattT[:, j, :], bank(b), ["bank%d" % b], ["qattT"])
            for j in range(4):
                b = next_ring()
                for kk in range(8):
                    k.mm(bank(b), w1[:, kk, C_QIDX + 128 * j:C_QIDX + 128 * (j + 1)], hT[:, kk, :], kk == 0, kk == 7,
                         [HT] + W1R, ["bank%d" % b])
                k.cp("act", qidxT[:, j, :], bank(b), ["bank%d" % b], [QI])
            for tt in range(4):
                blk = st * 4 + tt
                bck_, bkw_ = next_ring(), next_ring()
                PCK, PKW = "bank%d" % bck_, "bank%d" % bkw_
                pck = bank(bck_, 128, 0)
                pkw = bank(bkw_, 72, 0)
                for kk in range(8):
                    k.mm(pck, hT[:, kk, tt * 128:(tt + 1) * 128], w1[:, kk, C_CKV:C_CKV + 128], kk == 0, kk == 7,
                         [HT] + W1R, [PCK])
                for kk in range(8):
                    k.mm(pkw, hT[:, kk, tt * 128:(tt + 1) * 128], w1[:, kk, C_KIDX:C_KIDX + 72], kk == 0, kk == 7,
                         [HT] + W1R, [PKW])
                k.cp("act", craw[:], pck, [PCK], ["craw"])
                k.op("dve", lambda: nc.vector.scalar_tensor_tensor(out=craw2[:], in0=craw[:], scalar=1.0, in1=craw[:],
                                                                    op0=ALU.mult, op1=ALU.mult, accum_out=sm[:, 0:1]),
                     ["craw"], ["craw2", "sm_c"])
                k.ts("dve", sm[:, 1:2], sm[:, 0:1], 1.0 / 128.0, LN_EPS, ALU.mult, ALU.add, ["sm_c"], ["sm_c"])
                k.act(sm[:, 2:3], sm[:, 1:2], AF.Ln, ["sm_c"], ["sm_c"])
                k.act(sm[:, 2:3], sm[:, 2:3], AF.Exp, ["sm_c"], ["sm_c"], scale=-0.5)
                k.stt(ckv_tm[:], craw[:], sm[:, 2:3], kvg_bc[:], ALU.mult, ALU.mult, ["craw", "sm_c", "kvg_bc"], ["ckv_tm"])
                rstd_k, nmr_k = k.ln_stats(pkw, "k", [PKW], 1, 64)
                k.act(kn_f[:], pkw[:, 0:64], AF.Identity, [PKW, "lnst_k"], ["kn_f"], scale=rstd_k, bias=nmr_k)
                k.tt("pool", kn_f[:], kn_f[:], kig_bc[:], ALU.mult, ["kn_f", "kig_bc"], ["kn_f"])
                k.tt("pool", kn2[:, 0:64], kn_f[:], kib_bc[:], ALU.add, ["kn_f", "kib_bc"], ["kn2"])
                k.tt("pool", kn2[:, 64:128], kn_f[:], kib_bc[:], ALU.add, ["kn_f", "kib_bc"], ["kn2"])
                k.act(sm[:, 8:16], pkw[:, 64:72], AF.Copy, [PKW], ["sm_w"], scale=CW)
                k.stt(absw4[:, tt, :], sm[:, 8:16], -1.0, sm[:, 8:16], ALU.mult, ALU.max, ["sm_w"], [AW])
                k.act(sgn4[:, tt, :], pkw[:, 64:72], AF.Sign, [PKW], [SG])
                tb = bank_bf(BT)
                k.tr(tb[:, 512:640], ckv_tm[:], ident[:], ["ckv_tm", "ident"], ["bank0b"])
                k.tr(tb[:, 640:768], kn2[:], ident[:], ["kn2", "ident"], ["bank0b"])
                k.cp("act", ckvT[:, blk * 128:(blk + 1) * 128], tb[:, 512:640], ["bank0b"], ["ckvT"])
                k.cp("act", kT2[:, blk * 128:(blk + 1) * 128], tb[:, 640:768], ["bank0b"], ["kT2"])
                b = next_ring()
                k.mm(bank(b), ckvT[:, blk * 128:(blk + 1) * 128], wuv_sb[:], True, True, ["ckvT", "wuv"], ["bank%d" % b])
                k.cp("dve", vext[:, blk, :, 0:64], bank(b).rearrange("p (h d) -> p h d", h=8), ["bank%d" % b], ["vext"])
            for h in range(8):
                e, j = h % 2, h // 2
                b = next_ring()
                k.mm(bank(b), wuk_sb[64 * e:64 * e + 64, j, :], qattT[64 * e:64 * e + 64, j, :], True, True,
                     ["qattT", "wuk"], ["bank%d" % b])
                k.act(qlatT[:, h, :], bank(b), AF.Copy, ["bank%d" % b], [QL], scale=0.125)
            for i in range(4):
                I = st * 4 + i
                nk = 128 * (I + 1)
                q0 = i * 128
                ip_ = I % 2
                score, mask01, dsgn, pv_sb, rden = score2[ip_], mask012[ip_], dsgn2[ip_], pv_sb2[ip_], rden2[ip_]
                junk = mask01
                SC, MK, DS, PVS, RD = "score%d" % ip_, "mask01%d" % ip_, "dsgn%d" % ip_, "pv_sb0", "rden0"
                for h in range(8):
                    k.ts("dve", dsgn[:, h, :], ident[:], sgn4[:, i, h:h + 1], None, ALU.mult, None,
                         ["ident", SG], [DS])
                nkb = (nk + 511) // 512
                for kb in range(nkb):
                    wk = min(512, nk - 512 * kb)
                    for h in range(8):
                        e, j = h % 2, h // 2
                        b = next_ring()
                        k.mm(bank(b, wk), qidxT[64 * e:64 * e + 64, j, q0:q0 + 128],
                             kT2[64 * e:64 * e + 64, 512 * kb:512 * kb + wk], True, True,
                             [QI, "kT2"], ["bank%d" % b])
                        ri = rstate["relu"] % 3
                        rstate["relu"] += 1
                        k.act(relu_sb[ri][:, 0:wk], bank(b, wk), AF.Relu, ["bank%d" % b, AW], ["relu%d" % ri],
                              scale=absw4[:, i, h:h + 1])
                        k.mm(bank(BSC, wk), dsgn[:, h, :], relu_sb[ri][:, 0:wk], h == 0, h == 7,
                             [DS, "relu%d" % ri], ["bank%d" % BSC])
                    last = (kb == nkb - 1)
                    ncopy = wk - 128 if last else wk
                    if ncopy > 0:
                        k.cp("act", score[:, 512 * kb:512 * kb + ncopy], bank(BSC, ncopy), ["bank%d" % BSC], [SC])
                    if last:
                        k.tt("dve", score[:, nk - 128:nk], bank(BSC, 128, wk - 128), negm[:], ALU.add,
                             ["bank%d" % BSC, "negm"], [SC])
                if I >= 2:
                    k.op("dve", lambda nk=nk, sc_=score: nc.vector.tensor_reduce(out=bis[:, 0:1], in_=sc_[:, 0:nk], axis=AX.X,
                                                                                 op=ALU.max), [SC], ["bis"],
                         cost=100.0 + 1.05 * nk)
                    k.op("dve", lambda sc_=score: nc.vector.tensor_reduce(out=bis[:, 1:2], in_=sc_[:, 0:256], axis=AX.X,
                                                                          op=ALU.min), [SC], ["bis"], cost=400.0)
                    k.tt("dve", bis[:, 2:3], bis[:, 0:1], bis[:, 1:2], ALU.subtract, ["bis"], ["bis"])
                    k.ts("dve", bis[:, 2:3], bis[:, 2:3], 1.001, 1e-6, ALU.mult, ALU.add, ["bis"], ["bis"])
                    k.ts("dve", wks[:], pow2[:], bis[:, 2:3], None, ALU.mult, None, ["bis", "pow2"], ["wks"])
                    for it in range(N_BISECT):
                        k.tt("dve", bis[:, 3:4], bis[:, 1:2], wks[:, it:it + 1], ALU.add, ["bis", "wks"], ["bis"])
                        k.ts("dve", junk[:, 0:nk], score[:, 0:nk], bis[:, 3:4], 0.0, ALU.is_ge, ALU.add,
                             [SC, "bis"], [MK, "bis"], accum_out=bis[:, 4:5])
                        k.ts("dve", bis[:, 5:6], bis[:, 4:5], float(TOPK) - 0.5, wks[:, it:it + 1], ALU.is_ge, ALU.mult,
                             ["bis", "wks"], ["bis"])
                        k.tt("dve", bis[:, 1:2], bis[:, 1:2], bis[:, 5:6], ALU.add, ["bis"], ["bis"])
                    k.ts("dve", mask01[:, 0:nk], score[:, 0:nk], bis[:, 1:2], None, ALU.is_ge, None,
                         [SC, "bis"], [MK])
                else:
                    k.ts("dve", mask01[:, 0:nk], score[:, 0:nk], -1.0e29, None, ALU.is_ge, None, [SC], [MK])
                tb = bank_bf(BT)
                for g0 in range(0, I + 1, 8):
                    g1 = min(I + 1, g0 + 8)
                    for jb in range(g0, g1):
                        k.tr(tb[:, (jb - g0) * 128:(jb - g0 + 1) * 128], mask01[:, jb * 128:(jb + 1) * 128], ident[:],
                             [MK, "ident"], ["bank0", "bank0b"])
                    k.cp("act", maskT[:, g0:g1, :], tb[:, 0:(g1 - g0) * 128].rearrange("p (a b) -> p a b", b=128),
                         ["bank0", "bank0b"], ["maskT"])
                pv = ps[0:65, BV0 * 512:BV0 * 512 + 1024].rearrange("p (h q) -> p h q", h=8)
                k.op("dve", lambda: nc.vector.memset(ps[0:65, BV0 * 512:BV0 * 512 + 1024], 0.0), [], ["bankpv"])
                for jb in range(I + 1):
                    for g in range(2):
                        lb = [BL0, BL1][rstate["lg"] % 2]
                        rstate["lg"] += 1
                        k.mm(bank(lb), ckvT[:, jb * 128:(jb + 1) * 128], qlatT[:, 4 * g:4 * g + 4, q0:q0 + 128],
                             True, True, ["ckvT", QL], ["bank%d" % lb])
                        pi = rstate["pt"] % 4
                        rstate["pt"] += 1
                        k.act(PT[pi][:], bank(lb).rearrange("p (h q) -> p h q", h=4), AF.Exp, ["bank%d" % lb],
                              ["PT%d" % pi])
                        k.tt("pool", PT[pi][:], PT[pi][:], AP(maskT, jb * 128, [[32 * 128, 128], [0, 4], [1, 128]]),
                             ALU.mult, ["PT%d" % pi, "maskT"], ["PT%d" % pi])
                        for hh in range(4):
                            h = 4 * g + hh
                            k.op("pe", (lambda o=pv[:, h, :], l=vext[:, jb, h, :], rr=PT[pi][:, hh, :], sp_=(jb == I):
                                        nc.tensor.matmul(o, lhsT=l, rhs=rr, start=False, stop=sp_,
                                                         skip_group_check=True)),
                                 ["vext", "vext_ones", "PT%d" % pi], ["bankpv"])
                k.cp("act", pv_sb[:], ps[0:65, BV0 * 512:BV0 * 512 + 1024], ["bankpv"], [PVS])
                k.act(rden[64:65, :], pv_sb[64:65, :], AF.Ln, [PVS], [RD])
                k.act(rden[64:65, :], rden[64:65, :], AF.Exp, [RD], [RD], scale=-1.0)
                ai = rstate["an"] % 2
                rstate["an"] += 1
                for g in range(2):
                    lb = [BL0, BL1][rstate["lg"] % 2]
                    rstate["lg"] += 1
                    k.mm(bank(lb, 512, 0, 64), ones_r[64:65, :], rden[64:65, g * 512:(g + 1) * 512], True, True,
                         ["ones_r", RD], ["bank%d" % lb])
                    k.tt("dve", att_n[ai][:, 4 * g:4 * g + 4, :],
                         pv_sb[0:64, g * 512:(g + 1) * 512].rearrange("p (h q) -> p h q", h=4),
                         bank(lb, 512, 0, 64).rearrange("p (h q) -> p h q", h=4), ALU.mult,
                         [PVS, "bank%d" % lb], ["att_n%d" % ai])
                tok0 = T0 + q0
                k.dma("sp", AP(attT_d, tok0, [[S, 64], [64 * S, 8], [1, 128]]), att_n[ai][:],
                      ["att_n%d" % ai], ["attT_d"], semkey="att_n%d" % ai)
        k.final_wait(["att_n0", "att_n1"])

    def ln_hT(self, x, t0, tt, xbuf, rstate, xn_bf, hT, g_fm, b_fm, BT=0):
        k = self
        nc = self.nc
        xb_i = rstate["xb"] % 2
        rstate["xb"] += 1
        xb = xbuf[xb_i]
        XR = "xbuf%d" % xb_i
        k.dma("sp", xb[:], x[t0:t0 + 128, :], [], [XR], semkey=XR)
        rstd, nmr = k.ln_stats(xb, "x", [XR], 2, 512)
        k.act(xn_bf[:], xb[:], AF.Identity, ["lnst_x", XR], ["xn_bf"], scale=rstd, bias=nmr)
        tb = k.bank_bf(BT)
        for kk in range(8):
            k.tr(tb[:, kk * 128:(kk + 1) * 128], xn_bf[:, kk * 128:(kk + 1) * 128], k.ident[:],
                 ["xn_bf", "ident"], ["bank0", "bank0b"])
        for kk in range(8):
            k.act(hT[:, kk, tt * 128:(tt + 1) * 128], tb[:, kk * 128:(kk + 1) * 128], AF.Identity,
                  ["bank0", "bank0b", "g_fm", "b_fm"], ["hT"], scale=g_fm[:, kk:kk + 1], bias=b_fm[:, kk:kk + 1])

    def phase2(self, x, w_in, lng_fm, lnb_fm, bgate, mcw, mcb, wbra, wbrc, attT_d, mrgT_d):
        k = self
        nc = self.nc
        bank, bank_bf, ident, ps = k.bank, k.bank_bf, k.ident, k.ps
        w2 = k.sb("w2", [128, 8, W2C], BF16)
        wa = k.sb("wa", [128, 4, D], BF16)
        wc = k.sb("wc", [128, 4, D], BF16)
        g_fm = k.sb("g_fm", [128, 8], F32)
        b_fm = k.sb("b_fm", [128, 8], F32)
        hb = k.sb("hb", [128, 16], F32)
        mcw_sb = k.sb("mcw_sb", [128, 4, 3], F32)
        mcb_sb = k.sb("mcb_sb", [128, 4], F32)
        xbuf = [k.sb("xbuf%d" % i, [128, D], F32) for i in range(2)]
        xn_bf = k.sb("xn_bf", [128, D], BF16)
        hT = k.sb("hT", [128, 8, 512], BF16)
        att_in = k.sb("att_in", [128, 4, 512], BF16)
        u = k.sb("u", [128, 4, 514], F32)
        tmpc = k.sb("tmpc", [128, 512], F32)
        a_sb = k.sb("a_sb", [128, 512], F32)
        cyT = k.sb("cyT", [128, 4, 512], BF16)
        ta = k.sb("ta", [128, 512], F32)
        tc2 = k.sb("tc2", [128, 512], F32)
        m1 = k.sb("m1", [128, 512], F32)
        m2 = k.sb("m2", [128, 512], F32)
        mrg = [k.sb("mrg%d" % i, [128, 8, 512], BF16) for i in range(2)]
        k.alloc_lnst("x")
        k.guard()
        for kk in range(8):
            k.dma("pool", w2[:, kk, :], w_in[kk * 128:(kk + 1) * 128, W1C:IN_W], [], ["w2"], semkey="w2", nodep=True)
        W2R = ["w2"]
        k.dma("pool", wa[:], wbra.rearrange("(k p) f -> p k f", p=128), [], ["wa"])
        k.dma("pool", wc[:], wbrc.rearrange("(k p) f -> p k f", p=128), [], ["wc"])
        k.dma("sp", g_fm[:], lng_fm[:], [], ["g_fm"])
        k.dma("sp", b_fm[:], lnb_fm[:], [], ["b_fm"])
        k.dma("sp", hb[:], bgate[:], [], ["hb"])
        k.dma("sp", mcw_sb[:], mcw[:], [], ["mcw"])
        k.dma("sp", mcb_sb[:], mcb[:], [], ["mcb"])
        k.ts("dve", hb[:], hb[:], 0.5, None, ALU.mult, None, ["hb"], ["hb"])
        k.memset("pool", u[:], 0.0, [], ["u"])
        rstate = {"xb": 0, "ring": 0, "mrg": 0}
        ringb = [1, 2, 3, 4, 5, 6, 7]

        def nb():
            b = ringb[rstate["ring"] % 7]
            rstate["ring"] += 1
            return b

        def proj(col0, b):
            c0 = col0 - W1C
            for kk in range(8):
                k.mm(bank(b), w2[:, kk, c0:c0 + 128], hT[:, kk, :], kk == 0, kk == 7, ["hT"] + W2R, ["bank%d" % b])

        for st in range(8):
            T0 = st * 512
            for tt in range(4):
                k.ln_hT(x, T0 + tt * 128, tt, xbuf, rstate, xn_bf, hT, g_fm, b_fm)
            k.dma("sp", att_in[:], AP(attT_d, T0, [[S, 128], [128 * S, 4], [1, 512]]), [], ["att_in"], semkey="att_in")
            for j in range(4):
                bc_, bx_, bb_ = nb(), nb(), nb()
                proj(C_CVC + 128 * j, bc_)
                proj(C_CVX + 128 * j, bx_)
                proj(C_CVB + 128 * j, bb_)
                if st > 0:
                    k.cp("pool", u[:, j, 0:2], u[:, j, 512:514], ["u"], ["u"])
                k.cp("act", tmpc[:], bank(bc_), ["bank%d" % bc_], ["tmpc"])
                k.tt("dve", u[:, j, 2:514], tmpc[:], bank(bx_), ALU.mult, ["tmpc", "bank%d" % bx_], ["u"])
                k.act(a_sb[:], u[:, j, 2:514], AF.Identity, ["u", "mcw", "mcb"], ["a_sb"],
                      scale=mcw_sb[:, j, 2:3], bias=mcb_sb[:, j:j + 1])
                k.stt(a_sb[:], u[:, j, 1:513], mcw_sb[:, j, 1:2], a_sb[:], ALU.mult, ALU.add, ["u", "a_sb", "mcw"], ["a_sb"])
                k.stt(a_sb[:], u[:, j, 0:512], mcw_sb[:, j, 0:1], a_sb[:], ALU.mult, ALU.add, ["u", "a_sb", "mcw"], ["a_sb"])
                k.tt("dve", cyT[:, j, :], a_sb[:], bank(bb_), ALU.mult, ["a_sb", "bank%d" % bb_], ["cyT"])
            mi = rstate["mrg"] % 2
            rstate["mrg"] += 1
            for c in range(8):
                bga, bgc, bra, brc = nb(), nb(), nb(), nb()
                proj(C_GATT + 128 * c, bga)
                proj(C_GCONV + 128 * c, bgc)
                for kk in range(4):
                    k.mm(bank(bra), wa[:, kk, 128 * c:128 * (c + 1)], att_in[:, kk, :], kk == 0, kk == 3,
                         ["wa", "att_in"], ["bank%d" % bra])
                for kk in range(4):
                    k.mm(bank(brc), wc[:, kk, 128 * c:128 * (c + 1)], cyT[:, kk, :], kk == 0, kk == 3,
                         ["wc", "cyT"], ["bank%d" % brc])
                k.act(ta[:], bank(bga), AF.Tanh, ["bank%d" % bga, "hb"], ["ta"], scale=0.5, bias=hb[:, c:c + 1])
                k.act(tc2[:], bank(bgc), AF.Tanh, ["bank%d" % bgc, "hb"], ["tc2"], scale=0.5, bias=hb[:, 8 + c:9 + c])
                k.stt(m1[:], ta[:], 1.0, bank(bra), ALU.add, ALU.mult, ["ta", "bank%d" % bra], ["m1"])
                k.stt(m2[:], tc2[:], 1.0, bank(brc), ALU.add, ALU.mult, ["tc2", "bank%d" % brc], ["m2"])
                k.tt("pool", mrg[mi][:, c, :], m1[:], m2[:], ALU.add, ["m1", "m2"], ["mrg%d" % mi])
            k.dma("sp", AP(mrgT_d, T0, [[S, 128], [128 * S, 8], [1, 512]]), mrg[mi][:], ["mrg%d" % mi], ["mrgT_d"],
                  semkey="mrg%d" % mi)
        k.final_wait(["mrg0", "mrg1"])

    def phase3(self, x, p, lng, lnb, wo, ln1g, ln1b, wpg, bpg, wple, mrgT_d, r_d, h1T_d):
        k = self
        nc = self.nc
        bank, bank_bf, ident, ps = k.bank, k.bank_bf, k.ident, k.ps
        wo_sb = k.sb("wo_sb", [128, 8, D], BF16)
        wpg_sb = k.sb("wpg_sb", [128, 8, D], BF16)
        wpl_sb = k.sb("wpl_sb", [128, 2, D], BF16)
        Ga = k.sb("Ga", [128, D], F32)
        Ba = k.sb("Ba", [128, D], F32)
        G1 = k.sb("G1", [128, D], F32)
        B1 = k.sb("B1", [128, D], F32)
        HB = k.sb("HB", [128, D], F32)
        xbuf = [k.sb("xbuf%d" % i, [128, D], F32) for i in range(2)]
        pbuf = [k.sb("pbuf%d" % i, [128, 256], F32) for i in range(2)]
        m_in = [k.sb("m_in%d" % i, [128, 8, 128], BF16) for i in range(2)]
        hA_2 = [k.sb("hA%d" % i, [128, D], F32) for i in range(2)]
        y_2 = [k.sb("y%d" % i, [128, D], F32) for i in range(2)]
        h1_2 = [k.sb("h1%d" % i, [128, D], F32) for i in range(2)]
        h1_bf_2 = [k.sb("h1_bf%d" % i, [128, D], BF16) for i in range(2)]
        h1T = [k.sb("h1T%d" % i, [128, 8, 128], BF16) for i in range(2)]
        p_bf_2 = [k.sb("p_bf%d" % i, [128, 256], BF16) for i in range(2)]
        pT_2 = [k.sb("pT%d" % i, [128, 2, 128], BF16) for i in range(2)]
        tg_2 = [k.sb("tg%d" % i, [128, D], F32) for i in range(2)]
        pl2_2 = [k.sb("pl2%d" % i, [128, D], F32) for i in range(2)]
        r2 = [k.sb("r2_%d" % i, [128, D], F32) for i in range(2)]
        k.alloc_lnst("x")
        k.alloc_lnst("y")
        k.guard()
        k.dma("pool", wo_sb[:], wo.rearrange("(k p) f -> p k f", p=128), [], ["wo"])
        k.dma("pool", wpg_sb[:], wpg.rearrange("(k p) f -> p k f", p=128), [], ["wpg"])
        k.dma("pool", wpl_sb[:], wple.rearrange("(k p) f -> p k f", p=128), [], ["wpl"])
        for t_, src, nm in ((Ga, lng, "Ga"), (Ba, lnb, "Ba"), (G1, ln1g, "G1"), (B1, ln1b, "B1"), (HB, bpg, "HB")):
            k.dma("sp", t_[:], AP(src, 0, [[0, 128], [1, D]]), [], [nm])
        k.ts("dve", Ga[:], Ga[:], ALPHA, None, ALU.mult, None, ["Ga"], ["Ga"])
        k.ts("dve", Ba[:], Ba[:], ALPHA, None, ALU.mult, None, ["Ba"], ["Ba"])
        k.ts("dve", HB[:], HB[:], 0.5, None, ALU.mult, None, ["HB"], ["HB"])
        for t in range(32):
            t0 = t * 128
            bi = t % 2
            XR, PR, MR = "xbuf%d" % bi, "pbuf%d" % bi, "m_in%d" % bi
            hA, y, h1, h1_bf, p_bf, pT, tg, pl2 = hA_2[bi], y_2[bi], h1_2[bi], h1_bf_2[bi], p_bf_2[bi], pT_2[bi], tg_2[bi], pl2_2[bi]
            R_hA, R_y, R_h1, R_h1bf, R_pbf, R_pT, R_tg, R_pl2 = ["%s%d" % (n_, bi) for n_ in ("hA", "y", "h1", "h1_bf", "p_bf", "pT", "tg", "pl2")]
            k.dma("sp", xbuf[bi][:], x[t0:t0 + 128, :], [], [XR], semkey=XR)
            k.dma("sp", pbuf[bi][:], p[t0:t0 + 128, :], [], [PR], semkey=PR)
            k.dma("sp", m_in[bi][:], AP(mrgT_d, t0, [[S, 128], [128 * S, 8], [1, 128]]), [], [MR], semkey=MR)
            rstd, nmr = k.ln_stats(xbuf[bi], "x", [XR], 2, 512)
            k.act(hA[:], xbuf[bi][:], AF.Identity, ["lnst_x", XR], [R_hA], scale=rstd, bias=nmr)
            k.tt("pool", hA[:], hA[:], Ga[:], ALU.mult, [R_hA, "Ga"], [R_hA])
            k.tt("pool", hA[:], hA[:], Ba[:], ALU.add, [R_hA, "Ba"], [R_hA])
            for n in range(2):
                b = 1 + n
                for kk in range(8):
                    k.mm(bank(b), m_in[bi][:, kk, :], wo_sb[:, kk, 512 * n:512 * (n + 1)], kk == 0, kk == 7,
                         [MR, "wo"], ["bank%d" % b])
                k.stt(y[:, 512 * n:512 * (n + 1)], bank(b), 0.5, hA[:, 512 * n:512 * (n + 1)], ALU.mult, ALU.add,
                      ["bank%d" % b, R_hA], [R_y])
            rstd1, nmr1 = k.ln_stats(y, "y", [R_y], 2, 512)
            k.act(h1[:], y[:], AF.Identity, ["lnst_y", R_y], [R_h1], scale=rstd1, bias=nmr1)
            k.tt("pool", h1[:], h1[:], G1[:], ALU.mult, [R_h1, "G1"], [R_h1])
            k.tt("pool", h1[:], h1[:], B1[:], ALU.add, [R_h1, "B1"], [R_h1])
            k.cp("pool", h1_bf[:], h1[:], [R_h1], [R_h1bf])
            tb = bank_bf(0)
            for kk in range(8):
                k.tr(tb[:, kk * 128:(kk + 1) * 128], h1_bf[:, kk * 128:(kk + 1) * 128], ident[:], [R_h1bf, "ident"], ["bank0"])
            HR = "h1T%d" % bi
            k.cp("act", h1T[bi][:], tb[:, 0:1024].rearrange("p (a b) -> p a b", b=128), ["bank0"], [HR])
            k.dma("sp", AP(h1T_d, t0, [[S, 128], [128 * S, 8], [1, 128]]), h1T[bi][:], [HR], ["h1T_d"], semkey=HR)
            for n in range(2):
                b = 3 + n
                for kk in range(8):
                    k.mm(bank(b), h1T[bi][:, kk, :], wpg_sb[:, kk, 512 * n:512 * (n + 1)], kk == 0, kk == 7,
                         [HR, "wpg"], ["bank%d" % b])
                k.stt(tg[:, 512 * n:512 * (n + 1)], bank(b), 0.5, HB[:, 512 * n:512 * (n + 1)], ALU.mult, ALU.add,
                      ["bank%d" % b, "HB"], [R_tg])
            k.act(tg[:], tg[:], AF.Tanh, [R_tg], [R_tg])
            k.cp("pool", p_bf[:], pbuf[bi][:], [PR], [R_pbf])
            tb2 = bank_bf(7)
            for kk in range(2):
                k.tr(tb2[:, kk * 128:(kk + 1) * 128], p_bf[:, kk * 128:(kk + 1) * 128], ident[:], [R_pbf, "ident"], ["bank7"])
            k.cp("act", pT[:], tb2[:, 0:256].rearrange("p (a b) -> p a b", b=128), ["bank7"], [R_pT])
            RR = "r2_%d" % bi
            for n in range(2):
                b = 5 + n
                for kk in range(2):
                    k.mm(bank(b), pT[:, kk, :], wpl_sb[:, kk, 512 * n:512 * (n + 1)], kk == 0, kk == 1,
                         [R_pT, "wpl"], ["bank%d" % b])
                k.stt(pl2[:, 512 * n:512 * (n + 1)], tg[:, 512 * n:512 * (n + 1)], 1.0, bank(b), ALU.add, ALU.mult,
                      [R_tg, "bank%d" % b], [R_pl2])
            k.stt(r2[bi][:], h1[:], 2.0 * ALPHA, pl2[:], ALU.mult, ALU.add, [R_h1, R_pl2], [RR])
            k.dma("sp", r_d[t0:t0 + 128, :], r2[bi][:], [RR], ["r_d"], semkey=RR)
        k.final_wait(["r2_0", "r2_1", "h1T0", "h1T1"])

    def phase4(self, wup, fcw, fcb, wdn, ln2g, ln2b, r_d, h1T_d, out):
        k = self
        nc = self.nc
        bank, bank_bf, ident, ps = k.bank, k.bank_bf, k.ident, k.ps
        NT = 256
        wup_sb = k.sb("wup_sb", [128, 8, 2 * DFF], BF16)
        wdn_sb = k.sb("wdn_sb", [128, 22, D], BF16)
        fcw_sb = k.sb("fcw_sb", [128, 44, 3], F32)
        fcb_sb = k.sb("fcb_sb", [128, 44], F32)
        G2 = k.sb("G2", [128, D], F32)
        B2 = k.sb("B2", [128, D], F32)
        h1T2 = [k.sb("h1T%d" % i, [128, 8, NT], BF16) for i in range(2)]
        actT2 = [k.sb("actT%d" % i, [128, 22, NT], BF16) for i in range(2)]
        abuf = [[k.sb("abuf%d_%d" % (h_, i), [128, NT], F32) for i in range(2)] for h_ in range(2)]
        gbuf = [[k.sb("gbuf%d_%d" % (h_, i), [128, NT + 2], F32) for i in range(2)] for h_ in range(2)]
        sgb = [k.sb("sg%d" % i, [128, NT], F32) for i in range(2)]
        halo = k.sb("halo", [128, 44, 2], F32)
        r2b = [k.sb("r2_%d" % i, [128, D], F32) for i in range(2)]
        yb = [k.sb("yb%d" % i, [128, D], F32) for i in range(2)]
        k.alloc_lnst("y")
        k.guard()
        for kk in range(8):
            k.dma("pool", wup_sb[:, kk, :], wup[kk * 128:(kk + 1) * 128, :], [], ["wup"], semkey="wup", nodep=True)
        WUR = ["wup"]
        for c in range(22):
            k.dma("pool", wdn_sb[:, c, :], wdn[c * 128:(c + 1) * 128, :], [], ["wdn"], semkey="wdn", nodep=True)
        WDR = ["wdn"]
        k.dma("sp", fcw_sb[:], fcw[:], [], ["fcw"])
        k.dma("sp", fcb_sb[:], fcb[:], [], ["fcb"])
        k.dma("sp", G2[:], AP(ln2g, 0, [[0, 128], [1, D]]), [], ["G2"])
        k.dma("sp", B2[:], AP(ln2b, 0, [[0, 128], [1, D]]), [], ["B2"])
        k.memset("pool", halo[:], 0.0, [], ["halo%d" % ch for ch in range(44)])
        cnt = {"bank": 0, "ab0": 0, "ab1": 0, "sg": 0, "tile": 0}

        for st in range(S // NT):
            T0 = st * NT
            hb = st % 2
            h1T, aT = h1T2[hb], actT2[hb]
            HR, ATR = "h1T%d" % hb, "actT%d" % hb
            k.dma("sp", h1T[:], AP(h1T_d, T0, [[S, 128], [128 * S, 8], [1, NT]]), [], [HR], semkey=HR)
            for c in range(22):
                cur = []
                for half in range(2):
                    ch = c + 22 * half
                    b = cnt["bank"] % 4
                    cnt["bank"] += 1
                    BR = "bank%d" % b
                    for kk in range(8):
                        k.mm(bank(b, NT), wup_sb[:, kk, 128 * ch:128 * (ch + 1)], h1T[:, kk, :], kk == 0, kk == 7,
                             [HR] + WUR, [BR])
                    ai = cnt["ab%d" % half] % 2
                    cnt["ab%d" % half] += 1
                    ab, gb = abuf[half][ai], gbuf[half][ai]
                    AR, GR = "abuf%d_%d" % (half, ai), "gbuf%d_%d" % (half, ai)
                    HL = "halo%d" % ch
                    pb = bank(b, NT)
                    k.cp("pool", gb[:, 0:2], halo[:, ch, :], [HL], [GR])
                    k.cp("act", gb[:, 2:NT + 2], pb, [BR], [GR])
                    k.act(ab[:], pb, AF.Identity, [BR, "fcw", "fcb"], [AR], scale=fcw_sb[:, ch, 2:3], bias=fcb_sb[:, ch:ch + 1])
                    k.cp("pool", halo[:, ch, :], gb[:, NT:NT + 2], [GR], [HL])
                    k.stt(ab[:], gb[:, 1:NT + 1], fcw_sb[:, ch, 1:2], ab[:], ALU.mult, ALU.add, [GR, AR, "fcw"], [AR])
                    k.stt(ab[:], gb[:, 0:NT], fcw_sb[:, ch, 0:1], ab[:], ALU.mult, ALU.add, [GR, AR, "fcw"], [AR])
                    cur.append((ab, AR))
                si = cnt["sg"] % 2
                cnt["sg"] += 1
                SGR = "sg%d" % si
                k.act(sgb[si][:], cur[0][0][:], AF.Silu, [cur[0][1]], [SGR])
                k.tt("pool", aT[:, c, :], sgb[si][:], cur[1][0][:], ALU.mult, [SGR, cur[1][1]], [ATR])
            for tt in range(NT // 128):
                t0 = T0 + tt * 128
                ti = cnt["tile"] % 2
                cnt["tile"] += 1
                RR, YR = "r2_%d" % ti, "yb%d" % ti
                r2, yv = r2b[ti], yb[ti]
                k.dma("sp", r2[:], r_d[t0:t0 + 128, :], [], [RR], semkey=RR)
                for n in range(2):
                    b = 4 + 2 * ti + n
                    for c in range(22):
                        k.mm(bank(b), aT[:, c, tt * 128:(tt + 1) * 128], wdn_sb[:, c, 512 * n:512 * (n + 1)],
                             c == 0, c == 21, [ATR] + WDR, ["bank%d" % b])
                    k.stt(yv[:, 512 * n:512 * (n + 1)], r2[:, 512 * n:512 * (n + 1)], 0.5, bank(b), ALU.mult, ALU.add,
                          [RR, "bank%d" % b], [YR])
                rstd, nmr = k.ln_stats(yv, "y", [YR], 2, 512)
                k.act(yv[:], yv[:], AF.Identity, ["lnst_y", YR], [YR], scale=rstd, bias=nmr)
                k.tt("pool", yv[:], yv[:], G2[:], ALU.mult, [YR, "G2"], [YR])
                k.tt("pool", yv[:], yv[:], B2[:], ALU.add, [YR, "B2"], [YR])
                k.dma("sp", out[t0:t0 + 128, :], yv[:], [YR], ["out_d"], semkey=YR)
        k.final_wait(["yb0", "yb1"])

    def final_wait(self, names):
        nc = self.nc
        self.op("sp", lambda: nc.sync.nop(), names, names)


def _prep_inputs(inputs, b):
    f = lambda a: np.ascontiguousarray(np.asarray(a, dtype=np.float32))
    m = {}
    m["x"] = f(inputs["x"][b])
    m["p"] = f(inputs["p"][0, b])
    m["w_in"] = f(inputs["w_in"][0])
    m["lng_fm"] = f(np.asarray(inputs["ln_emb_g"]).reshape(8, 128).T)
    m["lnb_fm"] = f(np.asarray(inputs["ln_emb_b"]).reshape(8, 128).T)
    m["lng"] = f(np.asarray(inputs["ln_emb_g"]).reshape(1, D))
    m["lnb"] = f(np.asarray(inputs["ln_emb_b"]).reshape(1, D))
    m["bgate"] = f(np.asarray(inputs["b_gate"][0]).reshape(2, 8, 128).transpose(2, 0, 1).reshape(128, 16))
    m["kvg"] = f(np.asarray(inputs["kv_norm_g"][0]).reshape(1, 128))
    wuk = np.asarray(inputs["w_uk"][0])
    m["wukT"] = f(wuk.reshape(4, 2, 128, 64).transpose(1, 3, 0, 2).reshape(128, 4, 128))
    wuv = np.asarray(inputs["w_uv"][0])
    m["wuvr"] = f(wuv.transpose(1, 0, 2).reshape(128, 512))
    m["kig"] = f(np.asarray(inputs["k_idx_ln_g"][0]).reshape(1, 64))
    m["kib"] = f(np.asarray(inputs["k_idx_ln_b"][0]).reshape(1, 64))
    m["mcw"] = f(np.asarray(inputs["mix_conv_w"][0]).reshape(3, 4, 128).transpose(2, 1, 0))
    m["mcb"] = f(np.asarray(inputs["mix_conv_b"][0]).reshape(4, 128).T)
    m["wbra"] = f(inputs["w_br_att"][0])
    m["wbrc"] = f(inputs["w_br_conv"][0])
    m["wo"] = f(inputs["w_o"][0])
    m["ln1g"] = f(np.asarray(inputs["ln1_g"][0]).reshape(1, D))
    m["ln1b"] = f(np.asarray(inputs["ln1_b"][0]).reshape(1, D))
    m["wup"] = f(inputs["w_ffn_up"][0])
    m["fcw"] = f(np.asarray(inputs["ffn_conv_w"][0]).reshape(3, 44, 128).transpose(2, 1, 0))
    m["fcb"] = f(np.asarray(inputs["ffn_conv_b"][0]).reshape(44, 128).T)
    m["wdn"] = f(inputs["w_ffn_down"][0])
    m["wpg"] = f(inputs["w_ple_gate"][0])
    m["bpg"] = f(np.asarray(inputs["b_ple_gate"][0]).reshape(1, D))
    m["wple"] = f(inputs["w_ple"][0])
    m["ln2g"] = f(np.asarray(inputs["ln2_g"][0]).reshape(1, D))
    m["ln2b"] = f(np.asarray(inputs["ln2_b"][0]).reshape(1, D))
    return m


def kernel(**inputs):
    kern = Kern()
    nc = kern.build()
    in_maps = [_prep_inputs(inputs, b) for b in range(NCORES)]
    res = run_bass_kernel_spmd(nc, in_maps, core_ids=list(range(NCORES)))
    return np.stack([np.asarray(r["out"], dtype=np.float32) for r in res.results], axis=0)
```

```python
import numpy as np
from contextlib import ExitStack
import concourse.bass as bass
import concourse.mybir as mybir
from concourse.bass_utils import run_bass_kernel_spmd

F32 = mybir.dt.float32
BF16 = mybir.dt.bfloat16
ALU = mybir.AluOpType
AF = mybir.ActivationFunctionType
AX = mybir.AxisListType

S = 4096
D = 1024
NCORES = 8
IN_W = 4808
DFF = 2816
LN_EPS = 1e-5
ALPHA = 2.0 ** 0.25
TOPK = 256
NEG = -1.0e30
N_BISECT = 16
C_QATT, C_CKV, C_QIDX, C_KIDX, C_WIDX, C_CVB, C_CVC, C_CVX, C_GATT, C_GCONV = (
    0, 512, 640, 1152, 1216, 1224, 1736, 2248, 2760, 3784)
W1C = 1224
W2C = IN_W - W1C
CW = (64 ** -0.5) * (8 ** -0.5)


class Res:
    __slots__ = ("name", "last_w", "readers")

    def __init__(self, name):
        self.name = name
        self.last_w = None
        self.readers = []


class Op:
    __slots__ = ("eng", "fn", "deps", "alldeps", "orderdeps", "signal", "sem", "ticket", "is_dma", "idx", "eidx",
                 "semkey", "cost", "lat", "start")


class Sched:
    NSEM = 0
    ENGS = ("pe", "act", "dve", "pool", "sp")

    def __init__(self, nc, es, reorder=True):
        self.nc = nc
        self.es = es
        self.ops = []
        self.last_dma = {}
        self.reorder = reorder
        self.engs = {"pe": nc.tensor, "act": nc.scalar, "dve": nc.vector, "pool": nc.gpsimd, "sp": nc.sync}

    def add(self, eng, fn, reads=(), writes=(), dma=False, semkey=None, nodep=False, cost=200.0, lat=0.0):
        op = Op()
        op.eng = eng
        op.fn = fn
        op.is_dma = dma
        op.signal = False
        op.sem = None
        op.ticket = 0
        op.idx = len(self.ops)
        op.eidx = 0
        op.semkey = semkey
        op.cost = cost
        op.lat = lat
        op.start = 0.0
        deps = {}
        for r in reads:
            if r.last_w is not None:
                deps[r.last_w.idx] = r.last_w
        for w in writes:
            if w.last_w is not None:
                deps[w.last_w.idx] = w.last_w
            for rd in w.readers:
                deps[rd.idx] = rd
        deps.pop(op.idx, None)
        if nodep:
            deps = {}
        for r in reads:
            r.readers.append(op)
        for w in writes:
            w.last_w = op
            w.readers = []
        op.alldeps = list(deps.values())
        op.orderdeps = []
        if dma and nodep:
            prev = self.last_dma.get((eng, semkey))
            if prev is not None:
                op.orderdeps.append(prev)
            self.last_dma[(eng, semkey)] = op
        self.ops.append(op)
        return op

    def schedule(self):
        import heapq
        ops = self.ops
        n = len(ops)
        succ = [[] for _ in range(n)]
        ndeps = [0] * n
        for op in ops:
            ds = set(d.idx for d in op.alldeps) | set(d.idx for d in op.orderdeps)
            ndeps[op.idx] = len(ds)
            for di in ds:
                succ[di].append(op.idx)
        ready = [0.0] * n
        finish = [0.0] * n
        blev = [0.0] * n
        for op in reversed(ops):
            i = op.idx
            m = 0.0
            for j in succ[i]:
                if blev[j] > m:
                    m = blev[j]
            blev[i] = op.cost + op.lat + m + 100.0
        eng_free = {e: 0.0 for e in self.ENGS}
        future = {e: [] for e in self.ENGS}
        avail = {e: [] for e in self.ENGS}
        for op in ops:
            if ndeps[op.idx] == 0:
                heapq.heappush(future[op.eng], (0.0, op.idx))
        order = []
        XLAT = 150.0
        SLAT = 200.0
        while len(order) < n:
            best = None
            for e in self.ENGS:
                T = eng_free[e]
                fu, av = future[e], avail[e]
                while fu and fu[0][0] <= T:
                    _, i = heapq.heappop(fu)
                    heapq.heappush(av, (-blev[i], i))
                if av:
                    st = T
                elif fu:
                    st = fu[0][0]
                else:
                    continue
                if best is None or st < best[0]:
                    best = (st, e)
            st, e = best
            if avail[e]:
                _, i = heapq.heappop(avail[e])
            else:
                _, i = heapq.heappop(future[e])
            op = ops[i]
            op.start = st
            eng_free[e] = st + op.cost
            finish[i] = st + op.cost + op.lat
            order.append(op)
            for j in succ[i]:
                r_ = finish[i] + (XLAT if ops[j].eng != e or op.is_dma else (0.0 if e == "pe" else SLAT))
                if r_ > ready[j]:
                    ready[j] = r_
                ndeps[j] -= 1
                if ndeps[j] == 0:
                    heapq.heappush(future[ops[j].eng], (ready[j], j))
        self.est_ns = max(finish) if n else 0.0
        self.ops = order

    def emit(self):
        nc = self.nc
        if self.reorder:
            self.schedule()
        ecount = {}
        for op in self.ops:
            op.eidx = ecount.get(op.eng, 0)
            ecount[op.eng] = op.eidx + 1
        for op in self.ops:
            keep = []
            for d in op.alldeps:
                if not d.is_dma and d.eng == op.eng and not op.is_dma:
                    if op.eng == "pe":
                        continue
                    if op.eidx - d.eidx > 3:
                        continue
                keep.append(d)
            op.deps = keep
        for op in self.ops:
            if op.is_dma:
                op.signal = True
            for d in op.deps:
                d.signal = True
        sems = {}
        counts = {}

        def get_sem(key):
            if key not in sems:
                Sched.NSEM += 1
                sems[key] = self.es.enter_context(nc.semaphore("s%d" % Sched.NSEM))
                counts[key] = 0
            return sems[key]

        for op in self.ops:
            if not op.signal:
                continue
            if op.is_dma:
                key = ("dma", op.semkey if op.semkey is not None else op.idx)
                op.sem = get_sem(key)
                counts[key] += 16
                op.ticket = counts[key]
            else:
                key = ("eng", op.eng)
                op.sem = get_sem(key)
                counts[key] += 1
                op.ticket = counts[key]
        waited = {}
        nwait = 0
        for op in self.ops:
            e = self.engs[op.eng]
            need = {}
            for d in op.deps:
                k = id(d.sem)
                if waited.get((op.eng, k), 0) >= d.ticket:
                    continue
                if k not in need or need[k][1] < d.ticket:
                    need[k] = (d.sem, d.ticket)
            for k, (sem, val) in need.items():
                e.wait_ge(sem, val)
                waited[(op.eng, k)] = val
                nwait += 1
            ins = op.fn()
            if op.signal:
                ins.then_inc(op.sem, 16 if op.is_dma else 1)
        self.nsems = len(sems)
        self.nwait = nwait


def fsz(ap):
    n = 1
    for d in ap.shape[1:]:
        n *= int(d)
    return n


def AP(t, off, dims):
    return bass.AP(t, off, [list(d) for d in dims])


class Kern:
    def __init__(self, phases=(1, 2, 3, 4), debug=False, reorder=True):
        self.reorder = reorder
        self.phases = phases
        self.debug = debug
        self.nc = bass.Bass("TRN2", target_bir_lowering=False)
        self.es = ExitStack()
        self.ges = self.es
        self.semcount = 0
        self.dram = {}
        self.res = {}

    def din(self, name, shape, dt=F32):
        t = self.nc.dram_tensor(name, list(shape), dt, kind="ExternalInput")
        self.dram[name] = t
        return t

    def dscr(self, name, shape, dt):
        kind = "ExternalOutput" if self.debug else "Internal"
        t = self.nc.dram_tensor(name, list(shape), dt, kind=kind)
        self.dram[name] = t
        return t

    def sb(self, name, shape, dt):
        nm = "p%d_%s" % (getattr(self, "nphase", 0), name)
        return self.es.enter_context(self.nc.sbuf_tensor(nm, list(shape), dt))

    def guard(self, kb=8):
        with self.nc.sbuf_tensor("guard%d" % self.nphase, [128, kb * 256], F32):
            pass

    def R(self, name):
        if name not in self.res:
            self.res[name] = Res(name)
        return self.res[name]

    def Rs(self, *names):
        return [self.R(n) for n in names]

    def op(self, eng, fn, r=(), w=(), dma=False, semkey=None, nodep=False, cost=200.0, lat=0.0):
        return self.sc.add(eng, fn, [self.R(x) if isinstance(x, str) else x for x in r],
                           [self.R(x) if isinstance(x, str) else x for x in w], dma=dma, semkey=semkey, nodep=nodep,
                           cost=cost, lat=lat)

    def dma(self, q, out, in_, r=(), w=(), semkey=None, nodep=False):
        e = {"sp": self.nc.sync, "pool": self.nc.gpsimd, "act": self.nc.scalar}[q]
        nbytes = fsz(out) * int(out.shape[0]) * 4
        return self.op(q, lambda: e.dma_start(out=out, in_=in_), r, w, dma=True, semkey=semkey, nodep=nodep,
                       cost=(600.0 if q == "pool" else 100.0), lat=2500.0 + nbytes / 250.0)

    def mm(self, out, lhsT, rhs, start, stop, r=(), w=()):
        nc = self.nc
        return self.op("pe", lambda: nc.tensor.matmul(out, lhsT=lhsT, rhs=rhs, start=start, stop=stop), r, w,
                       cost=64.0 + 0.5 * fsz(rhs))

    def tr(self, out, in_, ident, r=(), w=()):
        nc = self.nc
        return self.op("pe", lambda: nc.tensor.transpose(out, in_, ident), r, w, cost=130.0)

    def act(self, out, in_, func, r=(), w=(), **kw):
        nc = self.nc
        return self.op("act", lambda: nc.scalar.activation(out=out, in_=in_, func=func, **kw), r, w,
                       cost=200.0 + 0.85 * fsz(out))

    def ts(self, eng, out, in0, s1, s2, op0, op1=None, r=(), w=(), accum_out=None):
        e = self.nc.vector if eng == "dve" else self.nc.gpsimd
        c = 70.0 + 1.05 * fsz(out)
        if op1 is None:
            return self.op(eng, lambda: e.tensor_scalar(out=out, in0=in0, scalar1=s1, scalar2=None, op0=op0), r, w, cost=c)
        if accum_out is not None:
            return self.op(eng, lambda: e.tensor_scalar(out=out, in0=in0, scalar1=s1, scalar2=s2, op0=op0, op1=op1,
                                                        accum_out=accum_out), r, w, cost=c)
        return self.op(eng, lambda: e.tensor_scalar(out=out, in0=in0, scalar1=s1, scalar2=s2, op0=op0, op1=op1), r, w, cost=c)

    def tt(self, eng, out, in0, in1, op, r=(), w=()):
        e = self.nc.vector if eng == "dve" else self.nc.gpsimd
        c = (70.0 + 1.05 * fsz(out)) if eng == "dve" else (150.0 + 2.0 * fsz(out))
        return self.op(eng, lambda: e.tensor_tensor(out=out, in0=in0, in1=in1, op=op), r, w, cost=c)

    def stt(self, out, in0, scalar, in1, op0, op1, r=(), w=()):
        nc = self.nc
        return self.op("dve", lambda: nc.vector.scalar_tensor_tensor(out=out, in0=in0, scalar=scalar, in1=in1,
                                                                      op0=op0, op1=op1), r, w, cost=70.0 + 1.05 * fsz(out))

    def cp(self, eng, out, in_, r=(), w=()):
        nc = self.nc
        if eng == "act":
            return self.op("act", lambda: nc.scalar.copy(out=out, in_=in_), r, w, cost=200.0 + 0.85 * fsz(out))
        e = nc.vector if eng == "dve" else nc.gpsimd
        c = (70.0 + 1.05 * fsz(out)) if eng == "dve" else (150.0 + 1.1 * fsz(out))
        return self.op(eng, lambda: e.tensor_copy(out=out, in_=in_), r, w, cost=c)

    def memset(self, eng, ap, val, r=(), w=()):
        e = self.nc.vector if eng == "dve" else self.nc.gpsimd
        return self.op(eng, lambda: e.memset(ap, val), r, w, cost=100.0 + 1.0 * fsz(ap))

    def ln_stats(self, src, tag, r, nchunk, width):
        nc = self.nc
        st = self.lnst[tag]
        stats, mv, rs = st
        resn = "lnst_" + tag
        for c in range(nchunk):
            self.op("dve", (lambda c=c: nc.vector.bn_stats(out=stats[:, 6 * c:6 * c + 6],
                                                           in_=src[:, c * width:(c + 1) * width])), r, [resn],
                    cost=100.0 + 1.1 * width)
        self.op("dve", lambda: nc.vector.bn_aggr(out=mv[:, 0:2], in_=stats[:, 0:6 * nchunk]), [resn], [resn])
        self.ts("dve", rs[:, 0:1], mv[:, 1:2], LN_EPS, None, ALU.add, None, [resn], [resn])
        self.act(rs[:, 0:1], rs[:, 0:1], AF.Ln, [resn], [resn])
        self.act(rs[:, 0:1], rs[:, 0:1], AF.Exp, [resn], [resn], scale=-0.5)
        self.ts("dve", rs[:, 1:2], mv[:, 0:1], -1.0, rs[:, 0:1], ALU.mult, ALU.mult, [resn], [resn])
        return rs[:, 0:1], rs[:, 1:2]

    def alloc_lnst(self, tag):
        if not hasattr(self, "lnst"):
            self.lnst = {}
        self.lnst[tag] = (self.sb("lnstats_" + tag, [128, 12], F32), self.sb("lnmv_" + tag, [128, 2], F32),
                          self.sb("lnrs_" + tag, [128, 2], F32))

    def build(self):
        nc = self.nc
        k = self
        x = k.din("x", [S, D])
        p = k.din("p", [S, 256])
        w_in = k.din("w_in", [D, IN_W])
        lng_fm = k.din("lng_fm", [128, 8])
        lnb_fm = k.din("lnb_fm", [128, 8])
        lng = k.din("lng", [1, D])
        lnb = k.din("lnb", [1, D])
        bgate = k.din("bgate", [128, 16])
        kvg = k.din("kvg", [1, 128])
        wukT = k.din("wukT", [128, 4, 128])
        wuvr = k.din("wuvr", [128, 512])
        kig = k.din("kig", [1, 64])
        kib = k.din("kib", [1, 64])
        mcw = k.din("mcw", [128, 4, 3])
        mcb = k.din("mcb", [128, 4])
        wbra = k.din("wbra", [512, D])
        wbrc = k.din("wbrc", [512, D])
        wo = k.din("wo", [D, D])
        ln1g = k.din("ln1g", [1, D])
        ln1b = k.din("ln1b", [1, D])
        wup = k.din("wup", [D, 2 * DFF])
        fcw = k.din("fcw", [128, 44, 3])
        fcb = k.din("fcb", [128, 44])
        wdn = k.din("wdn", [DFF, D])
        wpg = k.din("wpg", [D, D])
        bpg = k.din("bpg", [1, D])
        wple = k.din("wple", [256, D])
        ln2g = k.din("ln2g", [1, D])
        ln2b = k.din("ln2b", [1, D])
        out = nc.dram_tensor("out", [S, D], F32, kind="ExternalOutput")
        k.dram["out"] = out
        attT_d = k.dscr("attT_d", [512, S], BF16)
        mrgT_d = k.dscr("mrgT_d", [D, S], BF16)
        r_d = k.dscr("r_d", [S, D], F32)
        h1T_d = k.dscr("h1T_d", [D, S], BF16)

        ps = k.es.enter_context(nc.psum_tensor("ps", [128, 4096], F32))
        k.ps = ps

        def bank(b, n=512, off=0, parts=128):
            return ps[0:parts, b * 512 + off: b * 512 + off + n]

        def bank_bf(b):
            return ps[:, b * 512:(b + 1) * 512].bitcast(BF16)

        k.bank = bank
        k.bank_bf = bank_bf

        k.bar_tile = k.sb("bar_tile", [128, 8], F32)
        k.bar_bf = k.sb("bar_bf", [128, 8], BF16)
        ident = k.sb("ident", [128, 128], BF16)
        k.ident = ident
        k.nphase = 0
        with ExitStack() as pes:
            k.begin_phase(pes)
            k.memset("pool", ident[:], 0.0, [], ["ident"])
            k.op("pool", lambda: nc.gpsimd.affine_select(out=ident[:], in_=ident[:], pattern=[[-1, 128]],
                                                          compare_op=ALU.not_equal, fill=1.0, base=0,
                                                          channel_multiplier=1), ["ident"], ["ident"])
            k.memset("pool", k.bar_bf[:], 0.0, [], ["bar_bf"])
            k.end_phase()
        if 1 in k.phases:
            with ExitStack() as pes:
                k.begin_phase(pes)
                k.phase1(x, w_in, lng_fm, lnb_fm, kvg, wukT, wuvr, kig, kib, attT_d)
                k.end_phase()
        if 2 in k.phases:
            with ExitStack() as pes:
                k.begin_phase(pes)
                k.phase2(x, w_in, lng_fm, lnb_fm, bgate, mcw, mcb, wbra, wbrc, attT_d, mrgT_d)
                k.end_phase()
        if 3 in k.phases:
            with ExitStack() as pes:
                k.begin_phase(pes)
                k.phase3(x, p, lng, lnb, wo, ln1g, ln1b, wpg, bpg, wple, mrgT_d, r_d, h1T_d)
                k.end_phase()
        if 4 in k.phases:
            with ExitStack() as pes:
                k.begin_phase(pes)
                k.phase4(wup, fcw, fcb, wdn, ln2g, ln2b, r_d, h1T_d, out)
                k.end_phase()
        return nc

    def begin_phase(self, pes):
        self.es = pes
        self.sc = Sched(self.nc, self.ges, reorder=self.reorder)
        self.res = {}
        self.lnst = {}

    def end_phase(self):
        nc = self.nc
        k = self
        self.sc.emit()
        bar = self.ges.enter_context(nc.semaphore("bar%d" % self.nphase))
        self.nphase += 1
        nc.vector.memset(k.bar_tile[:, 0:1], 0.0).then_inc(bar, 1)
        nc.gpsimd.memset(k.bar_tile[:, 1:2], 0.0).then_inc(bar, 1)
        nc.scalar.copy(out=k.bar_tile[:, 2:3], in_=k.bar_tile[:, 3:4]).then_inc(bar, 1)
        nc.tensor.matmul(k.ps[0:8, 0:8], lhsT=k.bar_bf[:, 0:8], rhs=k.bar_bf[:, 0:8], start=True, stop=True).then_inc(bar, 1)
        nc.sync.nop().then_inc(bar, 1)
        for e in (nc.vector, nc.gpsimd, nc.scalar, nc.tensor, nc.sync):
            e.wait_ge(bar, 5)

    def phase1(self, x, w_in, lng_fm, lnb_fm, kvg, wukT, wuvr, kig, kib, attT_d):
        k = self
        nc = self.nc
        bank, bank_bf, ident, ps = k.bank, k.bank_bf, k.ident, k.ps
        w1 = k.sb("w1", [128, 8, W1C], BF16)
        g_fm = k.sb("g_fm", [128, 8], F32)
        b_fm = k.sb("b_fm", [128, 8], F32)
        wuk_sb = k.sb("wuk_sb", [128, 4, 128], BF16)
        wuv_sb = k.sb("wuv_sb", [128, 512], BF16)
        kvg_bc = k.sb("kvg_bc", [128, 128], F32)
        kig_bc = k.sb("kig_bc", [128, 64], F32)
        kib_bc = k.sb("kib_bc", [128, 64], F32)
        negm = k.sb("negm", [128, 128], F32)
        pow2 = k.sb("pow2", [128, N_BISECT], F32)
        kT2 = k.sb("kT2", [128, S], BF16)
        ckvT = k.sb("ckvT", [128, S], BF16)
        vext = k.sb("vext", [128, 32, 8, 65], BF16)
        xbuf = [k.sb("xbuf%d" % i, [128, D], F32) for i in range(2)]
        xn_bf = k.sb("xn_bf", [128, D], BF16)
        hT2 = [k.sb("hT0", [128, 8, 512], BF16)] * 2
        qattT = k.sb("qattT", [128, 4, 512], BF16)
        qlatT2 = [k.sb("qlatT%d" % i, [128, 8, 512], BF16) for i in range(2)]
        qidxT2 = [k.sb("qidxT%d" % i, [128, 4, 512], BF16) for i in range(2)]
        absw42 = [k.sb("absw4%d" % i, [128, 4, 8], F32) for i in range(2)]
        sgn42 = [k.sb("sgn4%d" % i, [128, 4, 8], F32) for i in range(2)]
        dsgn2 = [k.sb("dsgn%d" % i, [128, 8, 128], BF16) for i in range(2)]
        relu_sb = [k.sb("relu%d" % i, [128, 512], BF16) for i in range(3)]
        score2 = [k.sb("score%d" % i, [128, S], F32) for i in range(2)]
        mask012 = [k.sb("mask01%d" % i, [128, S], BF16) for i in range(2)]
        maskT = k.sb("maskT", [128, 32, 128], BF16)
        PT = [k.sb("PT%d" % i, [128, 4, 128], BF16) for i in range(4)]
        ckv_tm = k.sb("ckv_tm", [128, 128], BF16)
        craw = k.sb("craw", [128, 128], F32)
        craw2 = k.sb("craw2", [128, 128], F32)
        kn_f = k.sb("kn_f", [128, 64], F32)
        kn2 = k.sb("kn2", [128, 128], BF16)
        sm = k.sb("sm", [128, 16], F32)
        wks2 = [k.sb("wks%d" % i, [128, N_BISECT], F32) for i in range(2)]
        bis2 = [k.sb("bis%d" % i, [128, 8], F32) for i in range(2)]
        pv_sb2 = [k.sb("pv_sb0", [65, 1024], F32)] * 2
        rden2 = [k.sb("rden0", [65, 1024], F32)] * 2
        ones_r = k.sb("ones_r", [65, 64], F32)
        att_n = [k.sb("att_n%d" % i, [64, 8, 128], BF16) for i in range(2)]
        k.alloc_lnst("x")
        k.alloc_lnst("k")
        k.guard()

        for kk in range(8):
            k.dma("pool", w1[:, kk, :], w_in[kk * 128:(kk + 1) * 128, 0:W1C], [], ["w1"], semkey="w1", nodep=True)
        W1R = ["w1"]
        k.dma("sp", g_fm[:], lng_fm[:], [], ["g_fm"])
        k.dma("sp", b_fm[:], lnb_fm[:], [], ["b_fm"])
        k.dma("pool", wuk_sb[:], wukT[:], [], ["wuk"])
        k.dma("pool", wuv_sb[:], wuvr[:], [], ["wuv"])
        k.dma("sp", kvg_bc[:], AP(kvg, 0, [[0, 128], [1, 128]]), [], ["kvg_bc"])
        k.dma("sp", kig_bc[:], AP(kig, 0, [[0, 128], [1, 64]]), [], ["kig_bc"])
        k.dma("sp", kib_bc[:], AP(kib, 0, [[0, 128], [1, 64]]), [], ["kib_bc"])
        k.memset("pool", negm[:], 0.0, [], ["negm"])
        k.op("pool", lambda: nc.gpsimd.affine_select(out=negm[:], in_=negm[:], pattern=[[-1, 128]],
                                                      compare_op=ALU.is_ge, fill=NEG, base=0,
                                                      channel_multiplier=1), ["negm"], ["negm"])
        for i in range(N_BISECT):
            k.memset("pool", pow2[:, i:i + 1], 2.0 ** (-(i + 1)), [], ["pow2"])
        k.memset("pool", vext[:, :, :, 64:65], 1.0, [], ["vext_ones"])
        k.memset("pool", ones_r[:], 1.0, [], ["ones_r"])

        BT, BR0, BR1, BSC, BL0, BL1, BV0, BV1 = 0, 1, 2, 3, 4, 5, 6, 7
        ring = [BR0, BR1]
        rstate = {"i": 0, "relu": 0, "pt": 0, "lg": 0, "xb": 0, "an": 0}

        def next_ring():
            b = ring[rstate["i"] % 2]
            rstate["i"] += 1
            return b

        for st in range(8):
            T0 = st * 512
            sp_ = st % 2
            hT, qlatT, qidxT, absw4, sgn4 = hT2[sp_], qlatT2[sp_], qidxT2[sp_], absw42[sp_], sgn42[sp_]
            HT, QL, QI, AW, SG = "hT0", "qlatT%d" % sp_, "qidxT%d" % sp_, "absw4%d" % sp_, "sgn4%d" % sp_
            for tt in range(4):
                t0 = T0 + tt * 128
                xb_i = rstate["xb"] % 2
                rstate["xb"] += 1
                xb = xbuf[xb_i]
                XR = "xbuf%d" % xb_i
                k.dma("sp", xb[:], x[t0:t0 + 128, :], [], [XR], semkey=XR)
                rstd, nmr = k.ln_stats(xb, "x", [XR], 2, 512)
                k.act(xn_bf[:], xb[:], AF.Identity, ["lnst_x", XR], ["xn_bf"], scale=rstd, bias=nmr)
                tb = bank_bf(BT)
                for kk in range(8):
                    k.tr(tb[:, kk * 128:(kk + 1) * 128], xn_bf[:, kk * 128:(kk + 1) * 128], ident[:],
                         ["xn_bf", "ident"], ["bank0", "bank0b"])
                for kk in range(8):
                    k.act(hT[:, kk, tt * 128:(tt + 1) * 128], tb[:, kk * 128:(kk + 1) * 128], AF.Identity,
                          ["bank0", "bank0b", "g_fm", "b_fm"], [HT], scale=g_fm[:, kk:kk + 1], bias=b_fm[:, kk:kk + 1])
            for j in range(4):
                b = next_ring()
                for kk in range(8):
                    k.mm(bank(b), w1[:, kk, C_QATT + 128 * j:C_QATT + 128 * (j + 1)], hT[:, kk, :], kk == 0, kk == 7,
                         [HT] + W1R, ["bank%d" % b])
                k.cp("dve", qattT[:, j, :], bank(b), ["bank%d" % b], ["qattT"])
            for j in range(4):
                b = next_ring()
                for kk in range(8):
                    k.mm(bank(b), w1[:, kk, C_QIDX + 128 * j:C_QIDX + 128 * (j + 1)], hT[:, kk, :], kk == 0, kk == 7,
                         [HT] + W1R, ["bank%d" % b])
                k.cp("act", qidxT[:, j, :], bank(b), ["bank%d" % b], [QI])
            for tt in range(4):
                blk = st * 4 + tt
                bck_, bkw_ = next_ring(), next_ring()
                PCK, PKW = "bank%d" % bck_, "bank%d" % bkw_
                pck = bank(bck_, 128, 0)
                pkw = bank(bkw_, 72, 0)
                for kk in range(8):
                    k.mm(pck, hT[:, kk, tt * 128:(tt + 1) * 128], w1[:, kk, C_CKV:C_CKV + 128], kk == 0, kk == 7,
                         [HT] + W1R, [PCK])
                for kk in range(8):
                    k.mm(pkw, hT[:, kk, tt * 128:(tt + 1) * 128], w1[:, kk, C_KIDX:C_KIDX + 72], kk == 0, kk == 7,
                         [HT] + W1R, [PKW])
                k.cp("act", craw[:], pck, [PCK], ["craw"])
                k.op("dve", lambda: nc.vector.scalar_tensor_tensor(out=craw2[:], in0=craw[:], scalar=1.0, in1=craw[:],
                                                                    op0=ALU.mult, op1=ALU.mult, accum_out=sm[:, 0:1]),
                     ["craw"], ["craw2", "sm_c"])
                k.ts("dve", sm[:, 1:2], sm[:, 0:1], 1.0 / 128.0, LN_EPS, ALU.mult, ALU.add, ["sm_c"], ["sm_c"])
                k.act(sm[:, 2:3], sm[:, 1:2], AF.Ln, ["sm_c"], ["sm_c"])
                k.act(sm[:, 2:3], sm[:, 2:3], AF.Exp, ["sm_c"], ["sm_c"], scale=-0.5)
                k.stt(ckv_tm[:], craw[:], sm[:, 2:3], kvg_bc[:], ALU.mult, ALU.mult, ["craw", "sm_c", "kvg_bc"], ["ckv_tm"])
                rstd_k, nmr_k = k.ln_stats(pkw, "k", [PKW], 1, 64)
                k.act(kn_f[:], pkw[:, 0:64], AF.Identity, [PKW, "lnst_k"], ["kn_f"], scale=rstd_k, bias=nmr_k)
                k.tt("pool", kn_f[:], kn_f[:], kig_bc[:], ALU.mult, ["kn_f", "kig_bc"], ["kn_f"])
                k.tt("pool", kn2[:, 0:64], kn_f[:], kib_bc[:], ALU.add, ["kn_f", "kib_bc"], ["kn2"])
                k.tt("pool", kn2[:, 64:128], kn_f[:], kib_bc[:], ALU.add, ["kn_f", "kib_bc"], ["kn2"])
                k.act(sm[:, 8:16], pkw[:, 64:72], AF.Copy, [PKW], ["sm_w"], scale=CW)
                k.stt(absw4[:, tt, :], sm[:, 8:16], -1.0, sm[:, 8:16], ALU.mult, ALU.max, ["sm_w"], [AW])
                k.act(sgn4[:, tt, :], pkw[:, 64:72], AF.Sign, [PKW], [SG])
                tb = bank_bf(BT)
                k.tr(tb[:, 512:640], ckv_tm[:], ident[:], ["ckv_tm", "ident"], ["bank0b"])
                k.tr(tb[:, 640:768], kn2[:], ident[:], ["kn2", "ident"], ["bank0b"])
                k.cp("act", ckvT[:, blk * 128:(blk + 1) * 128], tb[:, 512:640], ["bank0b"], ["ckvT"])
                k.cp("act", kT2[:, blk * 128:(blk + 1) * 128], tb[:, 640:768], ["bank0b"], ["kT2"])
                b = next_ring()
                k.mm(bank(b), ckvT[:, blk * 128:(blk + 1) * 128], wuv_sb[:], True, True, ["ckvT", "wuv"], ["bank%d" % b])
                k.cp("dve", vext[:, blk, :, 0:64], bank(b).rearrange("p (h d) -> p h d", h=8), ["bank%d" % b], ["vext"])
            for h in range(8):
                e, j = h % 2, h // 2
                b = next_ring()
                k.mm(bank(b), wuk_sb[64 * e:64 * e + 64, j, :], qattT[64 * e:64 * e + 64, j, :], True, True,
                     ["qattT", "wuk"], ["bank%d" % b])
                k.act(qlatT[:, h, :], bank(b), AF.Copy, ["bank%d" % b], [QL], scale=0.125)
            for i in range(4):
                I = st * 4 + i
                nk = 128 * (I + 1)
                q0 = i * 128
                ip_ = I % 2
                score, mask01, dsgn, pv_sb, rden = score2[ip_], mask012[ip_], dsgn2[ip_], pv_sb2[ip_], rden2[ip_]
                junk = mask01
                SC, MK, DS, PVS, RD = "score%d" % ip_, "mask01%d" % ip_, "dsgn%d" % ip_, "pv_sb0", "rden0"
                for h in range(8):
                    k.ts("dve", dsgn[:, h, :], ident[:], sgn4[:, i, h:h + 1], None, ALU.mult, None,
                         ["ident", SG], [DS])
                nkb = (nk + 511) // 512
                for kb in range(nkb):
                    wk = min(512, nk - 512 * kb)
                    for h in range(8):
                        e, j = h % 2, h // 2
                        b = next_ring()
                        k.mm(bank(b, wk), qidxT[64 * e:64 * e + 64, j, q0:q0 + 128],
                             kT2[64 * e:64 * e + 64, 512 * kb:512 * kb + wk], True, True,
                             [QI, "kT2"], ["bank%d" % b])
                        ri = rstate["relu"] % 3
                        rstate["relu"] += 1
                        k.act(relu_sb[ri][:, 0:wk], bank(b, wk), AF.Relu, ["bank%d" % b, AW], ["relu%d" % ri],
                              scale=absw4[:, i, h:h + 1])
                        k.mm(bank(BSC, wk), dsgn[:, h, :], relu_sb[ri][:, 0:wk], h == 0, h == 7,
                             [DS, "relu%d" % ri], ["bank%d" % BSC])
                    last = (kb == nkb - 1)
                    ncopy = wk - 128 if last else wk
                    if ncopy > 0:
                        k.cp("act", score[:, 512 * kb:512 * kb + ncopy], bank(BSC, ncopy), ["bank%d" % BSC], [SC])
                    if last:
                        k.tt("dve", score[:, nk - 128:nk], bank(BSC, 128, wk - 128), negm[:], ALU.add,
                             ["bank%d" % BSC, "negm"], [SC])
                if I >= 2:
                    bis, wks = bis2[ip_], wks2[ip_]
                    BI, WK = "bis%d" % ip_, "wks%d" % ip_
                    k.op("dve", lambda nk=nk, sc_=score, b_=bis: nc.vector.tensor_reduce(out=b_[:, 0:1], in_=sc_[:, 0:nk], axis=AX.X,
                                                                                          op=ALU.max), [SC], [BI],
                         cost=100.0 + 1.05 * nk)
                    k.op("dve", lambda sc_=score, b_=bis: nc.vector.tensor_reduce(out=b_[:, 1:2], in_=sc_[:, 0:256], axis=AX.X,
                                                                                   op=ALU.min), [SC], [BI], cost=400.0)
                    k.tt("dve", bis[:, 2:3], bis[:, 0:1], bis[:, 1:2], ALU.subtract, [BI], [BI])
                    k.ts("dve", bis[:, 2:3], bis[:, 2:3], 1.001, 1e-6, ALU.mult, ALU.add, [BI], [BI])
                    k.ts("dve", wks[:], pow2[:], bis[:, 2:3], None, ALU.mult, None, [BI, "pow2"], [WK])
                    k.tt("dve", bis[:, 3:4], bis[:, 1:2], wks[:, 0:1], ALU.add, [BI, WK], [BI])
                    for it in range(N_BISECT):
                        k.ts("dve", junk[:, 0:nk], score[:, 0:nk], bis[:, 3:4], 0.0, ALU.is_ge, ALU.add,
                             [SC, BI], [MK, BI], accum_out=bis[:, 4:5])
                        k.stt(bis[:, 5:6], bis[:, 4:5], float(TOPK) - 0.5, wks[:, it:it + 1], ALU.is_ge, ALU.mult, [BI, WK], [BI])
                        nxt = it + 1 if it + 1 < N_BISECT else it
                        k.stt(bis[:, 3:4], bis[:, 5:6], bis[:, 3:4], wks[:, nxt:nxt + 1], ALU.add, ALU.subtract, [BI, WK], [BI])
                    k.ts("dve", mask01[:, 0:nk], score[:, 0:nk], bis[:, 3:4], None, ALU.is_ge, None,
                         [SC, BI], [MK])
                else:
                    k.ts("dve", mask01[:, 0:nk], score[:, 0:nk], -1.0e29, None, ALU.is_ge, None, [SC], [MK])
                tb = bank_bf(BT)
                for g0 in range(0, I + 1, 8):
                    g1 = min(I + 1, g0 + 8)
                    for jb in range(g0, g1):
                        k.tr(tb[:, (jb - g0) * 128:(jb - g0 + 1) * 128], mask01[:, jb * 128:(jb + 1) * 128], ident[:],
                             [MK, "ident"], ["bank0", "bank0b"])
                    k.cp("act", maskT[:, g0:g1, :], tb[:, 0:(g1 - g0) * 128].rearrange("p (a b) -> p a b", b=128),
                         ["bank0", "bank0b"], ["maskT"])
                pv = ps[0:65, BV0 * 512:BV0 * 512 + 1024].rearrange("p (h q) -> p h q", h=8)
                k.op("dve", lambda: nc.vector.memset(ps[0:65, BV0 * 512:BV0 * 512 + 1024], 0.0), [], ["bankpv"])
                for jb in range(I + 1):
                    for g in range(2):
                        lb = [BL0, BL1][rstate["lg"] % 2]
                        rstate["lg"] += 1
                        k.mm(bank(lb), ckvT[:, jb * 128:(jb + 1) * 128], qlatT[:, 4 * g:4 * g + 4, q0:q0 + 128],
                             True, True, ["ckvT", QL], ["bank%d" % lb])
                        pi = rstate["pt"] % 4
                        rstate["pt"] += 1
                        k.act(PT[pi][:], bank(lb).rearrange("p (h q) -> p h q", h=4), AF.Exp, ["bank%d" % lb],
                              ["PT%d" % pi])
                        k.tt("pool", PT[pi][:], PT[pi][:], AP(maskT, jb * 128, [[32 * 128, 128], [0, 4], [1, 128]]),
                             ALU.mult, ["PT%d" % pi, "maskT"], ["PT%d" % pi])
                        for hh in range(4):
                            h = 4 * g + hh
                            k.op("pe", (lambda o=pv[:, h, :], l=vext[:, jb, h, :], rr=PT[pi][:, hh, :], sp_=(jb == I):
                                        nc.tensor.matmul(o, lhsT=l, rhs=rr, start=False, stop=sp_,
                                                         skip_group_check=True)),
                                 ["vext", "vext_ones", "PT%d" % pi], ["bankpv"])
                k.cp("act", pv_sb[:], ps[0:65, BV0 * 512:BV0 * 512 + 1024], ["bankpv"], [PVS])
                k.act(rden[64:65, :], pv_sb[64:65, :], AF.Ln, [PVS], [RD])
                k.act(rden[64:65, :], rden[64:65, :], AF.Exp, [RD], [RD], scale=-1.0)
                ai = rstate["an"] % 2
                rstate["an"] += 1
                for g in range(2):
                    lb = [BL0, BL1][rstate["lg"] % 2]
                    rstate["lg"] += 1
                    k.mm(bank(lb, 512, 0, 64), ones_r[64:65, :], rden[64:65, g * 512:(g + 1) * 512], True, True,
                         ["ones_r", RD], ["bank%d" % lb])
                    k.tt("dve", att_n[ai][:, 4 * g:4 * g + 4, :],
                         pv_sb[0:64, g * 512:(g + 1) * 512].rearrange("p (h q) -> p h q", h=4),
                         bank(lb, 512, 0, 64).rearrange("p (h q) -> p h q", h=4), ALU.mult,
                         [PVS, "bank%d" % lb], ["att_n%d" % ai])
                tok0 = T0 + q0
                k.dma("sp", AP(attT_d, tok0, [[S, 64], [64 * S, 8], [1, 128]]), att_n[ai][:],
                      ["att_n%d" % ai], ["attT_d"], semkey="att_n%d" % ai)
        k.final_wait(["att_n0", "att_n1"])

    def ln_hT(self, x, t0, tt, xbuf, rstate, xn_bf, hT, g_fm, b_fm, BT=0):
        k = self
        nc = self.nc
        xb_i = rstate["xb"] % 2
        rstate["xb"] += 1
        xb = xbuf[xb_i]
        XR = "xbuf%d" % xb_i
        k.dma("sp", xb[:], x[t0:t0 + 128, :], [], [XR], semkey=XR)
        rstd, nmr = k.ln_stats(xb, "x", [XR], 2, 512)
        k.act(xn_bf[:], xb[:], AF.Identity, ["lnst_x", XR], ["xn_bf"], scale=rstd, bias=nmr)
        tb = k.bank_bf(BT)
        for kk in range(8):
            k.tr(tb[:, kk * 128:(kk + 1) * 128], xn_bf[:, kk * 128:(kk + 1) * 128], k.ident[:],
                 ["xn_bf", "ident"], ["bank0", "bank0b"])
        for kk in range(8):
            k.act(hT[:, kk, tt * 128:(tt + 1) * 128], tb[:, kk * 128:(kk + 1) * 128], AF.Identity,
                  ["bank0", "bank0b", "g_fm", "b_fm"], ["hT"], scale=g_fm[:, kk:kk + 1], bias=b_fm[:, kk:kk + 1])

    def phase2(self, x, w_in, lng_fm, lnb_fm, bgate, mcw, mcb, wbra, wbrc, attT_d, mrgT_d):
        k = self
        nc = self.nc
        bank, bank_bf, ident, ps = k.bank, k.bank_bf, k.ident, k.ps
        w2 = k.sb("w2", [128, 8, W2C], BF16)
        wa = k.sb("wa", [128, 4, D], BF16)
        wc = k.sb("wc", [128, 4, D], BF16)
        g_fm = k.sb("g_fm", [128, 8], F32)
        b_fm = k.sb("b_fm", [128, 8], F32)
        hb = k.sb("hb", [128, 16], F32)
        mcw_sb = k.sb("mcw_sb", [128, 4, 3], F32)
        mcb_sb = k.sb("mcb_sb", [128, 4], F32)
        xbuf = [k.sb("xbuf%d" % i, [128, D], F32) for i in range(2)]
        xn_bf = k.sb("xn_bf", [128, D], BF16)
        hT = k.sb("hT", [128, 8, 512], BF16)
        att_in = k.sb("att_in", [128, 4, 512], BF16)
        u = k.sb("u", [128, 4, 514], F32)
        tmpc = k.sb("tmpc", [128, 512], F32)
        a_sb = k.sb("a_sb", [128, 512], F32)
        cyT = k.sb("cyT", [128, 4, 512], BF16)
        ta = k.sb("ta", [128, 512], F32)
        tc2 = k.sb("tc2", [128, 512], F32)
        m1 = k.sb("m1", [128, 512], F32)
        m2 = k.sb("m2", [128, 512], F32)
        mrg = [k.sb("mrg%d" % i, [128, 8, 512], BF16) for i in range(2)]
        k.alloc_lnst("x")
        k.guard()
        for kk in range(8):
            k.dma("pool", w2[:, kk, :], w_in[kk * 128:(kk + 1) * 128, W1C:IN_W], [], ["w2"], semkey="w2", nodep=True)
        W2R = ["w2"]
        k.dma("pool", wa[:], wbra.rearrange("(k p) f -> p k f", p=128), [], ["wa"])
        k.dma("pool", wc[:], wbrc.rearrange("(k p) f -> p k f", p=128), [], ["wc"])
        k.dma("sp", g_fm[:], lng_fm[:], [], ["g_fm"])
        k.dma("sp", b_fm[:], lnb_fm[:], [], ["b_fm"])
        k.dma("sp", hb[:], bgate[:], [], ["hb"])
        k.dma("sp", mcw_sb[:], mcw[:], [], ["mcw"])
        k.dma("sp", mcb_sb[:], mcb[:], [], ["mcb"])
        k.ts("dve", hb[:], hb[:], 0.5, None, ALU.mult, None, ["hb"], ["hb"])
        k.memset("pool", u[:], 0.0, [], ["u"])
        rstate = {"xb": 0, "ring": 0, "mrg": 0}
        ringb = [1, 2, 3, 4, 5, 6, 7]

        def nb():
            b = ringb[rstate["ring"] % 7]
            rstate["ring"] += 1
            return b

        def proj(col0, b):
            c0 = col0 - W1C
            for kk in range(8):
                k.mm(bank(b), w2[:, kk, c0:c0 + 128], hT[:, kk, :], kk == 0, kk == 7, ["hT"] + W2R, ["bank%d" % b])

        for st in range(8):
            T0 = st * 512
            for tt in range(4):
                k.ln_hT(x, T0 + tt * 128, tt, xbuf, rstate, xn_bf, hT, g_fm, b_fm)
            k.dma("sp", att_in[:], AP(attT_d, T0, [[S, 128], [128 * S, 4], [1, 512]]), [], ["att_in"], semkey="att_in")
            for j in range(4):
                bc_, bx_, bb_ = nb(), nb(), nb()
                proj(C_CVC + 128 * j, bc_)
                proj(C_CVX + 128 * j, bx_)
                proj(C_CVB + 128 * j, bb_)
                if st > 0:
                    k.cp("pool", u[:, j, 0:2], u[:, j, 512:514], ["u"], ["u"])
                k.cp("act", tmpc[:], bank(bc_), ["bank%d" % bc_], ["tmpc"])
                k.tt("dve", u[:, j, 2:514], tmpc[:], bank(bx_), ALU.mult, ["tmpc", "bank%d" % bx_], ["u"])
                k.act(a_sb[:], u[:, j, 2:514], AF.Identity, ["u", "mcw", "mcb"], ["a_sb"],
                      scale=mcw_sb[:, j, 2:3], bias=mcb_sb[:, j:j + 1])
                k.stt(a_sb[:], u[:, j, 1:513], mcw_sb[:, j, 1:2], a_sb[:], ALU.mult, ALU.add, ["u", "a_sb", "mcw"], ["a_sb"])
                k.stt(a_sb[:], u[:, j, 0:512], mcw_sb[:, j, 0:1], a_sb[:], ALU.mult, ALU.add, ["u", "a_sb", "mcw"], ["a_sb"])
                k.tt("dve", cyT[:, j, :], a_sb[:], bank(bb_), ALU.mult, ["a_sb", "bank%d" % bb_], ["cyT"])
            mi = rstate["mrg"] % 2
            rstate["mrg"] += 1
            for c in range(8):
                bga, bgc, bra, brc = nb(), nb(), nb(), nb()
                proj(C_GATT + 128 * c, bga)
                proj(C_GCONV + 128 * c, bgc)
                for kk in range(4):
                    k.mm(bank(bra), wa[:, kk, 128 * c:128 * (c + 1)], att_in[:, kk, :], kk == 0, kk == 3,
                         ["wa", "att_in"], ["bank%d" % bra])
                for kk in range(4):
                    k.mm(bank(brc), wc[:, kk, 128 * c:128 * (c + 1)], cyT[:, kk, :], kk == 0, kk == 3,
                         ["wc", "cyT"], ["bank%d" % brc])
                k.act(ta[:], bank(bga), AF.Tanh, ["bank%d" % bga, "hb"], ["ta"], scale=0.5, bias=hb[:, c:c + 1])
                k.act(tc2[:], bank(bgc), AF.Tanh, ["bank%d" % bgc, "hb"], ["tc2"], scale=0.5, bias=hb[:, 8 + c:9 + c])
                k.stt(m1[:], ta[:], 1.0, bank(bra), ALU.add, ALU.mult, ["ta", "bank%d" % bra], ["m1"])
                k.stt(m2[:], tc2[:], 1.0, bank(brc), ALU.add, ALU.mult, ["tc2", "bank%d" % brc], ["m2"])
                k.tt("pool", mrg[mi][:, c, :], m1[:], m2[:], ALU.add, ["m1", "m2"], ["mrg%d" % mi])
            k.dma("sp", AP(mrgT_d, T0, [[S, 128], [128 * S, 8], [1, 512]]), mrg[mi][:], ["mrg%d" % mi], ["mrgT_d"],
                  semkey="mrg%d" % mi)
        k.final_wait(["mrg0", "mrg1"])

    def phase3(self, x, p, lng, lnb, wo, ln1g, ln1b, wpg, bpg, wple, mrgT_d, r_d, h1T_d):
        k = self
        nc = self.nc
        bank, bank_bf, ident, ps = k.bank, k.bank_bf, k.ident, k.ps
        wo_sb = k.sb("wo_sb", [128, 8, D], BF16)
        wpg_sb = k.sb("wpg_sb", [128, 8, D], BF16)
        wpl_sb = k.sb("wpl_sb", [128, 2, D], BF16)
        Ga = k.sb("Ga", [128, D], F32)
        Ba = k.sb("Ba", [128, D], F32)
        G1 = k.sb("G1", [128, D], F32)
        B1 = k.sb("B1", [128, D], F32)
        HB = k.sb("HB", [128, D], F32)
        xbuf = [k.sb("xbuf%d" % i, [128, D], F32) for i in range(2)]
        pbuf = [k.sb("pbuf%d" % i, [128, 256], F32) for i in range(2)]
        m_in = [k.sb("m_in%d" % i, [128, 8, 128], BF16) for i in range(2)]
        hA_2 = [k.sb("hA%d" % i, [128, D], F32) for i in range(2)]
        y_2 = [k.sb("y%d" % i, [128, D], F32) for i in range(2)]
        h1_2 = [k.sb("h1%d" % i, [128, D], F32) for i in range(2)]
        h1_bf_2 = [k.sb("h1_bf%d" % i, [128, D], BF16) for i in range(2)]
        h1T = [k.sb("h1T%d" % i, [128, 8, 128], BF16) for i in range(2)]
        p_bf_2 = [k.sb("p_bf%d" % i, [128, 256], BF16) for i in range(2)]
        pT_2 = [k.sb("pT%d" % i, [128, 2, 128], BF16) for i in range(2)]
        tg_2 = [k.sb("tg%d" % i, [128, D], F32) for i in range(2)]
        pl2_2 = [k.sb("pl2%d" % i, [128, D], F32) for i in range(2)]
        r2 = [k.sb("r2_%d" % i, [128, D], F32) for i in range(2)]
        k.alloc_lnst("x")
        k.alloc_lnst("y")
        k.guard()
        k.dma("pool", wo_sb[:], wo.rearrange("(k p) f -> p k f", p=128), [], ["wo"])
        k.dma("pool", wpg_sb[:], wpg.rearrange("(k p) f -> p k f", p=128), [], ["wpg"])
        k.dma("pool", wpl_sb[:], wple.rearrange("(k p) f -> p k f", p=128), [], ["wpl"])
        for t_, src, nm in ((Ga, lng, "Ga"), (Ba, lnb, "Ba"), (G1, ln1g, "G1"), (B1, ln1b, "B1"), (HB, bpg, "HB")):
            k.dma("sp", t_[:], AP(src, 0, [[0, 128], [1, D]]), [], [nm])
        k.ts("dve", Ga[:], Ga[:], ALPHA, None, ALU.mult, None, ["Ga"], ["Ga"])
        k.ts("dve", Ba[:], Ba[:], ALPHA, None, ALU.mult, None, ["Ba"], ["Ba"])
        k.ts("dve", HB[:], HB[:], 0.5, None, ALU.mult, None, ["HB"], ["HB"])
        for t in range(32):
            t0 = t * 128
            bi = t % 2
            XR, PR, MR = "xbuf%d" % bi, "pbuf%d" % bi, "m_in%d" % bi
            hA, y, h1, h1_bf, p_bf, pT, tg, pl2 = hA_2[bi], y_2[bi], h1_2[bi], h1_bf_2[bi], p_bf_2[bi], pT_2[bi], tg_2[bi], pl2_2[bi]
            R_hA, R_y, R_h1, R_h1bf, R_pbf, R_pT, R_tg, R_pl2 = ["%s%d" % (n_, bi) for n_ in ("hA", "y", "h1", "h1_bf", "p_bf", "pT", "tg", "pl2")]
            k.dma("sp", xbuf[bi][:], x[t0:t0 + 128, :], [], [XR], semkey=XR)
            k.dma("sp", pbuf[bi][:], p[t0:t0 + 128, :], [], [PR], semkey=PR)
            k.dma("sp", m_in[bi][:], AP(mrgT_d, t0, [[S, 128], [128 * S, 8], [1, 128]]), [], [MR], semkey=MR)
            rstd, nmr = k.ln_stats(xbuf[bi], "x", [XR], 2, 512)
            k.act(hA[:], xbuf[bi][:], AF.Identity, ["lnst_x", XR], [R_hA], scale=rstd, bias=nmr)
            k.tt("pool", hA[:], hA[:], Ga[:], ALU.mult, [R_hA, "Ga"], [R_hA])
            k.tt("pool", hA[:], hA[:], Ba[:], ALU.add, [R_hA, "Ba"], [R_hA])
            for n in range(2):
                b = 1 + n
                for kk in range(8):
                    k.mm(bank(b), m_in[bi][:, kk, :], wo_sb[:, kk, 512 * n:512 * (n + 1)], kk == 0, kk == 7,
                         [MR, "wo"], ["bank%d" % b])
                k.stt(y[:, 512 * n:512 * (n + 1)], bank(b), 0.5, hA[:, 512 * n:512 * (n + 1)], ALU.mult, ALU.add,
                      ["bank%d" % b, R_hA], [R_y])
            rstd1, nmr1 = k.ln_stats(y, "y", [R_y], 2, 512)
            k.act(h1[:], y[:], AF.Identity, ["lnst_y", R_y], [R_h1], scale=rstd1, bias=nmr1)
            k.tt("pool", h1[:], h1[:], G1[:], ALU.mult, [R_h1, "G1"], [R_h1])
            k.tt("pool", h1[:], h1[:], B1[:], ALU.add, [R_h1, "B1"], [R_h1])
            k.cp("pool", h1_bf[:], h1[:], [R_h1], [R_h1bf])
            tb = bank_bf(0)
            for kk in range(8):
                k.tr(tb[:, kk * 128:(kk + 1) * 128], h1_bf[:, kk * 128:(kk + 1) * 128], ident[:], [R_h1bf, "ident"], ["bank0"])
            HR = "h1T%d" % bi
            k.cp("act", h1T[bi][:], tb[:, 0:1024].rearrange("p (a b) -> p a b", b=128), ["bank0"], [HR])
            k.dma("sp", AP(h1T_d, t0, [[S, 128], [128 * S, 8], [1, 128]]), h1T[bi][:], [HR], ["h1T_d"], semkey=HR)
            for n in range(2):
                b = 3 + n
                for kk in range(8):
                    k.mm(bank(b), h1T[bi][:, kk, :], wpg_sb[:, kk, 512 * n:512 * (n + 1)], kk == 0, kk == 7,
                         [HR, "wpg"], ["bank%d" % b])
                k.stt(tg[:, 512 * n:512 * (n + 1)], bank(b), 0.5, HB[:, 512 * n:512 * (n + 1)], ALU.mult, ALU.add,
                      ["bank%d" % b, "HB"], [R_tg])
            k.act(tg[:], tg[:], AF.Tanh, [R_tg], [R_tg])
            k.cp("pool", p_bf[:], pbuf[bi][:], [PR], [R_pbf])
            tb2 = bank_bf(7)
            for kk in range(2):
                k.tr(tb2[:, kk * 128:(kk + 1) * 128], p_bf[:, kk * 128:(kk + 1) * 128], ident[:], [R_pbf, "ident"], ["bank7"])
            k.cp("act", pT[:], tb2[:, 0:256].rearrange("p (a b) -> p a b", b=128), ["bank7"], [R_pT])
            RR = "r2_%d" % bi
            for n in range(2):
                b = 5 + n
                for kk in range(2):
                    k.mm(bank(b), pT[:, kk, :], wpl_sb[:, kk, 512 * n:512 * (n + 1)], kk == 0, kk == 1,
                         [R_pT, "wpl"], ["bank%d" % b])
                k.stt(pl2[:, 512 * n:512 * (n + 1)], tg[:, 512 * n:512 * (n + 1)], 1.0, bank(b), ALU.add, ALU.mult,
                      [R_tg, "bank%d" % b], [R_pl2])
            k.stt(r2[bi][:], h1[:], 2.0 * ALPHA, pl2[:], ALU.mult, ALU.add, [R_h1, R_pl2], [RR])
            k.dma("sp", r_d[t0:t0 + 128, :], r2[bi][:], [RR], ["r_d"], semkey=RR)
        k.final_wait(["r2_0", "r2_1", "h1T0", "h1T1"])

    def phase4(self, wup, fcw, fcb, wdn, ln2g, ln2b, r_d, h1T_d, out):
        k = self
        nc = self.nc
        bank, bank_bf, ident, ps = k.bank, k.bank_bf, k.ident, k.ps
        NT = 256
        wup_sb = k.sb("wup_sb", [128, 8, 2 * DFF], BF16)
        wdn_sb = k.sb("wdn_sb", [128, 22, D], BF16)
        fcw_sb = k.sb("fcw_sb", [128, 44, 3], F32)
        fcb_sb = k.sb("fcb_sb", [128, 44], F32)
        G2 = k.sb("G2", [128, D], F32)
        B2 = k.sb("B2", [128, D], F32)
        h1T2 = [k.sb("h1T%d" % i, [128, 8, NT], BF16) for i in range(2)]
        actT2 = [k.sb("actT%d" % i, [128, 22, NT], BF16) for i in range(2)]
        abuf = [[k.sb("abuf%d_%d" % (h_, i), [128, NT], F32) for i in range(2)] for h_ in range(2)]
        gbuf = [[k.sb("gbuf%d_%d" % (h_, i), [128, NT + 2], F32) for i in range(2)] for h_ in range(2)]
        sgb = [k.sb("sg%d" % i, [128, NT], F32) for i in range(2)]
        halo = k.sb("halo", [128, 44, 2], F32)
        r2b = [k.sb("r2_%d" % i, [128, D], F32) for i in range(2)]
        yb = [k.sb("yb%d" % i, [128, D], F32) for i in range(2)]
        k.alloc_lnst("y")
        k.guard()
        for kk in range(8):
            k.dma("pool", wup_sb[:, kk, :], wup[kk * 128:(kk + 1) * 128, :], [], ["wup"], semkey="wup", nodep=True)
        WUR = ["wup"]
        for c in range(22):
            k.dma("pool", wdn_sb[:, c, :], wdn[c * 128:(c + 1) * 128, :], [], ["wdn"], semkey="wdn", nodep=True)
        WDR = ["wdn"]
        k.dma("sp", fcw_sb[:], fcw[:], [], ["fcw"])
        k.dma("sp", fcb_sb[:], fcb[:], [], ["fcb"])
        k.dma("sp", G2[:], AP(ln2g, 0, [[0, 128], [1, D]]), [], ["G2"])
        k.dma("sp", B2[:], AP(ln2b, 0, [[0, 128], [1, D]]), [], ["B2"])
        k.memset("pool", halo[:], 0.0, [], ["halo%d" % ch for ch in range(44)])
        cnt = {"bank": 0, "ab0": 0, "ab1": 0, "sg": 0, "tile": 0}

        for st in range(S // NT):
            T0 = st * NT
            hb = st % 2
            h1T, aT = h1T2[hb], actT2[hb]
            HR, ATR = "h1T%d" % hb, "actT%d" % hb
            k.dma("sp", h1T[:], AP(h1T_d, T0, [[S, 128], [128 * S, 8], [1, NT]]), [], [HR], semkey=HR)
            for c in range(22):
                cur = []
                for half in range(2):
                    ch = c + 22 * half
                    b = cnt["bank"] % 4
                    cnt["bank"] += 1
                    BR = "bank%d" % b
                    for kk in range(8):
                        k.mm(bank(b, NT), wup_sb[:, kk, 128 * ch:128 * (ch + 1)], h1T[:, kk, :], kk == 0, kk == 7,
                             [HR] + WUR, [BR])
                    ai = cnt["ab%d" % half] % 2
                    cnt["ab%d" % half] += 1
                    ab, gb = abuf[half][ai], gbuf[half][ai]
                    AR, GR = "abuf%d_%d" % (half, ai), "gbuf%d_%d" % (half, ai)
                    HL = "halo%d" % ch
                    pb = bank(b, NT)
                    k.cp("pool", gb[:, 0:2], halo[:, ch, :], [HL], [GR])
                    k.cp("act", gb[:, 2:NT + 2], pb, [BR], [GR])
                    k.act(ab[:], pb, AF.Identity, [BR, "fcw", "fcb"], [AR], scale=fcw_sb[:, ch, 2:3], bias=fcb_sb[:, ch:ch + 1])
                    k.cp("pool", halo[:, ch, :], gb[:, NT:NT + 2], [GR], [HL])
                    k.stt(ab[:], gb[:, 1:NT + 1], fcw_sb[:, ch, 1:2], ab[:], ALU.mult, ALU.add, [GR, AR, "fcw"], [AR])
                    k.stt(ab[:], gb[:, 0:NT], fcw_sb[:, ch, 0:1], ab[:], ALU.mult, ALU.add, [GR, AR, "fcw"], [AR])
                    cur.append((ab, AR))
                si = cnt["sg"] % 2
                cnt["sg"] += 1
                SGR = "sg%d" % si
                k.act(sgb[si][:], cur[0][0][:], AF.Silu, [cur[0][1]], [SGR])
                k.tt("pool", aT[:, c, :], sgb[si][:], cur[1][0][:], ALU.mult, [SGR, cur[1][1]], [ATR])
            for tt in range(NT // 128):
                t0 = T0 + tt * 128
                ti = cnt["tile"] % 2
                cnt["tile"] += 1
                RR, YR = "r2_%d" % ti, "yb%d" % ti
                r2, yv = r2b[ti], yb[ti]
                k.dma("sp", r2[:], r_d[t0:t0 + 128, :], [], [RR], semkey=RR)
                for n in range(2):
                    b = 4 + 2 * ti + n
                    for c in range(22):
                        k.mm(bank(b), aT[:, c, tt * 128:(tt + 1) * 128], wdn_sb[:, c, 512 * n:512 * (n + 1)],
                             c == 0, c == 21, [ATR] + WDR, ["bank%d" % b])
                    k.stt(yv[:, 512 * n:512 * (n + 1)], r2[:, 512 * n:512 * (n + 1)], 0.5, bank(b), ALU.mult, ALU.add,
                          [RR, "bank%d" % b], [YR])
                rstd, nmr = k.ln_stats(yv, "y", [YR], 2, 512)
                k.act(yv[:], yv[:], AF.Identity, ["lnst_y", YR], [YR], scale=rstd, bias=nmr)
                k.tt("pool", yv[:], yv[:], G2[:], ALU.mult, [YR, "G2"], [YR])
                k.tt("pool", yv[:], yv[:], B2[:], ALU.add, [YR, "B2"], [YR])
                k.dma("sp", out[t0:t0 + 128, :], yv[:], [YR], ["out_d"], semkey=YR)
        k.final_wait(["yb0", "yb1"])

    def final_wait(self, names):
        nc = self.nc
        self.op("sp", lambda: nc.sync.nop(), names, names)


def _prep_inputs(inputs, b):
    f = lambda a: np.ascontiguousarray(np.asarray(a, dtype=np.float32))
    m = {}
    m["x"] = f(inputs["x"][b])
    m["p"] = f(inputs["p"][0, b])
    m["w_in"] = f(inputs["w_in"][0])
    m["lng_fm"] = f(np.asarray(inputs["ln_emb_g"]).reshape(8, 128).T)
    m["lnb_fm"] = f(np.asarray(inputs["ln_emb_b"]).reshape(8, 128).T)
    m["lng"] = f(np.asarray(inputs["ln_emb_g"]).reshape(1, D))
    m["lnb"] = f(np.asarray(inputs["ln_emb_b"]).reshape(1, D))
    m["bgate"] = f(np.asarray(inputs["b_gate"][0]).reshape(2, 8, 128).transpose(2, 0, 1).reshape(128, 16))
    m["kvg"] = f(np.asarray(inputs["kv_norm_g"][0]).reshape(1, 128))
    wuk = np.asarray(inputs["w_uk"][0])
    m["wukT"] = f(wuk.reshape(4, 2, 128, 64).transpose(1, 3, 0, 2).reshape(128, 4, 128))
    wuv = np.asarray(inputs["w_uv"][0])
    m["wuvr"] = f(wuv.transpose(1, 0, 2).reshape(128, 512))
    m["kig"] = f(np.asarray(inputs["k_idx_ln_g"][0]).reshape(1, 64))
    m["kib"] = f(np.asarray(inputs["k_idx_ln_b"][0]).reshape(1, 64))
    m["mcw"] = f(np.asarray(inputs["mix_conv_w"][0]).reshape(3, 4, 128).transpose(2, 1, 0))
    m["mcb"] = f(np.asarray(inputs["mix_conv_b"][0]).reshape(4, 128).T)
    m["wbra"] = f(inputs["w_br_att"][0])
    m["wbrc"] = f(inputs["w_br_conv"][0])
    m["wo"] = f(inputs["w_o"][0])
    m["ln1g"] = f(np.asarray(inputs["ln1_g"][0]).reshape(1, D))
    m["ln1b"] = f(np.asarray(inputs["ln1_b"][0]).reshape(1, D))
    m["wup"] = f(inputs["w_ffn_up"][0])
    m["fcw"] = f(np.asarray(inputs["ffn_conv_w"][0]).reshape(3, 44, 128).transpose(2, 1, 0))
    m["fcb"] = f(np.asarray(inputs["ffn_conv_b"][0]).reshape(44, 128).T)
    m["wdn"] = f(inputs["w_ffn_down"][0])
    m["wpg"] = f(inputs["w_ple_gate"][0])
    m["bpg"] = f(np.asarray(inputs["b_ple_gate"][0]).reshape(1, D))
    m["wple"] = f(inputs["w_ple"][0])
    m["ln2g"] = f(np.asarray(inputs["ln2_g"][0]).reshape(1, D))
    m["ln2b"] = f(np.asarray(inputs["ln2_b"][0]).reshape(1, D))
    return m


def kernel(**inputs):
    kern = Kern()
    nc = kern.build()
    in_maps = [_prep_inputs(inputs, b) for b in range(NCORES)]
    res = run_bass_kernel_spmd(nc, in_maps, core_ids=list(range(NCORES)))
    return np.stack([np.asarray(r["out"], dtype=np.float32) for r in res.results], axis=0)
```

```python
import numpy as np
from contextlib import ExitStack
import concourse.bass as bass
import concourse.mybir as mybir
from concourse.bass_utils import run_bass_kernel_spmd

F32 = mybir.dt.float32
BF16 = mybir.dt.bfloat16
ALU = mybir.AluOpType
AF = mybir.ActivationFunctionType
AX = mybir.AxisListType

S = 4096
D = 1024
NCORES = 8
IN_W = 4808
DFF = 2816
LN_EPS = 1e-5
ALPHA = 2.0 ** 0.25
TOPK = 256
NEG = -1.0e30
N_BISECT = 16
C_QATT, C_CKV, C_QIDX, C_KIDX, C_WIDX, C_CVB, C_CVC, C_CVX, C_GATT, C_GCONV = (
    0, 512, 640, 1152, 1216, 1224, 1736, 2248, 2760, 3784)
W1C = 1224
W2C = IN_W - W1C
CW = (64 ** -0.5) * (8 ** -0.5)


class Res:
    __slots__ = ("name", "last_w", "readers")

    def __init__(self, name):
        self.name = name
        self.last_w = None
        self.readers = []


class Op:
    __slots__ = ("eng", "fn", "deps", "alldeps", "orderdeps", "signal", "sem", "ticket", "is_dma", "idx", "eidx",
                 "semkey", "cost", "lat", "start", "boost")


class Sched:
    NSEM = 0
    ENGS = ("pe", "act", "dve", "pool", "sp")

    def __init__(self, nc, es, reorder=True):
        self.nc = nc
        self.es = es
        self.ops = []
        self.last_dma = {}
        self.boost = 0.0
        self.reorder = reorder
        self.engs = {"pe": nc.tensor, "act": nc.scalar, "dve": nc.vector, "pool": nc.gpsimd, "sp": nc.sync}

    def add(self, eng, fn, reads=(), writes=(), dma=False, semkey=None, nodep=False, cost=200.0, lat=0.0):
        op = Op()
        op.boost = self.boost
        op.eng = eng
        op.fn = fn
        op.is_dma = dma
        op.signal = False
        op.sem = None
        op.ticket = 0
        op.idx = len(self.ops)
        op.eidx = 0
        op.semkey = semkey
        op.cost = cost
        op.lat = lat
        op.start = 0.0
        deps = {}
        for r in reads:
            if r.last_w is not None:
                deps[r.last_w.idx] = r.last_w
        for w in writes:
            if w.last_w is not None:
                deps[w.last_w.idx] = w.last_w
            for rd in w.readers:
                deps[rd.idx] = rd
        deps.pop(op.idx, None)
        if nodep:
            deps = {}
        for r in reads:
            r.readers.append(op)
        for w in writes:
            w.last_w = op
            w.readers = []
        op.alldeps = list(deps.values())
        op.orderdeps = []
        if dma and nodep:
            prev = self.last_dma.get((eng, semkey))
            if prev is not None:
                op.orderdeps.append(prev)
            self.last_dma[(eng, semkey)] = op
        self.ops.append(op)
        return op

    def schedule(self):
        import heapq
        ops = self.ops
        n = len(ops)
        succ = [[] for _ in range(n)]
        ndeps = [0] * n
        for op in ops:
            ds = set(d.idx for d in op.alldeps) | set(d.idx for d in op.orderdeps)
            ndeps[op.idx] = len(ds)
            for di in ds:
                succ[di].append(op.idx)
        ready = [0.0] * n
        finish = [0.0] * n
        blev = [0.0] * n
        for op in reversed(ops):
            i = op.idx
            m = 0.0
            for j in succ[i]:
                if blev[j] > m:
                    m = blev[j]
            blev[i] = op.cost + op.lat + m + 100.0
        for op in ops:
            blev[op.idx] += op.boost
        eng_free = {e: 0.0 for e in self.ENGS}
        future = {e: [] for e in self.ENGS}
        avail = {e: [] for e in self.ENGS}
        for op in ops:
            if ndeps[op.idx] == 0:
                heapq.heappush(future[op.eng], (0.0, op.idx))
        order = []
        XLAT = 150.0
        SLAT = 200.0
        while len(order) < n:
            best = None
            for e in self.ENGS:
                T = eng_free[e]
                fu, av = future[e], avail[e]
                while fu and fu[0][0] <= T:
                    _, i = heapq.heappop(fu)
                    heapq.heappush(av, (-blev[i], i))
                if av:
                    st = T
                elif fu:
                    st = fu[0][0]
                else:
                    continue
                if best is None or st < best[0]:
                    best = (st, e)
            st, e = best
            if avail[e]:
                _, i = heapq.heappop(avail[e])
            else:
                _, i = heapq.heappop(future[e])
            op = ops[i]
            op.start = st
            eng_free[e] = st + op.cost
            finish[i] = st + op.cost + op.lat
            order.append(op)
            for j in succ[i]:
                r_ = finish[i] + (XLAT if ops[j].eng != e or op.is_dma else (0.0 if e == "pe" else SLAT))
                if r_ > ready[j]:
                    ready[j] = r_
                ndeps[j] -= 1
                if ndeps[j] == 0:
                    heapq.heappush(future[ops[j].eng], (ready[j], j))
        self.est_ns = max(finish) if n else 0.0
        self.ops = order

    def emit(self):
        nc = self.nc
        if self.reorder:
            self.schedule()
        ecount = {}
        for op in self.ops:
            op.eidx = ecount.get(op.eng, 0)
            ecount[op.eng] = op.eidx + 1
        for op in self.ops:
            keep = []
            for d in op.alldeps:
                if not d.is_dma and d.eng == op.eng and not op.is_dma:
                    if op.eng == "pe":
                        continue
                    if op.eidx - d.eidx > 3:
                        continue
                keep.append(d)
            op.deps = keep
        for op in self.ops:
            if op.is_dma:
                op.signal = True
            for d in op.deps:
                d.signal = True
        sems = {}
        counts = {}

        def get_sem(key):
            if key not in sems:
                Sched.NSEM += 1
                sems[key] = self.es.enter_context(nc.semaphore("s%d" % Sched.NSEM))
                counts[key] = 0
            return sems[key]

        for op in self.ops:
            if not op.signal:
                continue
            if op.is_dma:
                key = ("dma", op.semkey if op.semkey is not None else op.idx)
                op.sem = get_sem(key)
                counts[key] += 16
                op.ticket = counts[key]
            else:
                key = ("eng", op.eng)
                op.sem = get_sem(key)
                counts[key] += 1
                op.ticket = counts[key]
        waited = {}
        nwait = 0
        for op in self.ops:
            e = self.engs[op.eng]
            need = {}
            for d in op.deps:
                k = id(d.sem)
                if waited.get((op.eng, k), 0) >= d.ticket:
                    continue
                if k not in need or need[k][1] < d.ticket:
                    need[k] = (d.sem, d.ticket)
            for k, (sem, val) in need.items():
                e.wait_ge(sem, val)
                waited[(op.eng, k)] = val
                nwait += 1
            ins = op.fn()
            if op.signal:
                ins.then_inc(op.sem, 16 if op.is_dma else 1)
        self.nsems = len(sems)
        self.nwait = nwait


def fsz(ap):
    n = 1
    for d in ap.shape[1:]:
        n *= int(d)
    return n


def AP(t, off, dims):
    return bass.AP(t, off, [list(d) for d in dims])


class Kern:
    def __init__(self, phases=(1, 2, 3, 4), debug=False, reorder=True):
        self.reorder = reorder
        self.phases = phases
        self.debug = debug
        self.nc = bass.Bass("TRN2", target_bir_lowering=False)
        self.es = ExitStack()
        self.ges = self.es
        self.semcount = 0
        self.dram = {}
        self.res = {}

    def din(self, name, shape, dt=F32):
        t = self.nc.dram_tensor(name, list(shape), dt, kind="ExternalInput")
        self.dram[name] = t
        return t

    def dscr(self, name, shape, dt):
        kind = "ExternalOutput" if self.debug else "Internal"
        t = self.nc.dram_tensor(name, list(shape), dt, kind=kind)
        self.dram[name] = t
        return t

    def sb(self, name, shape, dt):
        nm = "p%d_%s" % (getattr(self, "nphase", 0), name)
        return self.es.enter_context(self.nc.sbuf_tensor(nm, list(shape), dt))

    def guard(self, kb=8):
        with self.nc.sbuf_tensor("guard%d" % self.nphase, [128, kb * 256], F32):
            pass

    def R(self, name):
        if name not in self.res:
            self.res[name] = Res(name)
        return self.res[name]

    def Rs(self, *names):
        return [self.R(n) for n in names]

    def op(self, eng, fn, r=(), w=(), dma=False, semkey=None, nodep=False, cost=200.0, lat=0.0):
        return self.sc.add(eng, fn, [self.R(x) if isinstance(x, str) else x for x in r],
                           [self.R(x) if isinstance(x, str) else x for x in w], dma=dma, semkey=semkey, nodep=nodep,
                           cost=cost, lat=lat)

    def dma(self, q, out, in_, r=(), w=(), semkey=None, nodep=False):
        e = {"sp": self.nc.sync, "pool": self.nc.gpsimd, "act": self.nc.scalar}[q]
        nbytes = fsz(out) * int(out.shape[0]) * 4
        return self.op(q, lambda: e.dma_start(out=out, in_=in_), r, w, dma=True, semkey=semkey, nodep=nodep,
                       cost=(600.0 if q == "pool" else 100.0), lat=2500.0 + nbytes / 250.0)

    def mm(self, out, lhsT, rhs, start, stop, r=(), w=()):
        nc = self.nc
        return self.op("pe", lambda: nc.tensor.matmul(out, lhsT=lhsT, rhs=rhs, start=start, stop=stop), r, w,
                       cost=64.0 + 0.5 * fsz(rhs))

    def tr(self, out, in_, ident, r=(), w=()):
        nc = self.nc
        return self.op("pe", lambda: nc.tensor.transpose(out, in_, ident), r, w, cost=130.0)

    def act(self, out, in_, func, r=(), w=(), **kw):
        nc = self.nc
        return self.op("act", lambda: nc.scalar.activation(out=out, in_=in_, func=func, **kw), r, w,
                       cost=200.0 + 0.85 * fsz(out))

    def ts(self, eng, out, in0, s1, s2, op0, op1=None, r=(), w=(), accum_out=None):
        e = self.nc.vector if eng == "dve" else self.nc.gpsimd
        c = 70.0 + 1.05 * fsz(out)
        if op1 is None:
            return self.op(eng, lambda: e.tensor_scalar(out=out, in0=in0, scalar1=s1, scalar2=None, op0=op0), r, w, cost=c)
        if accum_out is not None:
            return self.op(eng, lambda: e.tensor_scalar(out=out, in0=in0, scalar1=s1, scalar2=s2, op0=op0, op1=op1,
                                                        accum_out=accum_out), r, w, cost=c)
        return self.op(eng, lambda: e.tensor_scalar(out=out, in0=in0, scalar1=s1, scalar2=s2, op0=op0, op1=op1), r, w, cost=c)

    def tt(self, eng, out, in0, in1, op, r=(), w=()):
        e = self.nc.vector if eng == "dve" else self.nc.gpsimd
        c = (70.0 + 1.05 * fsz(out)) if eng == "dve" else (150.0 + 2.0 * fsz(out))
        return self.op(eng, lambda: e.tensor_tensor(out=out, in0=in0, in1=in1, op=op), r, w, cost=c)

    def stt(self, out, in0, scalar, in1, op0, op1, r=(), w=()):
        nc = self.nc
        return self.op("dve", lambda: nc.vector.scalar_tensor_tensor(out=out, in0=in0, scalar=scalar, in1=in1,
                                                                      op0=op0, op1=op1), r, w, cost=70.0 + 1.05 * fsz(out))

    def cp(self, eng, out, in_, r=(), w=()):
        nc = self.nc
        if eng == "act":
            return self.op("act", lambda: nc.scalar.copy(out=out, in_=in_), r, w, cost=200.0 + 0.85 * fsz(out))
        e = nc.vector if eng == "dve" else nc.gpsimd
        c = (70.0 + 1.05 * fsz(out)) if eng == "dve" else (150.0 + 1.1 * fsz(out))
        return self.op(eng, lambda: e.tensor_copy(out=out, in_=in_), r, w, cost=c)

    def memset(self, eng, ap, val, r=(), w=()):
        e = self.nc.vector if eng == "dve" else self.nc.gpsimd
        return self.op(eng, lambda: e.memset(ap, val), r, w, cost=100.0 + 1.0 * fsz(ap))

    def ln_stats(self, src, tag, r, nchunk, width):
        nc = self.nc
        st = self.lnst[tag]
        stats, mv, rs = st
        resn = "lnst_" + tag
        for c in range(nchunk):
            self.op("dve", (lambda c=c: nc.vector.bn_stats(out=stats[:, 6 * c:6 * c + 6],
                                                           in_=src[:, c * width:(c + 1) * width])), r, [resn],
                    cost=100.0 + 1.1 * width)
        self.op("dve", lambda: nc.vector.bn_aggr(out=mv[:, 0:2], in_=stats[:, 0:6 * nchunk]), [resn], [resn])
        self.ts("dve", rs[:, 0:1], mv[:, 1:2], LN_EPS, None, ALU.add, None, [resn], [resn])
        self.act(rs[:, 0:1], rs[:, 0:1], AF.Ln, [resn], [resn])
        self.act(rs[:, 0:1], rs[:, 0:1], AF.Exp, [resn], [resn], scale=-0.5)
        self.ts("dve", rs[:, 1:2], mv[:, 0:1], -1.0, rs[:, 0:1], ALU.mult, ALU.mult, [resn], [resn])
        return rs[:, 0:1], rs[:, 1:2]

    def alloc_lnst(self, tag):
        if not hasattr(self, "lnst"):
            self.lnst = {}
        self.lnst[tag] = (self.sb("lnstats_" + tag, [128, 12], F32), self.sb("lnmv_" + tag, [128, 2], F32),
                          self.sb("lnrs_" + tag, [128, 2], F32))

    def build(self):
        nc = self.nc
        k = self
        x = k.din("x", [S, D])
        p = k.din("p", [S, 256])
        w_in = k.din("w_in", [D, IN_W])
        lng_fm = k.din("lng_fm", [128, 8])
        lnb_fm = k.din("lnb_fm", [128, 8])
        lng = k.din("lng", [1, D])
        lnb = k.din("lnb", [1, D])
        bgate = k.din("bgate", [128, 16])
        kvg = k.din("kvg", [1, 128])
        wukT = k.din("wukT", [128, 4, 128])
        wuvr = k.din("wuvr", [128, 512])
        kig = k.din("kig", [1, 64])
        kib = k.din("kib", [1, 64])
        mcw = k.din("mcw", [128, 4, 3])
        mcb = k.din("mcb", [128, 4])
        wbra = k.din("wbra", [512, D])
        wbrc = k.din("wbrc", [512, D])
        wo = k.din("wo", [D, D])
        ln1g = k.din("ln1g", [1, D])
        ln1b = k.din("ln1b", [1, D])
        wup = k.din("wup", [D, 2 * DFF])
        fcw = k.din("fcw", [128, 44, 3])
        fcb = k.din("fcb", [128, 44])
        wdn = k.din("wdn", [DFF, D])
        wpg = k.din("wpg", [D, D])
        bpg = k.din("bpg", [1, D])
        wple = k.din("wple", [256, D])
        ln2g = k.din("ln2g", [1, D])
        ln2b = k.din("ln2b", [1, D])
        out = nc.dram_tensor("out", [S, D], F32, kind="ExternalOutput")
        k.dram["out"] = out
        attT_d = k.dscr("attT_d", [512, S], BF16)
        mrgT_d = k.dscr("mrgT_d", [D, S], BF16)
        r_d = k.dscr("r_d", [S, D], F32)
        h1T_d = k.dscr("h1T_d", [D, S], BF16)

        ps = k.es.enter_context(nc.psum_tensor("ps", [128, 4096], F32))
        k.ps = ps

        def bank(b, n=512, off=0, parts=128):
            return ps[0:parts, b * 512 + off: b * 512 + off + n]

        def bank_bf(b):
            return ps[:, b * 512:(b + 1) * 512].bitcast(BF16)

        k.bank = bank
        k.bank_bf = bank_bf

        k.bar_tile = k.sb("bar_tile", [128, 8], F32)
        k.bar_bf = k.sb("bar_bf", [128, 8], BF16)
        ident = k.sb("ident", [128, 128], BF16)
        k.ident = ident
        k.nphase = 0
        with ExitStack() as pes:
            k.begin_phase(pes)
            k.memset("pool", ident[:], 0.0, [], ["ident"])
            k.op("pool", lambda: nc.gpsimd.affine_select(out=ident[:], in_=ident[:], pattern=[[-1, 128]],
                                                          compare_op=ALU.not_equal, fill=1.0, base=0,
                                                          channel_multiplier=1), ["ident"], ["ident"])
            k.memset("pool", k.bar_bf[:], 0.0, [], ["bar_bf"])
            k.end_phase()
        if 1 in k.phases:
            with ExitStack() as pes:
                k.begin_phase(pes)
                k.phase1(x, w_in, lng_fm, lnb_fm, kvg, wukT, wuvr, kig, kib, attT_d)
                k.end_phase()
        if 2 in k.phases:
            with ExitStack() as pes:
                k.begin_phase(pes)
                k.phase2(x, w_in, lng_fm, lnb_fm, bgate, mcw, mcb, wbra, wbrc, attT_d, mrgT_d)
                k.end_phase()
        if 3 in k.phases:
            with ExitStack() as pes:
                k.begin_phase(pes)
                k.phase3(x, p, lng, lnb, wo, ln1g, ln1b, wpg, bpg, wple, mrgT_d, r_d, h1T_d)
                k.end_phase()
        if 4 in k.phases:
            with ExitStack() as pes:
                k.begin_phase(pes)
                k.phase4(wup, fcw, fcb, wdn, ln2g, ln2b, r_d, h1T_d, out)
                k.end_phase()
        return nc

    def begin_phase(self, pes):
        self.es = pes
        self.sc = Sched(self.nc, self.ges, reorder=self.reorder)
        self.res = {}
        self.lnst = {}

    def end_phase(self):
        nc = self.nc
        k = self
        self.sc.emit()
        bar = self.ges.enter_context(nc.semaphore("bar%d" % self.nphase))
        self.nphase += 1
        nc.vector.memset(k.bar_tile[:, 0:1], 0.0).then_inc(bar, 1)
        nc.gpsimd.memset(k.bar_tile[:, 1:2], 0.0).then_inc(bar, 1)
        nc.scalar.copy(out=k.bar_tile[:, 2:3], in_=k.bar_tile[:, 3:4]).then_inc(bar, 1)
        nc.tensor.matmul(k.ps[0:8, 0:8], lhsT=k.bar_bf[:, 0:8], rhs=k.bar_bf[:, 0:8], start=True, stop=True).then_inc(bar, 1)
        nc.sync.nop().then_inc(bar, 1)
        for e in (nc.vector, nc.gpsimd, nc.scalar, nc.tensor, nc.sync):
            e.wait_ge(bar, 5)

    def phase1(self, x, w_in, lng_fm, lnb_fm, kvg, wukT, wuvr, kig, kib, attT_d):
        k = self
        nc = self.nc
        bank, bank_bf, ident, ps = k.bank, k.bank_bf, k.ident, k.ps
        w1 = k.sb("w1", [128, 8, W1C], BF16)
        g_fm = k.sb("g_fm", [128, 8], F32)
        b_fm = k.sb("b_fm", [128, 8], F32)
        wuk_sb = k.sb("wuk_sb", [128, 4, 128], BF16)
        wuv_sb = k.sb("wuv_sb", [128, 512], BF16)
        kvg_bc = k.sb("kvg_bc", [128, 128], F32)
        kig_bc = k.sb("kig_bc", [128, 64], F32)
        kib_bc = k.sb("kib_bc", [128, 64], F32)
        negm = k.sb("negm", [128, 128], F32)
        pow2 = k.sb("pow2", [128, N_BISECT], F32)
        kT2 = k.sb("kT2", [128, S], BF16)
        ckvT = k.sb("ckvT", [128, S], BF16)
        vext = k.sb("vext", [128, 32, 8, 65], BF16)
        xbuf = [k.sb("xbuf%d" % i, [128, D], F32) for i in range(2)]
        xn_bf = k.sb("xn_bf", [128, D], BF16)
        hT2 = [k.sb("hT0", [128, 8, 512], BF16)] * 2
        qattT = k.sb("qattT", [128, 4, 512], BF16)
        qlatT2 = [k.sb("qlatT%d" % i, [128, 8, 512], BF16) for i in range(2)]
        qidxT2 = [k.sb("qidxT%d" % i, [128, 4, 512], BF16) for i in range(2)]
        absw42 = [k.sb("absw4%d" % i, [128, 4, 8], F32) for i in range(2)]
        sgn42 = [k.sb("sgn4%d" % i, [128, 4, 8], F32) for i in range(2)]
        dsgn2 = [k.sb("dsgn%d" % i, [128, 8, 128], BF16) for i in range(2)]
        relu_sb = [k.sb("relu%d" % i, [128, 512], BF16) for i in range(3)]
        score2 = [k.sb("score%d" % i, [128, S], F32) for i in range(2)]
        mask012 = [k.sb("mask01%d" % i, [128, S], BF16) for i in range(2)]
        maskT = k.sb("maskT", [128, 32, 128], BF16)
        PT = [k.sb("PT%d" % i, [128, 4, 128], BF16) for i in range(4)]
        ckv_tm = k.sb("ckv_tm", [128, 128], BF16)
        craw = k.sb("craw", [128, 128], F32)
        craw2 = k.sb("craw2", [128, 128], F32)
        kn_f = k.sb("kn_f", [128, 64], F32)
        kn2 = k.sb("kn2", [128, 128], BF16)
        sm = k.sb("sm", [128, 16], F32)
        wks2 = [k.sb("wks%d" % i, [128, N_BISECT], F32) for i in range(2)]
        bis2 = [k.sb("bis%d" % i, [128, 8], F32) for i in range(2)]
        pv_sb2 = [k.sb("pv_sb0", [65, 1024], F32)] * 2
        rden2 = [k.sb("rden0", [65, 1024], F32)] * 2
        ones_r = k.sb("ones_r", [65, 64], F32)
        att_n = [k.sb("att_n%d" % i, [64, 8, 128], BF16) for i in range(2)]
        k.alloc_lnst("x")
        k.alloc_lnst("k")
        k.guard()

        w_in_v = w_in.rearrange("(k p) f -> p k f", p=128)
        for gi, (c0_, c1_) in enumerate(((C_QATT, C_CKV), (C_CKV, C_QIDX), (C_QIDX, C_KIDX), (C_KIDX, W1C))):
            k.dma("pool", w1[:, :, c0_:c1_], w_in_v[:, :, c0_:c1_], [], ["w1g%d" % gi])
        W1R = []
        k.dma("sp", g_fm[:], lng_fm[:], [], ["g_fm"])
        k.dma("sp", b_fm[:], lnb_fm[:], [], ["b_fm"])
        k.dma("pool", wuk_sb[:], wukT[:], [], ["wuk"])
        k.dma("pool", wuv_sb[:], wuvr[:], [], ["wuv"])
        k.dma("sp", kvg_bc[:], AP(kvg, 0, [[0, 128], [1, 128]]), [], ["kvg_bc"])
        k.dma("sp", kig_bc[:], AP(kig, 0, [[0, 128], [1, 64]]), [], ["kig_bc"])
        k.dma("sp", kib_bc[:], AP(kib, 0, [[0, 128], [1, 64]]), [], ["kib_bc"])
        k.memset("pool", negm[:], 0.0, [], ["negm"])
        k.op("pool", lambda: nc.gpsimd.affine_select(out=negm[:], in_=negm[:], pattern=[[-1, 128]],
                                                      compare_op=ALU.is_ge, fill=NEG, base=0,
                                                      channel_multiplier=1), ["negm"], ["negm"])
        for i in range(N_BISECT):
            k.memset("pool", pow2[:, i:i + 1], 2.0 ** (-(i + 1)), [], ["pow2"])
        k.memset("pool", vext[:, :, :, 64:65], 1.0, [], ["vext_ones"])
        k.memset("pool", ones_r[:], 1.0, [], ["ones_r"])

        BT, BR0, BR1, BSC, BL0, BL1, BV0, BV1 = 0, 1, 2, 3, 4, 5, 6, 7
        ring = [BR0, BR1]
        rstate = {"i": 0, "relu": 0, "pt": 0, "lg": 0, "xb": 0, "an": 0}

        def next_ring():
            b = ring[rstate["i"] % 2]
            rstate["i"] += 1
            return b

        for st in range(8):
            T0 = st * 512
            sp_ = st % 2
            hT, qlatT, qidxT, absw4, sgn4 = hT2[sp_], qlatT2[sp_], qidxT2[sp_], absw42[sp_], sgn42[sp_]
            HT, QL, QI, AW, SG = "hT0", "qlatT%d" % sp_, "qidxT%d" % sp_, "absw4%d" % sp_, "sgn4%d" % sp_
            k.sc.boost = 1.0e9
            for tt in range(4):
                t0 = T0 + tt * 128
                xb_i = rstate["xb"] % 2
                rstate["xb"] += 1
                xb = xbuf[xb_i]
                XR = "xbuf%d" % xb_i
                k.dma("sp", xb[:], x[t0:t0 + 128, :], [], [XR], semkey=XR)
                rstd, nmr = k.ln_stats(xb, "x", [XR], 2, 512)
                k.act(xn_bf[:], xb[:], AF.Identity, ["lnst_x", XR], ["xn_bf"], scale=rstd, bias=nmr)
                tb = bank_bf(BT)
                for kk in range(8):
                    k.tr(tb[:, kk * 128:(kk + 1) * 128], xn_bf[:, kk * 128:(kk + 1) * 128], ident[:],
                         ["xn_bf", "ident"], ["bank0", "bank0b"])
                for kk in range(8):
                    k.act(hT[:, kk, tt * 128:(tt + 1) * 128], tb[:, kk * 128:(kk + 1) * 128], AF.Identity,
                          ["bank0", "bank0b", "g_fm", "b_fm"], [HT], scale=g_fm[:, kk:kk + 1], bias=b_fm[:, kk:kk + 1])
            for j in range(4):
                b = next_ring()
                for kk in range(8):
                    k.mm(bank(b), w1[:, kk, C_QATT + 128 * j:C_QATT + 128 * (j + 1)], hT[:, kk, :], kk == 0, kk == 7,
                         [HT, "w1g0"], ["bank%d" % b])
                k.cp("dve", qattT[:, j, :], bank(b), ["bank%d" % b], ["qattT"])
            for j in range(4):
                b = next_ring()
                for kk in range(8):
                    k.mm(bank(b), w1[:, kk, C_QIDX + 128 * j:C_QIDX + 128 * (j + 1)], hT[:, kk, :], kk == 0, kk == 7,
                         [HT, "w1g2"], ["bank%d" % b])
                k.cp("act", qidxT[:, j, :], bank(b), ["bank%d" % b], [QI])
            for tt in range(4):
                blk = st * 4 + tt
                bck_, bkw_ = next_ring(), next_ring()
                PCK, PKW = "bank%d" % bck_, "bank%d" % bkw_
                pck = bank(bck_, 128, 0)
                pkw = bank(bkw_, 72, 0)
                for kk in range(8):
                    k.mm(pck, hT[:, kk, tt * 128:(tt + 1) * 128], w1[:, kk, C_CKV:C_CKV + 128], kk == 0, kk == 7,
                         [HT, "w1g1"], [PCK])
                for kk in range(8):
                    k.mm(pkw, hT[:, kk, tt * 128:(tt + 1) * 128], w1[:, kk, C_KIDX:C_KIDX + 72], kk == 0, kk == 7,
                         [HT, "w1g3"], [PKW])
                k.cp("act", craw[:], pck, [PCK], ["craw"])
                k.op("dve", lambda: nc.vector.scalar_tensor_tensor(out=craw2[:], in0=craw[:], scalar=1.0, in1=craw[:],
                                                                    op0=ALU.mult, op1=ALU.mult, accum_out=sm[:, 0:1]),
                     ["craw"], ["craw2", "sm_c"])
                k.ts("dve", sm[:, 1:2], sm[:, 0:1], 1.0 / 128.0, LN_EPS, ALU.mult, ALU.add, ["sm_c"], ["sm_c"])
                k.act(sm[:, 2:3], sm[:, 1:2], AF.Ln, ["sm_c"], ["sm_c"])
                k.act(sm[:, 2:3], sm[:, 2:3], AF.Exp, ["sm_c"], ["sm_c"], scale=-0.5)
                k.stt(ckv_tm[:], craw[:], sm[:, 2:3], kvg_bc[:], ALU.mult, ALU.mult, ["craw", "sm_c", "kvg_bc"], ["ckv_tm"])
                rstd_k, nmr_k = k.ln_stats(pkw, "k", [PKW], 1, 64)
                k.act(kn_f[:], pkw[:, 0:64], AF.Identity, [PKW, "lnst_k"], ["kn_f"], scale=rstd_k, bias=nmr_k)
                k.tt("pool", kn_f[:], kn_f[:], kig_bc[:], ALU.mult, ["kn_f", "kig_bc"], ["kn_f"])
                k.tt("pool", kn2[:, 0:64], kn_f[:], kib_bc[:], ALU.add, ["kn_f", "kib_bc"], ["kn2"])
                k.tt("pool", kn2[:, 64:128], kn_f[:], kib_bc[:], ALU.add, ["kn_f", "kib_bc"], ["kn2"])
                k.act(sm[:, 8:16], pkw[:, 64:72], AF.Copy, [PKW], ["sm_w"], scale=CW)
                k.stt(absw4[:, tt, :], sm[:, 8:16], -1.0, sm[:, 8:16], ALU.mult, ALU.max, ["sm_w"], [AW])
                k.act(sgn4[:, tt, :], pkw[:, 64:72], AF.Sign, [PKW], [SG])
                tb = bank_bf(BT)
                k.tr(tb[:, 512:640], ckv_tm[:], ident[:], ["ckv_tm", "ident"], ["bank0b"])
                k.tr(tb[:, 640:768], kn2[:], ident[:], ["kn2", "ident"], ["bank0b"])
                k.cp("act", ckvT[:, blk * 128:(blk + 1) * 128], tb[:, 512:640], ["bank0b"], ["ckvT_%d" % st])
                k.cp("act", kT2[:, blk * 128:(blk + 1) * 128], tb[:, 640:768], ["bank0b"], ["kT2_%d" % st])
                b = next_ring()
                k.mm(bank(b), ckvT[:, blk * 128:(blk + 1) * 128], wuv_sb[:], True, True, ["ckvT_%d" % st, "wuv"], ["bank%d" % b])
                k.cp("dve", vext[:, blk, :, 0:64], bank(b).rearrange("p (h d) -> p h d", h=8), ["bank%d" % b], ["vext_%d" % st])
            for h in range(8):
                e, j = h % 2, h // 2
                b = next_ring()
                k.mm(bank(b), wuk_sb[64 * e:64 * e + 64, j, :], qattT[64 * e:64 * e + 64, j, :], True, True,
                     ["qattT", "wuk"], ["bank%d" % b])
                k.act(qlatT[:, h, :], bank(b), AF.Copy, ["bank%d" % b], [QL], scale=0.125)
            k.sc.boost = 0.0
            for i in range(4):
                I = st * 4 + i
                nk = 128 * (I + 1)
                q0 = i * 128
                ip_ = I % 2
                score, mask01, dsgn, pv_sb, rden = score2[ip_], mask012[ip_], dsgn2[ip_], pv_sb2[ip_], rden2[ip_]
                junk = mask01
                SC, MK, DS, PVS, RD = "score%d" % ip_, "mask01%d" % ip_, "dsgn%d" % ip_, "pv_sb0", "rden0"
                for h in range(8):
                    k.tt("pool", dsgn[:, h, :], ident[:], AP(sgn4, i * 8 + h, [[32, 128], [0, 128]]), ALU.mult,
                         ["ident", SG], [DS])
                nkb = (nk + 511) // 512
                for kb in range(nkb):
                    wk = min(512, nk - 512 * kb)
                    for j in range(4):
                        b0_, b1_ = next_ring(), next_ring()

                        def pair(o0=bank(b0_, wk), o1=bank(b1_, wk), l0=qidxT[0:64, j, q0:q0 + 128],
                                 l1=qidxT[64:128, j, q0:q0 + 128], r0=kT2[0:64, 512 * kb:512 * kb + wk],
                                 r1=kT2[64:128, 512 * kb:512 * kb + wk]):
                            nc.tensor.matmul(o0, lhsT=l0, rhs=r0, start=True, stop=True)
                            return nc.tensor.matmul(o1, lhsT=l1, rhs=r1, start=True, stop=True)
                        k.op("pe", pair, [QI, "kT2_%d" % kb], ["bank%d" % b0_, "bank%d" % b1_], cost=2 * (64.0 + 0.5 * wk))
                        for e, b in ((0, b0_), (1, b1_)):
                            h = 2 * j + e
                            ri = rstate["relu"] % 3
                            rstate["relu"] += 1
                            k.act(relu_sb[ri][:, 0:wk], bank(b, wk), AF.Relu, ["bank%d" % b, AW], ["relu%d" % ri],
                                  scale=absw4[:, i, h:h + 1])
                            k.mm(bank(BSC, wk), dsgn[:, h, :], relu_sb[ri][:, 0:wk], h == 0, h == 7,
                                 [DS, "relu%d" % ri], ["bank%d" % BSC])
                    last = (kb == nkb - 1)
                    ncopy = wk - 128 if last else wk
                    if ncopy > 0:
                        k.cp("act", score[:, 512 * kb:512 * kb + ncopy], bank(BSC, ncopy), ["bank%d" % BSC], [SC])
                    if last:
                        k.tt("dve", score[:, nk - 128:nk], bank(BSC, 128, wk - 128), negm[:], ALU.add,
                             ["bank%d" % BSC, "negm"], [SC])
                if I >= 2:
                    bis, wks = bis2[ip_], wks2[ip_]
                    BI, WK = "bis%d" % ip_, "wks%d" % ip_
                    k.op("dve", lambda nk=nk, sc_=score, b_=bis: nc.vector.tensor_reduce(out=b_[:, 0:1], in_=sc_[:, 0:nk], axis=AX.X,
                                                                                          op=ALU.max), [SC], [BI],
                         cost=100.0 + 1.05 * nk)
                    k.op("dve", lambda sc_=score, b_=bis: nc.vector.tensor_reduce(out=b_[:, 1:2], in_=sc_[:, 0:256], axis=AX.X,
                                                                                   op=ALU.min), [SC], [BI], cost=400.0)
                    k.tt("dve", bis[:, 2:3], bis[:, 0:1], bis[:, 1:2], ALU.subtract, [BI], [BI])
                    k.ts("dve", bis[:, 2:3], bis[:, 2:3], 1.001, 1e-6, ALU.mult, ALU.add, [BI], [BI])
                    k.ts("dve", wks[:], pow2[:], bis[:, 2:3], None, ALU.mult, None, [BI, "pow2"], [WK])
                    k.tt("dve", bis[:, 3:4], bis[:, 1:2], wks[:, 0:1], ALU.add, [BI, WK], [BI])
                    for it in range(N_BISECT):
                        k.ts("dve", junk[:, 0:nk], score[:, 0:nk], bis[:, 3:4], 0.0, ALU.is_ge, ALU.add,
                             [SC, BI], [MK, BI], accum_out=bis[:, 4:5])
                        k.stt(bis[:, 5:6], bis[:, 4:5], float(TOPK) - 0.5, wks[:, it:it + 1], ALU.is_ge, ALU.mult, [BI, WK], [BI])
                        nxt = it + 1 if it + 1 < N_BISECT else it
                        k.stt(bis[:, 3:4], bis[:, 5:6], bis[:, 3:4], wks[:, nxt:nxt + 1], ALU.add, ALU.subtract, [BI, WK], [BI])
                    k.ts("dve", mask01[:, 0:nk], score[:, 0:nk], bis[:, 3:4], None, ALU.is_ge, None,
                         [SC, BI], [MK])
                else:
                    k.ts("dve", mask01[:, 0:nk], score[:, 0:nk], -1.0e29, None, ALU.is_ge, None, [SC], [MK])
                tb = bank_bf(BT)
                for g0 in range(0, I + 1, 8):
                    g1 = min(I + 1, g0 + 8)
                    for jb in range(g0, g1):
                        k.tr(tb[:, (jb - g0) * 128:(jb - g0 + 1) * 128], mask01[:, jb * 128:(jb + 1) * 128], ident[:],
                             [MK, "ident"], ["bank0", "bank0b", "maskT"])
                    k.cp("act", maskT[:, g0:g1, :], tb[:, 0:(g1 - g0) * 128].rearrange("p (a b) -> p a b", b=128),
                         ["bank0", "bank0b"], ["maskT"])
                pv = ps[0:65, BV0 * 512:BV0 * 512 + 1024].rearrange("p (h q) -> p h q", h=8)
                k.op("dve", lambda: nc.vector.memset(ps[0:65, BV0 * 512:BV0 * 512 + 1024], 0.0), [], ["bankpv"])
                for jb in range(I + 1):
                    for g in range(2):
                        lb = [BL0, BL1][rstate["lg"] % 2]
                        rstate["lg"] += 1
                        k.mm(bank(lb), ckvT[:, jb * 128:(jb + 1) * 128], qlatT[:, 4 * g:4 * g + 4, q0:q0 + 128],
                             True, True, ["ckvT_%d" % (jb // 4), QL], ["bank%d" % lb])
                        pi = rstate["pt"] % 4
                        rstate["pt"] += 1
                        k.act(PT[pi][:], bank(lb).rearrange("p (h q) -> p h q", h=4), AF.Exp, ["bank%d" % lb],
                              ["PT%d" % pi])
                        k.tt("pool", PT[pi][:], PT[pi][:], AP(maskT, jb * 128, [[32 * 128, 128], [0, 4], [1, 128]]),
                             ALU.mult, ["PT%d" % pi, "maskT"], ["PT%d" % pi])
                        for hh in range(4):
                            h = 4 * g + hh
                            k.op("pe", (lambda o=pv[:, h, :], l=vext[:, jb, h, :], rr=PT[pi][:, hh, :], sp_=(jb == I):
                                        nc.tensor.matmul(o, lhsT=l, rhs=rr, start=False, stop=sp_,
                                                         skip_group_check=True)),
                                 ["vext_%d" % (jb // 4), "vext_ones", "PT%d" % pi], ["bankpv"])
                k.cp("act", pv_sb[:], ps[0:65, BV0 * 512:BV0 * 512 + 1024], ["bankpv"], [PVS])
                k.act(rden[64:65, :], pv_sb[64:65, :], AF.Ln, [PVS], [RD])
                k.act(rden[64:65, :], rden[64:65, :], AF.Exp, [RD], [RD], scale=-1.0)
                ai = rstate["an"] % 2
                rstate["an"] += 1
                for g in range(2):
                    lb = [BL0, BL1][rstate["lg"] % 2]
                    rstate["lg"] += 1
                    k.mm(bank(lb, 512, 0, 64), ones_r[64:65, :], rden[64:65, g * 512:(g + 1) * 512], True, True,
                         ["ones_r", RD], ["bank%d" % lb])
                    k.tt("dve", att_n[ai][:, 4 * g:4 * g + 4, :],
                         pv_sb[0:64, g * 512:(g + 1) * 512].rearrange("p (h q) -> p h q", h=4),
                         bank(lb, 512, 0, 64).rearrange("p (h q) -> p h q", h=4), ALU.mult,
                         [PVS, "bank%d" % lb], ["att_n%d" % ai])
                tok0 = T0 + q0
                k.dma("sp", AP(attT_d, tok0, [[S, 64], [64 * S, 8], [1, 128]]), att_n[ai][:],
                      ["att_n%d" % ai], ["attT_d"], semkey="att_n%d" % ai)
        k.final_wait(["att_n0", "att_n1"])

    def ln_hT(self, x, t0, tt, xbuf, rstate, xn_bf, hT, g_fm, b_fm, BT=0, HT="hT"):
        k = self
        nc = self.nc
        xb_i = rstate["xb"] % 2
        rstate["xb"] += 1
        xb = xbuf[xb_i]
        XR = "xbuf%d" % xb_i
        k.dma("sp", xb[:], x[t0:t0 + 128, :], [], [XR], semkey=XR)
        rstd, nmr = k.ln_stats(xb, "x", [XR], 2, 512)
        k.act(xn_bf[:], xb[:], AF.Identity, ["lnst_x", XR], ["xn_bf"], scale=rstd, bias=nmr)
        tb = k.bank_bf(BT)
        for kk in range(8):
            k.tr(tb[:, kk * 128:(kk + 1) * 128], xn_bf[:, kk * 128:(kk + 1) * 128], k.ident[:],
                 ["xn_bf", "ident"], ["bank0", "bank0b"])
        for kk in range(8):
            k.act(hT[:, kk, tt * 128:(tt + 1) * 128], tb[:, kk * 128:(kk + 1) * 128], AF.Identity,
                  ["bank0", "bank0b", "g_fm", "b_fm"], [HT], scale=g_fm[:, kk:kk + 1], bias=b_fm[:, kk:kk + 1])

    def phase2(self, x, w_in, lng_fm, lnb_fm, bgate, mcw, mcb, wbra, wbrc, attT_d, mrgT_d):
        k = self
        nc = self.nc
        bank, bank_bf, ident, ps = k.bank, k.bank_bf, k.ident, k.ps
        w2 = k.sb("w2", [128, 8, W2C], BF16)
        wa = k.sb("wa", [128, 4, D], BF16)
        wc = k.sb("wc", [128, 4, D], BF16)
        g_fm = k.sb("g_fm", [128, 8], F32)
        b_fm = k.sb("b_fm", [128, 8], F32)
        hb = k.sb("hb", [128, 16], F32)
        mcw_sb = k.sb("mcw_sb", [128, 4, 3], F32)
        mcb_sb = k.sb("mcb_sb", [128, 4], F32)
        xbuf = [k.sb("xbuf%d" % i, [128, D], F32) for i in range(2)]
        xn_bf = k.sb("xn_bf", [128, D], BF16)
        hT2 = [k.sb("hT%d" % i, [128, 8, 512], BF16) for i in range(2)]
        att_in = k.sb("att_in", [128, 4, 512], BF16)
        u = k.sb("u", [128, 4, 514], F32)
        tmpc = k.sb("tmpc", [128, 512], F32)
        a_sb = k.sb("a_sb", [128, 512], F32)
        cyT = k.sb("cyT", [128, 4, 512], BF16)
        ta = k.sb("ta", [128, 512], F32)
        tc2 = k.sb("tc2", [128, 512], F32)
        m1 = k.sb("m1", [128, 512], F32)
        m2 = k.sb("m2", [128, 512], F32)
        mrg = [k.sb("mrg%d" % i, [128, 8, 512], BF16) for i in range(2)]
        k.alloc_lnst("x")
        k.guard()
        w_in_v = w_in.rearrange("(k p) f -> p k f", p=128)
        for gi in (1, 2, 0, 3, 5, 4, 6):
            k.dma("pool", w2[:, :, 512 * gi:512 * (gi + 1)], w_in_v[:, :, W1C + 512 * gi:W1C + 512 * (gi + 1)], [], ["w2g%d" % gi])
        W2R = []
        k.dma("pool", wa[:], wbra.rearrange("(k p) f -> p k f", p=128), [], ["wa"])
        k.dma("pool", wc[:], wbrc.rearrange("(k p) f -> p k f", p=128), [], ["wc"])
        k.dma("sp", g_fm[:], lng_fm[:], [], ["g_fm"])
        k.dma("sp", b_fm[:], lnb_fm[:], [], ["b_fm"])
        k.dma("sp", hb[:], bgate[:], [], ["hb"])
        k.dma("sp", mcw_sb[:], mcw[:], [], ["mcw"])
        k.dma("sp", mcb_sb[:], mcb[:], [], ["mcb"])
        k.ts("dve", hb[:], hb[:], 0.5, None, ALU.mult, None, ["hb"], ["hb"])
        k.memset("pool", u[:], 0.0, [], ["u"])
        rstate = {"xb": 0, "ring": 0, "mrg": 0}
        ringb = [1, 2, 3, 4, 5, 6, 7]

        def nb():
            b = ringb[rstate["ring"] % 7]
            rstate["ring"] += 1
            return b

        cur = {}

        def proj(col0, b):
            c0 = col0 - W1C
            hT, HT = cur["hT"], cur["HT"]
            for kk in range(8):
                k.mm(bank(b), w2[:, kk, c0:c0 + 128], hT[:, kk, :], kk == 0, kk == 7, [HT, "w2g%d" % (c0 // 512)], ["bank%d" % b])

        for st in range(8):
            T0 = st * 512
            cur["hT"], cur["HT"] = hT2[st % 2], "hT%d" % (st % 2)
            for tt in range(4):
                k.ln_hT(x, T0 + tt * 128, tt, xbuf, rstate, xn_bf, cur["hT"], g_fm, b_fm, HT=cur["HT"])
            k.dma("sp", att_in[:], AP(attT_d, T0, [[S, 128], [128 * S, 4], [1, 512]]), [], ["att_in"], semkey="att_in")
            for j in range(4):
                bc_, bx_, bb_ = nb(), nb(), nb()
                proj(C_CVC + 128 * j, bc_)
                proj(C_CVX + 128 * j, bx_)
                proj(C_CVB + 128 * j, bb_)
                if st > 0:
                    k.cp("pool", u[:, j, 0:2], u[:, j, 512:514], ["u"], ["u"])
                k.cp("act", tmpc[:], bank(bc_), ["bank%d" % bc_], ["tmpc"])
                k.tt("dve", u[:, j, 2:514], tmpc[:], bank(bx_), ALU.mult, ["tmpc", "bank%d" % bx_], ["u"])
                k.act(a_sb[:], u[:, j, 2:514], AF.Identity, ["u", "mcw", "mcb"], ["a_sb"],
                      scale=mcw_sb[:, j, 2:3], bias=mcb_sb[:, j:j + 1])
                k.stt(a_sb[:], u[:, j, 1:513], mcw_sb[:, j, 1:2], a_sb[:], ALU.mult, ALU.add, ["u", "a_sb", "mcw"], ["a_sb"])
                k.stt(a_sb[:], u[:, j, 0:512], mcw_sb[:, j, 0:1], a_sb[:], ALU.mult, ALU.add, ["u", "a_sb", "mcw"], ["a_sb"])
                k.tt("dve", cyT[:, j, :], a_sb[:], bank(bb_), ALU.mult, ["a_sb", "bank%d" % bb_], ["cyT"])
            mi = rstate["mrg"] % 2
            rstate["mrg"] += 1
            for c in range(8):
                bga, bgc, bra, brc = nb(), nb(), nb(), nb()
                proj(C_GATT + 128 * c, bga)
                proj(C_GCONV + 128 * c, bgc)
                for kk in range(4):
                    k.mm(bank(bra), wa[:, kk, 128 * c:128 * (c + 1)], att_in[:, kk, :], kk == 0, kk == 3,
                         ["wa", "att_in"], ["bank%d" % bra])
                for kk in range(4):
                    k.mm(bank(brc), wc[:, kk, 128 * c:128 * (c + 1)], cyT[:, kk, :], kk == 0, kk == 3,
                         ["wc", "cyT"], ["bank%d" % brc])
                k.act(ta[:], bank(bga), AF.Tanh, ["bank%d" % bga, "hb"], ["ta"], scale=0.5, bias=hb[:, c:c + 1])
                k.act(tc2[:], bank(bgc), AF.Tanh, ["bank%d" % bgc, "hb"], ["tc2"], scale=0.5, bias=hb[:, 8 + c:9 + c])
                k.stt(m1[:], ta[:], 1.0, bank(bra), ALU.add, ALU.mult, ["ta", "bank%d" % bra], ["m1"])
                k.stt(m2[:], tc2[:], 1.0, bank(brc), ALU.add, ALU.mult, ["tc2", "bank%d" % brc], ["m2"])
                k.tt("pool", mrg[mi][:, c, :], m1[:], m2[:], ALU.add, ["m1", "m2"], ["mrg%d" % mi])
            k.dma("sp", AP(mrgT_d, T0, [[S, 128], [128 * S, 8], [1, 512]]), mrg[mi][:], ["mrg%d" % mi], ["mrgT_d"],
                  semkey="mrg%d" % mi)
        k.final_wait(["mrg0", "mrg1"])

    def phase3(self, x, p, lng, lnb, wo, ln1g, ln1b, wpg, bpg, wple, mrgT_d, r_d, h1T_d):
        k = self
        nc = self.nc
        bank, bank_bf, ident, ps = k.bank, k.bank_bf, k.ident, k.ps
        wo_sb = k.sb("wo_sb", [128, 8, D], BF16)
        wpg_sb = k.sb("wpg_sb", [128, 8, D], BF16)
        wpl_sb = k.sb("wpl_sb", [128, 2, D], BF16)
        Ga = k.sb("Ga", [128, D], F32)
        Ba = k.sb("Ba", [128, D], F32)
        G1 = k.sb("G1", [128, D], F32)
        B1 = k.sb("B1", [128, D], F32)
        HB = k.sb("HB", [128, D], F32)
        xbuf = [k.sb("xbuf%d" % i, [128, D], F32) for i in range(2)]
        pbuf = [k.sb("pbuf%d" % i, [128, 256], F32) for i in range(2)]
        m_in = [k.sb("m_in%d" % i, [128, 8, 128], BF16) for i in range(2)]
        hA_2 = [k.sb("hA%d" % i, [128, D], F32) for i in range(2)]
        y_2 = [k.sb("y%d" % i, [128, D], F32) for i in range(2)]
        h1_2 = [k.sb("h1%d" % i, [128, D], F32) for i in range(2)]
        h1_bf_2 = [k.sb("h1_bf%d" % i, [128, D], BF16) for i in range(2)]
        h1T = [k.sb("h1T%d" % i, [128, 8, 128], BF16) for i in range(2)]
        p_bf_2 = [k.sb("p_bf%d" % i, [128, 256], BF16) for i in range(2)]
        pT_2 = [k.sb("pT%d" % i, [128, 2, 128], BF16) for i in range(2)]
        tg_2 = [k.sb("tg%d" % i, [128, D], F32) for i in range(2)]
        pl2_2 = [k.sb("pl2%d" % i, [128, D], F32) for i in range(2)]
        r2 = [k.sb("r2_%d" % i, [128, D], F32) for i in range(2)]
        k.alloc_lnst("x")
        k.alloc_lnst("y")
        k.guard()
        for n_ in range(2):
            k.dma("pool", wo_sb[:, :, 512 * n_:512 * (n_ + 1)], wo.rearrange("(k p) f -> p k f", p=128)[:, :, 512 * n_:512 * (n_ + 1)],
                  [], ["wo%d" % n_])
        for n_ in range(2):
            k.dma("pool", wpg_sb[:, :, 512 * n_:512 * (n_ + 1)], wpg.rearrange("(k p) f -> p k f", p=128)[:, :, 512 * n_:512 * (n_ + 1)],
                  [], ["wpg%d" % n_])
        k.dma("pool", wpl_sb[:], wple.rearrange("(k p) f -> p k f", p=128), [], ["wpl"])
        for t_, src, nm in ((Ga, lng, "Ga"), (Ba, lnb, "Ba"), (G1, ln1g, "G1"), (B1, ln1b, "B1"), (HB, bpg, "HB")):
            k.dma("sp", t_[:], AP(src, 0, [[0, 128], [1, D]]), [], [nm])
        k.ts("dve", Ga[:], Ga[:], ALPHA, None, ALU.mult, None, ["Ga"], ["Ga"])
        k.ts("dve", Ba[:], Ba[:], ALPHA, None, ALU.mult, None, ["Ba"], ["Ba"])
        k.ts("dve", HB[:], HB[:], 0.5, None, ALU.mult, None, ["HB"], ["HB"])
        for t in range(32):
            t0 = t * 128
            bi = t % 2
            XR, PR, MR = "xbuf%d" % bi, "pbuf%d" % bi, "m_in%d" % bi
            hA, y, h1, h1_bf, p_bf, pT, tg, pl2 = hA_2[bi], y_2[bi], h1_2[bi], h1_bf_2[bi], p_bf_2[bi], pT_2[bi], tg_2[bi], pl2_2[bi]
            R_hA, R_y, R_h1, R_h1bf, R_pbf, R_pT, R_tg, R_pl2 = ["%s%d" % (n_, bi) for n_ in ("hA", "y", "h1", "h1_bf", "p_bf", "pT", "tg", "pl2")]
            k.dma("sp", xbuf[bi][:], x[t0:t0 + 128, :], [], [XR], semkey=XR)
            k.dma("sp", pbuf[bi][:], p[t0:t0 + 128, :], [], [PR], semkey=PR)
            k.dma("sp", m_in[bi][:], AP(mrgT_d, t0, [[S, 128], [128 * S, 8], [1, 128]]), [], [MR], semkey=MR)
            rstd, nmr = k.ln_stats(xbuf[bi], "x", [XR], 2, 512)
            k.act(hA[:], xbuf[bi][:], AF.Identity, ["lnst_x", XR], [R_hA], scale=rstd, bias=nmr)
            k.tt("pool", hA[:], hA[:], Ga[:], ALU.mult, [R_hA, "Ga"], [R_hA])
            k.tt("pool", hA[:], hA[:], Ba[:], ALU.add, [R_hA, "Ba"], [R_hA])
            for n in range(2):
                b = 1 + n
                for kk in range(8):
                    k.mm(bank(b), m_in[bi][:, kk, :], wo_sb[:, kk, 512 * n:512 * (n + 1)], kk == 0, kk == 7,
                         [MR, "wo%d" % n], ["bank%d" % b])
                k.stt(y[:, 512 * n:512 * (n + 1)], bank(b), 0.5, hA[:, 512 * n:512 * (n + 1)], ALU.mult, ALU.add,
                      ["bank%d" % b, R_hA], [R_y])
            rstd1, nmr1 = k.ln_stats(y, "y", [R_y], 2, 512)
            k.act(h1[:], y[:], AF.Identity, ["lnst_y", R_y], [R_h1], scale=rstd1, bias=nmr1)
            k.tt("pool", h1[:], h1[:], G1[:], ALU.mult, [R_h1, "G1"], [R_h1])
            k.tt("pool", h1[:], h1[:], B1[:], ALU.add, [R_h1, "B1"], [R_h1])
            k.cp("pool", h1_bf[:], h1[:], [R_h1], [R_h1bf])
            tb = bank_bf(0)
            for kk in range(8):
                k.tr(tb[:, kk * 128:(kk + 1) * 128], h1_bf[:, kk * 128:(kk + 1) * 128], ident[:], [R_h1bf, "ident"], ["bank0"])
            HR = "h1T%d" % bi
            k.cp("act", h1T[bi][:], tb[:, 0:1024].rearrange("p (a b) -> p a b", b=128), ["bank0"], [HR])
            k.dma("sp", AP(h1T_d, t0, [[S, 128], [128 * S, 8], [1, 128]]), h1T[bi][:], [HR], ["h1T_d"], semkey=HR)
            for n in range(2):
                b = 3 + n
                for kk in range(8):
                    k.mm(bank(b), h1T[bi][:, kk, :], wpg_sb[:, kk, 512 * n:512 * (n + 1)], kk == 0, kk == 7,
                         [HR, "wpg%d" % n], ["bank%d" % b])
                k.stt(tg[:, 512 * n:512 * (n + 1)], bank(b), 0.5, HB[:, 512 * n:512 * (n + 1)], ALU.mult, ALU.add,
                      ["bank%d" % b, "HB"], [R_tg])
            k.act(tg[:], tg[:], AF.Tanh, [R_tg], [R_tg])
            k.cp("pool", p_bf[:], pbuf[bi][:], [PR], [R_pbf])
            tb2 = bank_bf(7)
            for kk in range(2):
                k.tr(tb2[:, kk * 128:(kk + 1) * 128], p_bf[:, kk * 128:(kk + 1) * 128], ident[:], [R_pbf, "ident"], ["bank7"])
            k.cp("act", pT[:], tb2[:, 0:256].rearrange("p (a b) -> p a b", b=128), ["bank7"], [R_pT])
            RR = "r2_%d" % bi
            for n in range(2):
                b = 5 + n
                for kk in range(2):
                    k.mm(bank(b), pT[:, kk, :], wpl_sb[:, kk, 512 * n:512 * (n + 1)], kk == 0, kk == 1,
                         [R_pT, "wpl"], ["bank%d" % b])
                k.stt(pl2[:, 512 * n:512 * (n + 1)], tg[:, 512 * n:512 * (n + 1)], 1.0, bank(b), ALU.add, ALU.mult,
                      [R_tg, "bank%d" % b], [R_pl2])
            k.stt(r2[bi][:], h1[:], 2.0 * ALPHA, pl2[:], ALU.mult, ALU.add, [R_h1, R_pl2], [RR])
            k.dma("sp", r_d[t0:t0 + 128, :], r2[bi][:], [RR], ["r_d"], semkey=RR)
        k.final_wait(["r2_0", "r2_1", "h1T0", "h1T1"])

    def phase4(self, wup, fcw, fcb, wdn, ln2g, ln2b, r_d, h1T_d, out):
        k = self
        nc = self.nc
        bank, bank_bf, ident, ps = k.bank, k.bank_bf, k.ident, k.ps
        NT = 256
        wup_sb = k.sb("wup_sb", [128, 8, 2 * DFF], BF16)
        wdn_sb = k.sb("wdn_sb", [128, 22, D], BF16)
        fcw_sb = k.sb("fcw_sb", [128, 44, 3], F32)
        fcb_sb = k.sb("fcb_sb", [128, 44], F32)
        G2 = k.sb("G2", [128, D], F32)
        B2 = k.sb("B2", [128, D], F32)
        h1T2 = [k.sb("h1T%d" % i, [128, 8, NT], BF16) for i in range(2)]
        actT2 = [k.sb("actT%d" % i, [128, 22, NT], BF16) for i in range(2)]
        abuf = [[k.sb("abuf%d_%d" % (h_, i), [128, NT], F32) for i in range(2)] for h_ in range(2)]
        gbuf = [[k.sb("gbuf%d_%d" % (h_, i), [128, NT + 2], F32) for i in range(2)] for h_ in range(2)]
        sgb = [k.sb("sg%d" % i, [128, NT], F32) for i in range(2)]
        halo = k.sb("halo", [128, 44, 2], F32)
        r2b = [k.sb("r2_%d" % i, [128, D], F32) for i in range(2)]
        yb = [k.sb("yb%d" % i, [128, D], F32) for i in range(2)]
        k.alloc_lnst("y")
        k.guard()
        wup_v = wup.rearrange("(k p) f -> p k f", p=128)
        for g_ in range(6):
            for half_ in range(2):
                c0_ = half_ * DFF + 512 * g_
                c1_ = min(c0_ + 512, half_ * DFF + DFF)
                k.dma("pool", wup_sb[:, :, c0_:c1_], wup_v[:, :, c0_:c1_], [], ["wupg%d_%d" % (g_, half_)])
        WUR = []
        for c in range(22):
            k.dma("pool", wdn_sb[:, c, :], wdn[c * 128:(c + 1) * 128, :], [], ["wdn"], semkey="wdn", nodep=True)
        WDR = ["wdn"]
        k.dma("sp", fcw_sb[:], fcw[:], [], ["fcw"])
        k.dma("sp", fcb_sb[:], fcb[:], [], ["fcb"])
        k.dma("sp", G2[:], AP(ln2g, 0, [[0, 128], [1, D]]), [], ["G2"])
        k.dma("sp", B2[:], AP(ln2b, 0, [[0, 128], [1, D]]), [], ["B2"])
        k.memset("pool", halo[:], 0.0, [], ["halo%d" % ch for ch in range(44)])
        cnt = {"bank": 0, "ab0": 0, "ab1": 0, "sg": 0, "tile": 0}

        for st in range(S // NT):
            T0 = st * NT
            hb = st % 2
            h1T, aT = h1T2[hb], actT2[hb]
            HR, ATR = "h1T%d" % hb, "actT%d" % hb
            k.dma("sp", h1T[:], AP(h1T_d, T0, [[S, 128], [128 * S, 8], [1, NT]]), [], [HR], semkey=HR)
            for c in range(22):
                cur = []
                for half in range(2):
                    ch = c + 22 * half
                    b = cnt["bank"] % 4
                    cnt["bank"] += 1
                    BR = "bank%d" % b
                    for kk in range(8):
                        k.mm(bank(b, NT), wup_sb[:, kk, 128 * ch:128 * (ch + 1)], h1T[:, kk, :], kk == 0, kk == 7,
                             [HR, "wupg%d_%d" % (c // 4, half)], [BR])
                    ai = cnt["ab%d" % half] % 2
                    cnt["ab%d" % half] += 1
                    ab, gb = abuf[half][ai], gbuf[half][ai]
                    AR, GR = "abuf%d_%d" % (half, ai), "gbuf%d_%d" % (half, ai)
                    HL = "halo%d" % ch
                    pb = bank(b, NT)
                    k.cp("pool", gb[:, 0:2], halo[:, ch, :], [HL], [GR])
                    k.cp("act", gb[:, 2:NT + 2], pb, [BR], [GR])
                    k.act(ab[:], pb, AF.Identity, [BR, "fcw", "fcb"], [AR], scale=fcw_sb[:, ch, 2:3], bias=fcb_sb[:, ch:ch + 1])
                    k.cp("pool", halo[:, ch, :], gb[:, NT:NT + 2], [GR], [HL])
                    k.stt(ab[:], gb[:, 1:NT + 1], fcw_sb[:, ch, 1:2], ab[:], ALU.mult, ALU.add, [GR, AR, "fcw"], [AR])
                    k.stt(ab[:], gb[:, 0:NT], fcw_sb[:, ch, 0:1], ab[:], ALU.mult, ALU.add, [GR, AR, "fcw"], [AR])
                    cur.append((ab, AR))
                si = cnt["sg"] % 2
                cnt["sg"] += 1
                SGR = "sg%d" % si
                k.act(sgb[si][:], cur[0][0][:], AF.Silu, [cur[0][1]], [SGR])
                k.tt("pool", aT[:, c, :], sgb[si][:], cur[1][0][:], ALU.mult, [SGR, cur[1][1]], [ATR])
            for tt in range(NT // 128):
                t0 = T0 + tt * 128
                ti = cnt["tile"] % 2
                cnt["tile"] += 1
                RR, YR = "r2_%d" % ti, "yb%d" % ti
                r2, yv = r2b[ti], yb[ti]
                k.dma("sp", r2[:], r_d[t0:t0 + 128, :], [], [RR], semkey=RR)
                for n in range(2):
                    b = 4 + 2 * ti + n
                    for c in range(22):
                        k.mm(bank(b), aT[:, c, tt * 128:(tt + 1) * 128], wdn_sb[:, c, 512 * n:512 * (n + 1)],
                             c == 0, c == 21, [ATR] + WDR, ["bank%d" % b])
                    k.stt(yv[:, 512 * n:512 * (n + 1)], r2[:, 512 * n:512 * (n + 1)], 0.5, bank(b), ALU.mult, ALU.add,
                          [RR, "bank%d" % b], [YR])
                rstd, nmr = k.ln_stats(yv, "y", [YR], 2, 512)
                k.act(yv[:], yv[:], AF.Identity, ["lnst_y", YR], [YR], scale=rstd, bias=nmr)
                k.tt("pool", yv[:], yv[:], G2[:], ALU.mult, [YR, "G2"], [YR])
                k.tt("pool", yv[:], yv[:], B2[:], ALU.add, [YR, "B2"], [YR])
                k.dma("sp", out[t0:t0 + 128, :], yv[:], [YR], ["out_d"], semkey=YR)
        k.final_wait(["yb0", "yb1"])

    def final_wait(self, names):
        nc = self.nc
        self.op("sp", lambda: nc.sync.nop(), names, names)


def _prep_inputs(inputs, b):
    f = lambda a: np.ascontiguousarray(np.asarray(a, dtype=np.float32))
    m = {}
    m["x"] = f(inputs["x"][b])
    m["p"] = f(inputs["p"][0, b])
    m["w_in"] = f(inputs["w_in"][0])
    m["lng_fm"] = f(np.asarray(inputs["ln_emb_g"]).reshape(8, 128).T)
    m["lnb_fm"] = f(np.asarray(inputs["ln_emb_b"]).reshape(8, 128).T)
    m["lng"] = f(np.asarray(inputs["ln_emb_g"]).reshape(1, D))
    m["lnb"] = f(np.asarray(inputs["ln_emb_b"]).reshape(1, D))
    m["bgate"] = f(np.asarray(inputs["b_gate"][0]).reshape(2, 8, 128).transpose(2, 0, 1).reshape(128, 16))
    m["kvg"] = f(np.asarray(inputs["kv_norm_g"][0]).reshape(1, 128))
    wuk = np.asarray(inputs["w_uk"][0])
    m["wukT"] = f(wuk.reshape(4, 2, 128, 64).transpose(1, 3, 0, 2).reshape(128, 4, 128))
    wuv = np.asarray(inputs["w_uv"][0])
    m["wuvr"] = f(wuv.transpose(1, 0, 2).reshape(128, 512))
    m["kig"] = f(np.asarray(inputs["k_idx_ln_g"][0]).reshape(1, 64))
    m["kib"] = f(np.asarray(inputs["k_idx_ln_b"][0]).reshape(1, 64))
    m["mcw"] = f(np.asarray(inputs["mix_conv_w"][0]).reshape(3, 4, 128).transpose(2, 1, 0))
    m["mcb"] = f(np.asarray(inputs["mix_conv_b"][0]).reshape(4, 128).T)
    m["wbra"] = f(inputs["w_br_att"][0])
    m["wbrc"] = f(inputs["w_br_conv"][0])
    m["wo"] = f(inputs["w_o"][0])
    m["ln1g"] = f(np.asarray(inputs["ln1_g"][0]).reshape(1, D))
    m["ln1b"] = f(np.asarray(inputs["ln1_b"][0]).reshape(1, D))
    m["wup"] = f(inputs["w_ffn_up"][0])
    m["fcw"] = f(np.asarray(inputs["ffn_conv_w"][0]).reshape(3, 44, 128).transpose(2, 1, 0))
    m["fcb"] = f(np.asarray(inputs["ffn_conv_b"][0]).reshape(44, 128).T)
    m["wdn"] = f(inputs["w_ffn_down"][0])
    m["wpg"] = f(inputs["w_ple_gate"][0])
    m["bpg"] = f(np.asarray(inputs["b_ple_gate"][0]).reshape(1, D))
    m["wple"] = f(inputs["w_ple"][0])
    m["ln2g"] = f(np.asarray(inputs["ln2_g"][0]).reshape(1, D))
    m["ln2b"] = f(np.asarray(inputs["ln2_b"][0]).reshape(1, D))
    return m


def kernel(**inputs):
    kern = Kern()
    nc = kern.build()
    in_maps = [_prep_inputs(inputs, b) for b in range(NCORES)]
    res = run_bass_kernel_spmd(nc, in_maps, core_ids=list(range(NCORES)))
    return np.stack([np.asarray(r["out"], dtype=np.float32) for r in res.results], axis=0)
```

```python
import numpy as np
from contextlib import ExitStack
import concourse.bass as bass
import concourse.mybir as mybir
from concourse.bass_utils import run_bass_kernel_spmd

F32 = mybir.dt.float32
BF16 = mybir.dt.bfloat16
ALU = mybir.AluOpType
AF = mybir.ActivationFunctionType
AX = mybir.AxisListType

S = 4096
D = 1024
NCORES = 8
IN_W = 4808
DFF = 2816
LN_EPS = 1e-5
ALPHA = 2.0 ** 0.25
TOPK = 256
NEG = -1.0e30
N_BISECT = 16
C_QATT, C_CKV, C_QIDX, C_KIDX, C_WIDX, C_CVB, C_CVC, C_CVX, C_GATT, C_GCONV = (
    0, 512, 640, 1152, 1216, 1224, 1736, 2248, 2760, 3784)
W1C = 1224
W2C = IN_W - W1C
CW = (64 ** -0.5) * (8 ** -0.5)


class Res:
    __slots__ = ("name", "last_w", "readers")

    def __init__(self, name):
        self.name = name
        self.last_w = None
        self.readers = []


class Op:
    __slots__ = ("eng", "fn", "deps", "alldeps", "orderdeps", "signal", "sem", "ticket", "is_dma", "idx", "eidx",
                 "semkey", "cost", "lat", "start", "boost")


class Sched:
    NSEM = 0
    ENGS = ("pe", "act", "dve", "pool", "sp")

    def __init__(self, nc, es, reorder=True):
        self.nc = nc
        self.es = es
        self.ops = []
        self.last_dma = {}
        self.boost = 0.0
        self.reorder = reorder
        self.engs = {"pe": nc.tensor, "act": nc.scalar, "dve": nc.vector, "pool": nc.gpsimd, "sp": nc.sync}

    def add(self, eng, fn, reads=(), writes=(), dma=False, semkey=None, nodep=False, cost=200.0, lat=0.0):
        op = Op()
        op.boost = self.boost
        op.eng = eng
        op.fn = fn
        op.is_dma = dma
        op.signal = False
        op.sem = None
        op.ticket = 0
        op.idx = len(self.ops)
        op.eidx = 0
        op.semkey = semkey
        op.cost = cost
        op.lat = lat
        op.start = 0.0
        deps = {}
        for r in reads:
            if r.last_w is not None:
                deps[r.last_w.idx] = r.last_w
        for w in writes:
            if w.last_w is not None:
                deps[w.last_w.idx] = w.last_w
            for rd in w.readers:
                deps[rd.idx] = rd
        deps.pop(op.idx, None)
        if nodep:
            deps = {}
        for r in reads:
            r.readers.append(op)
        for w in writes:
            w.last_w = op
            w.readers = []
        op.alldeps = list(deps.values())
        op.orderdeps = []
        if dma and nodep:
            prev = self.last_dma.get((eng, semkey))
            if prev is not None:
                op.orderdeps.append(prev)
            self.last_dma[(eng, semkey)] = op
        self.ops.append(op)
        return op

    def schedule(self):
        import heapq
        ops = self.ops
        n = len(ops)
        succ = [[] for _ in range(n)]
        ndeps = [0] * n
        for op in ops:
            ds = set(d.idx for d in op.alldeps) | set(d.idx for d in op.orderdeps)
            ndeps[op.idx] = len(ds)
            for di in ds:
                succ[di].append(op.idx)
        ready = [0.0] * n
        finish = [0.0] * n
        blev = [0.0] * n
        for op in reversed(ops):
            i = op.idx
            m = 0.0
            for j in succ[i]:
                if blev[j] > m:
                    m = blev[j]
            blev[i] = op.cost + op.lat + m + 100.0
        for op in ops:
            blev[op.idx] += op.boost
        eng_free = {e: 0.0 for e in self.ENGS}
        future = {e: [] for e in self.ENGS}
        avail = {e: [] for e in self.ENGS}
        for op in ops:
            if ndeps[op.idx] == 0:
                heapq.heappush(future[op.eng], (0.0, op.idx))
        order = []
        XLAT = 150.0
        SLAT = 200.0
        while len(order) < n:
            best = None
            for e in self.ENGS:
                T = eng_free[e]
                fu, av = future[e], avail[e]
                while fu and fu[0][0] <= T:
                    _, i = heapq.heappop(fu)
                    heapq.heappush(av, (-blev[i], i))
                if av:
                    st = T
                elif fu:
                    st = fu[0][0]
                else:
                    continue
                if best is None or st < best[0]:
                    best = (st, e)
            st, e = best
            if avail[e]:
                _, i = heapq.heappop(avail[e])
            else:
                _, i = heapq.heappop(future[e])
            op = ops[i]
            op.start = st
            eng_free[e] = st + op.cost
            finish[i] = st + op.cost + op.lat
            order.append(op)
            for j in succ[i]:
                r_ = finish[i] + (XLAT if ops[j].eng != e or op.is_dma else (0.0 if e == "pe" else SLAT))
                if r_ > ready[j]:
                    ready[j] = r_
                ndeps[j] -= 1
                if ndeps[j] == 0:
                    heapq.heappush(future[ops[j].eng], (ready[j], j))
        self.est_ns = max(finish) if n else 0.0
        self.ops = order

    def emit(self):
        nc = self.nc
        if self.reorder:
            self.schedule()
        ecount = {}
        for op in self.ops:
            op.eidx = ecount.get(op.eng, 0)
            ecount[op.eng] = op.eidx + 1
        for op in self.ops:
            keep = []
            for d in op.alldeps:
                if not d.is_dma and d.eng == op.eng and not op.is_dma:
                    if op.eng == "pe":
                        continue
                    if op.eidx - d.eidx > 3:
                        continue
                keep.append(d)
            op.deps = keep
        for op in self.ops:
            if op.is_dma:
                op.signal = True
            for d in op.deps:
                d.signal = True
        sems = {}
        counts = {}

        def get_sem(key):
            if key not in sems:
                Sched.NSEM += 1
                sems[key] = self.es.enter_context(nc.semaphore("s%d" % Sched.NSEM))
                counts[key] = 0
            return sems[key]

        for op in self.ops:
            if not op.signal:
                continue
            if op.is_dma:
                key = ("dma", op.semkey if op.semkey is not None else op.idx)
                op.sem = get_sem(key)
                counts[key] += 16
                op.ticket = counts[key]
            else:
                key = ("eng", op.eng)
                op.sem = get_sem(key)
                counts[key] += 1
                op.ticket = counts[key]
        waited = {}
        nwait = 0
        for op in self.ops:
            e = self.engs[op.eng]
            need = {}
            for d in op.deps:
                k = id(d.sem)
                if waited.get((op.eng, k), 0) >= d.ticket:
                    continue
                if k not in need or need[k][1] < d.ticket:
                    need[k] = (d.sem, d.ticket)
            for k, (sem, val) in need.items():
                e.wait_ge(sem, val)
                waited[(op.eng, k)] = val
                nwait += 1
            ins = op.fn()
            if op.signal:
                ins.then_inc(op.sem, 16 if op.is_dma else 1)
        self.nsems = len(sems)
        self.nwait = nwait


def fsz(ap):
    n = 1
    for d in ap.shape[1:]:
        n *= int(d)
    return n


def AP(t, off, dims):
    return bass.AP(t, off, [list(d) for d in dims])


class Kern:
    def __init__(self, phases=(1, 2, 3, 4), debug=False, reorder=True):
        self.reorder = reorder
        self.phases = phases
        self.debug = debug
        self.nc = bass.Bass("TRN2", target_bir_lowering=False)
        self.es = ExitStack()
        self.ges = self.es
        self.semcount = 0
        self.dram = {}
        self.res = {}

    def din(self, name, shape, dt=F32):
        t = self.nc.dram_tensor(name, list(shape), dt, kind="ExternalInput")
        self.dram[name] = t
        return t

    def dscr(self, name, shape, dt):
        kind = "ExternalOutput" if self.debug else "Internal"
        t = self.nc.dram_tensor(name, list(shape), dt, kind=kind)
        self.dram[name] = t
        return t

    def sb(self, name, shape, dt):
        nm = "p%d_%s" % (getattr(self, "nphase", 0), name)
        return self.es.enter_context(self.nc.sbuf_tensor(nm, list(shape), dt))

    def guard(self, kb=8):
        with self.nc.sbuf_tensor("guard%d" % self.nphase, [128, kb * 256], F32):
            pass

    def R(self, name):
        if name not in self.res:
            self.res[name] = Res(name)
        return self.res[name]

    def Rs(self, *names):
        return [self.R(n) for n in names]

    def op(self, eng, fn, r=(), w=(), dma=False, semkey=None, nodep=False, cost=200.0, lat=0.0):
        return self.sc.add(eng, fn, [self.R(x) if isinstance(x, str) else x for x in r],
                           [self.R(x) if isinstance(x, str) else x for x in w], dma=dma, semkey=semkey, nodep=nodep,
                           cost=cost, lat=lat)

    def dma(self, q, out, in_, r=(), w=(), semkey=None, nodep=False):
        e = {"sp": self.nc.sync, "pool": self.nc.gpsimd, "act": self.nc.scalar}[q]
        nbytes = fsz(out) * int(out.shape[0]) * 4
        return self.op(q, lambda: e.dma_start(out=out, in_=in_), r, w, dma=True, semkey=semkey, nodep=nodep,
                       cost=(600.0 if q == "pool" else 100.0), lat=2500.0 + nbytes / 250.0)

    def mm(self, out, lhsT, rhs, start, stop, r=(), w=()):
        nc = self.nc
        return self.op("pe", lambda: nc.tensor.matmul(out, lhsT=lhsT, rhs=rhs, start=start, stop=stop), r, w,
                       cost=(30.0 + 0.37 * fsz(rhs)) if self.cal else (64.0 + 0.5 * fsz(rhs)))

    def tr(self, out, in_, ident, r=(), w=()):
        nc = self.nc
        return self.op("pe", lambda: nc.tensor.transpose(out, in_, ident), r, w, cost=130.0)

    def act(self, out, in_, func, r=(), w=(), **kw):
        nc = self.nc
        return self.op("act", lambda: nc.scalar.activation(out=out, in_=in_, func=func, **kw), r, w,
                       cost=200.0 + 0.85 * fsz(out))

    def ts(self, eng, out, in0, s1, s2, op0, op1=None, r=(), w=(), accum_out=None):
        e = self.nc.vector if eng == "dve" else self.nc.gpsimd
        c = (190.0 if self.cal else 70.0) + 1.05 * fsz(out)
        if op1 is None:
            return self.op(eng, lambda: e.tensor_scalar(out=out, in0=in0, scalar1=s1, scalar2=None, op0=op0), r, w, cost=c)
        if accum_out is not None:
            return self.op(eng, lambda: e.tensor_scalar(out=out, in0=in0, scalar1=s1, scalar2=s2, op0=op0, op1=op1,
                                                        accum_out=accum_out), r, w, cost=c)
        return self.op(eng, lambda: e.tensor_scalar(out=out, in0=in0, scalar1=s1, scalar2=s2, op0=op0, op1=op1), r, w, cost=c)

    def tt(self, eng, out, in0, in1, op, r=(), w=()):
        e = self.nc.vector if eng == "dve" else self.nc.gpsimd
        if self.cal:
            c = (150.0 + 1.5 * fsz(out)) if eng == "dve" else (300.0 + 1.6 * fsz(out))
        else:
            c = (70.0 + 1.05 * fsz(out)) if eng == "dve" else (150.0 + 2.0 * fsz(out))
        return self.op(eng, lambda: e.tensor_tensor(out=out, in0=in0, in1=in1, op=op), r, w, cost=c)

    def stt(self, out, in0, scalar, in1, op0, op1, r=(), w=()):
        nc = self.nc
        return self.op("dve", lambda: nc.vector.scalar_tensor_tensor(out=out, in0=in0, scalar=scalar, in1=in1,
                                                                      op0=op0, op1=op1), r, w,
                       cost=(150.0 + 1.5 * fsz(out)) if self.cal else (70.0 + 1.05 * fsz(out)))

    def cp(self, eng, out, in_, r=(), w=()):
        nc = self.nc
        if eng == "act":
            return self.op("act", lambda: nc.scalar.copy(out=out, in_=in_), r, w, cost=200.0 + 0.85 * fsz(out))
        e = nc.vector if eng == "dve" else nc.gpsimd
        if self.cal:
            c = (150.0 + 1.05 * fsz(out)) if eng == "dve" else (200.0 + 1.1 * fsz(out))
        else:
            c = (70.0 + 1.05 * fsz(out)) if eng == "dve" else (150.0 + 1.1 * fsz(out))
        return self.op(eng, lambda: e.tensor_copy(out=out, in_=in_), r, w, cost=c)

    def memset(self, eng, ap, val, r=(), w=()):
        e = self.nc.vector if eng == "dve" else self.nc.gpsimd
        return self.op(eng, lambda: e.memset(ap, val), r, w, cost=100.0 + 1.0 * fsz(ap))

    def ln_stats(self, src, tag, r, nchunk, width):
        nc = self.nc
        st = self.lnst[tag]
        stats, mv, rs = st
        resn = "lnst_" + tag
        for c in range(nchunk):
            self.op("dve", (lambda c=c: nc.vector.bn_stats(out=stats[:, 6 * c:6 * c + 6],
                                                           in_=src[:, c * width:(c + 1) * width])), r, [resn],
                    cost=100.0 + 1.1 * width)
        self.op("dve", lambda: nc.vector.bn_aggr(out=mv[:, 0:2], in_=stats[:, 0:6 * nchunk]), [resn], [resn])
        self.ts("dve", rs[:, 0:1], mv[:, 1:2], LN_EPS, None, ALU.add, None, [resn], [resn])
        self.act(rs[:, 0:1], rs[:, 0:1], AF.Ln, [resn], [resn])
        self.act(rs[:, 0:1], rs[:, 0:1], AF.Exp, [resn], [resn], scale=-0.5)
        self.ts("dve", rs[:, 1:2], mv[:, 0:1], -1.0, rs[:, 0:1], ALU.mult, ALU.mult, [resn], [resn])
        return rs[:, 0:1], rs[:, 1:2]

    def alloc_lnst(self, tag):
        if not hasattr(self, "lnst"):
            self.lnst = {}
        self.lnst[tag] = (self.sb("lnstats_" + tag, [128, 12], F32), self.sb("lnmv_" + tag, [128, 2], F32),
                          self.sb("lnrs_" + tag, [128, 2], F32))

    def build(self):
        nc = self.nc
        k = self
        x = k.din("x", [S, D])
        p = k.din("p", [S, 256])
        w_in = k.din("w_in", [D, IN_W])
        lng_fm = k.din("lng_fm", [128, 8])
        lnb_fm = k.din("lnb_fm", [128, 8])
        lng = k.din("lng", [1, D])
        lnb = k.din("lnb", [1, D])
        bgate = k.din("bgate", [128, 16])
        kvg = k.din("kvg", [1, 128])
        wukT = k.din("wukT", [128, 4, 128])
        wuvr = k.din("wuvr", [128, 512])
        kig = k.din("kig", [1, 64])
        kib = k.din("kib", [1, 64])
        mcw = k.din("mcw", [128, 4, 3])
        mcb = k.din("mcb", [128, 4])
        wbra = k.din("wbra", [512, D])
        wbrc = k.din("wbrc", [512, D])
        wo = k.din("wo", [D, D])
        ln1g = k.din("ln1g", [1, D])
        ln1b = k.din("ln1b", [1, D])
        wup = k.din("wup", [D, 2 * DFF])
        fcw = k.din("fcw", [128, 44, 3])
        fcb = k.din("fcb", [128, 44])
        wdn = k.din("wdn", [DFF, D])
        wpg = k.din("wpg", [D, D])
        bpg = k.din("bpg", [1, D])
        wple = k.din("wple", [256, D])
        ln2g = k.din("ln2g", [1, D])
        ln2b = k.din("ln2b", [1, D])
        out = nc.dram_tensor("out", [S, D], F32, kind="ExternalOutput")
        k.dram["out"] = out
        attT_d = k.dscr("attT_d", [512, S], BF16)
        mrgT_d = k.dscr("mrgT_d", [D, S], BF16)
        r_d = k.dscr("r_d", [S, D], F32)
        h1T_d = k.dscr("h1T_d", [D, S], BF16)

        ps = k.es.enter_context(nc.psum_tensor("ps", [128, 4096], F32))
        k.ps = ps

        def bank(b, n=512, off=0, parts=128):
            return ps[0:parts, b * 512 + off: b * 512 + off + n]

        def bank_bf(b):
            return ps[:, b * 512:(b + 1) * 512].bitcast(BF16)

        k.bank = bank
        k.bank_bf = bank_bf

        k.bar_tile = k.sb("bar_tile", [128, 8], F32)
        k.bar_bf = k.sb("bar_bf", [128, 8], BF16)
        ident = k.sb("ident", [128, 128], BF16)
        k.ident = ident
        k.nphase = 0
        with ExitStack() as pes:
            k.begin_phase(pes)
            k.memset("pool", ident[:], 0.0, [], ["ident"])
            k.op("pool", lambda: nc.gpsimd.affine_select(out=ident[:], in_=ident[:], pattern=[[-1, 128]],
                                                          compare_op=ALU.not_equal, fill=1.0, base=0,
                                                          channel_multiplier=1), ["ident"], ["ident"])
            k.memset("pool", k.bar_bf[:], 0.0, [], ["bar_bf"])
            k.end_phase()
        if 1 in k.phases:
            with ExitStack() as pes:
                k.begin_phase(pes, cal=False)
                k.phase1(x, w_in, lng_fm, lnb_fm, kvg, wukT, wuvr, kig, kib, attT_d)
                k.end_phase()
        if 2 in k.phases:
            with ExitStack() as pes:
                k.begin_phase(pes)
                k.phase2(x, w_in, lng_fm, lnb_fm, bgate, mcw, mcb, wbra, wbrc, attT_d, mrgT_d)
                k.end_phase()
        if 3 in k.phases:
            with ExitStack() as pes:
                k.begin_phase(pes)
                k.phase3(x, p, lng, lnb, wo, ln1g, ln1b, wpg, bpg, wple, mrgT_d, r_d, h1T_d)
                k.end_phase()
        if 4 in k.phases:
            with ExitStack() as pes:
                k.begin_phase(pes)
                k.phase4(wup, fcw, fcb, wdn, ln2g, ln2b, r_d, h1T_d, out)
                k.end_phase()
        return nc

    def begin_phase(self, pes, cal=True):
        self.cal = cal
        self.es = pes
        self.sc = Sched(self.nc, self.ges, reorder=self.reorder)
        self.res = {}
        self.lnst = {}

    def end_phase(self):
        nc = self.nc
        k = self
        self.sc.emit()
        bar = self.ges.enter_context(nc.semaphore("bar%d" % self.nphase))
        self.nphase += 1
        nc.vector.memset(k.bar_tile[:, 0:1], 0.0).then_inc(bar, 1)
        nc.gpsimd.memset(k.bar_tile[:, 1:2], 0.0).then_inc(bar, 1)
        nc.scalar.copy(out=k.bar_tile[:, 2:3], in_=k.bar_tile[:, 3:4]).then_inc(bar, 1)
        nc.tensor.matmul(k.ps[0:8, 0:8], lhsT=k.bar_bf[:, 0:8], rhs=k.bar_bf[:, 0:8], start=True, stop=True).then_inc(bar, 1)
        nc.sync.nop().then_inc(bar, 1)
        for e in (nc.vector, nc.gpsimd, nc.scalar, nc.tensor, nc.sync):
            e.wait_ge(bar, 5)

    def phase1(self, x, w_in, lng_fm, lnb_fm, kvg, wukT, wuvr, kig, kib, attT_d):
        k = self
        nc = self.nc
        bank, bank_bf, ident, ps = k.bank, k.bank_bf, k.ident, k.ps
        w1 = k.sb("w1", [128, 8, W1C], BF16)
        g_fm = k.sb("g_fm", [128, 8], F32)
        b_fm = k.sb("b_fm", [128, 8], F32)
        wuk_sb = k.sb("wuk_sb", [128, 4, 128], BF16)
        wuv_sb = k.sb("wuv_sb", [128, 512], BF16)
        kvg_bc = k.sb("kvg_bc", [128, 128], F32)
        kig_bc = k.sb("kig_bc", [128, 64], F32)
        kib_bc = k.sb("kib_bc", [128, 64], F32)
        negm = k.sb("negm", [128, 128], F32)
        pow2 = k.sb("pow2", [128, N_BISECT], F32)
        kT2 = k.sb("kT2", [128, S], BF16)
        ckvT = k.sb("ckvT", [128, S], BF16)
        vext = k.sb("vext", [128, 32, 8, 65], BF16)
        xbuf = [k.sb("xbuf%d" % i, [128, D], F32) for i in range(2)]
        xn_bf = k.sb("xn_bf", [128, D], BF16)
        hT2 = [k.sb("hT0", [128, 8, 512], BF16)] * 2
        qattT = k.sb("qattT", [128, 4, 512], BF16)
        qlatT2 = [k.sb("qlatT%d" % i, [128, 8, 512], BF16) for i in range(2)]
        qidxT2 = [k.sb("qidxT%d" % i, [128, 4, 512], BF16) for i in range(2)]
        absw42 = [k.sb("absw4%d" % i, [128, 4, 8], F32) for i in range(2)]
        sgn42 = [k.sb("sgn4%d" % i, [128, 4, 8], F32) for i in range(2)]
        dsgn2 = [k.sb("dsgn%d" % i, [128, 8, 128], BF16) for i in range(2)]
        relu_sb = [k.sb("relu%d" % i, [128, 512], BF16) for i in range(3)]
        score2 = [k.sb("score%d" % i, [128, S], F32) for i in range(2)]
        mask012 = [k.sb("mask01%d" % i, [128, S], BF16) for i in range(2)]
        maskT = k.sb("maskT", [128, 32, 128], BF16)
        PT = [k.sb("PT%d" % i, [128, 4, 128], BF16) for i in range(4)]
        ckv_tm = k.sb("ckv_tm", [128, 128], BF16)
        craw = k.sb("craw", [128, 128], F32)
        craw2 = k.sb("craw2", [128, 128], F32)
        kn_f = k.sb("kn_f", [128, 64], F32)
        kn2 = k.sb("kn2", [128, 128], BF16)
        sm = k.sb("sm", [128, 16], F32)
        wks2 = [k.sb("wks%d" % i, [128, N_BISECT], F32) for i in range(2)]
        bis2 = [k.sb("bis%d" % i, [128, 8], F32) for i in range(2)]
        pv_sb2 = [k.sb("pv_sb0", [65, 1024], F32)] * 2
        rden2 = [k.sb("rden0", [65, 1024], F32)] * 2
        ones_r = k.sb("ones_r", [65, 64], F32)
        att_n = [k.sb("att_n%d" % i, [64, 8, 128], BF16) for i in range(2)]
        k.alloc_lnst("x")
        k.alloc_lnst("k")
        k.guard()

        w_in_v = w_in.rearrange("(k p) f -> p k f", p=128)
        for gi, (c0_, c1_) in enumerate(((C_QATT, C_CKV), (C_CKV, C_QIDX), (C_QIDX, C_KIDX), (C_KIDX, W1C))):
            k.dma("pool", w1[:, :, c0_:c1_], w_in_v[:, :, c0_:c1_], [], ["w1g%d" % gi])
        W1R = []
        k.dma("sp", g_fm[:], lng_fm[:], [], ["g_fm"])
        k.dma("sp", b_fm[:], lnb_fm[:], [], ["b_fm"])
        k.dma("pool", wuk_sb[:], wukT[:], [], ["wuk"])
        k.dma("pool", wuv_sb[:], wuvr[:], [], ["wuv"])
        k.dma("sp", kvg_bc[:], AP(kvg, 0, [[0, 128], [1, 128]]), [], ["kvg_bc"])
        k.dma("sp", kig_bc[:], AP(kig, 0, [[0, 128], [1, 64]]), [], ["kig_bc"])
        k.dma("sp", kib_bc[:], AP(kib, 0, [[0, 128], [1, 64]]), [], ["kib_bc"])
        k.memset("pool", negm[:], 0.0, [], ["negm"])
        k.op("pool", lambda: nc.gpsimd.affine_select(out=negm[:], in_=negm[:], pattern=[[-1, 128]],
                                                      compare_op=ALU.is_ge, fill=NEG, base=0,
                                                      channel_multiplier=1), ["negm"], ["negm"])
        for i in range(N_BISECT):
            k.memset("pool", pow2[:, i:i + 1], 2.0 ** (-(i + 1)), [], ["pow2"])
        k.memset("pool", vext[:, :, :, 64:65], 1.0, [], ["vext_ones"])
        k.memset("pool", ones_r[:], 1.0, [], ["ones_r"])

        BT, BR0, BR1, BSC, BL0, BL1, BV0, BV1 = 0, 1, 2, 3, 4, 5, 6, 7
        ring = [BR0, BR1]
        rstate = {"i": 0, "relu": 0, "pt": 0, "lg": 0, "xb": 0, "an": 0}

        def next_ring():
            b = ring[rstate["i"] % 2]
            rstate["i"] += 1
            return b

        for st in range(8):
            T0 = st * 512
            sp_ = st % 2
            hT, qlatT, qidxT, absw4, sgn4 = hT2[sp_], qlatT2[sp_], qidxT2[sp_], absw42[sp_], sgn42[sp_]
            HT, QL, QI, AW, SG = "hT0", "qlatT%d" % sp_, "qidxT%d" % sp_, "absw4%d" % sp_, "sgn4%d" % sp_
            k.sc.boost = 1.0e9
            for tt in range(4):
                t0 = T0 + tt * 128
                xb_i = rstate["xb"] % 2
                rstate["xb"] += 1
                xb = xbuf[xb_i]
                XR = "xbuf%d" % xb_i
                k.dma("sp", xb[:], x[t0:t0 + 128, :], [], [XR], semkey=XR)
                rstd, nmr = k.ln_stats(xb, "x", [XR], 2, 512)
                k.act(xn_bf[:], xb[:], AF.Identity, ["lnst_x", XR], ["xn_bf"], scale=rstd, bias=nmr)
                tb = bank_bf(BT)
                for kk in range(8):
                    k.tr(tb[:, kk * 128:(kk + 1) * 128], xn_bf[:, kk * 128:(kk + 1) * 128], ident[:],
                         ["xn_bf", "ident"], ["bank0", "bank0b"])
                for kk in range(8):
                    k.act(hT[:, kk, tt * 128:(tt + 1) * 128], tb[:, kk * 128:(kk + 1) * 128], AF.Identity,
                          ["bank0", "bank0b", "g_fm", "b_fm"], [HT], scale=g_fm[:, kk:kk + 1], bias=b_fm[:, kk:kk + 1])
            for j in range(4):
                b = next_ring()
                for kk in range(8):
                    k.mm(bank(b), w1[:, kk, C_QATT + 128 * j:C_QATT + 128 * (j + 1)], hT[:, kk, :], kk == 0, kk == 7,
                         [HT, "w1g0"], ["bank%d" % b])
                k.cp("dve", qattT[:, j, :], bank(b), ["bank%d" % b], ["qattT"])
            for j in range(4):
                b = next_ring()
                for kk in range(8):
                    k.mm(bank(b), w1[:, kk, C_QIDX + 128 * j:C_QIDX + 128 * (j + 1)], hT[:, kk, :], kk == 0, kk == 7,
                         [HT, "w1g2"], ["bank%d" % b])
                k.cp("act", qidxT[:, j, :], bank(b), ["bank%d" % b], [QI])
            for tt in range(4):
                blk = st * 4 + tt
                bck_, bkw_ = next_ring(), next_ring()
                PCK, PKW = "bank%d" % bck_, "bank%d" % bkw_
                pck = bank(bck_, 128, 0)
                pkw = bank(bkw_, 72, 0)
                for kk in range(8):
                    k.mm(pck, hT[:, kk, tt * 128:(tt + 1) * 128], w1[:, kk, C_CKV:C_CKV + 128], kk == 0, kk == 7,
                         [HT, "w1g1"], [PCK])
                for kk in range(8):
                    k.mm(pkw, hT[:, kk, tt * 128:(tt + 1) * 128], w1[:, kk, C_KIDX:C_KIDX + 72], kk == 0, kk == 7,
                         [HT, "w1g3"], [PKW])
                k.cp("act", craw[:], pck, [PCK], ["craw"])
                k.op("dve", lambda: nc.vector.scalar_tensor_tensor(out=craw2[:], in0=craw[:], scalar=1.0, in1=craw[:],
                                                                    op0=ALU.mult, op1=ALU.mult, accum_out=sm[:, 0:1]),
                     ["craw"], ["craw2", "sm_c"])
                k.ts("dve", sm[:, 1:2], sm[:, 0:1], 1.0 / 128.0, LN_EPS, ALU.mult, ALU.add, ["sm_c"], ["sm_c"])
                k.act(sm[:, 2:3], sm[:, 1:2], AF.Ln, ["sm_c"], ["sm_c"])
                k.act(sm[:, 2:3], sm[:, 2:3], AF.Exp, ["sm_c"], ["sm_c"], scale=-0.5)
                k.stt(ckv_tm[:], craw[:], sm[:, 2:3], kvg_bc[:], ALU.mult, ALU.mult, ["craw", "sm_c", "kvg_bc"], ["ckv_tm"])
                rstd_k, nmr_k = k.ln_stats(pkw, "k", [PKW], 1, 64)
                k.act(kn_f[:], pkw[:, 0:64], AF.Identity, [PKW, "lnst_k"], ["kn_f"], scale=rstd_k, bias=nmr_k)
                k.tt("pool", kn_f[:], kn_f[:], kig_bc[:], ALU.mult, ["kn_f", "kig_bc"], ["kn_f"])
                k.tt("pool", kn2[:, 0:64], kn_f[:], kib_bc[:], ALU.add, ["kn_f", "kib_bc"], ["kn2"])
                k.tt("pool", kn2[:, 64:128], kn_f[:], kib_bc[:], ALU.add, ["kn_f", "kib_bc"], ["kn2"])
                k.act(sm[:, 8:16], pkw[:, 64:72], AF.Copy, [PKW], ["sm_w"], scale=CW)
                k.stt(absw4[:, tt, :], sm[:, 8:16], -1.0, sm[:, 8:16], ALU.mult, ALU.max, ["sm_w"], [AW])
                k.act(sgn4[:, tt, :], pkw[:, 64:72], AF.Sign, [PKW], [SG])
                tb = bank_bf(BT)
                k.tr(tb[:, 512:640], ckv_tm[:], ident[:], ["ckv_tm", "ident"], ["bank0b"])
                k.tr(tb[:, 640:768], kn2[:], ident[:], ["kn2", "ident"], ["bank0b"])
                k.cp("act", ckvT[:, blk * 128:(blk + 1) * 128], tb[:, 512:640], ["bank0b"], ["ckvT_%d" % st])
                k.cp("act", kT2[:, blk * 128:(blk + 1) * 128], tb[:, 640:768], ["bank0b"], ["kT2_%d" % st])
                b = next_ring()
                k.mm(bank(b), ckvT[:, blk * 128:(blk + 1) * 128], wuv_sb[:], True, True, ["ckvT_%d" % st, "wuv"], ["bank%d" % b])
                k.cp("dve", vext[:, blk, :, 0:64], bank(b).rearrange("p (h d) -> p h d", h=8), ["bank%d" % b], ["vext_%d" % st])
            for h in range(8):
                e, j = h % 2, h // 2
                b = next_ring()
                k.mm(bank(b), wuk_sb[64 * e:64 * e + 64, j, :], qattT[64 * e:64 * e + 64, j, :], True, True,
                     ["qattT", "wuk"], ["bank%d" % b])
                k.act(qlatT[:, h, :], bank(b), AF.Copy, ["bank%d" % b], [QL], scale=0.125)
            k.sc.boost = 0.0
            for i in range(4):
                I = st * 4 + i
                nk = 128 * (I + 1)
                q0 = i * 128
                ip_ = I % 2
                score, mask01, dsgn, pv_sb, rden = score2[ip_], mask012[ip_], dsgn2[ip_], pv_sb2[ip_], rden2[ip_]
                junk = mask01
                SC, MK, DS, PVS, RD = "score%d" % ip_, "mask01%d" % ip_, "dsgn%d" % ip_, "pv_sb0", "rden0"
                for h in range(8):
                    k.tt("pool", dsgn[:, h, :], ident[:], AP(sgn4, i * 8 + h, [[32, 128], [0, 128]]), ALU.mult,
                         ["ident", SG], [DS])
                nkb = (nk + 511) // 512
                for kb in range(nkb):
                    wk = min(512, nk - 512 * kb)
                    for j in range(4):
                        b0_, b1_ = next_ring(), next_ring()

                        def pair(o0=bank(b0_, wk), o1=bank(b1_, wk), l0=qidxT[0:64, j, q0:q0 + 128],
                                 l1=qidxT[64:128, j, q0:q0 + 128], r0=kT2[0:64, 512 * kb:512 * kb + wk],
                                 r1=kT2[64:128, 512 * kb:512 * kb + wk]):
                            nc.tensor.matmul(o0, lhsT=l0, rhs=r0, start=True, stop=True)
                            return nc.tensor.matmul(o1, lhsT=l1, rhs=r1, start=True, stop=True)
                        k.op("pe", pair, [QI, "kT2_%d" % kb], ["bank%d" % b0_, "bank%d" % b1_], cost=2 * (64.0 + 0.5 * wk))
                        for e, b in ((0, b0_), (1, b1_)):
                            h = 2 * j + e
                            ri = rstate["relu"] % 3
                            rstate["relu"] += 1
                            k.act(relu_sb[ri][:, 0:wk], bank(b, wk), AF.Relu, ["bank%d" % b, AW], ["relu%d" % ri],
                                  scale=absw4[:, i, h:h + 1])
                            k.mm(bank(BSC, wk), dsgn[:, h, :], relu_sb[ri][:, 0:wk], h == 0, h == 7,
                                 [DS, "relu%d" % ri], ["bank%d" % BSC])
                    last = (kb == nkb - 1)
                    ncopy = wk - 128 if last else wk
                    if ncopy > 0:
                        k.cp("act", score[:, 512 * kb:512 * kb + ncopy], bank(BSC, ncopy), ["bank%d" % BSC], [SC])
                    if last:
                        k.tt("dve", score[:, nk - 128:nk], bank(BSC, 128, wk - 128), negm[:], ALU.add,
                             ["bank%d" % BSC, "negm"], [SC])
                if I >= 2:
                    bis, wks = bis2[ip_], wks2[ip_]
                    BI, WK = "bis%d" % ip_, "wks%d" % ip_
                    k.op("dve", lambda nk=nk, sc_=score, b_=bis: nc.vector.tensor_reduce(out=b_[:, 0:1], in_=sc_[:, 0:nk], axis=AX.X,
                                                                                          op=ALU.max), [SC], [BI],
                         cost=100.0 + 1.05 * nk)
                    k.op("dve", lambda sc_=score, b_=bis: nc.vector.tensor_reduce(out=b_[:, 1:2], in_=sc_[:, 0:256], axis=AX.X,
                                                                                   op=ALU.min), [SC], [BI], cost=400.0)
                    k.tt("dve", bis[:, 2:3], bis[:, 0:1], bis[:, 1:2], ALU.subtract, [BI], [BI])
                    k.ts("dve", bis[:, 2:3], bis[:, 2:3], 1.001, 1e-6, ALU.mult, ALU.add, [BI], [BI])
                    k.ts("dve", wks[:], pow2[:], bis[:, 2:3], None, ALU.mult, None, [BI, "pow2"], [WK])
                    k.tt("dve", bis[:, 3:4], bis[:, 1:2], wks[:, 0:1], ALU.add, [BI, WK], [BI])
                    for it in range(N_BISECT):
                        k.ts("dve", junk[:, 0:nk], score[:, 0:nk], bis[:, 3:4], 0.0, ALU.is_ge, ALU.add,
                             [SC, BI], [MK, BI], accum_out=bis[:, 4:5])
                        k.stt(bis[:, 5:6], bis[:, 4:5], float(TOPK) - 0.5, wks[:, it:it + 1], ALU.is_ge, ALU.mult, [BI, WK], [BI])
                        nxt = it + 1 if it + 1 < N_BISECT else it
                        k.stt(bis[:, 3:4], bis[:, 5:6], bis[:, 3:4], wks[:, nxt:nxt + 1], ALU.add, ALU.subtract, [BI, WK], [BI])
                    k.ts("dve", mask01[:, 0:nk], score[:, 0:nk], bis[:, 3:4], None, ALU.is_ge, None,
                         [SC, BI], [MK])
                else:
                    k.ts("dve", mask01[:, 0:nk], score[:, 0:nk], -1.0e29, None, ALU.is_ge, None, [SC], [MK])
                tb = bank_bf(BT)
                for g0 in range(0, I + 1, 8):
                    g1 = min(I + 1, g0 + 8)
                    for jb in range(g0, g1):
                        k.tr(tb[:, (jb - g0) * 128:(jb - g0 + 1) * 128], mask01[:, jb * 128:(jb + 1) * 128], ident[:],
                             [MK, "ident"], ["bank0", "bank0b", "maskT"])
                    k.cp("act", maskT[:, g0:g1, :], tb[:, 0:(g1 - g0) * 128].rearrange("p (a b) -> p a b", b=128),
                         ["bank0", "bank0b"], ["maskT"])
                pv = ps[0:65, BV0 * 512:BV0 * 512 + 1024].rearrange("p (h q) -> p h q", h=8)
                k.op("dve", lambda: nc.vector.memset(ps[0:65, BV0 * 512:BV0 * 512 + 1024], 0.0), [], ["bankpv"])
                for jb in range(I + 1):
                    for g in range(2):
                        lb = [BL0, BL1][rstate["lg"] % 2]
                        rstate["lg"] += 1
                        k.mm(bank(lb), ckvT[:, jb * 128:(jb + 1) * 128], qlatT[:, 4 * g:4 * g + 4, q0:q0 + 128],
                             True, True, ["ckvT_%d" % (jb // 4), QL], ["bank%d" % lb])
                        pi = rstate["pt"] % 4
                        rstate["pt"] += 1
                        k.act(PT[pi][:], bank(lb).rearrange("p (h q) -> p h q", h=4), AF.Exp, ["bank%d" % lb],
                              ["PT%d" % pi])
                        k.tt("pool", PT[pi][:], PT[pi][:], AP(maskT, jb * 128, [[32 * 128, 128], [0, 4], [1, 128]]),
                             ALU.mult, ["PT%d" % pi, "maskT"], ["PT%d" % pi])
                        for hh in range(4):
                            h = 4 * g + hh
                            k.op("pe", (lambda o=pv[:, h, :], l=vext[:, jb, h, :], rr=PT[pi][:, hh, :], sp_=(jb == I):
                                        nc.tensor.matmul(o, lhsT=l, rhs=rr, start=False, stop=sp_,
                                                         skip_group_check=True)),
                                 ["vext_%d" % (jb // 4), "vext_ones", "PT%d" % pi], ["bankpv"])
                k.cp("act", pv_sb[:], ps[0:65, BV0 * 512:BV0 * 512 + 1024], ["bankpv"], [PVS])
                k.act(rden[64:65, :], pv_sb[64:65, :], AF.Ln, [PVS], [RD])
                k.act(rden[64:65, :], rden[64:65, :], AF.Exp, [RD], [RD], scale=-1.0)
                ai = rstate["an"] % 2
                rstate["an"] += 1
                for g in range(2):
                    lb = [BL0, BL1][rstate["lg"] % 2]
                    rstate["lg"] += 1
                    k.mm(bank(lb, 512, 0, 64), ones_r[64:65, :], rden[64:65, g * 512:(g + 1) * 512], True, True,
                         ["ones_r", RD], ["bank%d" % lb])
                    k.tt("dve", att_n[ai][:, 4 * g:4 * g + 4, :],
                         pv_sb[0:64, g * 512:(g + 1) * 512].rearrange("p (h q) -> p h q", h=4),
                         bank(lb, 512, 0, 64).rearrange("p (h q) -> p h q", h=4), ALU.mult,
                         [PVS, "bank%d" % lb], ["att_n%d" % ai])
                tok0 = T0 + q0
                k.dma("sp", AP(attT_d, tok0, [[S, 64], [64 * S, 8], [1, 128]]), att_n[ai][:],
                      ["att_n%d" % ai], ["attT_d"], semkey="att_n%d" % ai)
        k.final_wait(["att_n0", "att_n1"])

    def ln_hT(self, x, t0, tt, xbuf, rstate, xn_bf, hT, g_fm, b_fm, BT=0, HT="hT"):
        k = self
        nc = self.nc
        xb_i = rstate["xb"] % 2
        rstate["xb"] += 1
        xb = xbuf[xb_i]
        XR = "xbuf%d" % xb_i
        k.dma("sp", xb[:], x[t0:t0 + 128, :], [], [XR], semkey=XR)
        rstd, nmr = k.ln_stats(xb, "x", [XR], 2, 512)
        k.act(xn_bf[:], xb[:], AF.Identity, ["lnst_x", XR], ["xn_bf"], scale=rstd, bias=nmr)
        tb = k.bank_bf(BT)
        for kk in range(8):
            k.tr(tb[:, kk * 128:(kk + 1) * 128], xn_bf[:, kk * 128:(kk + 1) * 128], k.ident[:],
                 ["xn_bf", "ident"], ["bank0", "bank0b"])
        for kk in range(8):
            k.act(hT[:, kk, tt * 128:(tt + 1) * 128], tb[:, kk * 128:(kk + 1) * 128], AF.Identity,
                  ["bank0", "bank0b", "g_fm", "b_fm"], [HT], scale=g_fm[:, kk:kk + 1], bias=b_fm[:, kk:kk + 1])

    def phase2(self, x, w_in, lng_fm, lnb_fm, bgate, mcw, mcb, wbra, wbrc, attT_d, mrgT_d):
        k = self
        nc = self.nc
        bank, bank_bf, ident, ps = k.bank, k.bank_bf, k.ident, k.ps
        w2 = k.sb("w2", [128, 8, W2C], BF16)
        wa = k.sb("wa", [128, 4, D], BF16)
        wc = k.sb("wc", [128, 4, D], BF16)
        g_fm = k.sb("g_fm", [128, 8], F32)
        b_fm = k.sb("b_fm", [128, 8], F32)
        hb = k.sb("hb", [128, 16], F32)
        mcw_sb = k.sb("mcw_sb", [128, 4, 3], F32)
        mcb_sb = k.sb("mcb_sb", [128, 4], F32)
        xbuf = [k.sb("xbuf%d" % i, [128, D], F32) for i in range(2)]
        xn_bf = k.sb("xn_bf", [128, D], BF16)
        hT2 = [k.sb("hT%d" % i, [128, 8, 512], BF16) for i in range(2)]
        att_in = k.sb("att_in", [128, 4, 512], BF16)
        u = k.sb("u", [128, 4, 514], F32)
        tmpc = k.sb("tmpc", [128, 512], F32)
        a_sb = k.sb("a_sb", [128, 512], F32)
        cyT = k.sb("cyT", [128, 4, 512], BF16)
        ta = k.sb("ta", [128, 512], F32)
        tc2 = k.sb("tc2", [128, 512], F32)
        m1 = k.sb("m1", [128, 512], F32)
        m2 = k.sb("m2", [128, 512], F32)
        mrg = [k.sb("mrg%d" % i, [128, 8, 512], BF16) for i in range(2)]
        k.alloc_lnst("x")
        k.guard()
        w_in_v = w_in.rearrange("(k p) f -> p k f", p=128)
        for gi in (1, 2, 0, 3, 5, 4, 6):
            k.dma("pool", w2[:, :, 512 * gi:512 * (gi + 1)], w_in_v[:, :, W1C + 512 * gi:W1C + 512 * (gi + 1)], [], ["w2g%d" % gi])
        W2R = []
        k.dma("pool", wa[:], wbra.rearrange("(k p) f -> p k f", p=128), [], ["wa"])
        k.dma("pool", wc[:], wbrc.rearrange("(k p) f -> p k f", p=128), [], ["wc"])
        k.dma("sp", g_fm[:], lng_fm[:], [], ["g_fm"])
        k.dma("sp", b_fm[:], lnb_fm[:], [], ["b_fm"])
        k.dma("sp", hb[:], bgate[:], [], ["hb"])
        k.dma("sp", mcw_sb[:], mcw[:], [], ["mcw"])
        k.dma("sp", mcb_sb[:], mcb[:], [], ["mcb"])
        k.ts("dve", hb[:], hb[:], 0.5, None, ALU.mult, None, ["hb"], ["hb"])
        k.memset("pool", u[:], 0.0, [], ["u"])
        rstate = {"xb": 0, "ring": 0, "mrg": 0}
        ringb = [1, 2, 3, 4, 5, 6, 7]

        def nb():
            b = ringb[rstate["ring"] % 7]
            rstate["ring"] += 1
            return b

        cur = {}

        def proj(col0, b):
            c0 = col0 - W1C
            hT, HT = cur["hT"], cur["HT"]
            for kk in range(8):
                k.mm(bank(b), w2[:, kk, c0:c0 + 128], hT[:, kk, :], kk == 0, kk == 7, [HT, "w2g%d" % (c0 // 512)], ["bank%d" % b])

        for st in range(8):
            T0 = st * 512
            cur["hT"], cur["HT"] = hT2[st % 2], "hT%d" % (st % 2)
            for tt in range(4):
                k.ln_hT(x, T0 + tt * 128, tt, xbuf, rstate, xn_bf, cur["hT"], g_fm, b_fm, HT=cur["HT"])
            k.dma("sp", att_in[:], AP(attT_d, T0, [[S, 128], [128 * S, 4], [1, 512]]), [], ["att_in"], semkey="att_in")
            for j in range(4):
                bc_, bx_, bb_ = nb(), nb(), nb()
                proj(C_CVC + 128 * j, bc_)
                proj(C_CVX + 128 * j, bx_)
                proj(C_CVB + 128 * j, bb_)
                if st > 0:
                    k.cp("pool", u[:, j, 0:2], u[:, j, 512:514], ["u"], ["u"])
                k.cp("act", tmpc[:], bank(bc_), ["bank%d" % bc_], ["tmpc"])
                k.tt("dve", u[:, j, 2:514], tmpc[:], bank(bx_), ALU.mult, ["tmpc", "bank%d" % bx_], ["u"])
                k.act(a_sb[:], u[:, j, 2:514], AF.Identity, ["u", "mcw", "mcb"], ["a_sb"],
                      scale=mcw_sb[:, j, 2:3], bias=mcb_sb[:, j:j + 1])
                k.stt(a_sb[:], u[:, j, 1:513], mcw_sb[:, j, 1:2], a_sb[:], ALU.mult, ALU.add, ["u", "a_sb", "mcw"], ["a_sb"])
                k.stt(a_sb[:], u[:, j, 0:512], mcw_sb[:, j, 0:1], a_sb[:], ALU.mult, ALU.add, ["u", "a_sb", "mcw"], ["a_sb"])
                k.tt("dve", cyT[:, j, :], a_sb[:], bank(bb_), ALU.mult, ["a_sb", "bank%d" % bb_], ["cyT"])
            mi = rstate["mrg"] % 2
            rstate["mrg"] += 1
            for c in range(8):
                bga, bgc, bra, brc = nb(), nb(), nb(), nb()
                proj(C_GATT + 128 * c, bga)
                proj(C_GCONV + 128 * c, bgc)
                for kk in range(4):
                    k.mm(bank(bra), wa[:, kk, 128 * c:128 * (c + 1)], att_in[:, kk, :], kk == 0, kk == 3,
                         ["wa", "att_in"], ["bank%d" % bra])
                for kk in range(4):
                    k.mm(bank(brc), wc[:, kk, 128 * c:128 * (c + 1)], cyT[:, kk, :], kk == 0, kk == 3,
                         ["wc", "cyT"], ["bank%d" % brc])
                k.act(ta[:], bank(bga), AF.Tanh, ["bank%d" % bga, "hb"], ["ta"], scale=0.5, bias=hb[:, c:c + 1])
                k.act(tc2[:], bank(bgc), AF.Tanh, ["bank%d" % bgc, "hb"], ["tc2"], scale=0.5, bias=hb[:, 8 + c:9 + c])
                k.stt(m1[:], ta[:], 1.0, bank(bra), ALU.add, ALU.mult, ["ta", "bank%d" % bra], ["m1"])
                k.stt(m2[:], tc2[:], 1.0, bank(brc), ALU.add, ALU.mult, ["tc2", "bank%d" % brc], ["m2"])
                k.tt("pool", mrg[mi][:, c, :], m1[:], m2[:], ALU.add, ["m1", "m2"], ["mrg%d" % mi])
            k.dma("sp", AP(mrgT_d, T0, [[S, 128], [128 * S, 8], [1, 512]]), mrg[mi][:], ["mrg%d" % mi], ["mrgT_d"],
                  semkey="mrg%d" % mi)
        k.final_wait(["mrg0", "mrg1"])

    def phase3(self, x, p, lng, lnb, wo, ln1g, ln1b, wpg, bpg, wple, mrgT_d, r_d, h1T_d):
        k = self
        nc = self.nc
        bank, bank_bf, ident, ps = k.bank, k.bank_bf, k.ident, k.ps
        wo_sb = k.sb("wo_sb", [128, 8, D], BF16)
        wpg_sb = k.sb("wpg_sb", [128, 8, D], BF16)
        wpl_sb = k.sb("wpl_sb", [128, 2, D], BF16)
        Ga = k.sb("Ga", [128, D], F32)
        Ba = k.sb("Ba", [128, D], F32)
        G1 = k.sb("G1", [128, D], F32)
        B1 = k.sb("B1", [128, D], F32)
        HB = k.sb("HB", [128, D], F32)
        xbuf = [k.sb("xbuf%d" % i, [128, D], F32) for i in range(2)]
        pbuf = [k.sb("pbuf%d" % i, [128, 256], F32) for i in range(2)]
        m_in = [k.sb("m_in%d" % i, [128, 8, 128], BF16) for i in range(2)]
        hA_2 = [k.sb("hA%d" % i, [128, D], F32) for i in range(2)]
        y_2 = [k.sb("y%d" % i, [128, D], F32) for i in range(2)]
        h1_2 = [k.sb("h1%d" % i, [128, D], F32) for i in range(2)]
        h1_bf_2 = [k.sb("h1_bf%d" % i, [128, D], BF16) for i in range(2)]
        h1T = [k.sb("h1T%d" % i, [128, 8, 128], BF16) for i in range(2)]
        p_bf_2 = [k.sb("p_bf%d" % i, [128, 256], BF16) for i in range(2)]
        pT_2 = [k.sb("pT%d" % i, [128, 2, 128], BF16) for i in range(2)]
        tg_2 = [k.sb("tg%d" % i, [128, D], F32) for i in range(2)]
        pl2_2 = [k.sb("pl2%d" % i, [128, D], F32) for i in range(2)]
        r2 = [k.sb("r2_%d" % i, [128, D], F32) for i in range(2)]
        k.alloc_lnst("x")
        k.alloc_lnst("y")
        k.guard()
        for n_ in range(2):
            k.dma("pool", wo_sb[:, :, 512 * n_:512 * (n_ + 1)], wo.rearrange("(k p) f -> p k f", p=128)[:, :, 512 * n_:512 * (n_ + 1)],
                  [], ["wo%d" % n_])
        for n_ in range(2):
            k.dma("pool", wpg_sb[:, :, 512 * n_:512 * (n_ + 1)], wpg.rearrange("(k p) f -> p k f", p=128)[:, :, 512 * n_:512 * (n_ + 1)],
                  [], ["wpg%d" % n_])
        k.dma("pool", wpl_sb[:], wple.rearrange("(k p) f -> p k f", p=128), [], ["wpl"])
        for t_, src, nm in ((Ga, lng, "Ga"), (Ba, lnb, "Ba"), (G1, ln1g, "G1"), (B1, ln1b, "B1"), (HB, bpg, "HB")):
            k.dma("sp", t_[:], AP(src, 0, [[0, 128], [1, D]]), [], [nm])
        k.ts("dve", Ga[:], Ga[:], ALPHA, None, ALU.mult, None, ["Ga"], ["Ga"])
        k.ts("dve", Ba[:], Ba[:], ALPHA, None, ALU.mult, None, ["Ba"], ["Ba"])
        k.ts("dve", HB[:], HB[:], 0.5, None, ALU.mult, None, ["HB"], ["HB"])
        for t in range(32):
            t0 = t * 128
            bi = t % 2
            XR, PR, MR = "xbuf%d" % bi, "pbuf%d" % bi, "m_in%d" % bi
            hA, y, h1, h1_bf, p_bf, pT, tg, pl2 = hA_2[bi], y_2[bi], h1_2[bi], h1_bf_2[bi], p_bf_2[bi], pT_2[bi], tg_2[bi], pl2_2[bi]
            R_hA, R_y, R_h1, R_h1bf, R_pbf, R_pT, R_tg, R_pl2 = ["%s%d" % (n_, bi) for n_ in ("hA", "y", "h1", "h1_bf", "p_bf", "pT", "tg", "pl2")]
            k.dma("sp", xbuf[bi][:], x[t0:t0 + 128, :], [], [XR], semkey=XR)
            k.dma("sp", pbuf[bi][:], p[t0:t0 + 128, :], [], [PR], semkey=PR)
            k.dma("sp", m_in[bi][:], AP(mrgT_d, t0, [[S, 128], [128 * S, 8], [1, 128]]), [], [MR], semkey=MR)
            rstd, nmr = k.ln_stats(xbuf[bi], "x", [XR], 2, 512)
            k.act(hA[:], xbuf[bi][:], AF.Identity, ["lnst_x", XR], [R_hA], scale=rstd, bias=nmr)
            k.tt("pool", hA[:], hA[:], Ga[:], ALU.mult, [R_hA, "Ga"], [R_hA])
            k.tt("pool", hA[:], hA[:], Ba[:], ALU.add, [R_hA, "Ba"], [R_hA])
            for n in range(2):
                b = 1 + n
                for kk in range(8):
                    k.mm(bank(b), m_in[bi][:, kk, :], wo_sb[:, kk, 512 * n:512 * (n + 1)], kk == 0, kk == 7,
                         [MR, "wo%d" % n], ["bank%d" % b])
                k.stt(y[:, 512 * n:512 * (n + 1)], bank(b), 0.5, hA[:, 512 * n:512 * (n + 1)], ALU.mult, ALU.add,
                      ["bank%d" % b, R_hA], [R_y])
            rstd1, nmr1 = k.ln_stats(y, "y", [R_y], 2, 512)
            k.act(h1[:], y[:], AF.Identity, ["lnst_y", R_y], [R_h1], scale=rstd1, bias=nmr1)
            k.tt("pool", h1[:], h1[:], G1[:], ALU.mult, [R_h1, "G1"], [R_h1])
            k.tt("pool", h1[:], h1[:], B1[:], ALU.add, [R_h1, "B1"], [R_h1])
            k.cp("pool", h1_bf[:], h1[:], [R_h1], [R_h1bf])
            tb = bank_bf(0)
            for kk in range(8):
                k.tr(tb[:, kk * 128:(kk + 1) * 128], h1_bf[:, kk * 128:(kk + 1) * 128], ident[:], [R_h1bf, "ident"], ["bank0"])
            HR = "h1T%d" % bi
            k.cp("act", h1T[bi][:], tb[:, 0:1024].rearrange("p (a b) -> p a b", b=128), ["bank0"], [HR])
            k.dma("sp", AP(h1T_d, t0, [[S, 128], [128 * S, 8], [1, 128]]), h1T[bi][:], [HR], ["h1T_d"], semkey=HR)
            for n in range(2):
                b = 3 + n
                for kk in range(8):
                    k.mm(bank(b), h1T[bi][:, kk, :], wpg_sb[:, kk, 512 * n:512 * (n + 1)], kk == 0, kk == 7,
                         [HR, "wpg%d" % n], ["bank%d" % b])
                k.stt(tg[:, 512 * n:512 * (n + 1)], bank(b), 0.5, HB[:, 512 * n:512 * (n + 1)], ALU.mult, ALU.add,
                      ["bank%d" % b, "HB"], [R_tg])
            k.act(tg[:], tg[:], AF.Tanh, [R_tg], [R_tg])
            k.cp("pool", p_bf[:], pbuf[bi][:], [PR], [R_pbf])
            tb2 = bank_bf(7)
            for kk in range(2):
                k.tr(tb2[:, kk * 128:(kk + 1) * 128], p_bf[:, kk * 128:(kk + 1) * 128], ident[:], [R_pbf, "ident"], ["bank7"])
            k.cp("act", pT[:], tb2[:, 0:256].rearrange("p (a b) -> p a b", b=128), ["bank7"], [R_pT])
            RR = "r2_%d" % bi
            for n in range(2):
                b = 5 + n
                for kk in range(2):
                    k.mm(bank(b), pT[:, kk, :], wpl_sb[:, kk, 512 * n:512 * (n + 1)], kk == 0, kk == 1,
                         [R_pT, "wpl"], ["bank%d" % b])
                k.stt(pl2[:, 512 * n:512 * (n + 1)], tg[:, 512 * n:512 * (n + 1)], 1.0, bank(b), ALU.add, ALU.mult,
                      [R_tg, "bank%d" % b], [R_pl2])
            k.stt(r2[bi][:], h1[:], 2.0 * ALPHA, pl2[:], ALU.mult, ALU.add, [R_h1, R_pl2], [RR])
            k.dma("sp", r_d[t0:t0 + 128, :], r2[bi][:], [RR], ["r_d"], semkey=RR)
        k.final_wait(["r2_0", "r2_1", "h1T0", "h1T1"])

    def phase4(self, wup, fcw, fcb, wdn, ln2g, ln2b, r_d, h1T_d, out):
        k = self
        nc = self.nc
        bank, bank_bf, ident, ps = k.bank, k.bank_bf, k.ident, k.ps
        NT = 256
        wup_sb = k.sb("wup_sb", [128, 8, 2 * DFF], BF16)
        wdn_sb = k.sb("wdn_sb", [128, 22, D], BF16)
        fcw_sb = k.sb("fcw_sb", [128, 44, 3], F32)
        fcb_sb = k.sb("fcb_sb", [128, 44], F32)
        G2 = k.sb("G2", [128, D], F32)
        B2 = k.sb("B2", [128, D], F32)
        h1T2 = [k.sb("h1T%d" % i, [128, 8, NT], BF16) for i in range(2)]
        actT2 = [k.sb("actT%d" % i, [128, 22, NT], BF16) for i in range(2)]
        abuf = [[k.sb("abuf%d_%d" % (h_, i), [128, NT], F32) for i in range(4)] for h_ in range(2)]
        gbuf = [[k.sb("gbuf%d_%d" % (h_, i), [128, NT + 2], F32) for i in range(4)] for h_ in range(2)]
        sgb = [k.sb("sg%d" % i, [128, NT], F32) for i in range(3)]
        halo = k.sb("halo", [128, 44, 2], F32)
        r2b = [k.sb("r2_%d" % i, [128, D], F32) for i in range(2)]
        k.alloc_lnst("y")
        k.guard()
        wup_v = wup.rearrange("(k p) f -> p k f", p=128)
        for g_ in range(6):
            for half_ in range(2):
                c0_ = half_ * DFF + 512 * g_
                c1_ = min(c0_ + 512, half_ * DFF + DFF)
                k.dma("pool", wup_sb[:, :, c0_:c1_], wup_v[:, :, c0_:c1_], [], ["wupg%d_%d" % (g_, half_)])
        WUR = []
        for c in range(22):
            k.dma("pool", wdn_sb[:, c, :], wdn[c * 128:(c + 1) * 128, :], [], ["wdn"], semkey="wdn", nodep=True)
        WDR = ["wdn"]
        k.dma("sp", fcw_sb[:], fcw[:], [], ["fcw"])
        k.dma("sp", fcb_sb[:], fcb[:], [], ["fcb"])
        k.dma("sp", G2[:], AP(ln2g, 0, [[0, 128], [1, D]]), [], ["G2"])
        k.dma("sp", B2[:], AP(ln2b, 0, [[0, 128], [1, D]]), [], ["B2"])
        k.memset("pool", halo[:], 0.0, [], ["halo%d" % ch for ch in range(44)])
        cnt = {"bank": 0, "ab0": 0, "ab1": 0, "sg": 0, "tile": 0}

        for st in range(S // NT):
            T0 = st * NT
            hb = st % 2
            h1T, aT = h1T2[hb], actT2[hb]
            HR, ATR = "h1T%d" % hb, "actT%d" % hb
            k.dma("sp", h1T[:], AP(h1T_d, T0, [[S, 128], [128 * S, 8], [1, NT]]), [], [HR], semkey=HR)
            for c in range(22):
                cur = []
                for half in range(2):
                    ch = c + 22 * half
                    b = cnt["bank"] % 4
                    cnt["bank"] += 1
                    BR = "bank%d" % b
                    for kk in range(8):
                        k.mm(bank(b, NT), wup_sb[:, kk, 128 * ch:128 * (ch + 1)], h1T[:, kk, :], kk == 0, kk == 7,
                             [HR, "wupg%d_%d" % (c // 4, half)], [BR])
                    ai = cnt["ab%d" % half] % 4
                    cnt["ab%d" % half] += 1
                    ab, gb = abuf[half][ai], gbuf[half][ai]
                    AR, GR = "abuf%d_%d" % (half, ai), "gbuf%d_%d" % (half, ai)
                    HL = "halo%d" % ch
                    pb = bank(b, NT)
                    k.cp("pool", gb[:, 0:2], halo[:, ch, :], [HL], [GR])
                    k.cp("act", gb[:, 2:NT + 2], pb, [BR], [GR])
                    k.act(ab[:], pb, AF.Identity, [BR, "fcw", "fcb"], [AR], scale=fcw_sb[:, ch, 2:3], bias=fcb_sb[:, ch:ch + 1])
                    k.cp("pool", halo[:, ch, :], gb[:, NT:NT + 2], [GR], [HL])
                    k.stt(ab[:], gb[:, 1:NT + 1], fcw_sb[:, ch, 1:2], ab[:], ALU.mult, ALU.add, [GR, AR, "fcw"], [AR])
                    k.stt(ab[:], gb[:, 0:NT], fcw_sb[:, ch, 0:1], ab[:], ALU.mult, ALU.add, [GR, AR, "fcw"], [AR])
                    cur.append((ab, AR))
                si = cnt["sg"] % 3
                cnt["sg"] += 1
                SGR = "sg%d" % si
                k.act(sgb[si][:], cur[0][0][:], AF.Silu, [cur[0][1]], [SGR])
                k.tt("pool", aT[:, c, :], sgb[si][:], cur[1][0][:], ALU.mult, [SGR, cur[1][1]], [ATR])
            for tt in range(NT // 128):
                t0 = T0 + tt * 128
                ti = cnt["tile"] % 2
                cnt["tile"] += 1
                RR, YR = "r2_%d" % ti, "r2_%d" % ti
                r2, yv = r2b[ti], r2b[ti]
                k.dma("sp", r2[:], r_d[t0:t0 + 128, :], [], [RR], semkey=RR)
                for n in range(2):
                    b = 4 + 2 * ti + n
                    for c in range(22):
                        k.mm(bank(b), aT[:, c, tt * 128:(tt + 1) * 128], wdn_sb[:, c, 512 * n:512 * (n + 1)],
                             c == 0, c == 21, [ATR] + WDR, ["bank%d" % b])
                    k.stt(yv[:, 512 * n:512 * (n + 1)], r2[:, 512 * n:512 * (n + 1)], 0.5, bank(b), ALU.mult, ALU.add,
                          [RR, "bank%d" % b], [YR])
                rstd, nmr = k.ln_stats(yv, "y", [YR], 2, 512)
                k.act(yv[:], yv[:], AF.Identity, ["lnst_y", YR], [YR], scale=rstd, bias=nmr)
                k.tt("pool", yv[:], yv[:], G2[:], ALU.mult, [YR, "G2"], [YR])
                k.tt("pool", yv[:], yv[:], B2[:], ALU.add, [YR, "B2"], [YR])
                k.dma("sp", out[t0:t0 + 128, :], yv[:], [YR], ["out_d"], semkey=YR)
        k.final_wait(["r2_0", "r2_1"])

    def final_wait(self, names):
        nc = self.nc
        self.op("sp", lambda: nc.sync.nop(), names, names)


def _prep_inputs(inputs, b):
    f = lambda a: np.ascontiguousarray(np.asarray(a, dtype=np.float32))
    m = {}
    m["x"] = f(inputs["x"][b])
    m["p"] = f(inputs["p"][0, b])
    m["w_in"] = f(inputs["w_in"][0])
    m["lng_fm"] = f(np.asarray(inputs["ln_emb_g"]).reshape(8, 128).T)
    m["lnb_fm"] = f(np.asarray(inputs["ln_emb_b"]).reshape(8, 128).T)
    m["lng"] = f(np.asarray(inputs["ln_emb_g"]).reshape(1, D))
    m["lnb"] = f(np.asarray(inputs["ln_emb_b"]).reshape(1, D))
    m["bgate"] = f(np.asarray(inputs["b_gate"][0]).reshape(2, 8, 128).transpose(2, 0, 1).reshape(128, 16))
    m["kvg"] = f(np.asarray(inputs["kv_norm_g"][0]).reshape(1, 128))
    wuk = np.asarray(inputs["w_uk"][0])
    m["wukT"] = f(wuk.reshape(4, 2, 128, 64).transpose(1, 3, 0, 2).reshape(128, 4, 128))
    wuv = np.asarray(inputs["w_uv"][0])
    m["wuvr"] = f(wuv.transpose(1, 0, 2).reshape(128, 512))
    m["kig"] = f(np.asarray(inputs["k_idx_ln_g"][0]).reshape(1, 64))
    m["kib"] = f(np.asarray(inputs["k_idx_ln_b"][0]).reshape(1, 64))
    m["mcw"] = f(np.asarray(inputs["mix_conv_w"][0]).reshape(3, 4, 128).transpose(2, 1, 0))
    m["mcb"] = f(np.asarray(inputs["mix_conv_b"][0]).reshape(4, 128).T)
    m["wbra"] = f(inputs["w_br_att"][0])
    m["wbrc"] = f(inputs["w_br_conv"][0])
    m["wo"] = f(inputs["w_o"][0])
    m["ln1g"] = f(np.asarray(inputs["ln1_g"][0]).reshape(1, D))
    m["ln1b"] = f(np.asarray(inputs["ln1_b"][0]).reshape(1, D))
    m["wup"] = f(inputs["w_ffn_up"][0])
    m["fcw"] = f(np.asarray(inputs["ffn_conv_w"][0]).reshape(3, 44, 128).transpose(2, 1, 0))
    m["fcb"] = f(np.asarray(inputs["ffn_conv_b"][0]).reshape(44, 128).T)
    m["wdn"] = f(inputs["w_ffn_down"][0])
    m["wpg"] = f(inputs["w_ple_gate"][0])
    m["bpg"] = f(np.asarray(inputs["b_ple_gate"][0]).reshape(1, D))
    m["wple"] = f(inputs["w_ple"][0])
    m["ln2g"] = f(np.asarray(inputs["ln2_g"][0]).reshape(1, D))
    m["ln2b"] = f(np.asarray(inputs["ln2_b"][0]).reshape(1, D))
    return m


def kernel(**inputs):
    kern = Kern()
    nc = kern.build()
    in_maps = [_prep_inputs(inputs, b) for b in range(NCORES)]
    res = run_bass_kernel_spmd(nc, in_maps, core_ids=list(range(NCORES)))
    return np.stack([np.asarray(r["out"], dtype=np.float32) for r in res.results], axis=0)
```

```python
import numpy as np
from contextlib import ExitStack
import concourse.bass as bass
import concourse.mybir as mybir
from concourse.bass_utils import run_bass_kernel_spmd

F32 = mybir.dt.float32
BF16 = mybir.dt.bfloat16
ALU = mybir.AluOpType
AF = mybir.ActivationFunctionType
AX = mybir.AxisListType

S = 4096
D = 1024
NCORES = 8
IN_W = 4808
DFF = 2816
LN_EPS = 1e-5
ALPHA = 2.0 ** 0.25
TOPK = 256
NEG = -1.0e30
N_BISECT = 16
C_QATT, C_CKV, C_QIDX, C_KIDX, C_WIDX, C_CVB, C_CVC, C_CVX, C_GATT, C_GCONV = (
    0, 512, 640, 1152, 1216, 1224, 1736, 2248, 2760, 3784)
W1C = 1224
W2C = IN_W - W1C
CW = (64 ** -0.5) * (8 ** -0.5)


class Res:
    __slots__ = ("name", "last_w", "readers")

    def __init__(self, name):
        self.name = name
        self.last_w = None
        self.readers = []


class Op:
    __slots__ = ("eng", "fn", "deps", "alldeps", "orderdeps", "signal", "sem", "ticket", "is_dma", "idx", "eidx",
                 "semkey", "cost", "lat", "start", "boost")


class Sched:
    NSEM = 0
    ENGS = ("pe", "act", "dve", "pool", "sp")

    def __init__(self, nc, es, reorder=True, cal=True):
        self.cal = cal
        self.nc = nc
        self.es = es
        self.ops = []
        self.last_dma = {}
        self.boost = 0.0
        self.reorder = reorder
        self.engs = {"pe": nc.tensor, "act": nc.scalar, "dve": nc.vector, "pool": nc.gpsimd, "sp": nc.sync}

    def add(self, eng, fn, reads=(), writes=(), dma=False, semkey=None, nodep=False, cost=200.0, lat=0.0):
        op = Op()
        op.boost = self.boost
        op.eng = eng
        op.fn = fn
        op.is_dma = dma
        op.signal = False
        op.sem = None
        op.ticket = 0
        op.idx = len(self.ops)
        op.eidx = 0
        op.semkey = semkey
        op.cost = cost
        op.lat = lat
        op.start = 0.0
        deps = {}
        for r in reads:
            if r.last_w is not None:
                deps[r.last_w.idx] = r.last_w
        for w in writes:
            if w.last_w is not None:
                deps[w.last_w.idx] = w.last_w
            for rd in w.readers:
                deps[rd.idx] = rd
        deps.pop(op.idx, None)
        if nodep:
            deps = {}
        for r in reads:
            r.readers.append(op)
        for w in writes:
            w.last_w = op
            w.readers = []
        op.alldeps = list(deps.values())
        op.orderdeps = []
        if dma and nodep:
            prev = self.last_dma.get((eng, semkey))
            if prev is not None:
                op.orderdeps.append(prev)
            self.last_dma[(eng, semkey)] = op
        self.ops.append(op)
        return op

    def schedule(self):
        import heapq
        ops = self.ops
        n = len(ops)
        succ = [[] for _ in range(n)]
        ndeps = [0] * n
        for op in ops:
            ds = set(d.idx for d in op.alldeps) | set(d.idx for d in op.orderdeps)
            ndeps[op.idx] = len(ds)
            for di in ds:
                succ[di].append(op.idx)
        ready = [0.0] * n
        finish = [0.0] * n
        blev = [0.0] * n
        for op in reversed(ops):
            i = op.idx
            m = 0.0
            for j in succ[i]:
                if blev[j] > m:
                    m = blev[j]
            blev[i] = op.cost + op.lat + m + 100.0
        for op in ops:
            blev[op.idx] += op.boost
        eng_free = {e: 0.0 for e in self.ENGS}
        future = {e: [] for e in self.ENGS}
        avail = {e: [] for e in self.ENGS}
        for op in ops:
            if ndeps[op.idx] == 0:
                heapq.heappush(future[op.eng], (0.0, op.idx))
        order = []
        XLAT = 300.0 if self.cal else 150.0
        SLAT = 300.0 if self.cal else 200.0
        while len(order) < n:
            best = None
            for e in self.ENGS:
                T = eng_free[e]
                fu, av = future[e], avail[e]
                while fu and fu[0][0] <= T:
                    _, i = heapq.heappop(fu)
                    heapq.heappush(av, (-blev[i], i))
                if av:
                    st = T
                elif fu:
                    st = fu[0][0]
                else:
                    continue
                if best is None or st < best[0]:
                    best = (st, e)
            st, e = best
            if avail[e]:
                _, i = heapq.heappop(avail[e])
            else:
                _, i = heapq.heappop(future[e])
            op = ops[i]
            op.start = st
            eng_free[e] = st + op.cost
            finish[i] = st + op.cost + op.lat
            order.append(op)
            for j in succ[i]:
                r_ = finish[i] + (XLAT if ops[j].eng != e or op.is_dma else (0.0 if e == "pe" else SLAT))
                if r_ > ready[j]:
                    ready[j] = r_
                ndeps[j] -= 1
                if ndeps[j] == 0:
                    heapq.heappush(future[ops[j].eng], (ready[j], j))
        self.est_ns = max(finish) if n else 0.0
        self.ops = order

    def emit(self):
        nc = self.nc
        if self.reorder:
            self.schedule()
        ecount = {}
        for op in self.ops:
            op.eidx = ecount.get(op.eng, 0)
            ecount[op.eng] = op.eidx + 1
        for op in self.ops:
            keep = []
            for d in op.alldeps:
                if not d.is_dma and d.eng == op.eng and not op.is_dma:
                    if op.eng == "pe":
                        continue
                    if op.eidx - d.eidx > 3:
                        continue
                keep.append(d)
            op.deps = keep
        for op in self.ops:
            if op.is_dma:
                op.signal = True
            for d in op.deps:
                d.signal = True
        sems = {}
        counts = {}

        def get_sem(key):
            if key not in sems:
                Sched.NSEM += 1
                sems[key] = self.es.enter_context(nc.semaphore("s%d" % Sched.NSEM))
                counts[key] = 0
            return sems[key]

        for op in self.ops:
            if not op.signal:
                continue
            if op.is_dma:
                key = ("dma", op.semkey if op.semkey is not None else op.idx)
                op.sem = get_sem(key)
                counts[key] += 16
                op.ticket = counts[key]
            else:
                key = ("eng", op.eng)
                op.sem = get_sem(key)
                counts[key] += 1
                op.ticket = counts[key]
        waited = {}
        nwait = 0
        for op in self.ops:
            e = self.engs[op.eng]
            need = {}
            for d in op.deps:
                k = id(d.sem)
                if waited.get((op.eng, k), 0) >= d.ticket:
                    continue
                if k not in need or need[k][1] < d.ticket:
                    need[k] = (d.sem, d.ticket)
            for k, (sem, val) in need.items():
                e.wait_ge(sem, val)
                waited[(op.eng, k)] = val
                nwait += 1
            ins = op.fn()
            if op.signal:
                ins.then_inc(op.sem, 16 if op.is_dma else 1)
        self.nsems = len(sems)
        self.nwait = nwait


def fsz(ap):
    n = 1
    for d in ap.shape[1:]:
        n *= int(d)
    return n


def AP(t, off, dims):
    return bass.AP(t, off, [list(d) for d in dims])


class Kern:
    def __init__(self, phases=(1, 2, 3, 4), debug=False, reorder=True):
        self.reorder = reorder
        self.phases = phases
        self.debug = debug
        self.nc = bass.Bass("TRN2", target_bir_lowering=False)
        self.es = ExitStack()
        self.ges = self.es
        self.semcount = 0
        self.dram = {}
        self.res = {}

    def din(self, name, shape, dt=F32):
        t = self.nc.dram_tensor(name, list(shape), dt, kind="ExternalInput")
        self.dram[name] = t
        return t

    def dscr(self, name, shape, dt):
        kind = "ExternalOutput" if self.debug else "Internal"
        t = self.nc.dram_tensor(name, list(shape), dt, kind=kind)
        self.dram[name] = t
        return t

    def sb(self, name, shape, dt):
        nm = "p%d_%s" % (getattr(self, "nphase", 0), name)
        return self.es.enter_context(self.nc.sbuf_tensor(nm, list(shape), dt))

    def guard(self, kb=8):
        with self.nc.sbuf_tensor("guard%d" % self.nphase, [128, kb * 256], F32):
            pass

    def R(self, name):
        if name not in self.res:
            self.res[name] = Res(name)
        return self.res[name]

    def Rs(self, *names):
        return [self.R(n) for n in names]

    def op(self, eng, fn, r=(), w=(), dma=False, semkey=None, nodep=False, cost=200.0, lat=0.0):
        return self.sc.add(eng, fn, [self.R(x) if isinstance(x, str) else x for x in r],
                           [self.R(x) if isinstance(x, str) else x for x in w], dma=dma, semkey=semkey, nodep=nodep,
                           cost=cost, lat=lat)

    def dma(self, q, out, in_, r=(), w=(), semkey=None, nodep=False):
        e = {"sp": self.nc.sync, "pool": self.nc.gpsimd, "act": self.nc.scalar}[q]
        nbytes = fsz(out) * int(out.shape[0]) * 4
        return self.op(q, lambda: e.dma_start(out=out, in_=in_), r, w, dma=True, semkey=semkey, nodep=nodep,
                       cost=(600.0 if q == "pool" else 100.0), lat=2500.0 + nbytes / 250.0)

    def mm(self, out, lhsT, rhs, start, stop, r=(), w=()):
        nc = self.nc
        return self.op("pe", lambda: nc.tensor.matmul(out, lhsT=lhsT, rhs=rhs, start=start, stop=stop), r, w,
                       cost=(30.0 + 0.37 * fsz(rhs)) if self.cal else (64.0 + 0.5 * fsz(rhs)))

    def tr(self, out, in_, ident, r=(), w=()):
        nc = self.nc
        return self.op("pe", lambda: nc.tensor.transpose(out, in_, ident), r, w, cost=130.0)

    def act(self, out, in_, func, r=(), w=(), **kw):
        nc = self.nc
        return self.op("act", lambda: nc.scalar.activation(out=out, in_=in_, func=func, **kw), r, w,
                       cost=200.0 + 0.85 * fsz(out))

    def ts(self, eng, out, in0, s1, s2, op0, op1=None, r=(), w=(), accum_out=None):
        e = self.nc.vector if eng == "dve" else self.nc.gpsimd
        c = (190.0 if self.cal else 70.0) + 1.05 * fsz(out)
        if op1 is None:
            return self.op(eng, lambda: e.tensor_scalar(out=out, in0=in0, scalar1=s1, scalar2=None, op0=op0), r, w, cost=c)
        if accum_out is not None:
            return self.op(eng, lambda: e.tensor_scalar(out=out, in0=in0, scalar1=s1, scalar2=s2, op0=op0, op1=op1,
                                                        accum_out=accum_out), r, w, cost=c)
        return self.op(eng, lambda: e.tensor_scalar(out=out, in0=in0, scalar1=s1, scalar2=s2, op0=op0, op1=op1), r, w, cost=c)

    def tt(self, eng, out, in0, in1, op, r=(), w=()):
        e = self.nc.vector if eng == "dve" else self.nc.gpsimd
        if self.cal:
            c = (150.0 + 1.5 * fsz(out)) if eng == "dve" else (300.0 + 1.6 * fsz(out))
        else:
            c = (70.0 + 1.05 * fsz(out)) if eng == "dve" else (150.0 + 2.0 * fsz(out))
        return self.op(eng, lambda: e.tensor_tensor(out=out, in0=in0, in1=in1, op=op), r, w, cost=c)

    def stt(self, out, in0, scalar, in1, op0, op1, r=(), w=()):
        nc = self.nc
        return self.op("dve", lambda: nc.vector.scalar_tensor_tensor(out=out, in0=in0, scalar=scalar, in1=in1,
                                                                      op0=op0, op1=op1), r, w,
                       cost=(150.0 + 1.5 * fsz(out)) if self.cal else (70.0 + 1.05 * fsz(out)))

    def cp(self, eng, out, in_, r=(), w=()):
        nc = self.nc
        if eng == "act":
            return self.op("act", lambda: nc.scalar.copy(out=out, in_=in_), r, w, cost=200.0 + 0.85 * fsz(out))
        e = nc.vector if eng == "dve" else nc.gpsimd
        if self.cal:
            c = (150.0 + 1.05 * fsz(out)) if eng == "dve" else (200.0 + 1.1 * fsz(out))
        else:
            c = (70.0 + 1.05 * fsz(out)) if eng == "dve" else (150.0 + 1.1 * fsz(out))
        return self.op(eng, lambda: e.tensor_copy(out=out, in_=in_), r, w, cost=c)

    def memset(self, eng, ap, val, r=(), w=()):
        e = self.nc.vector if eng == "dve" else self.nc.gpsimd
        return self.op(eng, lambda: e.memset(ap, val), r, w, cost=100.0 + 1.0 * fsz(ap))

    def ln_stats(self, src, tag, r, nchunk, width):
        nc = self.nc
        st = self.lnst[tag]
        stats, mv, rs = st
        resn = "lnst_" + tag
        for c in range(nchunk):
            self.op("dve", (lambda c=c: nc.vector.bn_stats(out=stats[:, 6 * c:6 * c + 6],
                                                           in_=src[:, c * width:(c + 1) * width])), r, [resn],
                    cost=100.0 + 1.1 * width)
        self.op("dve", lambda: nc.vector.bn_aggr(out=mv[:, 0:2], in_=stats[:, 0:6 * nchunk]), [resn], [resn])
        self.ts("dve", rs[:, 0:1], mv[:, 1:2], LN_EPS, None, ALU.add, None, [resn], [resn])
        self.act(rs[:, 0:1], rs[:, 0:1], AF.Ln, [resn], [resn])
        self.act(rs[:, 0:1], rs[:, 0:1], AF.Exp, [resn], [resn], scale=-0.5)
        self.ts("dve", rs[:, 1:2], mv[:, 0:1], -1.0, rs[:, 0:1], ALU.mult, ALU.mult, [resn], [resn])
        return rs[:, 0:1], rs[:, 1:2]

    def alloc_lnst(self, tag):
        if not hasattr(self, "lnst"):
            self.lnst = {}
        self.lnst[tag] = (self.sb("lnstats_" + tag, [128, 12], F32), self.sb("lnmv_" + tag, [128, 2], F32),
                          self.sb("lnrs_" + tag, [128, 2], F32))

    def build(self):
        nc = self.nc
        k = self
        x = k.din("x", [S, D])
        p = k.din("p", [S, 256])
        w_in = k.din("w_in", [D, IN_W])
        lng_fm = k.din("lng_fm", [128, 8])
        lnb_fm = k.din("lnb_fm", [128, 8])
        lng = k.din("lng", [1, D])
        lnb = k.din("lnb", [1, D])
        bgate = k.din("bgate", [128, 16])
        kvg = k.din("kvg", [1, 128])
        wukT = k.din("wukT", [128, 4, 128])
        wuvr = k.din("wuvr", [128, 512])
        kig = k.din("kig", [1, 64])
        kib = k.din("kib", [1, 64])
        mcw = k.din("mcw", [128, 4, 3])
        mcb = k.din("mcb", [128, 4])
        wbra = k.din("wbra", [512, D])
        wbrc = k.din("wbrc", [512, D])
        wo = k.din("wo", [D, D])
        ln1g = k.din("ln1g", [1, D])
        ln1b = k.din("ln1b", [1, D])
        wup = k.din("wup", [D, 2 * DFF])
        fcw = k.din("fcw", [128, 44, 3])
        fcb = k.din("fcb", [128, 44])
        wdn = k.din("wdn", [DFF, D])
        wpg = k.din("wpg", [D, D])
        bpg = k.din("bpg", [1, D])
        wple = k.din("wple", [256, D])
        ln2g = k.din("ln2g", [1, D])
        ln2b = k.din("ln2b", [1, D])
        out = nc.dram_tensor("out", [S, D], F32, kind="ExternalOutput")
        k.dram["out"] = out
        attT_d = k.dscr("attT_d", [512, S], BF16)
        mrgT_d = k.dscr("mrgT_d", [D, S], BF16)
        r_d = k.dscr("r_d", [S, D], F32)
        h1T_d = k.dscr("h1T_d", [D, S], BF16)

        ps = k.es.enter_context(nc.psum_tensor("ps", [128, 4096], F32))
        k.ps = ps

        def bank(b, n=512, off=0, parts=128):
            return ps[0:parts, b * 512 + off: b * 512 + off + n]

        def bank_bf(b):
            return ps[:, b * 512:(b + 1) * 512].bitcast(BF16)

        k.bank = bank
        k.bank_bf = bank_bf

        k.bar_tile = k.sb("bar_tile", [128, 8], F32)
        k.bar_bf = k.sb("bar_bf", [128, 8], BF16)
        ident = k.sb("ident", [128, 128], BF16)
        k.ident = ident
        k.nphase = 0
        with ExitStack() as pes:
            k.begin_phase(pes)
            k.memset("pool", ident[:], 0.0, [], ["ident"])
            k.op("pool", lambda: nc.gpsimd.affine_select(out=ident[:], in_=ident[:], pattern=[[-1, 128]],
                                                          compare_op=ALU.not_equal, fill=1.0, base=0,
                                                          channel_multiplier=1), ["ident"], ["ident"])
            k.memset("pool", k.bar_bf[:], 0.0, [], ["bar_bf"])
            k.end_phase()
        if 1 in k.phases:
            with ExitStack() as pes:
                k.begin_phase(pes, cal=False)
                k.phase1(x, w_in, lng_fm, lnb_fm, kvg, wukT, wuvr, kig, kib, attT_d)
                k.end_phase()
        if 2 in k.phases:
            with ExitStack() as pes:
                k.begin_phase(pes)
                k.phase2(x, w_in, lng_fm, lnb_fm, bgate, mcw, mcb, wbra, wbrc, attT_d, mrgT_d)
                k.end_phase()
        if 3 in k.phases:
            with ExitStack() as pes:
                k.begin_phase(pes)
                k.phase3(x, p, lng, lnb, wo, ln1g, ln1b, wpg, bpg, wple, mrgT_d, r_d, h1T_d)
                k.end_phase()
        if 4 in k.phases:
            with ExitStack() as pes:
                k.begin_phase(pes)
                k.phase4(wup, fcw, fcb, wdn, ln2g, ln2b, r_d, h1T_d, out)
                k.end_phase()
        return nc

    def begin_phase(self, pes, cal=True):
        self.cal = cal
        self.es = pes
        self.sc = Sched(self.nc, self.ges, reorder=self.reorder, cal=cal)
        self.res = {}
        self.lnst = {}

    def end_phase(self):
        nc = self.nc
        k = self
        self.sc.emit()
        bar = self.ges.enter_context(nc.semaphore("bar%d" % self.nphase))
        self.nphase += 1
        nc.vector.memset(k.bar_tile[:, 0:1], 0.0).then_inc(bar, 1)
        nc.gpsimd.memset(k.bar_tile[:, 1:2], 0.0).then_inc(bar, 1)
        nc.scalar.copy(out=k.bar_tile[:, 2:3], in_=k.bar_tile[:, 3:4]).then_inc(bar, 1)
        nc.tensor.matmul(k.ps[0:8, 0:8], lhsT=k.bar_bf[:, 0:8], rhs=k.bar_bf[:, 0:8], start=True, stop=True).then_inc(bar, 1)
        nc.sync.nop().then_inc(bar, 1)
        for e in (nc.vector, nc.gpsimd, nc.scalar, nc.tensor, nc.sync):
            e.wait_ge(bar, 5)

    def phase1(self, x, w_in, lng_fm, lnb_fm, kvg, wukT, wuvr, kig, kib, attT_d):
        k = self
        nc = self.nc
        bank, bank_bf, ident, ps = k.bank, k.bank_bf, k.ident, k.ps
        w1 = k.sb("w1", [128, 8, W1C], BF16)
        g_fm = k.sb("g_fm", [128, 8], F32)
        b_fm = k.sb("b_fm", [128, 8], F32)
        wuk_sb = k.sb("wuk_sb", [128, 4, 128], BF16)
        wuv_sb = k.sb("wuv_sb", [128, 512], BF16)
        kvg_bc = k.sb("kvg_bc", [128, 128], F32)
        kig_bc = k.sb("kig_bc", [128, 64], F32)
        kib_bc = k.sb("kib_bc", [128, 64], F32)
        negm = k.sb("negm", [128, 128], F32)
        pow2 = k.sb("pow2", [128, N_BISECT], F32)
        kT2 = k.sb("kT2", [128, S], BF16)
        ckvT = k.sb("ckvT", [128, S], BF16)
        vext = k.sb("vext", [128, 32, 8, 65], BF16)
        xbuf = [k.sb("xbuf%d" % i, [128, D], F32) for i in range(2)]
        xn_bf = k.sb("xn_bf", [128, D], BF16)
        hT2 = [k.sb("hT0", [128, 8, 512], BF16)] * 2
        qattT = k.sb("qattT", [128, 4, 512], BF16)
        qlatT2 = [k.sb("qlatT%d" % i, [128, 8, 512], BF16) for i in range(2)]
        qidxT2 = [k.sb("qidxT%d" % i, [128, 4, 512], BF16) for i in range(2)]
        absw42 = [k.sb("absw4%d" % i, [128, 4, 8], F32) for i in range(2)]
        sgn42 = [k.sb("sgn4%d" % i, [128, 4, 8], F32) for i in range(2)]
        dsgn2 = [k.sb("dsgn%d" % i, [128, 8, 128], BF16) for i in range(2)]
        relu_sb = [k.sb("relu%d" % i, [128, 512], BF16) for i in range(3)]
        score2 = [k.sb("score%d" % i, [128, S], F32) for i in range(2)]
        mask012 = [k.sb("mask01%d" % i, [128, S], BF16) for i in range(2)]
        maskT = k.sb("maskT", [128, 32, 128], BF16)
        PT = [k.sb("PT%d" % i, [128, 4, 128], BF16) for i in range(4)]
        ckv_tm = k.sb("ckv_tm", [128, 128], BF16)
        craw = k.sb("craw", [128, 128], F32)
        craw2 = k.sb("craw2", [128, 128], F32)
        kn_f = k.sb("kn_f", [128, 64], F32)
        kn2 = k.sb("kn2", [128, 128], BF16)
        sm = k.sb("sm", [128, 16], F32)
        wks2 = [k.sb("wks%d" % i, [128, N_BISECT], F32) for i in range(2)]
        bis2 = [k.sb("bis%d" % i, [128, 8], F32) for i in range(2)]
        pv_sb2 = [k.sb("pv_sb0", [65, 1024], F32)] * 2
        rden2 = [k.sb("rden0", [65, 1024], F32)] * 2
        ones_r = k.sb("ones_r", [65, 64], F32)
        att_n = [k.sb("att_n%d" % i, [64, 8, 128], BF16) for i in range(2)]
        k.alloc_lnst("x")
        k.alloc_lnst("k")
        k.guard()

        w_in_v = w_in.rearrange("(k p) f -> p k f", p=128)
        for gi, (c0_, c1_) in enumerate(((C_QATT, C_CKV), (C_CKV, C_QIDX), (C_QIDX, C_KIDX), (C_KIDX, W1C))):
            k.dma("pool", w1[:, :, c0_:c1_], w_in_v[:, :, c0_:c1_], [], ["w1g%d" % gi])
        W1R = []
        k.dma("sp", g_fm[:], lng_fm[:], [], ["g_fm"])
        k.dma("sp", b_fm[:], lnb_fm[:], [], ["b_fm"])
        k.dma("pool", wuk_sb[:], wukT[:], [], ["wuk"])
        k.dma("pool", wuv_sb[:], wuvr[:], [], ["wuv"])
        k.dma("sp", kvg_bc[:], AP(kvg, 0, [[0, 128], [1, 128]]), [], ["kvg_bc"])
        k.dma("sp", kig_bc[:], AP(kig, 0, [[0, 128], [1, 64]]), [], ["kig_bc"])
        k.dma("sp", kib_bc[:], AP(kib, 0, [[0, 128], [1, 64]]), [], ["kib_bc"])
        k.memset("pool", negm[:], 0.0, [], ["negm"])
        k.op("pool", lambda: nc.gpsimd.affine_select(out=negm[:], in_=negm[:], pattern=[[-1, 128]],
                                                      compare_op=ALU.is_ge, fill=NEG, base=0,
                                                      channel_multiplier=1), ["negm"], ["negm"])
        for i in range(N_BISECT):
            k.memset("pool", pow2[:, i:i + 1], 2.0 ** (-(i + 1)), [], ["pow2"])
        k.memset("pool", vext[:, :, :, 64:65], 1.0, [], ["vext_ones"])
        k.memset("pool", ones_r[:], 1.0, [], ["ones_r"])

        BT, BR0, BR1, BSC, BL0, BL1, BV0, BV1 = 0, 1, 2, 3, 4, 5, 6, 7
        ring = [BR0, BR1]
        rstate = {"i": 0, "relu": 0, "pt": 0, "lg": 0, "xb": 0, "an": 0}

        def next_ring():
            b = ring[rstate["i"] % 2]
            rstate["i"] += 1
            return b

        for st in range(8):
            T0 = st * 512
            sp_ = st % 2
            hT, qlatT, qidxT, absw4, sgn4 = hT2[sp_], qlatT2[sp_], qidxT2[sp_], absw42[sp_], sgn42[sp_]
            HT, QL, QI, AW, SG = "hT0", "qlatT%d" % sp_, "qidxT%d" % sp_, "absw4%d" % sp_, "sgn4%d" % sp_
            k.sc.boost = 1.0e9
            for tt in range(4):
                t0 = T0 + tt * 128
                xb_i = rstate["xb"] % 2
                rstate["xb"] += 1
                xb = xbuf[xb_i]
                XR = "xbuf%d" % xb_i
                k.dma("sp", xb[:], x[t0:t0 + 128, :], [], [XR], semkey=XR)
                rstd, nmr = k.ln_stats(xb, "x", [XR], 2, 512)
                k.act(xn_bf[:], xb[:], AF.Identity, ["lnst_x", XR], ["xn_bf"], scale=rstd, bias=nmr)
                tb = bank_bf(BT)
                for kk in range(8):
                    k.tr(tb[:, kk * 128:(kk + 1) * 128], xn_bf[:, kk * 128:(kk + 1) * 128], ident[:],
                         ["xn_bf", "ident"], ["bank0", "bank0b"])
                for kk in range(8):
                    k.act(hT[:, kk, tt * 128:(tt + 1) * 128], tb[:, kk * 128:(kk + 1) * 128], AF.Identity,
                          ["bank0", "bank0b", "g_fm", "b_fm"], [HT], scale=g_fm[:, kk:kk + 1], bias=b_fm[:, kk:kk + 1])
            for j in range(4):
                b = next_ring()
                for kk in range(8):
                    k.mm(bank(b), w1[:, kk, C_QATT + 128 * j:C_QATT + 128 * (j + 1)], hT[:, kk, :], kk == 0, kk == 7,
                         [HT, "w1g0"], ["bank%d" % b])
                k.cp("dve", qattT[:, j, :], bank(b), ["bank%d" % b], ["qattT"])
            for j in range(4):
                b = next_ring()
                for kk in range(8):
                    k.mm(bank(b), w1[:, kk, C_QIDX + 128 * j:C_QIDX + 128 * (j + 1)], hT[:, kk, :], kk == 0, kk == 7,
                         [HT, "w1g2"], ["bank%d" % b])
                k.cp("act", qidxT[:, j, :], bank(b), ["bank%d" % b], [QI])
            for tt in range(4):
                blk = st * 4 + tt
                bck_, bkw_ = next_ring(), next_ring()
                PCK, PKW = "bank%d" % bck_, "bank%d" % bkw_
                pck = bank(bck_, 128, 0)
                pkw = bank(bkw_, 72, 0)
                for kk in range(8):
                    k.mm(pck, hT[:, kk, tt * 128:(tt + 1) * 128], w1[:, kk, C_CKV:C_CKV + 128], kk == 0, kk == 7,
                         [HT, "w1g1"], [PCK])
                for kk in range(8):
                    k.mm(pkw, hT[:, kk, tt * 128:(tt + 1) * 128], w1[:, kk, C_KIDX:C_KIDX + 72], kk == 0, kk == 7,
                         [HT, "w1g3"], [PKW])
                k.cp("act", craw[:], pck, [PCK], ["craw"])
                k.op("dve", lambda: nc.vector.scalar_tensor_tensor(out=craw2[:], in0=craw[:], scalar=1.0, in1=craw[:],
                                                                    op0=ALU.mult, op1=ALU.mult, accum_out=sm[:, 0:1]),
                     ["craw"], ["craw2", "sm_c"])
                k.ts("dve", sm[:, 1:2], sm[:, 0:1], 1.0 / 128.0, LN_EPS, ALU.mult, ALU.add, ["sm_c"], ["sm_c"])
                k.act(sm[:, 2:3], sm[:, 1:2], AF.Ln, ["sm_c"], ["sm_c"])
                k.act(sm[:, 2:3], sm[:, 2:3], AF.Exp, ["sm_c"], ["sm_c"], scale=-0.5)
                k.stt(ckv_tm[:], craw[:], sm[:, 2:3], kvg_bc[:], ALU.mult, ALU.mult, ["craw", "sm_c", "kvg_bc"], ["ckv_tm"])
                rstd_k, nmr_k = k.ln_stats(pkw, "k", [PKW], 1, 64)
                k.act(kn_f[:], pkw[:, 0:64], AF.Identity, [PKW, "lnst_k"], ["kn_f"], scale=rstd_k, bias=nmr_k)
                k.tt("pool", kn_f[:], kn_f[:], kig_bc[:], ALU.mult, ["kn_f", "kig_bc"], ["kn_f"])
                k.tt("pool", kn2[:, 0:64], kn_f[:], kib_bc[:], ALU.add, ["kn_f", "kib_bc"], ["kn2"])
                k.tt("pool", kn2[:, 64:128], kn_f[:], kib_bc[:], ALU.add, ["kn_f", "kib_bc"], ["kn2"])
                k.act(sm[:, 8:16], pkw[:, 64:72], AF.Copy, [PKW], ["sm_w"], scale=CW)
                k.stt(absw4[:, tt, :], sm[:, 8:16], -1.0, sm[:, 8:16], ALU.mult, ALU.max, ["sm_w"], [AW])
                k.act(sgn4[:, tt, :], pkw[:, 64:72], AF.Sign, [PKW], [SG])
                tb = bank_bf(BT)
                k.tr(tb[:, 512:640], ckv_tm[:], ident[:], ["ckv_tm", "ident"], ["bank0b"])
                k.tr(tb[:, 640:768], kn2[:], ident[:], ["kn2", "ident"], ["bank0b"])
                k.cp("act", ckvT[:, blk * 128:(blk + 1) * 128], tb[:, 512:640], ["bank0b"], ["ckvT_%d" % st])
                k.cp("act", kT2[:, blk * 128:(blk + 1) * 128], tb[:, 640:768], ["bank0b"], ["kT2_%d" % st])
                b = next_ring()
                k.mm(bank(b), ckvT[:, blk * 128:(blk + 1) * 128], wuv_sb[:], True, True, ["ckvT_%d" % st, "wuv"], ["bank%d" % b])
                k.cp("dve", vext[:, blk, :, 0:64], bank(b).rearrange("p (h d) -> p h d", h=8), ["bank%d" % b], ["vext_%d" % st])
            for h in range(8):
                e, j = h % 2, h // 2
                b = next_ring()
                k.mm(bank(b), wuk_sb[64 * e:64 * e + 64, j, :], qattT[64 * e:64 * e + 64, j, :], True, True,
                     ["qattT", "wuk"], ["bank%d" % b])
                k.act(qlatT[:, h, :], bank(b), AF.Copy, ["bank%d" % b], [QL], scale=0.125)
            k.sc.boost = 0.0
            for i in range(4):
                I = st * 4 + i
                nk = 128 * (I + 1)
                q0 = i * 128
                ip_ = I % 2
                score, mask01, dsgn, pv_sb, rden = score2[ip_], mask012[ip_], dsgn2[ip_], pv_sb2[ip_], rden2[ip_]
                junk = mask01
                SC, MK, DS, PVS, RD = "score%d" % ip_, "mask01%d" % ip_, "dsgn%d" % ip_, "pv_sb0", "rden0"
                for h in range(8):
                    k.tt("pool", dsgn[:, h, :], ident[:], AP(sgn4, i * 8 + h, [[32, 128], [0, 128]]), ALU.mult,
                         ["ident", SG], [DS])
                nkb = (nk + 511) // 512
                for kb in range(nkb):
                    wk = min(512, nk - 512 * kb)
                    for j in range(4):
                        b0_, b1_ = next_ring(), next_ring()

                        def pair(o0=bank(b0_, wk), o1=bank(b1_, wk), l0=qidxT[0:64, j, q0:q0 + 128],
                                 l1=qidxT[64:128, j, q0:q0 + 128], r0=kT2[0:64, 512 * kb:512 * kb + wk],
                                 r1=kT2[64:128, 512 * kb:512 * kb + wk]):
                            nc.tensor.matmul(o0, lhsT=l0, rhs=r0, start=True, stop=True)
                            return nc.tensor.matmul(o1, lhsT=l1, rhs=r1, start=True, stop=True)
                        k.op("pe", pair, [QI, "kT2_%d" % kb], ["bank%d" % b0_, "bank%d" % b1_], cost=2 * (64.0 + 0.5 * wk))
                        for e, b in ((0, b0_), (1, b1_)):
                            h = 2 * j + e
                            ri = rstate["relu"] % 3
                            rstate["relu"] += 1
                            k.act(relu_sb[ri][:, 0:wk], bank(b, wk), AF.Relu, ["bank%d" % b, AW], ["relu%d" % ri],
                                  scale=absw4[:, i, h:h + 1])
                            k.mm(bank(BSC, wk), dsgn[:, h, :], relu_sb[ri][:, 0:wk], h == 0, h == 7,
                                 [DS, "relu%d" % ri], ["bank%d" % BSC])
                    last = (kb == nkb - 1)
                    ncopy = wk - 128 if last else wk
                    if ncopy > 0:
                        k.cp("act", score[:, 512 * kb:512 * kb + ncopy], bank(BSC, ncopy), ["bank%d" % BSC], [SC])
                    if last:
                        k.tt("dve", score[:, nk - 128:nk], bank(BSC, 128, wk - 128), negm[:], ALU.add,
                             ["bank%d" % BSC, "negm"], [SC])
                if I >= 2:
                    bis, wks = bis2[ip_], wks2[ip_]
                    BI, WK = "bis%d" % ip_, "wks%d" % ip_
                    k.op("dve", lambda nk=nk, sc_=score, b_=bis: nc.vector.tensor_reduce(out=b_[:, 0:1], in_=sc_[:, 0:nk], axis=AX.X,
                                                                                          op=ALU.max), [SC], [BI],
                         cost=100.0 + 1.05 * nk)
                    k.op("dve", lambda sc_=score, b_=bis: nc.vector.tensor_reduce(out=b_[:, 1:2], in_=sc_[:, 0:256], axis=AX.X,
                                                                                   op=ALU.min), [SC], [BI], cost=400.0)
                    k.tt("dve", bis[:, 2:3], bis[:, 0:1], bis[:, 1:2], ALU.subtract, [BI], [BI])
                    k.ts("dve", bis[:, 2:3], bis[:, 2:3], 1.001, 1e-6, ALU.mult, ALU.add, [BI], [BI])
                    k.ts("dve", wks[:], pow2[:], bis[:, 2:3], None, ALU.mult, None, [BI, "pow2"], [WK])
                    k.tt("dve", bis[:, 3:4], bis[:, 1:2], wks[:, 0:1], ALU.add, [BI, WK], [BI])
                    for it in range(N_BISECT):
                        k.ts("dve", junk[:, 0:nk], score[:, 0:nk], bis[:, 3:4], 0.0, ALU.is_ge, ALU.add,
                             [SC, BI], [MK, BI], accum_out=bis[:, 4:5])
                        k.stt(bis[:, 5:6], bis[:, 4:5], float(TOPK) - 0.5, wks[:, it:it + 1], ALU.is_ge, ALU.mult, [BI, WK], [BI])
                        nxt = it + 1 if it + 1 < N_BISECT else it
                        k.stt(bis[:, 3:4], bis[:, 5:6], bis[:, 3:4], wks[:, nxt:nxt + 1], ALU.add, ALU.subtract, [BI, WK], [BI])
                    k.ts("dve", mask01[:, 0:nk], score[:, 0:nk], bis[:, 3:4], None, ALU.is_ge, None,
                         [SC, BI], [MK])
                else:
                    k.ts("dve", mask01[:, 0:nk], score[:, 0:nk], -1.0e29, None, ALU.is_ge, None, [SC], [MK])
                tb = bank_bf(BT)
                for g0 in range(0, I + 1, 8):
                    g1 = min(I + 1, g0 + 8)
                    for jb in range(g0, g1):
                        k.tr(tb[:, (jb - g0) * 128:(jb - g0 + 1) * 128], mask01[:, jb * 128:(jb + 1) * 128], ident[:],
                             [MK, "ident"], ["bank0", "bank0b", "maskT"])
                    k.cp("act", maskT[:, g0:g1, :], tb[:, 0:(g1 - g0) * 128].rearrange("p (a b) -> p a b", b=128),
                         ["bank0", "bank0b"], ["maskT"])
                pv = ps[0:65, BV0 * 512:BV0 * 512 + 1024].rearrange("p (h q) -> p h q", h=8)
                k.op("dve", lambda: nc.vector.memset(ps[0:65, BV0 * 512:BV0 * 512 + 1024], 0.0), [], ["bankpv"])
                for jb in range(I + 1):
                    for g in range(2):
                        lb = [BL0, BL1][rstate["lg"] % 2]
                        rstate["lg"] += 1
                        k.mm(bank(lb), ckvT[:, jb * 128:(jb + 1) * 128], qlatT[:, 4 * g:4 * g + 4, q0:q0 + 128],
                             True, True, ["ckvT_%d" % (jb // 4), QL], ["bank%d" % lb])
                        pi = rstate["pt"] % 4
                        rstate["pt"] += 1
                        k.act(PT[pi][:], bank(lb).rearrange("p (h q) -> p h q", h=4), AF.Exp, ["bank%d" % lb],
                              ["PT%d" % pi])
                        k.tt("pool", PT[pi][:], PT[pi][:], AP(maskT, jb * 128, [[32 * 128, 128], [0, 4], [1, 128]]),
                             ALU.mult, ["PT%d" % pi, "maskT"], ["PT%d" % pi])
                        for hh in range(4):
                            h = 4 * g + hh
                            k.op("pe", (lambda o=pv[:, h, :], l=vext[:, jb, h, :], rr=PT[pi][:, hh, :], sp_=(jb == I):
                                        nc.tensor.matmul(o, lhsT=l, rhs=rr, start=False, stop=sp_,
                                                         skip_group_check=True)),
                                 ["vext_%d" % (jb // 4), "vext_ones", "PT%d" % pi], ["bankpv"])
                k.cp("act", pv_sb[:], ps[0:65, BV0 * 512:BV0 * 512 + 1024], ["bankpv"], [PVS])
                k.act(rden[64:65, :], pv_sb[64:65, :], AF.Ln, [PVS], [RD])
                k.act(rden[64:65, :], rden[64:65, :], AF.Exp, [RD], [RD], scale=-1.0)
                ai = rstate["an"] % 2
                rstate["an"] += 1
                for g in range(2):
                    lb = [BL0, BL1][rstate["lg"] % 2]
                    rstate["lg"] += 1
                    k.mm(bank(lb, 512, 0, 64), ones_r[64:65, :], rden[64:65, g * 512:(g + 1) * 512], True, True,
                         ["ones_r", RD], ["bank%d" % lb])
                    k.tt("dve", att_n[ai][:, 4 * g:4 * g + 4, :],
                         pv_sb[0:64, g * 512:(g + 1) * 512].rearrange("p (h q) -> p h q", h=4),
                         bank(lb, 512, 0, 64).rearrange("p (h q) -> p h q", h=4), ALU.mult,
                         [PVS, "bank%d" % lb], ["att_n%d" % ai])
                tok0 = T0 + q0
                k.dma("sp", AP(attT_d, tok0, [[S, 64], [64 * S, 8], [1, 128]]), att_n[ai][:],
                      ["att_n%d" % ai], ["attT_d"], semkey="att_n%d" % ai)
        k.final_wait(["att_n0", "att_n1"])

    def ln_hT(self, x, t0, tt, xbuf, rstate, xn_bf, hT, g_fm, b_fm, BT=0, HT="hT"):
        k = self
        nc = self.nc
        xb_i = rstate["xb"] % 2
        rstate["xb"] += 1
        xb = xbuf[xb_i]
        XR = "xbuf%d" % xb_i
        k.dma("sp", xb[:], x[t0:t0 + 128, :], [], [XR], semkey=XR)
        rstd, nmr = k.ln_stats(xb, "x", [XR], 2, 512)
        k.act(xn_bf[:], xb[:], AF.Identity, ["lnst_x", XR], ["xn_bf"], scale=rstd, bias=nmr)
        tb = k.bank_bf(BT)
        for kk in range(8):
            k.tr(tb[:, kk * 128:(kk + 1) * 128], xn_bf[:, kk * 128:(kk + 1) * 128], k.ident[:],
                 ["xn_bf", "ident"], ["bank0", "bank0b"])
        for kk in range(8):
            k.act(hT[:, kk, tt * 128:(tt + 1) * 128], tb[:, kk * 128:(kk + 1) * 128], AF.Identity,
                  ["bank0", "bank0b", "g_fm", "b_fm"], [HT], scale=g_fm[:, kk:kk + 1], bias=b_fm[:, kk:kk + 1])

    def phase2(self, x, w_in, lng_fm, lnb_fm, bgate, mcw, mcb, wbra, wbrc, attT_d, mrgT_d):
        k = self
        nc = self.nc
        bank, bank_bf, ident, ps = k.bank, k.bank_bf, k.ident, k.ps
        w2 = k.sb("w2", [128, 8, W2C], BF16)
        wa = k.sb("wa", [128, 4, D], BF16)
        wc = k.sb("wc", [128, 4, D], BF16)
        g_fm = k.sb("g_fm", [128, 8], F32)
        b_fm = k.sb("b_fm", [128, 8], F32)
        hb = k.sb("hb", [128, 16], F32)
        mcw_sb = k.sb("mcw_sb", [128, 4, 3], F32)
        mcb_sb = k.sb("mcb_sb", [128, 4], F32)
        xbuf = [k.sb("xbuf%d" % i, [128, D], F32) for i in range(2)]
        xn_bf = k.sb("xn_bf", [128, D], BF16)
        hT2 = [k.sb("hT%d" % i, [128, 8, 512], BF16) for i in range(2)]
        att_in = k.sb("att_in", [128, 4, 512], BF16)
        u = k.sb("u", [128, 4, 514], F32)
        tmpc = k.sb("tmpc", [128, 512], F32)
        a_sb = k.sb("a_sb", [128, 512], F32)
        cyT = k.sb("cyT", [128, 4, 512], BF16)
        ta = k.sb("ta", [128, 512], F32)
        tc2 = k.sb("tc2", [128, 512], F32)
        m1 = k.sb("m1", [128, 512], F32)
        m2 = k.sb("m2", [128, 512], F32)
        mrg = [k.sb("mrg%d" % i, [128, 8, 512], BF16) for i in range(2)]
        k.alloc_lnst("x")
        k.guard()
        w_in_v = w_in.rearrange("(k p) f -> p k f", p=128)
        for gi in (1, 2, 0, 3, 5, 4, 6):
            k.dma("pool", w2[:, :, 512 * gi:512 * (gi + 1)], w_in_v[:, :, W1C + 512 * gi:W1C + 512 * (gi + 1)], [], ["w2g%d" % gi])
        W2R = []
        k.dma("pool", wa[:], wbra.rearrange("(k p) f -> p k f", p=128), [], ["wa"])
        k.dma("pool", wc[:], wbrc.rearrange("(k p) f -> p k f", p=128), [], ["wc"])
        k.dma("sp", g_fm[:], lng_fm[:], [], ["g_fm"])
        k.dma("sp", b_fm[:], lnb_fm[:], [], ["b_fm"])
        k.dma("sp", hb[:], bgate[:], [], ["hb"])
        k.dma("sp", mcw_sb[:], mcw[:], [], ["mcw"])
        k.dma("sp", mcb_sb[:], mcb[:], [], ["mcb"])
        k.ts("dve", hb[:], hb[:], 0.5, None, ALU.mult, None, ["hb"], ["hb"])
        k.memset("pool", u[:], 0.0, [], ["u"])
        rstate = {"xb": 0, "ring": 0, "mrg": 0}
        ringb = [1, 2, 3, 4, 5, 6, 7]

        def nb():
            b = ringb[rstate["ring"] % 7]
            rstate["ring"] += 1
            return b

        cur = {}

        def proj(col0, b):
            c0 = col0 - W1C
            hT, HT = cur["hT"], cur["HT"]
            for kk in range(8):
                k.mm(bank(b), w2[:, kk, c0:c0 + 128], hT[:, kk, :], kk == 0, kk == 7, [HT, "w2g%d" % (c0 // 512)], ["bank%d" % b])

        for st in range(8):
            T0 = st * 512
            cur["hT"], cur["HT"] = hT2[st % 2], "hT%d" % (st % 2)
            for tt in range(4):
                k.ln_hT(x, T0 + tt * 128, tt, xbuf, rstate, xn_bf, cur["hT"], g_fm, b_fm, HT=cur["HT"])
            k.dma("sp", att_in[:], AP(attT_d, T0, [[S, 128], [128 * S, 4], [1, 512]]), [], ["att_in"], semkey="att_in")
            for j in range(4):
                bc_, bx_, bb_ = nb(), nb(), nb()
                proj(C_CVC + 128 * j, bc_)
                proj(C_CVX + 128 * j, bx_)
                proj(C_CVB + 128 * j, bb_)
                if st > 0:
                    k.cp("pool", u[:, j, 0:2], u[:, j, 512:514], ["u"], ["u"])
                k.cp("act", tmpc[:], bank(bc_), ["bank%d" % bc_], ["tmpc"])
                k.tt("dve", u[:, j, 2:514], tmpc[:], bank(bx_), ALU.mult, ["tmpc", "bank%d" % bx_], ["u"])
                k.act(a_sb[:], u[:, j, 2:514], AF.Identity, ["u", "mcw", "mcb"], ["a_sb"],
                      scale=mcw_sb[:, j, 2:3], bias=mcb_sb[:, j:j + 1])
                k.stt(a_sb[:], u[:, j, 1:513], mcw_sb[:, j, 1:2], a_sb[:], ALU.mult, ALU.add, ["u", "a_sb", "mcw"], ["a_sb"])
                k.stt(a_sb[:], u[:, j, 0:512], mcw_sb[:, j, 0:1], a_sb[:], ALU.mult, ALU.add, ["u", "a_sb", "mcw"], ["a_sb"])
                k.tt("dve", cyT[:, j, :], a_sb[:], bank(bb_), ALU.mult, ["a_sb", "bank%d" % bb_], ["cyT"])
            mi = rstate["mrg"] % 2
            rstate["mrg"] += 1
            for c in range(8):
                bga, bgc, bra, brc = nb(), nb(), nb(), nb()
                proj(C_GATT + 128 * c, bga)
                proj(C_GCONV + 128 * c, bgc)
                for kk in range(4):
                    k.mm(bank(bra), wa[:, kk, 128 * c:128 * (c + 1)], att_in[:, kk, :], kk == 0, kk == 3,
                         ["wa", "att_in"], ["bank%d" % bra])
                for kk in range(4):
                    k.mm(bank(brc), wc[:, kk, 128 * c:128 * (c + 1)], cyT[:, kk, :], kk == 0, kk == 3,
                         ["wc", "cyT"], ["bank%d" % brc])
                k.act(ta[:], bank(bga), AF.Tanh, ["bank%d" % bga, "hb"], ["ta"], scale=0.5, bias=hb[:, c:c + 1])
                k.act(tc2[:], bank(bgc), AF.Tanh, ["bank%d" % bgc, "hb"], ["tc2"], scale=0.5, bias=hb[:, 8 + c:9 + c])
                k.stt(m1[:], ta[:], 1.0, bank(bra), ALU.add, ALU.mult, ["ta", "bank%d" % bra], ["m1"])
                k.stt(m2[:], tc2[:], 1.0, bank(brc), ALU.add, ALU.mult, ["tc2", "bank%d" % brc], ["m2"])
                k.tt("pool", mrg[mi][:, c, :], m1[:], m2[:], ALU.add, ["m1", "m2"], ["mrg%d" % mi])
            k.dma("sp", AP(mrgT_d, T0, [[S, 128], [128 * S, 8], [1, 512]]), mrg[mi][:], ["mrg%d" % mi], ["mrgT_d"],
                  semkey="mrg%d" % mi)
        k.final_wait(["mrg0", "mrg1"])

    def phase3(self, x, p, lng, lnb, wo, ln1g, ln1b, wpg, bpg, wple, mrgT_d, r_d, h1T_d):
        k = self
        nc = self.nc
        bank, bank_bf, ident, ps = k.bank, k.bank_bf, k.ident, k.ps
        wo_sb = k.sb("wo_sb", [128, 8, D], BF16)
        wpg_sb = k.sb("wpg_sb", [128, 8, D], BF16)
        wpl_sb = k.sb("wpl_sb", [128, 2, D], BF16)
        Ga = k.sb("Ga", [128, D], F32)
        Ba = k.sb("Ba", [128, D], F32)
        G1 = k.sb("G1", [128, D], F32)
        B1 = k.sb("B1", [128, D], F32)
        HB = k.sb("HB", [128, D], F32)
        xbuf = [k.sb("xbuf%d" % i, [128, D], F32) for i in range(2)]
        pbuf = [k.sb("pbuf%d" % i, [128, 256], F32) for i in range(2)]
        m_in = [k.sb("m_in%d" % i, [128, 8, 128], BF16) for i in range(2)]
        hA_2 = [k.sb("hA%d" % i, [128, D], F32) for i in range(2)]
        y_2 = [k.sb("y%d" % i, [128, D], F32) for i in range(2)]
        h1_2 = [k.sb("h1%d" % i, [128, D], F32) for i in range(2)]
        h1_bf_2 = [k.sb("h1_bf%d" % i, [128, D], BF16) for i in range(2)]
        h1T = [k.sb("h1T%d" % i, [128, 8, 128], BF16) for i in range(2)]
        p_bf_2 = [k.sb("p_bf%d" % i, [128, 256], BF16) for i in range(2)]
        pT_2 = [k.sb("pT%d" % i, [128, 2, 128], BF16) for i in range(2)]
        tg_2 = [k.sb("tg%d" % i, [128, D], F32) for i in range(2)]
        pl2_2 = [k.sb("pl2%d" % i, [128, D], F32) for i in range(2)]
        r2 = [k.sb("r2_%d" % i, [128, D], F32) for i in range(2)]
        k.alloc_lnst("x")
        k.alloc_lnst("y")
        k.guard()
        for n_ in range(2):
            k.dma("pool", wo_sb[:, :, 512 * n_:512 * (n_ + 1)], wo.rearrange("(k p) f -> p k f", p=128)[:, :, 512 * n_:512 * (n_ + 1)],
                  [], ["wo%d" % n_])
        for n_ in range(2):
            k.dma("pool", wpg_sb[:, :, 512 * n_:512 * (n_ + 1)], wpg.rearrange("(k p) f -> p k f", p=128)[:, :, 512 * n_:512 * (n_ + 1)],
                  [], ["wpg%d" % n_])
        k.dma("pool", wpl_sb[:], wple.rearrange("(k p) f -> p k f", p=128), [], ["wpl"])
        for t_, src, nm in ((Ga, lng, "Ga"), (Ba, lnb, "Ba"), (G1, ln1g, "G1"), (B1, ln1b, "B1"), (HB, bpg, "HB")):
            k.dma("sp", t_[:], AP(src, 0, [[0, 128], [1, D]]), [], [nm])
        k.ts("dve", Ga[:], Ga[:], ALPHA, None, ALU.mult, None, ["Ga"], ["Ga"])
        k.ts("dve", Ba[:], Ba[:], ALPHA, None, ALU.mult, None, ["Ba"], ["Ba"])
        k.ts("dve", HB[:], HB[:], 0.5, None, ALU.mult, None, ["HB"], ["HB"])
        for t in range(32):
            t0 = t * 128
            bi = t % 2
            XR, PR, MR = "xbuf%d" % bi, "pbuf%d" % bi, "m_in%d" % bi
            hA, y, h1, h1_bf, p_bf, pT, tg, pl2 = hA_2[bi], y_2[bi], h1_2[bi], h1_bf_2[bi], p_bf_2[bi], pT_2[bi], tg_2[bi], pl2_2[bi]
            R_hA, R_y, R_h1, R_h1bf, R_pbf, R_pT, R_tg, R_pl2 = ["%s%d" % (n_, bi) for n_ in ("hA", "y", "h1", "h1_bf", "p_bf", "pT", "tg", "pl2")]
            k.dma("sp", xbuf[bi][:], x[t0:t0 + 128, :], [], [XR], semkey=XR)
            k.dma("sp", pbuf[bi][:], p[t0:t0 + 128, :], [], [PR], semkey=PR)
            k.dma("sp", m_in[bi][:], AP(mrgT_d, t0, [[S, 128], [128 * S, 8], [1, 128]]), [], [MR], semkey=MR)
            rstd, nmr = k.ln_stats(xbuf[bi], "x", [XR], 2, 512)
            k.act(hA[:], xbuf[bi][:], AF.Identity, ["lnst_x", XR], [R_hA], scale=rstd, bias=nmr)
            k.tt("pool", hA[:], hA[:], Ga[:], ALU.mult, [R_hA, "Ga"], [R_hA])
            k.tt("pool", hA[:], hA[:], Ba[:], ALU.add, [R_hA, "Ba"], [R_hA])
            for n in range(2):
                b = 1 + n
                for kk in range(8):
                    k.mm(bank(b), m_in[bi][:, kk, :], wo_sb[:, kk, 512 * n:512 * (n + 1)], kk == 0, kk == 7,
                         [MR, "wo%d" % n], ["bank%d" % b])
                k.stt(y[:, 512 * n:512 * (n + 1)], bank(b), 0.5, hA[:, 512 * n:512 * (n + 1)], ALU.mult, ALU.add,
                      ["bank%d" % b, R_hA], [R_y])
            rstd1, nmr1 = k.ln_stats(y, "y", [R_y], 2, 512)
            k.act(h1[:], y[:], AF.Identity, ["lnst_y", R_y], [R_h1], scale=rstd1, bias=nmr1)
            k.tt("pool", h1[:], h1[:], G1[:], ALU.mult, [R_h1, "G1"], [R_h1])
            k.tt("pool", h1[:], h1[:], B1[:], ALU.add, [R_h1, "B1"], [R_h1])
            k.cp("pool", h1_bf[:], h1[:], [R_h1], [R_h1bf])
            tb = bank_bf(0)
            for kk in range(8):
                k.tr(tb[:, kk * 128:(kk + 1) * 128], h1_bf[:, kk * 128:(kk + 1) * 128], ident[:], [R_h1bf, "ident"], ["bank0"])
            HR = "h1T%d" % bi
            k.cp("act", h1T[bi][:], tb[:, 0:1024].rearrange("p (a b) -> p a b", b=128), ["bank0"], [HR])
            k.dma("sp", AP(h1T_d, t0, [[S, 128], [128 * S, 8], [1, 128]]), h1T[bi][:], [HR], ["h1T_d"], semkey=HR)
            for n in range(2):
                b = 3 + n
                for kk in range(8):
                    k.mm(bank(b), h1T[bi][:, kk, :], wpg_sb[:, kk, 512 * n:512 * (n + 1)], kk == 0, kk == 7,
                         [HR, "wpg%d" % n], ["bank%d" % b])
                k.stt(tg[:, 512 * n:512 * (n + 1)], bank(b), 0.5, HB[:, 512 * n:512 * (n + 1)], ALU.mult, ALU.add,
                      ["bank%d" % b, "HB"], [R_tg])
            k.act(tg[:], tg[:], AF.Tanh, [R_tg], [R_tg])
            k.cp("pool", p_bf[:], pbuf[bi][:], [PR], [R_pbf])
            tb2 = bank_bf(7)
            for kk in range(2):
                k.tr(tb2[:, kk * 128:(kk + 1) * 128], p_bf[:, kk * 128:(kk + 1) * 128], ident[:], [R_pbf, "ident"], ["bank7"])
            k.cp("act", pT[:], tb2[:, 0:256].rearrange("p (a b) -> p a b", b=128), ["bank7"], [R_pT])
            RR = "r2_%d" % bi
            for n in range(2):
                b = 5 + n
                for kk in range(2):
                    k.mm(bank(b), pT[:, kk, :], wpl_sb[:, kk, 512 * n:512 * (n + 1)], kk == 0, kk == 1,
                         [R_pT, "wpl"], ["bank%d" % b])
                k.stt(pl2[:, 512 * n:512 * (n + 1)], tg[:, 512 * n:512 * (n + 1)], 1.0, bank(b), ALU.add, ALU.mult,
                      [R_tg, "bank%d" % b], [R_pl2])
            k.stt(r2[bi][:], h1[:], 2.0 * ALPHA, pl2[:], ALU.mult, ALU.add, [R_h1, R_pl2], [RR])
            k.dma("sp", r_d[t0:t0 + 128, :], r2[bi][:], [RR], ["r_d"], semkey=RR)
        k.final_wait(["r2_0", "r2_1", "h1T0", "h1T1"])

    def phase4(self, wup, fcw, fcb, wdn, ln2g, ln2b, r_d, h1T_d, out):
        k = self
        nc = self.nc
        bank, bank_bf, ident, ps = k.bank, k.bank_bf, k.ident, k.ps
        NT = 256
        wup_sb = k.sb("wup_sb", [128, 8, 2 * DFF], BF16)
        wdn_sb = k.sb("wdn_sb", [128, 22, D], BF16)
        fcw_sb = k.sb("fcw_sb", [128, 44, 3], F32)
        fcb_sb = k.sb("fcb_sb", [128, 44], F32)
        G2 = k.sb("G2", [128, D], F32)
        B2 = k.sb("B2", [128, D], F32)
        h1T2 = [k.sb("h1T%d" % i, [128, 8, NT], BF16) for i in range(2)]
        actT2 = [k.sb("actT%d" % i, [128, 22, NT], BF16) for i in range(2)]
        abuf = [[k.sb("abuf%d_%d" % (h_, i), [128, NT], F32) for i in range(4)] for h_ in range(2)]
        gbuf = [[k.sb("gbuf%d_%d" % (h_, i), [128, NT + 2], F32) for i in range(4)] for h_ in range(2)]
        sgb = [k.sb("sg%d" % i, [128, NT], F32) for i in range(3)]
        halo = k.sb("halo", [128, 44, 2], F32)
        r2b = [k.sb("r2_%d" % i, [128, D], F32) for i in range(2)]
        k.alloc_lnst("y")
        k.guard()
        wup_v = wup.rearrange("(k p) f -> p k f", p=128)
        for g_ in range(6):
            for half_ in range(2):
                c0_ = half_ * DFF + 512 * g_
                c1_ = min(c0_ + 512, half_ * DFF + DFF)
                k.dma("pool", wup_sb[:, :, c0_:c1_], wup_v[:, :, c0_:c1_], [], ["wupg%d_%d" % (g_, half_)])
        WUR = []
        for c in range(22):
            k.dma("pool", wdn_sb[:, c, :], wdn[c * 128:(c + 1) * 128, :], [], ["wdn"], semkey="wdn", nodep=True)
        WDR = ["wdn"]
        k.dma("sp", fcw_sb[:], fcw[:], [], ["fcw"])
        k.dma("sp", fcb_sb[:], fcb[:], [], ["fcb"])
        k.dma("sp", G2[:], AP(ln2g, 0, [[0, 128], [1, D]]), [], ["G2"])
        k.dma("sp", B2[:], AP(ln2b, 0, [[0, 128], [1, D]]), [], ["B2"])
        k.memset("pool", halo[:], 0.0, [], ["halo%d" % ch for ch in range(44)])
        cnt = {"bank": 0, "ab0": 0, "ab1": 0, "sg": 0, "tile": 0}

        for st in range(S // NT):
            T0 = st * NT
            hb = st % 2
            h1T, aT = h1T2[hb], actT2[hb]
            HR, ATR = "h1T%d" % hb, "actT%d" % hb
            k.dma("sp", h1T[:], AP(h1T_d, T0, [[S, 128], [128 * S, 8], [1, NT]]), [], [HR], semkey=HR)
            for c in range(22):
                cur = []
                for half in range(2):
                    ch = c + 22 * half
                    b = cnt["bank"] % 4
                    cnt["bank"] += 1
                    BR = "bank%d" % b
                    for kk in range(8):
                        k.mm(bank(b, NT), wup_sb[:, kk, 128 * ch:128 * (ch + 1)], h1T[:, kk, :], kk == 0, kk == 7,
                             [HR, "wupg%d_%d" % (c // 4, half)], [BR])
                    ai = cnt["ab%d" % half] % 4
                    cnt["ab%d" % half] += 1
                    ab, gb = abuf[half][ai], gbuf[half][ai]
                    AR, GR = "abuf%d_%d" % (half, ai), "gbuf%d_%d" % (half, ai)
                    HL = "halo%d" % ch
                    pb = bank(b, NT)
                    k.cp("pool", gb[:, 0:2], halo[:, ch, :], [HL], [GR])
                    k.cp("act", gb[:, 2:NT + 2], pb, [BR], [GR])
                    k.act(ab[:], pb, AF.Identity, [BR, "fcw", "fcb"], [AR], scale=fcw_sb[:, ch, 2:3], bias=fcb_sb[:, ch:ch + 1])
                    k.cp("pool", halo[:, ch, :], gb[:, NT:NT + 2], [GR], [HL])
                    k.stt(ab[:], gb[:, 1:NT + 1], fcw_sb[:, ch, 1:2], ab[:], ALU.mult, ALU.add, [GR, AR, "fcw"], [AR])
                    k.stt(ab[:], gb[:, 0:NT], fcw_sb[:, ch, 0:1], ab[:], ALU.mult, ALU.add, [GR, AR, "fcw"], [AR])
                    cur.append((ab, AR))
                si = cnt["sg"] % 3
                cnt["sg"] += 1
                SGR = "sg%d" % si
                k.act(sgb[si][:], cur[0][0][:], AF.Silu, [cur[0][1]], [SGR])
                k.tt("pool", aT[:, c, :], sgb[si][:], cur[1][0][:], ALU.mult, [SGR, cur[1][1]], [ATR])
            for tt in range(NT // 128):
                t0 = T0 + tt * 128
                ti = cnt["tile"] % 2
                cnt["tile"] += 1
                RR, YR = "r2_%d" % ti, "r2_%d" % ti
                r2, yv = r2b[ti], r2b[ti]
                k.dma("sp", r2[:], r_d[t0:t0 + 128, :], [], [RR], semkey=RR)
                for n in range(2):
                    b = 4 + 2 * ti + n
                    for c in range(22):
                        k.mm(bank(b), aT[:, c, tt * 128:(tt + 1) * 128], wdn_sb[:, c, 512 * n:512 * (n + 1)],
                             c == 0, c == 21, [ATR] + WDR, ["bank%d" % b])
                    k.stt(yv[:, 512 * n:512 * (n + 1)], r2[:, 512 * n:512 * (n + 1)], 0.5, bank(b), ALU.mult, ALU.add,
                          [RR, "bank%d" % b], [YR])
                rstd, nmr = k.ln_stats(yv, "y", [YR], 2, 512)
                k.act(yv[:], yv[:], AF.Identity, ["lnst_y", YR], [YR], scale=rstd, bias=nmr)
                k.tt("pool", yv[:], yv[:], G2[:], ALU.mult, [YR, "G2"], [YR])
                k.tt("pool", yv[:], yv[:], B2[:], ALU.add, [YR, "B2"], [YR])
                k.dma("sp", out[t0:t0 + 128, :], yv[:], [YR], ["out_d"], semkey=YR)
        k.final_wait(["r2_0", "r2_1"])

    def final_wait(self, names):
        nc = self.nc
        self.op("sp", lambda: nc.sync.nop(), names, names)


def _prep_inputs(inputs, b):
    f = lambda a: np.ascontiguousarray(np.asarray(a, dtype=np.float32))
    m = {}
    m["x"] = f(inputs["x"][b])
    m["p"] = f(inputs["p"][0, b])
    m["w_in"] = f(inputs["w_in"][0])
    m["lng_fm"] = f(np.asarray(inputs["ln_emb_g"]).reshape(8, 128).T)
    m["lnb_fm"] = f(np.asarray(inputs["ln_emb_b"]).reshape(8, 128).T)
    m["lng"] = f(np.asarray(inputs["ln_emb_g"]).reshape(1, D))
    m["lnb"] = f(np.asarray(inputs["ln_emb_b"]).reshape(1, D))
    m["bgate"] = f(np.asarray(inputs["b_gate"][0]).reshape(2, 8, 128).transpose(2, 0, 1).reshape(128, 16))
    m["kvg"] = f(np.asarray(inputs["kv_norm_g"][0]).reshape(1, 128))
    wuk = np.asarray(inputs["w_uk"][0])
    m["wukT"] = f(wuk.reshape(4, 2, 128, 64).transpose(1, 3, 0, 2).reshape(128, 4, 128))
    wuv = np.asarray(inputs["w_uv"][0])
    m["wuvr"] = f(wuv.transpose(1, 0, 2).reshape(128, 512))
    m["kig"] = f(np.asarray(inputs["k_idx_ln_g"][0]).reshape(1, 64))
    m["kib"] = f(np.asarray(inputs["k_idx_ln_b"][0]).reshape(1, 64))
    m["mcw"] = f(np.asarray(inputs["mix_conv_w"][0]).reshape(3, 4, 128).transpose(2, 1, 0))
    m["mcb"] = f(np.asarray(inputs["mix_conv_b"][0]).reshape(4, 128).T)
    m["wbra"] = f(inputs["w_br_att"][0])
    m["wbrc"] = f(inputs["w_br_conv"][0])
    m["wo"] = f(inputs["w_o"][0])
    m["ln1g"] = f(np.asarray(inputs["ln1_g"][0]).reshape(1, D))
    m["ln1b"] = f(np.asarray(inputs["ln1_b"][0]).reshape(1, D))
    m["wup"] = f(inputs["w_ffn_up"][0])
    m["fcw"] = f(np.asarray(inputs["ffn_conv_w"][0]).reshape(3, 44, 128).transpose(2, 1, 0))
    m["fcb"] = f(np.asarray(inputs["ffn_conv_b"][0]).reshape(44, 128).T)
    m["wdn"] = f(inputs["w_ffn_down"][0])
    m["wpg"] = f(inputs["w_ple_gate"][0])
    m["bpg"] = f(np.asarray(inputs["b_ple_gate"][0]).reshape(1, D))
    m["wple"] = f(inputs["w_ple"][0])
    m["ln2g"] = f(np.asarray(inputs["ln2_g"][0]).reshape(1, D))
    m["ln2b"] = f(np.asarray(inputs["ln2_b"][0]).reshape(1, D))
    return m


def kernel(**inputs):
    kern = Kern()
    nc = kern.build()
    in_maps = [_prep_inputs(inputs, b) for b in range(NCORES)]
    res = run_bass_kernel_spmd(nc, in_maps, core_ids=list(range(NCORES)))
    return np.stack([np.asarray(r["out"], dtype=np.float32) for r in res.results], axis=0)
```
